# Optimizing a Trainium2 kernel written in Bass

```python
import math
import jax, jax.numpy as jnp
from jax import lax
import numpy as np


D_MODEL = 1024
BATCH = 4
SEQ = 4096
DEPTH = 2

D_FF = 2816
MLSTM_HEADS = 4
MLSTM_HEAD_DIM = 64
MLSTM_WIDTH = MLSTM_HEADS * MLSTM_HEAD_DIM
MLSTM_CHUNK = 64
SSM_HEADS = 8
SSM_HEAD_DIM = 64
SSM_WIDTH = SSM_HEADS * SSM_HEAD_DIM
SSM_STATE = 128
SSM_GROUPS = 2
SSM_CONV = 4
SSM_CONV_DIM = SSM_WIDTH + 2 * SSM_GROUPS * SSM_STATE
SSM_CHUNK = 128
DIFF_HEADS = 4
DIFF_QK_DIM = 32
DIFF_V_DIM = 64
DIFF_WIDTH = DIFF_HEADS * DIFF_V_DIM
Q_BLOCK = 128
REL_BUCKETS = 32
REL_MAX_DIST = 128
D_MIX = MLSTM_WIDTH + SSM_WIDTH + DIFF_WIDTH
IN_SPLIT_SIZES = (MLSTM_WIDTH, MLSTM_WIDTH, MLSTM_WIDTH, MLSTM_WIDTH, MLSTM_HEADS, MLSTM_HEADS,
                  SSM_WIDTH, SSM_CONV_DIM, SSM_HEADS,
                  2 * DIFF_HEADS * DIFF_QK_DIM, 2 * DIFF_HEADS * DIFF_QK_DIM, DIFF_WIDTH)
D_IN = sum(IN_SPLIT_SIZES)
NORM_EPS = 1e-6

kernel_name = 'hymba_style_mlstm_ssd_diffattn_macaron'


def rms_norm(x, w):
    xf = x.astype(jnp.float32)
    y = xf * lax.rsqrt(jnp.mean(xf * xf, axis=-1, keepdims=True) + NORM_EPS)
    return (y * w.astype(jnp.float32)).astype(x.dtype)


def swiglu_ffn(x, w_gate, w_up, w_down):
    return (jax.nn.silu(x @ w_gate) * (x @ w_up)) @ w_down


def t5_bucket(rel):
    n = jnp.maximum(rel, 0)
    max_exact = REL_BUCKETS // 2
    nf = jnp.maximum(n, 1).astype(jnp.float32)
    large = max_exact + (jnp.log(nf / max_exact) / math.log(REL_MAX_DIST / max_exact)
                         * (REL_BUCKETS - max_exact)).astype(jnp.int32)
    large = jnp.minimum(large, REL_BUCKETS - 1)
    return jnp.where(n < max_exact, n, large)


def causal_dwconv(x, w, b):
    K, C = w.shape
    y = lax.conv_general_dilated(x, w[:, None, :], window_strides=(1,), padding=((K - 1, 0),),
                                 dimension_numbers=('NWC', 'WIO', 'NWC'), feature_group_count=C)
    return y + b


def mlstm_chunkwise(q, k, v, i_pre, log_f):
    Bsz, H, S, DK = q.shape
    DV = v.shape[-1]
    L = MLSTM_CHUNK
    nc = S // L

    def to_chunks(a):
        return jnp.moveaxis(a.reshape(a.shape[:2] + (nc, L) + a.shape[3:]), 2, 0)

    causal = jnp.tril(jnp.ones((L, L), dtype=bool))

    def step(carry, inp):
        C, n, m = carry
        qc, kc, vc, ic, fc = inp
        b = jnp.cumsum(fc, axis=-1)
        log_d = jnp.where(causal, b[..., :, None] - b[..., None, :] + ic[..., None, :], -jnp.inf)
        m_inter = b + m[..., None]
        m_t = jnp.maximum(m_inter, jnp.max(log_d, axis=-1))
        d = jnp.exp(log_d - m_t[..., None])
        inter = jnp.exp(m_inter - m_t)
        s = jnp.einsum('bhtd,bhsd->bhts', qc, kc) * d
        num = jnp.einsum('bhts,bhsv->bhtv', s, vc) + inter[..., None] * jnp.einsum('bhtd,bhdv->bhtv', qc, C)
        den = jnp.sum(s, axis=-1) + inter * jnp.einsum('bhtd,bhd->bht', qc, n)
        h = num / jnp.maximum(jnp.abs(den), jnp.exp(-m_t))[..., None]
        b_last = b[..., -1]
        log_w = b_last[..., None] - b + ic
        m_new = jnp.maximum(b_last + m, jnp.max(log_w, axis=-1))
        w = jnp.exp(log_w - m_new[..., None])
        decay = jnp.exp(b_last + m - m_new)
        C_new = decay[..., None, None] * C + jnp.einsum('bhs,bhsd,bhsv->bhdv', w, kc, vc)
        n_new = decay[..., None] * n + jnp.einsum('bhs,bhsd->bhd', w, kc)
        return (C_new, n_new, m_new), h

    init = (jnp.zeros((Bsz, H, DK, DV), jnp.float32), jnp.zeros((Bsz, H, DK), jnp.float32),
            jnp.zeros((Bsz, H), jnp.float32))
    xs = (to_chunks(q), to_chunks(k), to_chunks(v), to_chunks(i_pre), to_chunks(log_f))
    _, h = lax.scan(step, init, xs)
    return jnp.moveaxis(h, 0, 2).reshape(Bsz, H, S, DV)


def ssd_chunked(x, dt, A, Bm, Cm):
    Bsz, S, H, P = x.shape
    N = Bm.shape[-1]
    L = SSM_CHUNK
    nc = S // L
    xc = (x * dt[..., None]).reshape(Bsz, nc, L, H, P)
    Bc = Bm.reshape(Bsz, nc, L, H, N)
    Cc = Cm.reshape(Bsz, nc, L, H, N)
    a_cs = jnp.cumsum((dt * A).reshape(Bsz, nc, L, H).transpose(0, 3, 1, 2), axis=-1)
    causal = jnp.tril(jnp.ones((L, L), dtype=bool))
    seg = jnp.exp(jnp.where(causal, a_cs[..., :, None] - a_cs[..., None, :], -jnp.inf))
    scores = jnp.einsum('bclhn,bcshn->bhcls', Cc, Bc) * seg
    y_diag = jnp.einsum('bhcls,bcshp->bclhp', scores, xc)
    decay_in = jnp.exp(a_cs[..., -1:] - a_cs)
    states = jnp.einsum('bcshn,bhcs,bcshp->bchpn', Bc, decay_in, xc)
    chunk_decay = jnp.exp(a_cs[..., -1])

    def step(s, inp):
        st, dec = inp
        return dec[..., None, None] * s + st, s

    _, prev = lax.scan(step, jnp.zeros((Bsz, H, P, N), jnp.float32),
                       (jnp.moveaxis(states, 1, 0), jnp.moveaxis(chunk_decay, 2, 0)))
    prev = jnp.moveaxis(prev, 0, 1)
    y_off = jnp.einsum('bclhn,bchpn,bhcl->bclhp', Cc, prev, jnp.exp(a_cs))
    return (y_diag + y_off).reshape(Bsz, S, H, P)


def diff_attention(q, k, v, rel_bias, lam):
    Bsz, H, S = q.shape[:3]
    DV = v.shape[-1]
    nb = S // Q_BLOCK
    scale = DIFF_QK_DIM ** -0.5
    k1 = k[..., 0, :]
    k2 = k[..., 1, :]
    qb = jnp.moveaxis(q.reshape(Bsz, H, nb, Q_BLOCK, 2, DIFF_QK_DIM), 2, 0)
    k_pos = jnp.arange(S)

    def block(args):
        qblk, bi = args
        q_pos = bi * Q_BLOCK + jnp.arange(Q_BLOCK)
        rel = q_pos[:, None] - k_pos[None, :]
        mask = rel >= 0
        bias = jnp.moveaxis(rel_bias[t5_bucket(rel)], -1, 0).astype(jnp.float32)

        def probs(qi, ki):
            logits = jnp.einsum('bhqd,bhkd->bhqk', qi, ki).astype(jnp.float32) * scale + bias
            return jax.nn.softmax(jnp.where(mask, logits, -jnp.inf), axis=-1)

        a = probs(qblk[..., 0, :], k1) - lam * probs(qblk[..., 1, :], k2)
        return jnp.einsum('bhqk,bhkv->bhqv', a.astype(v.dtype), v)

    out = lax.map(block, (qb, jnp.arange(nb)))
    return jnp.moveaxis(out, 0, 2).reshape(Bsz, H, S, DV)


def hybrid_mixer(h, layer_idx, w_in, w_out, mlstm_gate_bias, mlstm_norm_w, conv_w, conv_b,
                 dt_bias, A_log, D_skip, ssm_norm_w, q_norm_w, k_norm_w, lambdas, subln_w, rel_bias):
    Bsz, S, _ = h.shape
    f32 = jnp.float32
    proj = h @ w_in
    split_idx = np.cumsum(IN_SPLIT_SIZES)[:-1].tolist()
    mq, mk, mv, mo, mi, mf, z, xbc, dt_raw, dq, dk, dv = jnp.split(proj, split_idx, axis=-1)

    def heads(a, nh):
        return a.reshape(Bsz, S, nh, -1).transpose(0, 2, 1, 3)

    i_pre = (mi + mlstm_gate_bias[0]).astype(f32).transpose(0, 2, 1)
    log_f = jax.nn.log_sigmoid((mf + mlstm_gate_bias[1]).astype(f32)).transpose(0, 2, 1)
    hm = mlstm_chunkwise(heads(mq, MLSTM_HEADS).astype(f32),
                         heads(mk, MLSTM_HEADS).astype(f32) * (MLSTM_HEAD_DIM ** -0.5),
                         heads(mv, MLSTM_HEADS).astype(f32), i_pre, log_f)
    hm = rms_norm(hm.transpose(0, 2, 1, 3), mlstm_norm_w.reshape(MLSTM_HEADS, MLSTM_HEAD_DIM))
    y_mlstm = jax.nn.sigmoid(mo) * hm.reshape(Bsz, S, MLSTM_WIDTH).astype(h.dtype)

    xbc = jax.nn.silu(causal_dwconv(xbc, conv_w, conv_b))
    xs, Bm, Cm = jnp.split(xbc, [SSM_WIDTH, SSM_WIDTH + SSM_GROUPS * SSM_STATE], axis=-1)
    xs = xs.reshape(Bsz, S, SSM_HEADS, SSM_HEAD_DIM).astype(f32)
    rep = SSM_HEADS // SSM_GROUPS
    Bm = jnp.repeat(Bm.reshape(Bsz, S, SSM_GROUPS, SSM_STATE), rep, axis=2).astype(f32)
    Cm = jnp.repeat(Cm.reshape(Bsz, S, SSM_GROUPS, SSM_STATE), rep, axis=2).astype(f32)
    dt = jax.nn.softplus((dt_raw + dt_bias).astype(f32))
    A = -jnp.exp(A_log.astype(f32))
    y = ssd_chunked(xs, dt, A, Bm, Cm) + D_skip.astype(f32)[:, None] * xs
    y = y.reshape(Bsz, S, SSM_WIDTH).astype(h.dtype) * jax.nn.silu(z)
    y_ssm = rms_norm(y.reshape(Bsz, S, SSM_GROUPS, -1),
                     ssm_norm_w.reshape(SSM_GROUPS, -1)).reshape(Bsz, S, SSM_WIDTH)

    dq = rms_norm(dq.reshape(Bsz, S, DIFF_HEADS, 2, DIFF_QK_DIM), q_norm_w).transpose(0, 2, 1, 3, 4)
    dk = rms_norm(dk.reshape(Bsz, S, DIFF_HEADS, 2, DIFF_QK_DIM), k_norm_w).transpose(0, 2, 1, 3, 4)
    dv = heads(dv, DIFF_HEADS)
    lam_init = 0.8 - 0.6 * math.exp(-0.3 * layer_idx)
    lf = lambdas.astype(f32)
    lam = jnp.exp(jnp.sum(lf[0] * lf[1])) - jnp.exp(jnp.sum(lf[2] * lf[3])) + lam_init
    o = diff_attention(dq, dk, dv, rel_bias, lam)
    o = rms_norm(o.transpose(0, 2, 1, 3), subln_w) * (1.0 - lam_init)
    y_diff = o.reshape(Bsz, S, DIFF_WIDTH)

    return jnp.concatenate([y_mlstm, y_ssm, y_diff], axis=-1) @ w_out


def setup_inputs(seed: int = 0) -> dict:
    key = jax.random.key(seed)
    ks = jax.random.split(key, 32)
    L = DEPTH
    f32 = jnp.float32

    def nrm(i, shape, scale):
        return scale * jax.random.normal(ks[i], shape, f32)

    def gain(i, shape):
        return 1.0 + 0.1 * jax.random.normal(ks[i], shape, f32)

    x = jax.random.normal(ks[0], (BATCH, SEQ, D_MODEL), f32)
    ffn1_norm_w = gain(1, (L, D_MODEL))
    ffn1_w_gate = nrm(2, (L, D_MODEL, D_FF), D_MODEL ** -0.5)
    ffn1_w_up = nrm(3, (L, D_MODEL, D_FF), D_MODEL ** -0.5)
    ffn1_w_down = nrm(4, (L, D_FF, D_MODEL), D_FF ** -0.5)
    mix_norm_w = gain(5, (L, D_MODEL))
    w_in = nrm(6, (L, D_MODEL, D_IN), D_MODEL ** -0.5)
    mlstm_gate_bias = jnp.stack([nrm(7, (L, MLSTM_HEADS), 0.1),
                                 jnp.linspace(3.0, 6.0, MLSTM_HEADS)[None, :] + nrm(8, (L, MLSTM_HEADS), 0.1)],
                                axis=1)
    mlstm_norm_w = gain(9, (L, MLSTM_WIDTH))
    ssm_conv_w = nrm(10, (L, SSM_CONV, SSM_CONV_DIM), SSM_CONV ** -0.5)
    ssm_conv_b = nrm(11, (L, SSM_CONV_DIM), 0.01)
    dt0 = jnp.exp(jax.random.uniform(ks[12], (L, SSM_HEADS), f32, math.log(1e-3), math.log(1e-1)))
    ssm_dt_bias = dt0 + jnp.log(-jnp.expm1(-dt0))
    ssm_A_log = jnp.log(jax.random.uniform(ks[13], (L, SSM_HEADS), f32, 1.0, 16.0))
    ssm_D = gain(14, (L, SSM_HEADS))
    ssm_norm_w = gain(15, (L, SSM_WIDTH))
    diff_q_norm_w = gain(16, (L, 2, DIFF_QK_DIM))
    diff_k_norm_w = gain(17, (L, 2, DIFF_QK_DIM))
    diff_lambda = nrm(18, (L, 4, DIFF_QK_DIM), 0.1)
    diff_subln_w = gain(19, (L, DIFF_V_DIM))
    rel_bias = nrm(20, (REL_BUCKETS, DIFF_HEADS), 0.5)
    w_out = nrm(21, (L, D_MIX, D_MODEL), D_MIX ** -0.5)
    ffn2_norm_w = gain(22, (L, D_MODEL))
    ffn2_w_gate = nrm(23, (L, D_MODEL, D_FF), D_MODEL ** -0.5)
    ffn2_w_up = nrm(24, (L, D_MODEL, D_FF), D_MODEL ** -0.5)
    ffn2_w_down = nrm(25, (L, D_FF, D_MODEL), D_FF ** -0.5)
    return {'x': x, 'ffn1_norm_w': ffn1_norm_w, 'ffn1_w_gate': ffn1_w_gate, 'ffn1_w_up': ffn1_w_up,
            'ffn1_w_down': ffn1_w_down, 'mix_norm_w': mix_norm_w, 'w_in': w_in,
            'mlstm_gate_bias': mlstm_gate_bias, 'mlstm_norm_w': mlstm_norm_w,
            'ssm_conv_w': ssm_conv_w, 'ssm_conv_b': ssm_conv_b, 'ssm_dt_bias': ssm_dt_bias,
            'ssm_A_log': ssm_A_log, 'ssm_D': ssm_D, 'ssm_norm_w': ssm_norm_w,
            'diff_q_norm_w': diff_q_norm_w, 'diff_k_norm_w': diff_k_norm_w, 'diff_lambda': diff_lambda,
            'diff_subln_w': diff_subln_w, 'rel_bias': rel_bias, 'w_out': w_out,
            'ffn2_norm_w': ffn2_norm_w, 'ffn2_w_gate': ffn2_w_gate, 'ffn2_w_up': ffn2_w_up,
            'ffn2_w_down': ffn2_w_down}


def reference(x, ffn1_norm_w, ffn1_w_gate, ffn1_w_up, ffn1_w_down, mix_norm_w, w_in,
              mlstm_gate_bias, mlstm_norm_w, ssm_conv_w, ssm_conv_b, ssm_dt_bias, ssm_A_log, ssm_D,
              ssm_norm_w, diff_q_norm_w, diff_k_norm_w, diff_lambda, diff_subln_w, rel_bias, w_out,
              ffn2_norm_w, ffn2_w_gate, ffn2_w_up, ffn2_w_down):
    for l in range(DEPTH):
        x = x + 0.5 * swiglu_ffn(rms_norm(x, ffn1_norm_w[l]), ffn1_w_gate[l], ffn1_w_up[l], ffn1_w_down[l])
        x = x + hybrid_mixer(rms_norm(x, mix_norm_w[l]), l, w_in[l], w_out[l],
                             mlstm_gate_bias[l], mlstm_norm_w[l], ssm_conv_w[l], ssm_conv_b[l],
                             ssm_dt_bias[l], ssm_A_log[l], ssm_D[l], ssm_norm_w[l],
                             diff_q_norm_w[l], diff_k_norm_w[l], diff_lambda[l], diff_subln_w[l],
                             rel_bias)
        x = x + 0.5 * swiglu_ffn(rms_norm(x, ffn2_norm_w[l]), ffn2_w_gate[l], ffn2_w_up[l], ffn2_w_down[l])
    return x
```

```python
import math
import numpy as np
import ml_dtypes
from contextlib import ExitStack
import concourse.bass as bass
import concourse.mybir as mybir


F32 = mybir.dt.float32
BF16 = mybir.dt.bfloat16
AF = mybir.ActivationFunctionType
ALU = mybir.AluOpType
AX = mybir.AxisListType


class Reg:
    __slots__ = ("lw", "rd", "name", "excl")

    def __init__(self, name="", excl=False):
        self.excl = excl
        self.lw = None
        self.rd = {}
        self.name = name


class KB:
    def __init__(self, nc, stack):
        self.nc = nc
        self.stack = stack
        self.eng = {"pe": nc.tensor, "dve": nc.vector, "act": nc.scalar,
                    "pool": nc.gpsimd, "sp": nc.sync}
        self.sem = {}
        self.cnt = {}
        self.waited = {}
        for n in self.eng:
            self.sem[n] = stack.enter_context(nc.semaphore("s_" + n))
            self.cnt[n] = 0
            self.waited[n] = {}
        self.top_stack = stack
        self.free_sems = []
        self.phase_sems = []
        self.lazy_keys = set()
        self.dma_sems = []
        self.dma_by_key = {}
        self.n_dma_sem = 0
        self.uid = 0

    def sb(self, shape, dt, name=None):
        self.uid += 1
        return self.stack.enter_context(
            self.nc.sbuf_tensor(f"sb{self.uid}_{name or ""}", list(shape), dt))

    def ps(self, shape, dt, name=None):
        self.uid += 1
        full = 512 if dt == F32 else 1024
        t = self.stack.enter_context(
            self.nc.psum_tensor(f"ps{self.uid}_{name or ""}", [128, full], dt))
        n = 1
        for d in shape[1:]:
            n *= d
        assert n <= full
        v = t[0:shape[0], 0:n]
        if len(shape) == 3:
            v = v.rearrange("p (a b) -> p a b", b=shape[2])
        elif len(shape) == 4:
            v = v.rearrange("p (a b c) -> p a b c", b=shape[2], c=shape[3])
        return v

    def new_dma_sem(self):
        if self.free_sems:
            ent = self.free_sems.pop()
        else:
            s = self.top_stack.enter_context(self.nc.semaphore(f"d{self.n_dma_sem}"))
            self.n_dma_sem += 1
            ent = [s, 0, f"d{self.n_dma_sem}"]
            self.dma_sems.append(ent)
            self.dma_by_key[ent[2]] = ent
        if self.phase_sems:
            self.phase_sems[-1].append(ent)
        return ent

    def begin_phase(self):
        self.phase_sems.append([])

    def end_phase(self):
        for ent in self.phase_sems.pop():
            self.free_sems.append(ent)

    def collective(self, kind, src, dst, groups, csem, R=(), W=()):
        self._deps("pool", R, W, is_dma=True)
        ins = self.nc.gpsimd.collective_compute(kind, mybir.AluOpType.bypass, replica_groups=groups,
                                                ins=[src], outs=[dst])
        csem[1] += 1
        ins.then_inc(csem[0], 1)
        tag = (csem[2], csem[0], csem[1])
        for w in W:
            w.lw = tag
            w.rd = {}
        self.lazy_keys.add(csem[2])
        return ins

    def _wait(self, e, dep):
        key, sem, val = dep
        if key in self.dma_by_key:
            val = self.dma_by_key[key][1]
        w = self.waited[e]
        if w.get(key, 0) >= val:
            return
        self.eng[e].wait_ge(sem, val)
        w[key] = val

    def _deps(self, e, R, W, is_dma=False):
        deps = []
        for r in R:
            if r.lw is not None:
                deps.append(r.lw)
            if r.excl:
                for d in r.rd.values():
                    if d[0] != e:
                        deps.append(d)
        for w in W:
            if w.lw is not None:
                deps.append(w.lw)
            for d in w.rd.values():
                if d[0] != e or is_dma:
                    deps.append(d)
        for d in deps:
            if d[0] == e and e == "pe" and not is_dma:
                continue
            self._wait(e, d)

    def op(self, e, fn, R=(), W=()):
        self._deps(e, R, W)
        ins = fn()
        self.cnt[e] += 1
        ins.then_inc(self.sem[e], 1)
        tag = (e, self.sem[e], self.cnt[e])
        for w in W:
            w.lw = tag
            w.rd = {}
        for r in R:
            if r not in W:
                r.rd[e] = tag
        return ins

    def dma(self, q, out, in_, dsem, R=(), W=(), **kw):
        self._deps(q, R, W, is_dma=True)
        ins = self.eng[q].dma_start(out=out, in_=in_, **kw)
        dsem[1] += 16
        ins.then_inc(dsem[0], 16)
        tag = (dsem[2], dsem[0], dsem[1])
        for w in W:
            w.lw = tag
            w.rd = {}
        for r in R:
            if r not in W:
                r.rd[dsem[2]] = tag
        return ins

    def barrier(self, full=False):
        deps = [(n, self.sem[n], self.cnt[n]) for n in self.eng if self.cnt[n] > 0]
        deps += [(d[2], d[0], d[1]) for d in self.dma_sems if d[1] > 0 and (full or d[2] not in self.lazy_keys)]
        for e in self.eng:
            for d in deps:
                if d[0] != e:
                    self._wait(e, d)


def psbank(K, name):
    K.uid += 1
    t = K.stack.enter_context(K.nc.psum_tensor(f"pb{K.uid}_{name}", [128, 512], F32))
    return t[:, :]


D = 1024
DFF = 2816
NF = DFF // 128
EPS = 1e-6


def load_consts(K, ident_bf_d, ident_f_d):
    C = {}
    C["dsem"] = K.new_dma_sem()
    C["ident_bf"] = K.sb([128, 128], BF16, "ident_bf")
    C["ident_f"] = K.sb([128, 128], F32, "ident_f")
    C["r"] = Reg("consts")
    K.dma("sp", C["ident_bf"][:], ident_bf_d, C["dsem"], W=[C["r"]])
    K.dma("sp", C["ident_f"][:], ident_f_d, C["dsem"], W=[C["r"]])
    return C


def emit_norm_stats(K, C, xt, r_xt, S):
    nc = K.nc
    K.op("act", lambda: nc.scalar.activation(out=S["junk"][:], in_=xt, func=AF.Square,
                                             accum_out=S["ssq"][:]),
         R=[r_xt], W=[S["r_junk"], S["r_ssq"]])
    K.op("dve", lambda: nc.vector.tensor_scalar(out=S["ms"][:], in0=S["ssq"][:], scalar1=1.0 / D,
                                                scalar2=EPS, op0=ALU.mult, op1=ALU.add),
         R=[S["r_ssq"]], W=[S["r_ms"]])
    K.op("act", lambda: nc.scalar.activation(out=S["sd"][:], in_=S["ms"][:], func=AF.Sqrt),
         R=[S["r_ms"]], W=[S["r_sd"]])
    K.op("dve", lambda: nc.vector.reciprocal(out=S["rstd"][:], in_=S["sd"][:]),
         R=[S["r_sd"]], W=[S["r_rstd"]])
    K.op("dve", lambda: nc.vector.tensor_scalar(out=S["xn"][:], in0=xt, scalar1=S["rstd"][:],
                                                scalar2=None, op0=ALU.mult),
         R=[r_xt, S["r_rstd"]], W=[S["r_xn"]])


def emit_norm_tr(K, C, nw, r_nw, dst_fn, r_dst, S):
    nc = K.nc
    for k in range(8):
        K.op("pe", lambda k=k: nc.tensor.transpose(out=S["ptr"][:, k, :],
                                                   in_=S["xn"][:, k * 128:(k + 1) * 128],
                                                   identity=C["ident_bf"][:]),
             R=[S["r_xn"], C["r"]], W=[S["r_ptr"]])
    K.op("dve", lambda: nc.vector.tensor_tensor(out=dst_fn, in0=S["ptr"][:],
                                                in1=nw.unsqueeze(2).to_broadcast([128, 8, 128]),
                                                op=ALU.mult),
         R=[S["r_ptr"], r_nw], W=[r_dst])


def emit_norm_T(K, C, xt, r_xt, nw, r_nw, dst_fn, r_dst, S):
    emit_norm_stats(K, C, xt, r_xt, S)
    emit_norm_tr(K, C, nw, r_nw, dst_fn, r_dst, S)


def norm_scratch(K, tag, share=0, ptr=None, r_ptr=None):
    S = {}
    if share is not None:
        S["junk"] = K.sb([128, D], BF16, tag + "junk")
    S["ssq"] = K.sb([128, 1], F32, tag + "ssq")
    S["ms"] = K.sb([128, 1], F32, tag + "ms")
    S["sd"] = K.sb([128, 1], F32, tag + "sd")
    S["rstd"] = K.sb([128, 1], F32, tag + "rstd")
    S["xn"] = K.sb([128, D], BF16, tag + "xn")
    S["ptr"] = K.ps([128, 8, 128], BF16, tag + "ptr") if ptr is None else ptr
    for n in ["junk", "ssq", "ms", "sd", "rstd", "xn", "ptr"]:
        S["r_" + n] = Reg(tag + n)
    if r_ptr is not None:
        S["r_ptr"] = r_ptr
    return S


def emit_ffn(K, C, x_in, x_out, nw_d, wg_d, wu_d, wd_d, ntok, hn_out=None, nw2_d=None,
             blk=1024, on_block=None):
    nc = K.nc
    outer = K.stack
    with ExitStack() as st:
        K.stack = st
        K.begin_phase()
        nblk = ntok // blk
        nsub = blk // 128
        ntt = blk // 512
        nw = K.sb([128, 8], F32, "nw")
        r_nw = Reg("nw")
        ds_misc = K.new_dma_sem()
        K.dma("sp", nw[:], nw_d, ds_misc, W=[r_nw])
        if hn_out is not None:
            nw2 = K.sb([128, 8], F32, "nw2")
            r_nw2 = Reg("nw2")
            K.dma("sp", nw2[:], nw2_d, ds_misc, W=[r_nw2])
            hnb = K.sb([128, 8, blk], BF16, "hnb")
            r_hnb = Reg("hnb")
            ds_hn = K.new_dma_sem()
        hT = K.sb([128, 8, blk], BF16, "hT")
        r_hT = [Reg(f"hT{j}") for j in range(nsub)]
        aT = K.sb([128, NF, blk], BF16, "aT")
        r_aT = [[Reg(f"aT{f}_{t}") for t in range(ntt)] for f in range(NF)]
        NX = 3
        xt = [K.sb([128, D], F32, f"xt{i}") for i in range(NX)]
        r_xt = [Reg(f"xt{i}") for i in range(NX)]
        ds_xt = [K.new_dma_sem() for i in range(NX)]
        Ss = [norm_scratch(K, "n1"), norm_scratch(K, "n2", share=None)]
        Ss[1]["junk"] = Ss[0]["junk"]; Ss[1]["r_junk"] = Ss[0]["r_junk"]
        NW = 3
        wst = [[K.sb([128, 8 * 128], F32, f"wst{i}_{g}") for g in range(2)] for i in range(NW)]
        r_wst = [[Reg() for g in range(2)] for i in range(NW)]
        ds_w = [[K.new_dma_sem() for g in range(2)] for i in range(NW)]
        wb = [[K.sb([128, 8, 128], BF16, f"wb{i}_{g}") for g in range(2)] for i in range(NW)]
        r_wb = [[Reg() for g in range(2)] for i in range(NW)]
        wdb = K.sb([128, NF, D], BF16, "wdb")
        r_wdb = [Reg(f"wdb{f}") for f in range(NF)]
        NDS = 2
        wdst = [K.sb([128, D], F32, f"wdst{i}") for i in range(NDS)]
        r_wdst = [Reg() for i in range(NDS)]
        ds_wd = [K.new_dma_sem() for i in range(NDS)]
        pg = [K.ps([128, 512], F32, f"pg{i}") for i in range(2)]
        pu = [K.ps([128, 512], F32, f"pu{i}") for i in range(2)]
        r_pg = [Reg() for i in range(2)]
        r_pu = [Reg() for i in range(2)]
        sg = [K.sb([128, 512], F32, f"sg{i}") for i in range(2)]
        r_sg = [Reg() for i in range(2)]
        po = [K.ps([128, 512], F32, f"po{i}") for i in range(2)]
        r_po = [Reg() for i in range(2)]
        ot = [K.sb([128, D], F32, f"ot{i}") for i in range(2)]
        r_ot = [Reg() for i in range(2)]
        ds_ot = [K.new_dma_sem() for i in range(2)]

        Sn = norm_scratch(K, "n3", share=None, ptr=pg[0].bitcast(BF16).rearrange("p (a b) -> p a b", b=128),
                          r_ptr=r_pg[0])
        Sn["junk"] = Ss[0]["junk"]; Sn["r_junk"] = Ss[0]["r_junk"]
        xs = [0]

        def ld_row(row0):
            slot = xs[0] % NX
            xs[0] += 1
            K.dma("sp", xt[slot][:], x_in[row0: row0 + 128, :], ds_xt[slot], W=[r_xt[slot]])
            return slot
        for b in range(nblk):
            t0 = b * blk
            if b == 0:
                nslot = ld_row(t0)
                for j in range(nsub):
                    slot = nslot
                    if j + 1 < nsub:
                        nslot = ld_row(t0 + (j + 1) * 128)
                    emit_norm_T(K, C, xt[slot][:], r_xt[slot], nw[:], r_nw,
                                hT[:, :, j * 128:(j + 1) * 128], r_hT[j], Ss[j % 2])

            def ld_w(f):
                s = f % NW
                K.dma("sp", wst[s][0][:], wg_d[f], ds_w[s][0], W=[r_wst[s][0]])
                K.dma("sp", wst[s][1][:], wu_d[f], ds_w[s][1], W=[r_wst[s][1]])

            def ld_wd(f):
                s = f % NDS
                K.dma("act", wdst[s][:], wd_d[f], ds_wd[s], W=[r_wdst[s]])

            def cast_wd(f):
                s = f % NDS
                K.op("act", lambda: nc.scalar.copy(out=wdb[:, f, :], in_=wdst[s][:]),
                     R=[r_wdst[s]], W=[r_wdb[f]])

            def cast_w(f):
                s = f % NW
                K.op("dve", lambda: nc.vector.tensor_copy(
                    out=wb[s][0][:].rearrange("p k c -> p (k c)"), in_=wst[s][0][:]),
                    R=[r_wst[s][0]], W=[r_wb[s][0]])
                K.op("act", lambda: nc.scalar.copy(
                    out=wb[s][1][:].rearrange("p k c -> p (k c)"), in_=wst[s][1][:]),
                    R=[r_wst[s][1]], W=[r_wb[s][1]])

            ld_w(0)
            ld_w(1)
            ld_wd(0)
            ld_wd(1)
            cast_w(0)
            for f in range(NF):
                s = f % NW
                if f + 2 < NF:
                    ld_w(f + 2)
                if f + 1 < NF:
                    cast_w(f + 1)
                cast_wd(f)
                if f + 2 < NF:
                    ld_wd(f + 2)
                for tt in range(ntt):
                    pb = (f * ntt + tt) % 2
                    rr = [r_hT[j] for j in range(tt * 4, tt * 4 + 4)]
                    for k in range(8):
                        K.op("pe", lambda k=k: nc.tensor.matmul(
                            pg[pb][:], lhsT=wb[s][0][:, k, :], rhs=hT[:, k, tt * 512:(tt + 1) * 512],
                            start=(k == 0), stop=(k == 7)),
                            R=[r_wb[s][0]] + rr, W=[r_pg[pb]])
                    for k in range(8):
                        K.op("pe", lambda k=k: nc.tensor.matmul(
                            pu[pb][:], lhsT=wb[s][1][:, k, :], rhs=hT[:, k, tt * 512:(tt + 1) * 512],
                            start=(k == 0), stop=(k == 7)),
                            R=[r_wb[s][1]] + rr, W=[r_pu[pb]])
                    K.op("act", lambda: nc.scalar.activation(out=sg[pb][:], in_=pg[pb][:], func=AF.Silu),
                         R=[r_pg[pb]], W=[r_sg[pb]])
                    K.op("dve", lambda: nc.vector.tensor_tensor(
                        out=aT[:, f, tt * 512:(tt + 1) * 512], in0=sg[pb][:], in1=pu[pb][:], op=ALU.mult),
                        R=[r_sg[pb], r_pu[pb]], W=[r_aT[f][tt]])

            pend = None
            pend2 = None
            nxtb = (b + 1 < nblk)
            nslot = ld_row(t0)
            for j in range(nsub):
                slot = nslot
                if j + 1 < nsub:
                    nslot = ld_row(t0 + (j + 1) * 128)
                os_ = j % 2
                for half in range(2):
                    pb = (j * 2 + half) % 2
                    for f in range(NF):
                        K.op("pe", lambda f=f: nc.tensor.matmul(
                            po[pb][:], lhsT=aT[:, f, j * 128:(j + 1) * 128],
                            rhs=wdb[:, f, half * 512:(half + 1) * 512],
                            start=(f == 0), stop=(f == NF - 1)),
                            R=[r_aT[f][j // 4], r_wdb[f]], W=[r_po[pb]])
                    K.op("dve", lambda: nc.vector.scalar_tensor_tensor(
                        out=ot[os_][:, half * 512:(half + 1) * 512], in0=po[pb][:], scalar=0.5,
                        in1=xt[slot][:, half * 512:(half + 1) * 512], op0=ALU.mult, op1=ALU.add),
                        R=[r_po[pb], r_xt[slot]], W=[r_ot[os_]])
                K.dma("sp", x_out[t0 + j * 128: t0 + (j + 1) * 128, :], ot[os_][:], ds_ot[os_],
                      R=[r_ot[os_]])
                if hn_out is not None:
                    if pend is not None:
                        emit_norm_tr(K, C, nw2[:], r_nw2, *pend)
                    emit_norm_stats(K, C, ot[os_][:], r_ot[os_], Ss[j % 2])
                    pend = (hnb[:, :, j * 128:(j + 1) * 128], r_hnb, Ss[j % 2])
                if nxtb:
                    if pend2 is not None:
                        emit_norm_tr(K, C, nw[:], r_nw, *pend2)
                    s2 = ld_row(t0 + blk + j * 128)
                    emit_norm_stats(K, C, xt[s2][:], r_xt[s2], Sn)
                    pend2 = (hT[:, :, j * 128:(j + 1) * 128], r_hT[j], Sn)
            if hn_out is not None and pend is not None:
                emit_norm_tr(K, C, nw2[:], r_nw2, *pend)
                pend = None
            if pend2 is not None:
                emit_norm_tr(K, C, nw[:], r_nw, *pend2)
                pend2 = None
            if hn_out is not None:
                if isinstance(hn_out, (list, tuple)):
                    rd_ = Reg(f"hn_dram{b}")
                    K.dma("sp", hn_out[b].rearrange("k p t -> p k t"), hnb[:], ds_hn, R=[r_hnb], W=[rd_])
                    if on_block is not None:
                        on_block(b, rd_)
                else:
                    K.dma("sp", hn_out[:, :, t0:t0 + blk].rearrange("k p t -> p k t"), hnb[:], ds_hn,
                          R=[r_hnb])
        K.barrier()
        K.end_phase()
        K.stack = outer


def emit_outproj(K, C, x_in, y_in, wo_d, x_out, ntok):
    nc = K.nc
    outer = K.stack
    with ExitStack() as st:
        K.stack = st
        K.begin_phase()
        wob = K.sb([128, 8, D], BF16, "wob")
        r_wob = Reg()
        wst = [K.sb([128, D], F32, f"wost{i}") for i in range(2)]
        r_wst = [Reg() for i in range(2)]
        ds_w = [K.new_dma_sem() for i in range(2)]
        for k in range(8):
            K.dma("sp", wst[k % 2][:], wo_d[k], ds_w[k % 2], W=[r_wst[k % 2]])
            K.op("pool", lambda k=k: nc.gpsimd.tensor_copy(out=wob[:, k, :], in_=wst[k % 2][:]),
                 R=[r_wst[k % 2]], W=[r_wob])
        NS = 3
        yt = [K.sb([128, D], BF16, f"yt{i}") for i in range(NS)]
        r_yt = [Reg() for i in range(NS)]
        xt = [K.sb([128, D], F32, f"oxt{i}") for i in range(NS)]
        r_xt = [Reg() for i in range(NS)]
        ds_in = [K.new_dma_sem() for i in range(NS)]
        ptr = [K.ps([128, 8, 128], BF16, f"optr{i}") for i in range(2)]
        r_ptr = [Reg() for i in range(2)]
        yT = [K.sb([128, 8, 128], BF16, f"yT{i}") for i in range(2)]
        r_yT = [Reg() for i in range(2)]
        po = [K.ps([128, 512], F32, f"opo{i}") for i in range(2)]
        r_po = [Reg() for i in range(2)]
        ot = [K.sb([128, D], F32, f"oot{i}") for i in range(2)]
        r_ot = [Reg() for i in range(2)]
        ds_ot = [K.new_dma_sem() for i in range(2)]
        nsub = ntok // 128

        def ld(j):
            s = j % NS
            K.dma("sp", yt[s][:], y_in[j * 128:(j + 1) * 128, :], ds_in[s], W=[r_yt[s]])
            K.dma("sp", xt[s][:], x_in[j * 128:(j + 1) * 128, :], ds_in[s], W=[r_xt[s]])
        ld(0)
        for j in range(nsub):
            s = j % NS
            p2 = j % 2
            if j + 1 < nsub:
                ld(j + 1)
            for k in range(8):
                K.op("pe", lambda k=k: nc.tensor.transpose(out=ptr[p2][:, k, :],
                                                           in_=yt[s][:, k * 128:(k + 1) * 128],
                                                           identity=C["ident_bf"][:]),
                     R=[r_yt[s], C["r"]], W=[r_ptr[p2]])
            K.op("act", lambda: nc.scalar.copy(out=yT[p2][:], in_=ptr[p2][:]),
                 R=[r_ptr[p2]], W=[r_yT[p2]])
            for half in range(2):
                pb = half
                for k in range(8):
                    K.op("pe", lambda k=k: nc.tensor.matmul(
                        po[pb][:], lhsT=yT[p2][:, k, :], rhs=wob[:, k, half * 512:(half + 1) * 512],
                        start=(k == 0), stop=(k == 7)),
                        R=[r_yT[p2], r_wob], W=[r_po[pb]])
                K.op("dve", lambda: nc.vector.tensor_tensor(
                    out=ot[p2][:, half * 512:(half + 1) * 512], in0=po[pb][:],
                    in1=xt[s][:, half * 512:(half + 1) * 512], op=ALU.add),
                    R=[r_po[pb], r_xt[s]], W=[r_ot[p2]])
            K.dma("sp", x_out[j * 128:(j + 1) * 128, :], ot[p2][:], ds_ot[p2], R=[r_ot[p2]])
        K.barrier()
        K.end_phase()
        K.stack = outer


def emit_outproj_sel(K, C, x_in, yall, mh_d, wo_d, x_out, ntok, seq, yregs=None):
    nc = K.nc
    outer = K.stack
    with ExitStack() as st:
        K.stack = st
        K.begin_phase()
        wob = K.sb([128, 8, D], BF16, "wob")
        r_wob = Reg()
        wst = [K.sb([128, D], F32, f"wost{i}") for i in range(2)]
        r_wst = [Reg() for i in range(2)]
        ds_w = [K.new_dma_sem() for i in range(2)]
        mh = K.sb([128, 2], F32, "mh"); r_mh = Reg()
        K.dma("sp", mh[:], mh_d, ds_w[0], W=[r_mh])
        for k in range(8):
            K.dma("sp", wst[k % 2][:], wo_d[k], ds_w[k % 2], W=[r_wst[k % 2]])
            if k % 2 == 0:
                K.op("dve", lambda k=k: nc.vector.tensor_copy(out=wob[:, k, :], in_=wst[k % 2][:]),
                     R=[r_wst[k % 2]], W=[r_wob])
            else:
                K.op("act", lambda k=k: nc.scalar.copy(out=wob[:, k, :], in_=wst[k % 2][:]),
                     R=[r_wst[k % 2]], W=[r_wob])
        NS = 3
        ya = [[K.sb([128, 2, 512], BF16, f"ya{i}_{c}") for c in range(2)] for i in range(NS)]
        r_ya = [[Reg(), Reg()] for i in range(NS)]
        yt = [K.sb([128, D], BF16, f"yt{i}") for i in range(2)]
        r_yt = [Reg() for i in range(2)]
        xt = [K.sb([128, D], F32, f"oxt{i}") for i in range(NS)]
        r_xt = [Reg() for i in range(NS)]
        ds_in = [K.new_dma_sem() for i in range(NS)]
        ptr = [K.ps([128, 8, 128], BF16, f"optr{i}") for i in range(2)]
        r_ptr = [Reg() for i in range(2)]
        yT = [K.sb([128, 8, 128], BF16, f"yT{i}") for i in range(2)]
        r_yT = [Reg() for i in range(2)]
        po = [K.ps([128, 512], F32, f"opo{i}") for i in range(2)]
        r_po = [Reg() for i in range(2)]
        ot = [K.sb([128, D], F32, f"oot{i}") for i in range(2)]
        r_ot = [Reg() for i in range(2)]
        ds_ot = [K.new_dma_sem() for i in range(2)]
        nsub = ntok // 128
        yv = [ya_.rearrange("(r t) c -> t r c", r=2) for ya_ in yall]

        def ld(j):
            s = j % NS
            for c in range(2):
                K.dma("sp" if c == 0 else "act", ya[s][c][:], yv[c][j * 128:(j + 1) * 128, :, :],
                      ds_in[s], W=[r_ya[s][c]], R=([yregs[c]] if yregs is not None else []))
            K.dma("sp", xt[s][:], x_in[j * 128:(j + 1) * 128, :], ds_in[s], W=[r_xt[s]])
        def front(j):
            s = j % NS
            p2 = j % 2
            K.op("dve", lambda: nc.vector.tensor_scalar(out=yt[p2][:], in0=ya[s][0][:].rearrange("p r c -> p (r c)"),
                                                        scalar1=mh[:, 0:1], scalar2=None, op0=ALU.mult),
                 R=[r_ya[s][0], r_mh], W=[r_yt[p2]])
            K.op("dve", lambda: nc.vector.scalar_tensor_tensor(out=yt[p2][:], in0=ya[s][1][:].rearrange("p r c -> p (r c)"),
                                                               scalar=mh[:, 1:2], in1=yt[p2][:], op0=ALU.mult, op1=ALU.add),
                 R=[r_ya[s][1], r_mh, r_yt[p2]], W=[r_yt[p2]])
            for k in range(8):
                K.op("pe", lambda k=k: nc.tensor.transpose(out=ptr[p2][:, k, :],
                                                           in_=yt[p2][:, k * 128:(k + 1) * 128],
                                                           identity=C["ident_bf"][:]),
                     R=[r_yt[p2], C["r"]], W=[r_ptr[p2]])
            K.op("act", lambda: nc.scalar.copy(out=yT[p2][:], in_=ptr[p2][:]),
                 R=[r_ptr[p2]], W=[r_yT[p2]])

        ld(0)
        if nsub > 1:
            ld(1)
        front(0)
        for j in range(nsub):
            s = j % NS
            p2 = j % 2
            if j + 2 < nsub:
                ld(j + 2)
            if j + 1 < nsub:
                front(j + 1)
            for half in range(2):
                pb = half
                for k in range(8):
                    K.op("pe", lambda k=k: nc.tensor.matmul(
                        po[pb][:], lhsT=yT[p2][:, k, :], rhs=wob[:, k, half * 512:(half + 1) * 512],
                        start=(k == 0), stop=(k == 7)),
                        R=[r_yT[p2], r_wob], W=[r_po[pb]])
                K.op("dve", lambda: nc.vector.tensor_tensor(
                    out=ot[p2][:, half * 512:(half + 1) * 512], in0=po[pb][:],
                    in1=xt[s][:, half * 512:(half + 1) * 512], op=ALU.add),
                    R=[r_po[pb], r_xt[s]], W=[r_ot[p2]])
            K.dma("sp", x_out[j * 128:(j + 1) * 128, :], ot[p2][:], ds_ot[p2], R=[r_ot[p2]])
        K.barrier()
        K.end_phase()
        K.stack = outer


EPS = 1e-6
NEG = -30000.0


def load_w(K, w_d, ncols, name):
    nc = K.nc
    wb = K.sb([128, 8, ncols], BF16, name)
    r_wb = Reg(name)
    with ExitStack() as st:
        outer = K.stack
        K.stack = st
        K.begin_phase()
        stg = [K.sb([128, ncols], F32, f"{name}st{i}") for i in range(2)]
        r_st = [Reg() for i in range(2)]
        ds = [K.new_dma_sem() for i in range(2)]
        for k in range(8):
            K.dma("sp", stg[k % 2][:], w_d[:, k, :], ds[k % 2], W=[r_st[k % 2]])
            K.op("pool", lambda k=k: nc.gpsimd.tensor_copy(out=wb[:, k, :], in_=stg[k % 2][:]),
                 R=[r_st[k % 2]], W=[r_wb])
        K.barrier()
        K.end_phase()
        K.stack = outer
    return wb, r_wb


def load_w_all(K, specs, nslots=4):
    nc = K.nc
    out = {}
    for key, w_d, ncols in specs:
        out[key] = (K.sb([128, 8, ncols], BF16, "w_" + key), Reg("w_" + key))
    stg_stack = None
    stg = [K.sb([128, 768], F32, f"wstg{i}") for i in range(nslots)]
    r_st = [Reg() for _ in range(nslots)]
    ds = [K.new_dma_sem() for _ in range(nslots)]
    n = 0
    for key, w_d, ncols in specs:
        wb, r_wb = out[key]
        for k in range(8):
            s = n % nslots
            K.dma("sp" if n % 2 == 0 else "act", stg[s][:, 0:ncols], w_d[:, k, :], ds[s], W=[r_st[s]])
            if n % 2 == 0:
                K.op("dve", lambda k=k: nc.vector.tensor_copy(out=wb[:, k, :], in_=stg[s][:, 0:ncols]),
                     R=[r_st[s]], W=[r_wb])
            else:
                K.op("act", lambda k=k: nc.scalar.copy(out=wb[:, k, :], in_=stg[s][:, 0:ncols]),
                     R=[r_st[s]], W=[r_wb])
            n += 1
    return out, stg_stack


def rstd_from_ssq(K, ssq, r_ssq, n, scr, inv_n):
    nc = K.nc
    K.op("dve", lambda: nc.vector.tensor_scalar(out=scr["ms"], in0=ssq, scalar1=inv_n, scalar2=EPS,
                                                op0=ALU.mult, op1=ALU.add),
         R=[r_ssq], W=[scr["r_ms"]])
    K.op("act", lambda: nc.scalar.activation(out=scr["sd"], in_=scr["ms"], func=AF.Sqrt),
         R=[scr["r_ms"]], W=[scr["r_sd"]])
    K.op("dve", lambda: nc.vector.reciprocal(out=scr["rstd"], in_=scr["sd"]),
         R=[scr["r_sd"]], W=[scr["r_rstd"]])


def rstd_explog(K, ssq, r_ssq, scr, inv_n):
    nc = K.nc
    K.op("dve", lambda: nc.vector.tensor_scalar(out=scr["ms"], in0=ssq, scalar1=inv_n, scalar2=EPS,
                                                op0=ALU.mult, op1=ALU.add),
         R=[r_ssq], W=[scr["r_ms"]])
    K.op("act", lambda: nc.scalar.activation(out=scr["sd"], in_=scr["ms"], func=AF.Ln),
         R=[scr["r_ms"]], W=[scr["r_sd"]])
    K.op("act", lambda: nc.scalar.activation(out=scr["rstd"], in_=scr["sd"], func=AF.Exp, scale=-0.5),
         R=[scr["r_sd"]], W=[scr["r_rstd"]])


def mk_scr(K, shape, tag):
    d = {}
    for n in ["ms", "sd", "rstd"]:
        t = K.sb(shape, F32, tag + n)
        d[n] = t[:]
        d["r_" + n] = Reg(tag + n)
    return d


def emit_diff(K, C, hnT, r_hnT, P, y_d, S, lam_init, W=None):
    nc = K.nc
    nT = S // 128
    nQ = S // 512
    outer = K.stack
    with ExitStack() as st:
        K.stack = st
        K.begin_phase()
        wb, r_wb = W["wD"] if W is not None else load_w(K, P["wD"], 384, "wD")
        dsm = K.new_dma_sem()
        qkw = K.sb([128, 256], F32, "qkw"); r_c = Reg("dconst")
        sw = K.sb([128, 64], F32, "sw")
        lamb = K.sb([128, 4, 32], F32, "lamb")
        Bt = K.sb([128, 2, 1024], F32, "Bt")
        c31 = K.sb([128, 2], F32, "c31")
        K.dma("sp", qkw[:], P["qkw"], dsm, W=[r_c])
        K.dma("sp", sw[:], P["sw"], dsm, W=[r_c])
        K.dma("sp", lamb[:], P["lamb"], dsm, W=[r_c])
        K.dma("sp", Bt[:], P["Bt"], dsm, W=[r_c])
        K.dma("sp", c31[:], P["c31"], dsm, W=[r_c])
        qT = K.sb([128, S], BF16, "dqT"); r_qT = [Reg() for _ in range(nT)]
        kT = K.sb([128, S], BF16, "dkT"); r_kT = [Reg() for _ in range(nT)]
        qTb = K.sb([32, S], BF16, "dqTb")
        kTb = K.sb([32, S], BF16, "dkTb")
        vaug = K.sb([128, nT, 2, 65], BF16, "dvaug"); r_v = [Reg() for _ in range(nT)]
        zer = K.sb([128, 260], BF16, "zer"); r_z = Reg()
        K.op("dve", lambda: nc.vector.memset(zer[:], 0.0), W=[r_z])
        K.op("dve", lambda: nc.vector.memset(vaug[:].rearrange("p a b c -> p (a b c)"), 1.0), W=r_v)
        lt = K.sb([128, 2, 32], F32, "lt"); r_lt = Reg()
        ls = K.sb([128, 2], F32, "ls"); r_ls = Reg()
        le = K.sb([128, 2], F32, "le"); r_le = Reg()
        nlam = K.sb([128, 1], F32, "nlam"); r_nlam = Reg()
        swl = K.sb([128, 64], F32, "swl"); r_swl = Reg()
        K.op("dve", lambda: nc.vector.tensor_tensor(out=lt[:, 0, :], in0=lamb[:, 0, :], in1=lamb[:, 1, :], op=ALU.mult),
             R=[r_c], W=[r_lt])
        K.op("dve", lambda: nc.vector.tensor_tensor(out=lt[:, 1, :], in0=lamb[:, 2, :], in1=lamb[:, 3, :], op=ALU.mult),
             R=[r_c], W=[r_lt])
        K.op("dve", lambda: nc.vector.tensor_reduce(out=ls[:], in_=lt[:], axis=AX.X, op=ALU.add), R=[r_lt], W=[r_ls])
        K.op("act", lambda: nc.scalar.activation(out=le[:], in_=ls[:], func=AF.Exp), R=[r_ls], W=[r_le])
        K.op("dve", lambda: nc.vector.tensor_tensor(out=nlam[:], in0=le[:, 1:2], in1=le[:, 0:1], op=ALU.subtract),
             R=[r_le], W=[r_nlam])
        K.op("dve", lambda: nc.vector.tensor_scalar(out=nlam[:], in0=nlam[:], scalar1=-lam_init, scalar2=None, op0=ALU.add),
             R=[r_nlam], W=[r_nlam])
        K.op("dve", lambda: nc.vector.tensor_scalar(out=swl[:], in0=sw[:], scalar1=1.0 - lam_init, scalar2=None, op0=ALU.mult),
             R=[r_c], W=[r_swl])

        GD = 4
        st_d2 = ExitStack()
        K.stack = st_d2
        ppb = [psbank(K, f"dpp{i}") for i in range(GD)]; r_ppb = [Reg(excl=True) for _ in range(GD)]
        ptrb = [K.ps([128, 8, 128], BF16, f"dptr{i}") for i in range(GD // 2)]; r_ptrb = [Reg() for _ in range(GD // 2)]
        sq = K.sb([128, GD, 256], F32, "dsq"); r_sq = Reg()
        ssq = K.sb([128, GD * 8], F32, "dssq"); r_ssq = Reg()
        scr = mk_scr(K, [128, GD * 8], "dq")
        qn = K.sb([128, GD, 256], F32, "dqn"); r_qn = Reg()
        qnb = K.sb([128, GD, 256], BF16, "dqnb"); r_qnb = Reg()
        K.stack = st
        scale = 32 ** -0.5
        for i0 in range(0, nT, GD):
            tss = [slice((i0 + i) * 128, (i0 + i + 1) * 128) for i in range(GD)]
            for i in range(GD):
                for k in range(8):
                    K.op("pe", lambda k=k, i=i: nc.tensor.matmul(ppb[i][:, 0:384], lhsT=hnT[:, k, tss[i]], rhs=wb[:, k, :],
                                                                 start=(k == 0), stop=(k == 7)),
                         R=[r_hnT, r_wb], W=[r_ppb[i]])
            for i in range(GD):
                K.op("act", lambda i=i: nc.scalar.activation(out=sq[:, i, :], in_=ppb[i][:, 0:256], func=AF.Square),
                     R=[r_ppb[i]], W=[r_sq])
            K.op("dve", lambda: nc.vector.tensor_reduce(out=ssq[:], in_=sq[:].rearrange("p t (g d) -> p (t g) d", d=32),
                                                        axis=AX.X, op=ALU.add), R=[r_sq], W=[r_ssq])
            rstd_explog(K, ssq[:], r_ssq, scr, 1.0 / 32)
            rs3 = scr["rstd"].rearrange("p (t g) -> p t g", g=8)
            K.op("dve", lambda: nc.vector.tensor_scalar(out=rs3[:, :, 0:4], in0=rs3[:, :, 0:4], scalar1=scale,
                                                        scalar2=None, op0=ALU.mult), R=[scr["r_rstd"]], W=[scr["r_rstd"]])
            for i in range(GD):
                K.op("dve", lambda i=i: nc.vector.tensor_tensor(
                    out=qn[:, i, :].rearrange("p (g d) -> p g d", d=32), in0=ppb[i][:, 0:256].rearrange("p (g d) -> p g d", d=32),
                    in1=rs3[:, i, :].unsqueeze(2).to_broadcast([128, 8, 32]), op=ALU.mult),
                    R=[r_ppb[i], scr["r_rstd"]], W=[r_qn])
                K.op("dve", lambda i=i: nc.vector.tensor_copy(out=vaug[:, i0 + i, :, 0:64],
                                                              in_=ppb[i][:, 256:384].rearrange("p (h d) -> p h d", d=64)),
                     R=[r_ppb[i]], W=[r_v[i0 + i]])
            K.op("dve", lambda: nc.vector.tensor_tensor(out=qnb[:], in0=qn[:], in1=qkw[:].unsqueeze(1).to_broadcast([128, GD, 256]),
                                                        op=ALU.mult), R=[r_qn, r_c], W=[r_qnb])
            for i in range(GD):
                ptr = ptrb[i // 2][:, (i % 2) * 4:(i % 2) * 4 + 4, :]; r_ptr = r_ptrb[i // 2]
                for a in range(2):
                    K.op("pe", lambda a=a, i=i: nc.tensor.transpose(out=ptr[0:96, 2 * a, :], in_=qnb[:, i, a * 128:a * 128 + 96],
                                                                    identity=C["ident_bf"][:]),
                         R=[r_qnb, C["r"]], W=[r_ptr])
                    K.op("pe", lambda a=a, i=i: nc.tensor.transpose(out=ptr[0:32, 2 * a + 1, :], in_=qnb[:, i, a * 128 + 96:a * 128 + 128],
                                                                    identity=C["ident_bf"][:]),
                         R=[r_qnb, C["r"]], W=[r_ptr])
            for i in range(GD):
                ptr = ptrb[i // 2][:, (i % 2) * 4:(i % 2) * 4 + 4, :]; r_ptr = r_ptrb[i // 2]
                ti = i0 + i
                K.op("act", lambda: nc.scalar.copy(out=qT[0:96, tss[i]], in_=ptr[0:96, 0, :]), R=[r_ptr], W=[r_qT[ti]])
                K.op("act", lambda: nc.scalar.copy(out=qTb[0:32, tss[i]], in_=ptr[0:32, 1, :]), R=[r_ptr], W=[r_qT[ti]])
                K.op("act", lambda: nc.scalar.copy(out=kT[0:96, tss[i]], in_=ptr[0:96, 2, :]), R=[r_ptr], W=[r_kT[ti]])
                K.op("act", lambda: nc.scalar.copy(out=kTb[0:32, tss[i]], in_=ptr[0:32, 3, :]), R=[r_ptr], W=[r_kT[ti]])

        K.barrier()
        st_d2.close()
        NSB = 4
        sbk = [K.ps([128, 512], F32, f"dsb{i}") for i in range(NSB)]; r_sb = [Reg() for _ in range(NSB)]
        acc = [K.ps([128, 4, 65], F32, f"dacc{i}") for i in range(4)]; r_acc = [Reg() for _ in range(4)]
        NP = 8
        pT = [K.sb([128, 512], BF16, f"dpT{i}") for i in range(NP)]; r_pT = [Reg() for _ in range(NP)]
        tmp = [K.sb([128, 512], F32, f"dtmp{i}") for i in range(2)]; r_tmp = [Reg() for _ in range(2)]
        rd = K.sb([128, 2, 4], F32, "drd"); r_rd = Reg()
        o1 = K.sb([128, 4, 64], F32, "do1"); r_o1 = Reg()
        o2 = K.sb([128, 4, 64], F32, "do2"); r_o2 = Reg()
        osq = K.sb([128, 4, 64], F32, "dosq"); r_osq = Reg()
        oss = K.sb([128, 4], F32, "doss"); r_oss = Reg()
        oscr = mk_scr(K, [128, 4], "do")
        yb = [K.sb([128, 4, 128], BF16, f"dyb{i}") for i in range(2)]; r_yb = [Reg() for _ in range(2)]
        ds_y = [K.new_dma_sem() for _ in range(2)]
        cnt = 0
        ntmp = 0
        ai = 0
        for t in range(nQ):
            ybt = yb[t % 2]
            for hl in range(2):
                accs = []
                items = []
                for c in range(2):
                    g = hl * 2 + c
                    A = acc[ai % 4]; rA = r_acc[ai % 4]; ai += 1
                    accs.append((A, rA))
                    K.op("pe", lambda: nc.tensor.matmul(A[:].rearrange("p a b -> p (a b)"), lhsT=zer[:, 0:128],
                                                        rhs=zer[:, 0:260], start=True, stop=False),
                         R=[r_z], W=[rA])
                for j in range(4 * t + 4):
                    for c in range(2):
                        items.append((c, hl * 2 + c, accs[c][0], accs[c][1], j))

                def emit_S(it):
                    nonlocal cnt, ntmp
                    c, g, A, rA, j = it
                    pr = slice(32 * g, 32 * g + 32) if g < 3 else slice(0, 32)
                    qTg = qT if g < 3 else qTb
                    kTg = kT if g < 3 else kTb
                    m = 4 * t - j
                    c0 = max(0, -m) * 128
                    sb_ = sbk[cnt % NSB]; rs = r_sb[cnt % NSB]
                    p_ = pT[cnt % NP]; rp = r_pT[cnt % NP]
                    cnt += 1
                    K.op("pe", lambda: nc.tensor.matmul(sb_[:, c0:512], lhsT=kTg[pr, j * 128:(j + 1) * 128],
                                                        rhs=qTg[pr, t * 512 + c0:(t + 1) * 512], start=True, stop=True),
                         R=[r_kT[j]] + [r_qT[4 * t + x] for x in range(c0 // 128, 4)], W=[rs])
                    if m >= 2:
                        K.op("act", lambda: nc.scalar.activation(out=p_[:, c0:512], in_=sb_[:, c0:512], func=AF.Exp,
                                                                 bias=c31[:, hl:hl + 1]),
                             R=[rs, r_c], W=[rp])
                    else:
                        tm_ = tmp[ntmp % 2]; rt = r_tmp[ntmp % 2]; ntmp += 1
                        b0 = 128 * m + 384
                        K.op("dve", lambda: nc.vector.tensor_tensor(out=tm_[:, c0:512], in0=sb_[:, c0:512],
                                                                    in1=Bt[:, hl, b0 + c0:b0 + 512], op=ALU.add),
                             R=[rs, r_c], W=[rt])
                        K.op("act", lambda: nc.scalar.activation(out=p_[:, c0:512], in_=tm_[:, c0:512], func=AF.Exp),
                             R=[rt], W=[rp])
                    return (p_, rp, c0)

                def emit_AV(it, pinfo):
                    c, g, A, rA, j = it
                    p_, rp, c0 = pinfo
                    for sub in range(c0 // 128, 4):
                        K.op("pe", lambda sub=sub: nc.tensor.matmul(
                            A[:, sub, :], lhsT=p_[:, sub * 128:(sub + 1) * 128], rhs=vaug[:, j, hl, :],
                            start=False, stop=(j == 4 * t + 3 and sub == 3)),
                            R=[rp, r_v[j]], W=[rA])

                LOOK = 4
                pend = []
                for idx in range(0, len(items), 4):
                    for it in items[idx:idx + 4]:
                        pend.append((it, emit_S(it)))
                    while len(pend) > LOOK:
                        emit_AV(*pend.pop(0))
                while pend:
                    emit_AV(*pend.pop(0))
                (A1, rA1), (A2, rA2) = accs
                K.op("dve", lambda: nc.vector.reciprocal(out=rd[:, 0, :], in_=A1[:, :, 64]), R=[rA1], W=[r_rd])
                K.op("dve", lambda: nc.vector.reciprocal(out=rd[:, 1, :], in_=A2[:, :, 64]), R=[rA2], W=[r_rd])
                K.op("dve", lambda: nc.vector.tensor_scalar(out=rd[:, 1, :], in0=rd[:, 1, :], scalar1=nlam[:, 0:1],
                                                            scalar2=None, op0=ALU.mult), R=[r_rd, r_nlam], W=[r_rd])
                K.op("dve", lambda: nc.vector.tensor_tensor(out=o1[:], in0=A1[:, :, 0:64],
                                                            in1=rd[:, 0, :].unsqueeze(2).to_broadcast([128, 4, 64]),
                                                            op=ALU.mult), R=[rA1, r_rd], W=[r_o1])
                K.op("dve", lambda: nc.vector.tensor_tensor(out=o2[:], in0=A2[:, :, 0:64],
                                                            in1=rd[:, 1, :].unsqueeze(2).to_broadcast([128, 4, 64]),
                                                            op=ALU.mult), R=[rA2, r_rd], W=[r_o2])
                K.op("pool", lambda: nc.gpsimd.tensor_tensor(out=o1[:], in0=o1[:], in1=o2[:], op=ALU.add),
                     R=[r_o1, r_o2], W=[r_o1])
                K.op("dve", lambda: nc.vector.tensor_tensor(out=osq[:], in0=o1[:], in1=o1[:], op=ALU.mult), R=[r_o1], W=[r_osq])
                K.op("dve", lambda: nc.vector.tensor_reduce(out=oss[:], in_=osq[:], axis=AX.X, op=ALU.add),
                     R=[r_osq], W=[r_oss])
                rstd_explog(K, oss[:], r_oss, oscr, 1.0 / 64)
                K.op("dve", lambda: nc.vector.tensor_tensor(out=o2[:], in0=o1[:],
                                                            in1=oscr["rstd"].unsqueeze(2).to_broadcast([128, 4, 64]),
                                                            op=ALU.mult), R=[r_o1, oscr["r_rstd"]], W=[r_o2])
                K.op("dve", lambda: nc.vector.tensor_tensor(out=ybt[:, :, hl * 64:(hl + 1) * 64], in0=o2[:],
                                                            in1=swl[:].unsqueeze(1).to_broadcast([128, 4, 64]),
                                                            op=ALU.mult), R=[r_o2, r_swl], W=[r_yb[t % 2]])
            K.dma("sp", y_d[t * 512:(t + 1) * 512, 384:512].rearrange("(s p) c -> p s c", p=128), ybt[:],
                  ds_y[t % 2], R=[r_yb[t % 2]], W=([y_d.reg(t * 512)] if hasattr(y_d, "reg") else []))
            if getattr(y_d, "hook", None) is not None:
                y_d.hook(t)
        K.barrier()
        K.end_phase()
        K.stack = outer


def load_mixer_consts(K, C, D):
    ds = C["dsem"]
    C["cf"] = K.sb([128, 1280], F32, "cf")
    C["sel"] = K.sb([2, 2, 128], F32, "sel")
    C["hsel"] = K.sb([2, 128], F32, "hsel")
    C["rowc"] = K.sb([2, 2, 512], F32, "rowc")
    K.dma("sp", C["cf"][:], D["cf"], ds, W=[C["r"]])
    K.dma("sp", C["sel"][:], D["sel"], ds, W=[C["r"]])
    K.dma("sp", C["hsel"][:], D["hsel"], ds, W=[C["r"]])
    K.dma("sp", C["rowc"][:], D["rowc"], ds, W=[C["r"]])
    C["ones"] = C["cf"][:, 0:128]
    C["tri"] = C["cf"][:, 128:256]
    C["nm2"] = C["cf"][:, 256:768]
    C["nm1"] = C["cf"][:, 768:1280]


def load_hnT(K, hn_d, S):
    hnT = K.sb([128, 8, S], BF16, "hnT")
    r = Reg("hnT")
    ds = K.new_dma_sem()
    for k in range(8):
        K.dma("sp" if k % 2 == 0 else "act", hnT[:, k, :], hn_d[k], ds, W=[r])
    return hnT, r


def emit_ssd_gen(K, C, hnT, r_hnT, P, y_d, S, W, nsets=2):
    STOP = 99
    nc = K.nc
    nT = S // 128
    nTT = S // 512
    if True:
        wfm, r_wfm = W["wS_fm"]
        wtm, r_wtm = W["wS_tm"]
        dsm = K.new_dma_sem()
        r_c = Reg("sconst")
        cw = K.sb([128, 4, 4], F32, "cw"); cb = K.sb([128, 4], F32, "cb")
        dtb = K.sb([128, 4], F32, "dtb"); alog = K.sb([128, 4], F32, "alog")
        dsk = K.sb([128, 4], F32, "dsk"); snw = K.sb([128, 256], F32, "snw")
        for t_, n_ in [(cw, "cw"), (cb, "cb"), (dtb, "dtb"), (alog, "alog"), (dsk, "dsk"), (snw, "snw")]:
            K.dma("sp", t_[:], P[n_], dsm, W=[r_c])
        Aneg = K.sb([128, 4], F32, "Aneg"); r_A = Reg()
        K.op("act", lambda: nc.scalar.activation(out=Aneg[:], in_=alog[:], func=AF.Exp), R=[r_c], W=[r_A])
        K.op("dve", lambda: nc.vector.tensor_scalar(out=Aneg[:], in0=Aneg[:], scalar1=-1.0, scalar2=None, op0=ALU.mult),
             R=[r_A], W=[r_A])
        xc = [K.sb([128, S], BF16, f"xc{i}") for i in range(4)]
        r_xc = [Reg() for _ in range(4)]
        def T(shape, dt, name):
            return K.sb(shape, dt, name), Reg(name)
        names = [("sz", [128, 256], F32), ("dtx", [128, 4], F32), ("ax", [128, 4], F32), ("ex", [128, 4], F32),
                 ("lx", [128, 4], F32), ("dt", [128, 4], F32), ("aa", [128, 4], F32), ("acs", [128, 4], F32),
                 ("nacs", [128, 4], F32), ("el", [128, 4], F32), ("cd", [128, 4], F32), ("dd", [128, 4], F32),
                 ("dec", [128, 4], F32), ("dtdec", [128, 4], F32), ("rseg", [128, 4, 128], F32),
                 ("segT", [128, 4, 128], F32), ("xdt", [128, 4, 64], BF16), ("xdd", [128, 4, 64], BF16),
                 ("xD", [128, 4, 64], F32), ("Btm", [128, 128], BF16), ("Gm", [128, 128], F32),
                 ("scT", [128, 4, 128], BF16), ("t1", [128, 4, 64], F32), ("gg", [128, 256], F32),
                 ("junk", [128, 256], F32), ("ssq", [128, 1], F32)]
        sets = []
        for par in range(nsets):
            d_ = {}
            for (n_, shp, dt_) in names:
                d_[n_] = T(shp, dt_, f"s{par}{n_}")
            d_["nscr"] = mk_scr(K, [128, 1], f"sn{par}")
            bA = psbank(K, f"sA{par}"); bB = psbank(K, f"sB{par}"); bC = psbank(K, f"sC{par}"); bD = psbank(K, f"sD{par}")
            d_["bA"] = bA; d_["bB"] = bB
            d_["rA"] = Reg(excl=True); d_["rB"] = Reg(excl=True); d_["rC"] = Reg(excl=True); d_["rD"] = Reg(excl=True)
            d_["pz"] = bA[:, 0:256]; d_["pst"] = bA[:, 256:512]
            d_["pseg"] = bB.rearrange("p (a b) -> p a b", b=128)
            d_["ptr"] = bC[:, 0:192].bitcast(BF16).rearrange("p (a b) -> p a b", b=128)
            d_["pG"] = bC[:, 192:320]; d_["pa"] = bC[:, 320:328]; d_["pdt"] = bC[:, 328:332]
            d_["py"] = bD[:, 0:256]; d_["pyo"] = bD[:, 256:512]
            sets.append(d_)
        Sf, r_Sf = T([128, 4, 64], F32, "sSf")
        Sbf, r_Sbf = T([128, 256], BF16, "sSbf")
        yb = [K.sb([128, 256], BF16, f"syb{i}") for i in range(2)]; r_yb = [Reg() for _ in range(2)]
        ds_y = [K.new_dma_sem() for _ in range(2)]
        K.op("dve", lambda: nc.vector.memset(Sf[:].rearrange("p a b -> p (a b)"), 0.0), W=[r_Sf])
        K.op("dve", lambda: nc.vector.memset(Sbf[:], 0.0), W=[r_Sbf])
        ident_f = C["ident_f"]
        if True:
            SH = S // 2
            xpre = K.sb([128, SH + 3], F32, "xpre"); r_xpre = Reg()
            cacc = K.sb([128, SH], F32, "cacc"); r_cacc = Reg()
            pf = [sets[0]["bA"], sets[0]["bB"]]; r_pf = [sets[0]["rA"], sets[0]["rB"]]
            n = 0
            for ct in range(4):
                for hf in range(2):
                    if hf == 0:
                        K.op("dve", lambda: nc.vector.memset(xpre[:, 0:3], 0.0), W=[r_xpre])
                    else:
                        K.op("dve", lambda: nc.vector.tensor_copy(out=xpre[:, 0:3], in_=xpre[:, SH:SH + 3]),
                             R=[r_xpre], W=[r_xpre])
                    for tt in range(nTT // 2):
                        tg_ = hf * (nTT // 2) + tt
                        p_ = pf[n % 2]; rp = r_pf[n % 2]; n += 1
                        for k in range(8):
                            K.op("pe", lambda k=k: nc.tensor.matmul(p_[:], lhsT=wfm[:, k, ct * 128:(ct + 1) * 128],
                                                                    rhs=hnT[:, k, tg_ * 512:(tg_ + 1) * 512],
                                                                    start=(k == 0), stop=(k == 7)),
                                 R=[r_wfm, r_hnT], W=[rp])
                        K.op("act", lambda: nc.scalar.copy(out=xpre[:, 3 + tt * 512:3 + (tt + 1) * 512], in_=p_[:]),
                             R=[rp], W=[r_xpre])
                        yield
                    K.op("dve", lambda: nc.vector.tensor_scalar(out=cacc[:], in0=xpre[:, 0:SH], scalar1=cw[:, ct, 0:1],
                                                                scalar2=None, op0=ALU.mult), R=[r_xpre, r_c], W=[r_cacc])
                    for j in range(1, 4):
                        K.op("dve", lambda j=j: nc.vector.scalar_tensor_tensor(
                            out=cacc[:], in0=xpre[:, j:SH + j], scalar=cw[:, ct, j:j + 1], in1=cacc[:],
                            op0=ALU.mult, op1=ALU.add), R=[r_xpre, r_c, r_cacc], W=[r_cacc])
                        yield
                    K.op("act", lambda: nc.scalar.activation(out=xc[ct][:, hf * SH:(hf + 1) * SH], in_=cacc[:], func=AF.Silu,
                                                             bias=cb[:, ct:ct + 1]),
                         R=[r_cacc, r_c], W=[r_xc[ct]])
                    yield
        state_done = [-1]

        def chunk_flow(c):
                cs = slice(c * 128, (c + 1) * 128)
                S_ = sets[c % nsets]
                (sz, r_sz), (dtx, r_dtx), (ax, r_ax), (ex, r_ex), (lx, r_lx), (dt, r_dt), (aa, r_aa), (acs, r_acs), \
                    (nacs, r_nacs), (el, r_el), (cd, r_cd), (dd, r_dd), (dec, r_dec), (dtdec, r_dtdec), (rseg, r_rseg), \
                    (segT, r_segT), (xdt, r_xdt), (xdd, r_xdd), (xD, r_xD), (Btm, r_Btm), (Gm, r_Gm), (scT, r_scT), \
                    (t1, r_t1), (gg, r_gg), (junk, r_junk), (ssq, r_ssq) = [S_[n_[0]] for n_ in names]
                nscr = S_["nscr"]
                pz, pst, pseg, ptr, pG, pa, pdt, py, pyo = [S_[n_] for n_ in ["pz", "pst", "pseg", "ptr", "pG", "pa", "pdt", "py", "pyo"]]
                r_pz = r_pst = S_["rA"]; r_pseg = S_["rB"]; r_ptr = r_pG = r_pa = r_pdt = S_["rC"]; r_py = r_pyo = S_["rD"]
                for k in range(8):
                    K.op("pe", lambda k=k: nc.tensor.matmul(pz[:], lhsT=hnT[:, k, cs], rhs=wtm[:, k, 0:256],
                                                            start=(k == 0), stop=(k == 7)), R=[r_hnT, r_wtm], W=[r_pz])
                for k in range(8):
                    K.op("pe", lambda k=k: nc.tensor.matmul(pdt[:], lhsT=hnT[:, k, cs], rhs=wtm[:, k, 256:260],
                                                            start=(k == 0), stop=(k == 7)), R=[r_hnT, r_wtm], W=[r_pdt])
                yield
                K.op("act", lambda: nc.scalar.activation(out=sz[:], in_=pz[:, 0:256], func=AF.Silu), R=[r_pz], W=[r_sz])
                yield
                K.op("dve", lambda: nc.vector.tensor_tensor(out=dtx[:], in0=pdt[:], in1=dtb[:], op=ALU.add),
                     R=[r_pdt, r_c], W=[r_dtx])
                yield
                K.op("dve", lambda: nc.vector.scalar_tensor_tensor(out=ax[:], in0=dtx[:], scalar=-1.0, in1=dtx[:],
                                                                   op0=ALU.mult, op1=ALU.min), R=[r_dtx], W=[r_ax])
                yield
                K.op("act", lambda: nc.scalar.activation(out=ex[:], in_=ax[:], func=AF.Exp), R=[r_ax], W=[r_ex])
                yield
                K.op("dve", lambda: nc.vector.tensor_scalar(out=ex[:], in0=ex[:], scalar1=1.0, scalar2=None, op0=ALU.add),
                     R=[r_ex], W=[r_ex])
                yield
                K.op("act", lambda: nc.scalar.activation(out=lx[:], in_=ex[:], func=AF.Ln), R=[r_ex], W=[r_lx])
                yield
                K.op("dve", lambda: nc.vector.scalar_tensor_tensor(out=dt[:], in0=dtx[:], scalar=0.0, in1=lx[:],
                                                                   op0=ALU.max, op1=ALU.add), R=[r_dtx, r_lx], W=[r_dt])
                yield
                K.op("dve", lambda: nc.vector.tensor_tensor(out=aa[:], in0=dt[:], in1=Aneg[:], op=ALU.mult),
                     R=[r_dt, r_A], W=[r_aa])
                yield
                K.op("pe", lambda: nc.tensor.matmul(pa[:, 0:4], lhsT=C["tri"], rhs=aa[:], start=True, stop=True),
                     R=[r_aa, C["r"]], W=[r_pa])
                yield
                K.op("pe", lambda: nc.tensor.matmul(pa[:, 4:8], lhsT=C["ones"], rhs=aa[:], start=True, stop=True),
                     R=[r_aa, C["r"]], W=[r_pa])
                yield
                K.op("dve", lambda: nc.vector.tensor_copy(out=acs[:], in_=pa[:, 0:4]), R=[r_pa], W=[r_acs])
                yield
                K.op("dve", lambda: nc.vector.tensor_scalar(out=nacs[:], in0=pa[:, 0:4], scalar1=-1.0, scalar2=None,
                                                            op0=ALU.mult), R=[r_pa], W=[r_nacs])
                yield
                K.op("act", lambda: nc.scalar.activation(out=el[:], in_=pa[:, 0:4], func=AF.Exp), R=[r_pa], W=[r_el])
                yield
                K.op("act", lambda: nc.scalar.activation(out=cd[:], in_=pa[:, 4:8], func=AF.Exp), R=[r_pa], W=[r_cd])
                yield
                K.op("dve", lambda: nc.vector.tensor_tensor(out=dd[:], in0=pa[:, 4:8], in1=acs[:], op=ALU.subtract),
                     R=[r_pa, r_acs], W=[r_dd])
                yield
                K.op("act", lambda: nc.scalar.activation(out=dec[:], in_=dd[:], func=AF.Exp), R=[r_dd], W=[r_dec])
                yield
                K.op("dve", lambda: nc.vector.tensor_tensor(out=dtdec[:], in0=dt[:], in1=dec[:], op=ALU.mult),
                     R=[r_dt, r_dec], W=[r_dtdec])
                yield
                K.op("dve", lambda: nc.vector.tensor_tensor(out=rseg[:], in0=ident_f[:].unsqueeze(1).to_broadcast([128, 4, 128]),
                                                            in1=acs[:].unsqueeze(2).to_broadcast([128, 4, 128]), op=ALU.mult),
                     R=[C["r"], r_acs], W=[r_rseg])
                yield
                K.op("pe", lambda: nc.tensor.matmul(pseg[:].rearrange("p a b -> p (a b)"), lhsT=C["ones"],
                                                    rhs=rseg[:].rearrange("p a b -> p (a b)"), start=True, stop=False),
                     R=[r_rseg, C["r"]], W=[r_pseg])
                yield
                K.op("pe", lambda: nc.tensor.matmul(pseg[:].rearrange("p a b -> p (a b)"), lhsT=ident_f[:],
                                                    rhs=C["nm1"], start=False, stop=True),
                     R=[C["r"]], W=[r_pseg])
                for h in range(4):
                    K.op("act", lambda h=h: nc.scalar.activation(out=segT[:, h, :], in_=pseg[:, h, :], func=AF.Exp,
                                                                 bias=nacs[:, h:h + 1]), R=[r_pseg, r_nacs], W=[r_segT])
                yield
                for a in range(3):
                    K.op("pe", lambda a=a: nc.tensor.transpose(out=ptr[:, a, :], in_=xc[a][:, cs], identity=C["ident_bf"][:]),
                         R=[r_xc[a], C["r"]], W=[r_ptr])
                xs_v = ptr[:, 0:2, :].rearrange("p a (h d) -> p (a h) d", d=64)
                yield
                K.op("dve", lambda: nc.vector.tensor_tensor(out=xdt[:], in0=xs_v, in1=dt[:].unsqueeze(2).to_broadcast([128, 4, 64]),
                                                            op=ALU.mult), R=[r_ptr, r_dt], W=[r_xdt])
                yield
                K.op("dve", lambda: nc.vector.tensor_tensor(out=xdd[:], in0=xs_v, in1=dtdec[:].unsqueeze(2).to_broadcast([128, 4, 64]),
                                                            op=ALU.mult), R=[r_ptr, r_dtdec], W=[r_xdd])
                yield
                K.op("dve", lambda: nc.vector.tensor_tensor(out=xD[:], in0=xs_v, in1=dsk[:].unsqueeze(2).to_broadcast([128, 4, 64]),
                                                            op=ALU.mult), R=[r_ptr, r_c], W=[r_xD])
                yield
                K.op("act", lambda: nc.scalar.copy(out=Btm[:], in_=ptr[:, 2, :]), R=[r_ptr], W=[r_Btm])
                yield
                K.op("pe", lambda: nc.tensor.matmul(pG[:], lhsT=xc[2][:, cs], rhs=xc[3][:, cs], start=True, stop=True),
                     R=[r_xc[2], r_xc[3]], W=[r_pG])
                yield
                K.op("dve", lambda: nc.vector.tensor_tensor(out=Gm[:], in0=pG[:], in1=C["tri"], op=ALU.mult),
                     R=[r_pG, C["r"]], W=[r_Gm])
                yield
                K.op("dve", lambda: nc.vector.tensor_tensor(out=scT[:], in0=Gm[:].unsqueeze(1).to_broadcast([128, 4, 128]),
                                                            in1=segT[:], op=ALU.mult), R=[r_Gm, r_segT], W=[r_scT])
                for h in range(4):
                    K.op("pe", lambda h=h: nc.tensor.matmul(py[:, h * 64:(h + 1) * 64], lhsT=scT[:, h, :], rhs=xdt[:, h, :],
                                                            start=True, stop=True), R=[r_scT, r_xdt], W=[r_py])
                yield
                while state_done[0] < c - 1:
                    yield
                K.op("pe", lambda: nc.tensor.matmul(pyo[:], lhsT=xc[3][:, cs], rhs=Sbf[:], start=True, stop=True),
                     R=[r_xc[3], r_Sbf], W=[r_pyo])
                yield
                K.op("pe", lambda: nc.tensor.matmul(pst[:], lhsT=Btm[:], rhs=xdd[:].rearrange("p a b -> p (a b)"),
                                                    start=True, stop=True), R=[r_Btm, r_xdd], W=[r_pst])
                yield
                K.op("pool", lambda: nc.gpsimd.tensor_tensor(out=Sf[:], in0=Sf[:], in1=cd[:].unsqueeze(2).to_broadcast([128, 4, 64]),
                                                             op=ALU.mult), R=[r_Sf, r_cd], W=[r_Sf])
                yield
                K.op("dve", lambda: nc.vector.tensor_tensor(out=Sf[:].rearrange("p a b -> p (a b)"),
                                                            in0=Sf[:].rearrange("p a b -> p (a b)"), in1=pst[:], op=ALU.add),
                     R=[r_Sf, r_pst], W=[r_Sf])
                yield
                K.op("act", lambda: nc.scalar.copy(out=Sbf[:], in_=Sf[:].rearrange("p a b -> p (a b)")), R=[r_Sf], W=[r_Sbf])
                state_done[0] = c
                yield
                K.op("dve", lambda: nc.vector.tensor_tensor(out=t1[:], in0=pyo[:].rearrange("p (a b) -> p a b", b=64),
                                                            in1=el[:].unsqueeze(2).to_broadcast([128, 4, 64]), op=ALU.mult),
                     R=[r_pyo, r_el], W=[r_t1])
                yield
                K.op("dve", lambda: nc.vector.tensor_tensor(out=t1[:].rearrange("p a b -> p (a b)"),
                                                            in0=t1[:].rearrange("p a b -> p (a b)"), in1=py[:], op=ALU.add),
                     R=[r_t1, r_py], W=[r_t1])
                yield
                K.op("pool", lambda: nc.gpsimd.tensor_tensor(out=t1[:], in0=t1[:], in1=xD[:], op=ALU.add),
                     R=[r_t1, r_xD], W=[r_t1])
                yield
                K.op("pool", lambda: nc.gpsimd.tensor_tensor(out=gg[:], in0=t1[:].rearrange("p a b -> p (a b)"), in1=sz[:],
                                                             op=ALU.mult), R=[r_t1, r_sz], W=[r_gg])
                yield
                K.op("act", lambda: nc.scalar.activation(out=junk[:], in_=gg[:], func=AF.Square, accum_out=ssq[:]),
                     R=[r_gg], W=[r_junk, r_ssq])
                rstd_from_ssq(K, ssq[:], r_ssq, 1, nscr, 1.0 / 256)
                yb_ = yb[c % 2]
                yield
                K.op("dve", lambda: nc.vector.scalar_tensor_tensor(out=yb_[:], in0=gg[:], scalar=nscr["rstd"], in1=snw[:],
                                                                   op0=ALU.mult, op1=ALU.mult),
                     R=[r_gg, nscr["r_rstd"], r_c], W=[r_yb[c % 2]])
                K.dma("sp", y_d[cs, 128:384], yb_[:], ds_y[c % 2], R=[r_yb[c % 2]])
        nrun = nT if STOP > 1 else 0
        active = []
        nxt_c = 0
        while nxt_c < nrun or active:
            while len(active) < nsets and nxt_c < nrun:
                active.append(chunk_flow(nxt_c))
                nxt_c += 1
            for g_ in list(active):
                try:
                    next(g_)
                except StopIteration:
                    active.remove(g_)
            yield


def run_gen(g):
    for _ in g:
        pass


def emit_ssd(K, C, hnT, r_hnT, P, y_d, S):
    outer = K.stack
    with ExitStack() as st:
        K.stack = st
        K.begin_phase()
        W = {"wS_fm": load_w(K, P["wS_fm"], 512, "wSf"), "wS_tm": load_w(K, P["wS_tm"], 260, "wSt")}
        run_gen(emit_ssd_gen(K, C, hnT, r_hnT, P, y_d, S, W, nsets=2))
        K.barrier()
        K.end_phase()
        K.stack = outer


def emit_mlstm_gen(K, C, hnT, r_hnT, P, y_d, S, W):
    nc = K.nc
    nB = S // 512
    if True:
        wfm, r_wfm = W["wM_fm"]
        wg, r_wg = W["wM_g"]
        wtm, r_wtm = W["wM_tm"]
        dsm = K.new_dma_sem()
        r_c = Reg("mconst")
        gbias = K.sb([2, 2], F32, "gbias"); mnw = K.sb([128, 128], F32, "mnw")
        K.dma("sp", gbias[:], P["gbias"], dsm, W=[r_c])
        K.dma("sp", mnw[:], P["mnw"], dsm, W=[r_c])
        ident_f = C["ident_f"]
        B = [psbank(K, f"mb{i}") for i in range(4)]
        rB = [Reg(f"mb{i}", excl=True) for i in range(4)]
        pq, pk, pgi, pgf = B[2], B[3], B[0][0:2, :], B[1][0:2, :]
        ptl = B[2][:, 0:32].rearrange("p (q i h) -> p q i h", q=4, i=4)
        pdec = B[2][:, 32:40]
        pDt = B[2][:, 0:256].rearrange("p (h t) -> p h t", t=128)
        ptm = B[3][:, 0:384]

        def T(shape, dt, name):
            return K.sb(shape, dt, name), Reg(name)
        qTb, r_qTb = T([128, 512], BF16, "mqTb")
        kTb, r_kTb = T([128, 512], BF16, "mkTb")
        rows = {}
        for n_ in ["ipre", "yv", "e", "b", "al", "cma", "mu", "nmu", "wrow", "inter", "en", "tmp"]:
            rows[n_] = T([2, 512], F32, "mr_" + n_)
        rows["nab"] = rows["e"]; rows["l"] = rows["e"]
        rows["logf"] = rows["yv"]
        mnew, r_mnew = T([2, 8], F32, "mnew")
        mprev, r_mprev = T([2, 8], F32, "mprev")
        mcar, r_mcar = T([2, 1], F32, "mcar")
        decay, r_decay = T([2, 8], F32, "mdecay")
        tl, r_tl = T([128, 4, 4, 2], F32, "mtl")
        decr, r_decr = T([128, 8], F32, "mdecr")
        ktm, r_ktm = T([128, 128], F32, "mktm")
        vaug, r_vaug = T([128, 2, 65], BF16, "mvaug")
        og, r_og = T([128, 128], F32, "mog")
        dT, r_dT = T([128, 128], F32, "mdT")
        sdT, r_sdT = T([128, 128], BF16, "msdT")
        kw, r_kw = T([128, 64], BF16, "mkw")
        Cst, r_Cst = T([128, 65], F32, "mCst")
        Cbf, r_Cbf = T([128, 65], BF16, "mCbf")
        nmv, r_nmv = T([128, 65], F32, "mnmv")
        dn, r_dn = T([128, 1], F32, "mdn")
        rn, r_rn = T([128, 1], F32, "mrn")
        hm, r_hm = T([128, 64], F32, "mhm")
        junk, r_junk = T([128, 64], F32, "mjunk")
        ssq, r_ssq = T([128, 1], F32, "mssq")
        nscr = mk_scr(K, [128, 1], "mn")
        hn2, r_hn2 = T([128, 64], F32, "mhn2")
        yb = [K.sb([128, 128], BF16, f"myb{i}") for i in range(2)]; r_yb = [Reg() for _ in range(2)]
        ds_y = [K.new_dma_sem() for _ in range(2)]
        r_Cst = [Reg("Cst0"), Reg("Cst1")]
        r_Cbf = [Reg("Cbf0"), Reg("Cbf1")]
        K.op("dve", lambda: nc.vector.memset(Cst[:], 0.0), W=r_Cst)
        K.op("dve", lambda: nc.vector.memset(Cbf[:], 0.0), W=r_Cbf)
        K.op("dve", lambda: nc.vector.memset(mcar[:], 0.0), W=[r_mcar])
        ktm2 = [T([128, 128], F32, f"mktm{i}") for i in range(2)]
        vaug2 = [T([128, 2, 65], BF16, f"mvaug{i}") for i in range(2)]
        og2 = [T([128, 128], F32, f"mog{i}") for i in range(2)]
        for i_ in range(2):
            K.op("dve", lambda i_=i_: nc.vector.memset(vaug2[i_][0][:].rearrange("p a b -> p (a b)"), 1.0), W=[vaug2[i_][1]])
        r_yb2 = [[Reg(), Reg()] for _ in range(2)]
        HT = []
        for h_ in range(2):
            d_ = {}
            d_["dT"] = T([128, 128], F32, f"mdT{h_}")
            d_["sdT"] = T([128, 128], BF16, f"msdT{h_}")
            d_["kw"] = T([128, 64], BF16, f"mkw{h_}")
            d_["nmv"] = T([128, 65], F32, f"mnmv{h_}")
            d_["dn"] = T([128, 1], F32, f"mdn{h_}")
            d_["rn"] = T([128, 1], F32, f"mrn{h_}")
            d_["hm"] = T([128, 64], F32, f"mhm{h_}")
            d_["junk"] = T([128, 64], F32, f"mjunk{h_}")
            d_["ssq"] = T([128, 1], F32, f"mssq{h_}")
            d_["hn2"] = T([128, 64], F32, f"mhn2{h_}")
            d_["nscr"] = mk_scr(K, [128, 1], f"mn{h_}")
            HT.append(d_)

        def R_(n_):
            return rows[n_][0]

        def rr(n_):
            return rows[n_][1]
        rowc = C["rowc"]
        ntile = 0
        for b in range(nB):
            bs = slice(b * 512, (b + 1) * 512)
            for (pp_, rp_, c0, dst, rdst, sc) in [(pq, rB[2], 0, qTb, r_qTb, 1.0), (pk, rB[3], 128, kTb, r_kTb, 0.125)]:
                for k in range(8):
                    K.op("pe", lambda k=k: nc.tensor.matmul(pp_, lhsT=wfm[:, k, c0:c0 + 128], rhs=hnT[:, k, bs],
                                                            start=(k == 0), stop=(k == 7)), R=[r_wfm, r_hnT], W=[rp_])
                K.op("act", lambda: nc.scalar.mul(out=dst[:], in_=pp_, mul=sc), R=[rp_], W=[rdst])
            for (pp_, rp_, c0) in [(pgi, rB[0], 0), (pgf, rB[1], 2)]:
                for k in range(8):
                    K.op("pe", lambda k=k: nc.tensor.matmul(pp_, lhsT=wg[:, k, c0:c0 + 2], rhs=hnT[:, k, bs],
                                                            start=(k == 0), stop=(k == 7)), R=[r_wg, r_hnT], W=[rp_])
            yield
            K.op("dve", lambda: nc.vector.tensor_scalar(out=R_("ipre")[:], in0=pgi, scalar1=gbias[:, 0:1], scalar2=None,
                                                        op0=ALU.add), R=[rB[0], r_c], W=[rr("ipre")])
            yield
            K.op("dve", lambda: nc.vector.tensor_scalar(out=R_("yv")[:], in0=pgf, scalar1=gbias[:, 1:2], scalar2=-1.0,
                                                        op0=ALU.add, op1=ALU.mult), R=[rB[1], r_c], W=[rr("yv")])
            yield
            K.op("dve", lambda: nc.vector.scalar_tensor_tensor(out=R_("nab")[:], in0=R_("yv")[:], scalar=-1.0, in1=R_("yv")[:],
                                                               op0=ALU.mult, op1=ALU.min), R=[rr("yv")], W=[rr("nab")])
            yield
            K.op("act", lambda: nc.scalar.activation(out=R_("e")[:], in_=R_("nab")[:], func=AF.Exp), R=[rr("nab")], W=[rr("e")])
            yield
            K.op("dve", lambda: nc.vector.tensor_scalar(out=R_("e")[:], in0=R_("e")[:], scalar1=1.0, scalar2=None, op0=ALU.add),
                 R=[rr("e")], W=[rr("e")])
            yield
            K.op("act", lambda: nc.scalar.activation(out=R_("l")[:], in_=R_("e")[:], func=AF.Ln), R=[rr("e")], W=[rr("l")])
            yield
            K.op("dve", lambda: nc.vector.scalar_tensor_tensor(out=R_("logf")[:], in0=R_("yv")[:], scalar=0.0, in1=R_("l")[:],
                                                               op0=ALU.max, op1=ALU.add), R=[rr("yv"), rr("l")], W=[rr("logf")])
            yield
            K.op("dve", lambda: nc.vector.tensor_scalar(out=R_("logf")[:], in0=R_("logf")[:], scalar1=-1.0, scalar2=None,
                                                        op0=ALU.mult), R=[rr("logf")], W=[rr("logf")])
            yield
            K.op("dve", lambda: nc.vector.tensor_tensor_scan(out=R_("b")[:], data0=rowc[:, 0, :], data1=R_("logf")[:],
                                                             initial=0.0, op0=ALU.mult, op1=ALU.add),
                 R=[rr("logf"), C["r"]], W=[rr("b")])
            yield
            K.op("dve", lambda: nc.vector.tensor_tensor(out=R_("al")[:], in0=R_("ipre")[:], in1=R_("b")[:], op=ALU.subtract),
                 R=[rr("ipre"), rr("b")], W=[rr("al")])
            yield
            K.op("dve", lambda: nc.vector.tensor_tensor_scan(out=R_("cma")[:], data0=rowc[:, 1, :], data1=R_("al")[:],
                                                             initial=0.0, op0=ALU.add, op1=ALU.max),
                 R=[rr("al"), C["r"]], W=[rr("cma")])
            cma3 = R_("cma")[:].rearrange("p (c l) -> p c l", l=64)
            b3 = R_("b")[:].rearrange("p (c l) -> p c l", l=64)
            al3 = R_("al")[:].rearrange("p (c l) -> p c l", l=64)
            mu3 = R_("mu")[:].rearrange("p (c l) -> p c l", l=64)
            tmp3 = R_("tmp")[:].rearrange("p (c l) -> p c l", l=64)
            yield
            K.op("dve", lambda: nc.vector.tensor_tensor_scan(out=mnew[:], data0=cma3[:, :, 63], data1=b3[:, :, 63],
                                                             initial=mcar[:, 0:1], op0=ALU.max, op1=ALU.add),
                 R=[rr("cma"), rr("b"), r_mcar], W=[r_mnew])
            yield
            K.op("dve", lambda: nc.vector.tensor_copy(out=mprev[:, 0:1], in_=mcar[:]), R=[r_mcar], W=[r_mprev])
            yield
            K.op("dve", lambda: nc.vector.tensor_copy(out=mprev[:, 1:8], in_=mnew[:, 0:7]), R=[r_mnew], W=[r_mprev])
            yield
            K.op("dve", lambda: nc.vector.tensor_copy(out=mcar[:], in_=mnew[:, 7:8]), R=[r_mnew, r_mprev], W=[r_mcar])
            mpb = mprev[:].unsqueeze(2).to_broadcast([2, 8, 64])
            yield
            K.op("dve", lambda: nc.vector.tensor_tensor(out=mu3, in0=cma3, in1=mpb, op=ALU.max),
                 R=[rr("cma"), r_mprev], W=[rr("mu")])
            yield
            K.op("dve", lambda: nc.vector.tensor_scalar(out=R_("nmu")[:], in0=R_("mu")[:], scalar1=-1.0, scalar2=None,
                                                        op0=ALU.mult), R=[rr("mu")], W=[rr("nmu")])
            mcb = mu3[:, :, 63].unsqueeze(2).to_broadcast([2, 8, 64])
            yield
            K.op("dve", lambda: nc.vector.tensor_tensor(out=tmp3, in0=al3, in1=mcb, op=ALU.subtract),
                 R=[rr("al"), rr("mu")], W=[rr("tmp")])
            yield
            K.op("act", lambda: nc.scalar.activation(out=R_("wrow")[:], in_=R_("tmp")[:], func=AF.Exp), R=[rr("tmp")], W=[rr("wrow")])
            yield
            K.op("dve", lambda: nc.vector.tensor_tensor(out=decay[:], in0=mprev[:], in1=mu3[:, :, 63], op=ALU.subtract),
                 R=[r_mprev, rr("mu")], W=[r_decay])
            yield
            K.op("act", lambda: nc.scalar.activation(out=decay[:], in_=decay[:], func=AF.Exp), R=[r_decay], W=[r_decay])
            yield
            K.op("dve", lambda: nc.vector.tensor_tensor(out=tmp3, in0=mu3, in1=mpb, op=ALU.subtract),
                 R=[rr("mu"), r_mprev, rr("wrow")], W=[rr("tmp")])
            yield
            K.op("act", lambda: nc.scalar.activation(out=R_("inter")[:], in_=R_("tmp")[:], func=AF.Exp, scale=-1.0),
                 R=[rr("tmp")], W=[rr("inter")])
            yield
            K.op("dve", lambda: nc.vector.tensor_tensor(out=R_("tmp")[:], in0=R_("b")[:], in1=R_("mu")[:], op=ALU.add),
                 R=[rr("b"), rr("mu"), rr("inter")], W=[rr("tmp")])
            yield
            K.op("act", lambda: nc.scalar.activation(out=R_("en")[:], in_=R_("tmp")[:], func=AF.Exp, scale=-1.0),
                 R=[rr("tmp")], W=[rr("en")])
            for qi, qn_ in enumerate(["al", "wrow", "inter", "en"]):
                for i in range(4):
                    K.op("pe", lambda qi=qi, i=i, qn_=qn_: nc.tensor.transpose(
                        out=ptl[:, qi, i, :], in_=R_(qn_)[0:2, i * 128:(i + 1) * 128], identity=ident_f[0:2, 0:2]),
                        R=[rr(qn_), C["r"]], W=[rB[2]])
            yield
            K.op("pe", lambda: nc.tensor.matmul(pdec, lhsT=C["hsel"][:], rhs=decay[:], start=True, stop=True),
                 R=[r_decay, C["r"]], W=[rB[2]])
            yield
            K.op("dve", lambda: nc.vector.tensor_copy(out=tl[:], in_=ptl), R=[rB[2]], W=[r_tl])
            yield
            K.op("dve", lambda: nc.vector.tensor_copy(out=decr[:], in_=pdec), R=[rB[2]], W=[r_decr])
            for i in range(4):
                tg = b * 4 + i
                ts = slice(tg * 128, (tg + 1) * 128)
                tb = slice(i * 128, (i + 1) * 128)
                par = ntile % 2
                ktm_, r_ktm_ = ktm2[par]
                vaug_, r_vaug_ = vaug2[par]
                og_, r_og_ = og2[par]
                for k in range(8):
                    K.op("pe", lambda k=k: nc.tensor.matmul(ptm, lhsT=hnT[:, k, ts], rhs=wtm[:, k, :],
                                                            start=(k == 0), stop=(k == 7)), R=[r_hnT, r_wtm], W=[rB[3]])
                K.op("act", lambda: nc.scalar.mul(out=ktm_[:], in_=ptm[:, 0:128], mul=0.125), R=[rB[3]], W=[r_ktm_])
                K.op("dve", lambda: nc.vector.tensor_copy(out=vaug_[:, :, 0:64],
                                                          in_=ptm[:, 128:256].rearrange("p (h d) -> p h d", d=64)),
                     R=[rB[3]], W=[r_vaug_])
                K.op("act", lambda: nc.scalar.activation(out=og_[:], in_=ptm[:, 256:384], func=AF.Sigmoid), R=[rB[3]], W=[r_og_])
                yb_ = yb[par]
                ryb2 = r_yb2[par]
                for h in range(2):
                    K.op("pe", lambda h=h: nc.tensor.matmul(pDt[:, h, :], lhsT=C["sel"][:, h, :],
                                                            rhs=R_("nmu")[0:2, tb], start=True, stop=False),
                         R=[rr("nmu"), C["r"]], W=[rB[2]])
                    K.op("pe", lambda h=h: nc.tensor.matmul(pDt[:, h, :], lhsT=ident_f[:],
                                                            rhs=C["nm2"][:, 0:128], start=False, stop=True),
                         R=[C["r"]], W=[rB[2]])
                yield

                def head_flow(h):
                    hp = slice(64 * h, 64 * h + 64)
                    hc = slice(64 * h, 64 * h + 64)
                    PB = B[h]; rPB = rB[h]
                    pS = PB[:, 0:128]; pN = PB[:, 128:193]
                    pQs = [PB[:, 200:265], PB[:, 272:337]]
                    pC = PB[:, 344:409]
                    Hh = HT[h]
                    dT, r_dT = Hh["dT"]; sdT, r_sdT = Hh["sdT"]; kw, r_kw = Hh["kw"]; nmv, r_nmv = Hh["nmv"]
                    dn, r_dn = Hh["dn"]; rn, r_rn = Hh["rn"]; hm, r_hm = Hh["hm"]; junk, r_junk = Hh["junk"]
                    ssq, r_ssq = Hh["ssq"]; hn2, r_hn2 = Hh["hn2"]; nscr = Hh["nscr"]
                    K.op("pe", lambda: nc.tensor.matmul(pS, lhsT=kTb[hp, tb], rhs=qTb[hp, tb], start=True, stop=True),
                         R=[r_kTb, r_qTb], W=[rPB])
                    K.op("act", lambda: nc.scalar.activation(out=dT[:], in_=pDt[:, h, :], func=AF.Exp,
                                                             bias=tl[:, 0, i, h:h + 1]), R=[rB[2], r_tl], W=[r_dT])
                    yield
                    K.op("dve", lambda: nc.vector.tensor_tensor(out=sdT[:], in0=pS, in1=dT[:], op=ALU.mult),
                         R=[rPB, r_dT], W=[r_sdT])
                    K.op("pe", lambda: nc.tensor.matmul(pN, lhsT=sdT[:], rhs=vaug_[:, h, :], start=True, stop=True),
                         R=[r_sdT, r_vaug_], W=[rPB])
                    yield
                    K.op("dve", lambda: nc.vector.tensor_copy(out=nmv[:], in_=pN), R=[rPB], W=[r_nmv])
                    for half in range(2):
                        ce = 2 * i + half
                        rs_ = slice(64 * half, 64 * half + 64)
                        pQ = pQs[half]
                        K.op("pe", lambda: nc.tensor.matmul(pQ, lhsT=qTb[hp, tb], rhs=Cbf[hp, :], start=True, stop=True),
                             R=[r_qTb, r_Cbf[h]], W=[rPB])
                        K.op("dve", lambda: nc.vector.tensor_scalar(out=kw[rs_, :], in0=ktm_[rs_, hc], scalar1=tl[rs_, 1, i, h:h + 1],
                                                                    scalar2=None, op0=ALU.mult), R=[r_ktm_, r_tl], W=[r_kw])
                        yield
                        K.op("dve", lambda: nc.vector.scalar_tensor_tensor(
                            out=nmv[rs_, :], in0=pQ[rs_, :], scalar=tl[rs_, 2, i, h:h + 1], in1=nmv[rs_, :],
                            op0=ALU.mult, op1=ALU.add), R=[rPB, r_tl, r_nmv], W=[r_nmv])
                        K.op("pe", lambda: nc.tensor.matmul(pC[hp, :], lhsT=kw[rs_, :], rhs=vaug_[rs_, h, :], start=True, stop=True),
                             R=[r_kw, r_vaug_], W=[rPB])
                        yield
                        K.op("dve", lambda: nc.vector.scalar_tensor_tensor(
                            out=Cst[hp, :], in0=Cst[hp, :], scalar=decr[hp, ce:ce + 1], in1=pC[hp, :],
                            op0=ALU.mult, op1=ALU.add), R=[r_Cst[h], r_decr, rPB], W=[r_Cst[h]])
                        K.op("act", lambda: nc.scalar.copy(out=Cbf[hp, :], in_=Cst[hp, :]), R=[r_Cst[h]], W=[r_Cbf[h]])
                        yield
                    K.op("dve", lambda: nc.vector.scalar_tensor_tensor(out=dn[:], in0=nmv[:, 64:65], scalar=-1.0, in1=nmv[:, 64:65],
                                                                       op0=ALU.mult, op1=ALU.max), R=[r_nmv], W=[r_dn])
                    K.op("dve", lambda: nc.vector.tensor_tensor(out=dn[:], in0=dn[:], in1=tl[:, 3, i, h:h + 1], op=ALU.max),
                         R=[r_dn, r_tl], W=[r_dn])
                    yield
                    K.op("dve", lambda: nc.vector.reciprocal(out=rn[:], in_=dn[:]), R=[r_dn], W=[r_rn])
                    K.op("dve", lambda: nc.vector.tensor_scalar(out=hm[:], in0=nmv[:, 0:64], scalar1=rn[:, 0:1], scalar2=None,
                                                                op0=ALU.mult), R=[r_nmv, r_rn], W=[r_hm])
                    yield
                    K.op("act", lambda: nc.scalar.activation(out=junk[:], in_=hm[:], func=AF.Square, accum_out=ssq[:]),
                         R=[r_hm], W=[r_junk, r_ssq])
                    yield
                    rstd_from_ssq(K, ssq[:], r_ssq, 1, nscr, 1.0 / 64)
                    yield
                    K.op("dve", lambda: nc.vector.scalar_tensor_tensor(out=hn2[:], in0=hm[:], scalar=nscr["rstd"], in1=mnw[:, hc],
                                                                       op0=ALU.mult, op1=ALU.mult),
                         R=[r_hm, nscr["r_rstd"], r_c], W=[r_hn2])
                    K.op("dve", lambda: nc.vector.tensor_tensor(out=yb_[:, hc], in0=hn2[:], in1=og_[:, hc], op=ALU.mult),
                         R=[r_hn2, r_og_], W=[ryb2[h]])

                gens = [head_flow(0), head_flow(1)]
                while gens:
                    for g_ in list(gens):
                        try:
                            next(g_)
                        except StopIteration:
                            gens.remove(g_)
                    yield
                K.dma("sp", y_d[ts, 0:128], yb_[:], ds_y[par], R=ryb2)
                ntile += 1


def emit_mlstm(K, C, hnT, r_hnT, P, y_d, S):
    outer = K.stack
    with ExitStack() as st:
        K.stack = st
        K.begin_phase()
        W = {"wM_fm": load_w(K, P["wM_fm"], 256, "wMf"), "wM_g": load_w(K, P["wM_g"], 4, "wMg"),
             "wM_tm": load_w(K, P["wM_tm"], 384, "wMt")}
        run_gen(emit_mlstm_gen(K, C, hnT, r_hnT, P, y_d, S, W))
        K.barrier()
        K.end_phase()
        K.stack = outer


def load_hnT_pair(K, hn_all, S, blk=1024, regs=None):
    hnT = K.sb([128, 8, S], BF16, "hnT")
    r = Reg("hnT")
    ds = K.new_dma_sem()
    half = S // 2
    n = 0
    for b, ha in enumerate(hn_all):
        for rk in range(2):
            for k in range(8):
                c0 = rk * half + b * blk
                K.dma("sp" if n % 2 == 0 else "act", hnT[:, k, c0:c0 + blk],
                      ha[rk * 1024 + k * 128: rk * 1024 + (k + 1) * 128, :], ds, W=[r],
                      R=([regs[b]] if regs is not None else []))
                n += 1
    return hnT, r


class YSplit:
    def __init__(self, a, b, half):
        self.a, self.b, self.half = a, b, half
        self.regs = [Reg("yhalf0"), Reg("yhalf1")]
        self.hook = None

    def reg(self, row0):
        return self.regs[0 if row0 < self.half else 1]

    def __getitem__(self, key):
        rs, cs = key
        if rs.start < self.half:
            assert rs.stop <= self.half
            return self.a[rs, cs]
        return self.b[slice(rs.start - self.half, rs.stop - self.half), cs]


def emit_ms_concurrent(K, C, hnT, r_hnT, P, y_d, S):
    outer = K.stack
    with ExitStack() as st:
        K.stack = st
        K.begin_phase()
        W = {"wM_fm": load_w(K, P["wM_fm"], 256, "wMf"), "wM_g": load_w(K, P["wM_g"], 4, "wMg"),
             "wM_tm": load_w(K, P["wM_tm"], 384, "wMt"),
             "wS_fm": load_w(K, P["wS_fm"], 512, "wSf"), "wS_tm": load_w(K, P["wS_tm"], 260, "wSt")}
        gens = [emit_mlstm_gen(K, C, hnT, r_hnT, P, y_d, S, W),
                emit_ssd_gen(K, C, hnT, r_hnT, P, y_d, S, W, nsets=1)]
        while gens:
            for g_ in list(gens):
                try:
                    next(g_)
                except StopIteration:
                    gens.remove(g_)
        K.barrier()
        K.end_phase()
        K.stack = outer


def emit_ssd2(K, C, hnT, r_hnT, P, y_d, S, G=4, W=None):
    nc = K.nc
    nT = S // 128
    nTT = S // 512
    outer = K.stack
    with ExitStack() as st:
        K.stack = st
        K.begin_phase()
        wfm, r_wfm = W["wS_fm"] if W is not None else load_w(K, P["wS_fm"], 512, "wSf")
        wtm, r_wtm = W["wS_tm"] if W is not None else load_w(K, P["wS_tm"], 260, "wSt")
        dsm = K.new_dma_sem()
        r_c = Reg("sconst")
        cw = K.sb([128, 4, 4], F32, "cw"); cb = K.sb([128, 4], F32, "cb")
        dtb = K.sb([128, 4], F32, "dtb"); alog = K.sb([128, 4], F32, "alog")
        dsk = K.sb([128, 4], F32, "dsk"); snw = K.sb([128, 256], F32, "snw")
        for t_, n_ in [(cw, "cw"), (cb, "cb"), (dtb, "dtb"), (alog, "alog"), (dsk, "dsk"), (snw, "snw")]:
            K.dma("sp", t_[:], P[n_], dsm, W=[r_c])
        Aneg = K.sb([128, 4], F32, "Aneg"); r_A = Reg()
        K.op("act", lambda: nc.scalar.activation(out=Aneg[:], in_=alog[:], func=AF.Exp), R=[r_c], W=[r_A])
        K.op("dve", lambda: nc.vector.tensor_scalar(out=Aneg[:], in0=Aneg[:], scalar1=-1.0, scalar2=None, op0=ALU.mult),
             R=[r_A], W=[r_A])
        xc = [K.sb([128, S], BF16, f"xc{i}") for i in range(4)]
        r_xc = [Reg() for _ in range(4)]
        X = [psbank(K, f"sx{i}") for i in range(8)]
        rX = [Reg(f"sx{i}", excl=True) for i in range(8)]
        with ExitStack() as st2:
            K.stack = st2
            SH = S // 2
            xpre = K.sb([128, SH + 3], F32, "xpre"); r_xpre = Reg()
            cacc = K.sb([128, SH], F32, "cacc"); r_cacc = Reg()
            n = 0
            for ct in range(4):
                for hf in range(2):
                    if hf == 0:
                        K.op("dve", lambda: nc.vector.memset(xpre[:, 0:3], 0.0), W=[r_xpre])
                    else:
                        K.op("dve", lambda: nc.vector.tensor_copy(out=xpre[:, 0:3], in_=xpre[:, SH:SH + 3]),
                             R=[r_xpre], W=[r_xpre])
                    for tt in range(nTT // 2):
                        tg_ = hf * (nTT // 2) + tt
                        p_ = X[n % 4]; rp = rX[n % 4]; n += 1
                        for k in range(8):
                            K.op("pe", lambda k=k: nc.tensor.matmul(p_, lhsT=wfm[:, k, ct * 128:(ct + 1) * 128],
                                                                    rhs=hnT[:, k, tg_ * 512:(tg_ + 1) * 512],
                                                                    start=(k == 0), stop=(k == 7)),
                                 R=[r_wfm, r_hnT], W=[rp])
                        K.op("act", lambda: nc.scalar.copy(out=xpre[:, 3 + tt * 512:3 + (tt + 1) * 512], in_=p_),
                             R=[rp], W=[r_xpre])
                    K.op("dve", lambda: nc.vector.tensor_scalar(out=cacc[:], in0=xpre[:, 0:SH], scalar1=cw[:, ct, 0:1],
                                                                scalar2=None, op0=ALU.mult), R=[r_xpre, r_c], W=[r_cacc])
                    for j in range(1, 4):
                        K.op("dve", lambda j=j: nc.vector.scalar_tensor_tensor(
                            out=cacc[:], in0=xpre[:, j:SH + j], scalar=cw[:, ct, j:j + 1], in1=cacc[:],
                            op0=ALU.mult, op1=ALU.add), R=[r_xpre, r_c, r_cacc], W=[r_cacc])
                    K.op("act", lambda: nc.scalar.activation(out=xc[ct][:, hf * SH:(hf + 1) * SH], in_=cacc[:], func=AF.Silu,
                                                             bias=cb[:, ct:ct + 1]),
                         R=[r_cacc, r_c], W=[r_xc[ct]])
            K.barrier()
            K.stack = st

        def T(shape, dt, name):
            return K.sb(shape, dt, name), Reg(name)
        sz, r_sz = T([128, G, 256], BF16, "gsz")
        dtx, r_dtx = T([128, G, 4], F32, "gdtx"); ax, r_ax = T([128, G, 4], F32, "gax")
        ex, r_ex = T([128, G, 4], F32, "gex"); lx, r_lx = T([128, G, 4], F32, "glx")
        dt, r_dt = T([128, G, 4], F32, "gdt"); aa, r_aa = T([128, G, 4], F32, "gaa")
        acs, r_acs = T([128, G, 4], F32, "gacs"); nacs, r_nacs = T([128, G, 4], F32, "gnacs")
        el, r_el = T([128, G, 4], F32, "gel"); cd, r_cd = T([128, G, 4], F32, "gcd")
        dd, r_dd = T([128, G, 4], F32, "gdd"); dec, r_dec = T([128, G, 4], F32, "gdec")
        dtdec, r_dtdec = T([128, G, 4], F32, "gdtdec")
        rseg = [T([128, 4, 128], F32, f"grseg{i}") for i in range(2)]
        segT = [T([128, 4, 128], F32, f"gsegT{i}") for i in range(G)]
        xdt = [T([128, 4, 64], BF16, f"gxdt{i}") for i in range(G)]
        xdd = [T([128, 4, 64], BF16, f"gxdd{i}") for i in range(G)]
        xD = [T([128, 4, 64], F32, f"gxD{i}") for i in range(G)]
        Btm = [T([128, 128], BF16, f"gBtm{i}") for i in range(G)]
        Gm, r_Gm = T([128, G, 128], F32, "gGm")
        scT = [T([128, 4, 128], BF16, f"gscT{i}") for i in range(G)]
        t0 = [T([128, 256], F32, f"gt0{i}") for i in range(G)]
        Sf, r_Sf = T([128, 4, 64], F32, "gSf")
        Sbf = [T([128, 256], BF16, f"gSbf{i}") for i in range(G + 1)]
        gg = [T([128, 256], F32, f"ggg{i}") for i in range(G)]
        junk, r_junk = T([128, 256], BF16, "gjunk")
        ssq, r_ssq = T([128, G], F32, "gssq")
        nscr = mk_scr(K, [128, G], "gn")
        yb = [K.sb([128, 256], BF16, f"gyb{i}") for i in range(G)]; r_yb = [Reg() for _ in range(G)]
        ds_y = [K.new_dma_sem() for _ in range(G)]
        K.op("dve", lambda: nc.vector.memset(Sf[:].rearrange("p a b -> p (a b)"), 0.0), W=[r_Sf])
        K.op("dve", lambda: nc.vector.memset(Sbf[0][0][:], 0.0), W=[Sbf[0][1]])
        ident_f = C["ident_f"]
        fl = lambda t: t[:].rearrange("p a b -> p (a b)")
        for g0 in range(0, nT, G):
            cs = [slice((g0 + i) * 128, (g0 + i + 1) * 128) for i in range(G)]
            pz = [X[0][:, 0:256], X[0][:, 256:512], X[1][:, 0:256], X[1][:, 256:512]]
            rpz = [rX[0], rX[0], rX[1], rX[1]]
            pdt = X[2][:, 0:4 * G].rearrange("p (g h) -> p g h", h=4)
            for i in range(G):
                for k in range(8):
                    K.op("pe", lambda k=k, i=i: nc.tensor.matmul(pz[i], lhsT=hnT[:, k, cs[i]], rhs=wtm[:, k, 0:256],
                                                                 start=(k == 0), stop=(k == 7)), R=[r_hnT, r_wtm], W=[rpz[i]])
                for k in range(8):
                    K.op("pe", lambda k=k, i=i: nc.tensor.matmul(pdt[:, i, :], lhsT=hnT[:, k, cs[i]], rhs=wtm[:, k, 256:260],
                                                                 start=(k == 0), stop=(k == 7)), R=[r_hnT, r_wtm], W=[rX[2]])
            for i in range(0, G, 2):
                K.op("act", lambda i=i: nc.scalar.activation(out=sz[:, i:i + 2, :].rearrange("p a b -> p (a b)"),
                                                             in_=X[i // 2], func=AF.Silu), R=[rpz[i]], W=[r_sz])
            K.op("dve", lambda: nc.vector.tensor_tensor(out=dtx[:], in0=pdt, in1=dtb[:].unsqueeze(1).to_broadcast([128, G, 4]),
                                                        op=ALU.add), R=[rX[2], r_c], W=[r_dtx])
            K.op("dve", lambda: nc.vector.scalar_tensor_tensor(out=ax[:], in0=dtx[:], scalar=-1.0, in1=dtx[:],
                                                               op0=ALU.mult, op1=ALU.min), R=[r_dtx], W=[r_ax])
            K.op("act", lambda: nc.scalar.activation(out=ex[:], in_=ax[:], func=AF.Exp), R=[r_ax], W=[r_ex])
            K.op("dve", lambda: nc.vector.tensor_scalar(out=ex[:], in0=ex[:], scalar1=1.0, scalar2=None, op0=ALU.add),
                 R=[r_ex], W=[r_ex])
            K.op("act", lambda: nc.scalar.activation(out=lx[:], in_=ex[:], func=AF.Ln), R=[r_ex], W=[r_lx])
            K.op("dve", lambda: nc.vector.scalar_tensor_tensor(out=dt[:], in0=dtx[:], scalar=0.0, in1=lx[:],
                                                               op0=ALU.max, op1=ALU.add), R=[r_dtx, r_lx], W=[r_dt])
            K.op("dve", lambda: nc.vector.tensor_tensor(out=aa[:], in0=dt[:], in1=Aneg[:].unsqueeze(1).to_broadcast([128, G, 4]),
                                                        op=ALU.mult), R=[r_dt, r_A], W=[r_aa])
            pacs = X[2][:, 64:64 + 4 * G].rearrange("p (g h) -> p g h", h=4)
            plast = X[2][:, 128:128 + 4 * G].rearrange("p (g h) -> p g h", h=4)
            K.op("pe", lambda: nc.tensor.matmul(X[2][:, 64:64 + 4 * G], lhsT=C["tri"], rhs=fl(aa), start=True, stop=True),
                 R=[r_aa, C["r"]], W=[rX[2]])
            K.op("pe", lambda: nc.tensor.matmul(X[2][:, 128:128 + 4 * G], lhsT=C["ones"], rhs=fl(aa), start=True, stop=True),
                 R=[r_aa, C["r"]], W=[rX[2]])
            K.op("dve", lambda: nc.vector.tensor_copy(out=acs[:], in_=pacs), R=[rX[2]], W=[r_acs])
            K.op("dve", lambda: nc.vector.tensor_scalar(out=nacs[:], in0=pacs, scalar1=-1.0, scalar2=None, op0=ALU.mult),
                 R=[rX[2]], W=[r_nacs])
            K.op("dve", lambda: nc.vector.tensor_tensor(out=dd[:], in0=plast, in1=acs[:], op=ALU.subtract),
                 R=[rX[2], r_acs], W=[r_dd])
            K.op("act", lambda: nc.scalar.activation(out=el[:], in_=acs[:], func=AF.Exp), R=[r_acs], W=[r_el])
            K.op("act", lambda: nc.scalar.activation(out=cd[:], in_=plast, func=AF.Exp), R=[rX[2]], W=[r_cd])
            K.op("act", lambda: nc.scalar.activation(out=dec[:], in_=dd[:], func=AF.Exp), R=[r_dd], W=[r_dec])
            K.op("dve", lambda: nc.vector.tensor_tensor(out=dtdec[:], in0=dt[:], in1=dec[:], op=ALU.mult),
                 R=[r_dt, r_dec], W=[r_dtdec])
            for i in range(G):
                rs_, rrs = rseg[i % 2]
                ps_ = X[3 + i % 2]; rps = rX[3 + i % 2]
                K.op("dve", lambda i=i: nc.vector.tensor_tensor(out=rs_[:], in0=ident_f[:].unsqueeze(1).to_broadcast([128, 4, 128]),
                                                                in1=acs[:, i, :].unsqueeze(2).to_broadcast([128, 4, 128]), op=ALU.mult),
                     R=[C["r"], r_acs], W=[rrs])
                K.op("pe", lambda: nc.tensor.matmul(ps_, lhsT=C["ones"], rhs=fl(rs_), start=True, stop=False),
                     R=[rrs, C["r"]], W=[rps])
                K.op("pe", lambda: nc.tensor.matmul(ps_, lhsT=ident_f[:], rhs=C["nm1"], start=False, stop=True),
                     R=[C["r"]], W=[rps])
                for h in range(4):
                    K.op("act", lambda i=i, h=h: nc.scalar.activation(out=segT[i][0][:, h, :], in_=ps_[:, h * 128:(h + 1) * 128],
                                                                      func=AF.Exp, bias=nacs[:, i, h:h + 1]),
                         R=[rps, r_nacs], W=[segT[i][1]])
            for i in range(G):
                pt_ = X[5 + i % 2][:, 0:192].bitcast(BF16).rearrange("p (a b) -> p a b", b=128); rpt = rX[5 + i % 2]
                for a in range(3):
                    K.op("pe", lambda a=a, i=i: nc.tensor.transpose(out=pt_[:, a, :], in_=xc[a][:, cs[i]], identity=C["ident_bf"][:]),
                         R=[r_xc[a], C["r"]], W=[rpt])
                xs_v = pt_[:, 0:2, :].rearrange("p a (h d) -> p (a h) d", d=64)
                K.op("dve", lambda i=i: nc.vector.tensor_tensor(out=xdt[i][0][:], in0=xs_v, in1=dt[:, i, :].unsqueeze(2).to_broadcast([128, 4, 64]),
                                                                op=ALU.mult), R=[rpt, r_dt], W=[xdt[i][1]])
                K.op("dve", lambda i=i: nc.vector.tensor_tensor(out=xdd[i][0][:], in0=xs_v, in1=dtdec[:, i, :].unsqueeze(2).to_broadcast([128, 4, 64]),
                                                                op=ALU.mult), R=[rpt, r_dtdec], W=[xdd[i][1]])
                K.op("dve", lambda i=i: nc.vector.tensor_tensor(out=xD[i][0][:], in0=xs_v, in1=dsk[:].unsqueeze(2).to_broadcast([128, 4, 64]),
                                                                op=ALU.mult), R=[rpt, r_c], W=[xD[i][1]])
                K.op("dve", lambda i=i: nc.vector.tensor_copy(out=Btm[i][0][:], in_=pt_[:, 2, :]), R=[rpt], W=[Btm[i][1]])
            for i in range(G):
                K.op("pe", lambda i=i: nc.tensor.matmul(X[7][:, i * 128:(i + 1) * 128], lhsT=xc[2][:, cs[i]], rhs=xc[3][:, cs[i]],
                                                        start=True, stop=True), R=[r_xc[2], r_xc[3]], W=[rX[7]])
            K.op("dve", lambda: nc.vector.tensor_tensor(out=Gm[:], in0=X[7][:, 0:G * 128].rearrange("p (g l) -> p g l", l=128),
                                                        in1=C["tri"].unsqueeze(1).to_broadcast([128, G, 128]), op=ALU.mult),
                 R=[rX[7], C["r"]], W=[r_Gm])
            for i in range(G):
                K.op("dve", lambda i=i: nc.vector.tensor_tensor(out=scT[i][0][:], in0=Gm[:, i, :].unsqueeze(1).to_broadcast([128, 4, 128]),
                                                                in1=segT[i][0][:], op=ALU.mult), R=[r_Gm, segT[i][1]], W=[scT[i][1]])
            for i in range(G):
                py_ = X[i % 2][:, (i // 2 % 2) * 256:(i // 2 % 2) * 256 + 256]; rpy = rX[i % 2]
                for h in range(4):
                    K.op("pe", lambda i=i, h=h: nc.tensor.matmul(py_[:, h * 64:(h + 1) * 64], lhsT=scT[i][0][:, h, :], rhs=xdt[i][0][:, h, :],
                                                                 start=True, stop=True), R=[scT[i][1], xdt[i][1]], W=[rpy])
                K.op("dve", lambda i=i: nc.vector.tensor_tensor(out=t0[i][0][:], in0=py_, in1=fl(xD[i][0]), op=ALU.add),
                     R=[rpy, xD[i][1]], W=[t0[i][1]])
            for i in range(G):
                pst_ = X[3 + i % 2][:, 0:256]; rpst = rX[3 + i % 2]
                K.op("pe", lambda i=i: nc.tensor.matmul(pst_, lhsT=Btm[i][0][:], rhs=fl(xdd[i][0]), start=True, stop=True),
                     R=[Btm[i][1], xdd[i][1]], W=[rpst])
                K.op("dve", lambda i=i: nc.vector.tensor_tensor(out=Sf[:], in0=Sf[:], in1=cd[:, i, :].unsqueeze(2).to_broadcast([128, 4, 64]),
                                                                op=ALU.mult), R=[r_Sf, r_cd], W=[r_Sf])
                K.op("dve", lambda: nc.vector.tensor_tensor(out=fl(Sf), in0=fl(Sf), in1=pst_, op=ALU.add),
                     R=[r_Sf, rpst], W=[r_Sf])
                K.op("act", lambda i=i: nc.scalar.copy(out=Sbf[i + 1][0][:], in_=fl(Sf)), R=[r_Sf], W=[Sbf[i + 1][1]])
            for i in range(G):
                pyo_ = X[5 + i % 2][:, 256:512]; rpyo = rX[5 + i % 2]
                K.op("pe", lambda i=i: nc.tensor.matmul(pyo_, lhsT=xc[3][:, cs[i]], rhs=Sbf[i][0][:], start=True, stop=True),
                     R=[r_xc[3], Sbf[i][1]], W=[rpyo])
                K.op("dve", lambda i=i: nc.vector.tensor_tensor(out=gg[i][0][:].rearrange("p (a b) -> p a b", b=64),
                                                                in0=pyo_.rearrange("p (a b) -> p a b", b=64),
                                                                in1=el[:, i, :].unsqueeze(2).to_broadcast([128, 4, 64]), op=ALU.mult),
                     R=[rpyo, r_el], W=[gg[i][1]])
                K.op("dve", lambda i=i: nc.vector.tensor_tensor(out=gg[i][0][:], in0=gg[i][0][:], in1=t0[i][0][:], op=ALU.add),
                     R=[gg[i][1], t0[i][1]], W=[gg[i][1]])
                K.op("dve", lambda i=i: nc.vector.tensor_tensor(out=gg[i][0][:], in0=gg[i][0][:], in1=sz[:, i, :], op=ALU.mult),
                     R=[gg[i][1], r_sz], W=[gg[i][1]])
                K.op("act", lambda i=i: nc.scalar.activation(out=junk[:], in_=gg[i][0][:], func=AF.Square, accum_out=ssq[:, i:i + 1]),
                     R=[gg[i][1]], W=[r_junk, r_ssq])
            rstd_from_ssq(K, ssq[:], r_ssq, G, nscr, 1.0 / 256)
            for i in range(G):
                K.op("dve", lambda i=i: nc.vector.scalar_tensor_tensor(out=yb[i][:], in0=gg[i][0][:], scalar=nscr["rstd"][:, i:i + 1],
                                                                       in1=snw[:], op0=ALU.mult, op1=ALU.mult),
                     R=[gg[i][1], nscr["r_rstd"], r_c], W=[r_yb[i]])
                K.dma("sp", y_d[cs[i], 128:384], yb[i][:], ds_y[i], R=[r_yb[i]])
            K.op("act", lambda: nc.scalar.copy(out=Sbf[0][0][:], in_=Sbf[G][0][:]), R=[Sbf[G][1]], W=[Sbf[0][1]])
        K.barrier()
        K.end_phase()
        K.stack = outer


def emit_ssd3(K, C, hnT, r_hnT, P, y_d, S, G=4, AW=3):
    nc = K.nc
    nT = S // 128
    nTT = S // 512
    outer = K.stack
    with ExitStack() as st:
        K.stack = st
        K.begin_phase()
        wfm, r_wfm = load_w(K, P["wS_fm"], 512, "wSf")
        wtm, r_wtm = load_w(K, P["wS_tm"], 260, "wSt")
        dsm = K.new_dma_sem()
        r_c = Reg("sconst")
        cw = K.sb([128, 4, 4], F32, "cw"); cb = K.sb([128, 4], F32, "cb")
        dtb = K.sb([128, 4], F32, "dtb"); alog = K.sb([128, 4], F32, "alog")
        dsk = K.sb([128, 4], F32, "dsk"); snw = K.sb([128, 256], F32, "snw")
        for t_, n_ in [(cw, "cw"), (cb, "cb"), (dtb, "dtb"), (alog, "alog"), (dsk, "dsk"), (snw, "snw")]:
            K.dma("sp", t_[:], P[n_], dsm, W=[r_c])
        Aneg = K.sb([128, 4], F32, "Aneg"); r_A = Reg()
        K.op("act", lambda: nc.scalar.activation(out=Aneg[:], in_=alog[:], func=AF.Exp), R=[r_c], W=[r_A])
        K.op("dve", lambda: nc.vector.tensor_scalar(out=Aneg[:], in0=Aneg[:], scalar1=-1.0, scalar2=None, op0=ALU.mult),
             R=[r_A], W=[r_A])
        xc = [K.sb([128, S], BF16, f"xc{i}") for i in range(4)]
        r_xc = [Reg() for _ in range(4)]
        X = [psbank(K, f"sx{i}") for i in range(8)]
        rX = [Reg(f"sx{i}", excl=True) for i in range(8)]
        with ExitStack() as st2:
            K.stack = st2
            SH = S // 2
            xpre = K.sb([128, SH + 3], F32, "xpre"); r_xpre = Reg()
            cacc = K.sb([128, SH], F32, "cacc"); r_cacc = Reg()
            n = 0
            for ct in range(4):
                for hf in range(2):
                    if hf == 0:
                        K.op("dve", lambda: nc.vector.memset(xpre[:, 0:3], 0.0), W=[r_xpre])
                    else:
                        K.op("dve", lambda: nc.vector.tensor_copy(out=xpre[:, 0:3], in_=xpre[:, SH:SH + 3]),
                             R=[r_xpre], W=[r_xpre])
                    for tt in range(nTT // 2):
                        tg_ = hf * (nTT // 2) + tt
                        p_ = X[n % 4]; rp = rX[n % 4]; n += 1
                        for k in range(8):
                            K.op("pe", lambda k=k: nc.tensor.matmul(p_, lhsT=wfm[:, k, ct * 128:(ct + 1) * 128],
                                                                    rhs=hnT[:, k, tg_ * 512:(tg_ + 1) * 512],
                                                                    start=(k == 0), stop=(k == 7)),
                                 R=[r_wfm, r_hnT], W=[rp])
                        K.op("act", lambda: nc.scalar.copy(out=xpre[:, 3 + tt * 512:3 + (tt + 1) * 512], in_=p_),
                             R=[rp], W=[r_xpre])
                    K.op("dve", lambda: nc.vector.tensor_scalar(out=cacc[:], in0=xpre[:, 0:SH], scalar1=cw[:, ct, 0:1],
                                                                scalar2=None, op0=ALU.mult), R=[r_xpre, r_c], W=[r_cacc])
                    for j in range(1, 4):
                        K.op("dve", lambda j=j: nc.vector.scalar_tensor_tensor(
                            out=cacc[:], in0=xpre[:, j:SH + j], scalar=cw[:, ct, j:j + 1], in1=cacc[:],
                            op0=ALU.mult, op1=ALU.add), R=[r_xpre, r_c, r_cacc], W=[r_cacc])
                    K.op("act", lambda: nc.scalar.activation(out=xc[ct][:, hf * SH:(hf + 1) * SH], in_=cacc[:], func=AF.Silu,
                                                             bias=cb[:, ct:ct + 1]),
                         R=[r_cacc, r_c], W=[r_xc[ct]])
            K.barrier()
            K.stack = st

        def T(shape, dt, name):
            return K.sb(shape, dt, name + SFX[0]), Reg(name)
        SFX = ['']

        def mkset(par):
            SFX[0] = f'_{par}'
            sz, r_sz = T([128, G, 256], BF16, "gsz")
            dtx, r_dtx = T([128, G, 4], F32, "gdtx"); ax, r_ax = T([128, G, 4], F32, "gax")
            ex, r_ex = T([128, G, 4], F32, "gex"); lx, r_lx = T([128, G, 4], F32, "glx")
            dt, r_dt = T([128, G, 4], F32, "gdt"); aa, r_aa = T([128, G, 4], F32, "gaa")
            acs, r_acs = T([128, G, 4], F32, "gacs"); nacs, r_nacs = T([128, G, 4], F32, "gnacs")
            el, r_el = T([128, G, 4], F32, "gel"); cd, r_cd = T([128, G, 4], F32, "gcd")
            dd, r_dd = T([128, G, 4], F32, "gdd"); dec, r_dec = T([128, G, 4], F32, "gdec")
            dtdec, r_dtdec = T([128, G, 4], F32, "gdtdec")
            rseg = [T([128, 4, 128], F32, f"grseg{i}") for i in range(2)]
            segT = [T([128, 4, 128], F32, f"gsegT{i}") for i in range(G)]
            xdt = [T([128, 4, 64], BF16, f"gxdt{i}") for i in range(G)]
            xdd = [T([128, 4, 64], BF16, f"gxdd{i}") for i in range(G)]
            xD = [T([128, 4, 64], F32, f"gxD{i}") for i in range(G)]
            Btm = [T([128, 128], BF16, f"gBtm{i}") for i in range(G)]
            Gm, r_Gm = T([128, G, 128], F32, "gGm")
            scT = [T([128, 4, 128], BF16, f"gscT{i}") for i in range(G)]
            t0 = [T([128, 256], F32, f"gt0{i}") for i in range(G)]
            Sbf = [T([128, 256], BF16, f"gSbf{i}") for i in range(G + 1)]
            gg = [T([128, 256], F32, f"ggg{i}") for i in range(G)]
            junk, r_junk = T([128, 256], BF16, "gjunk")
            ssq, r_ssq = T([128, G], F32, "gssq")
            nscr = mk_scr(K, [128, G], "gn")
            yb = [K.sb([128, 256], BF16, f"gyb{i}") for i in range(G)]; r_yb = [Reg() for _ in range(G)]
            return dict(locals())
        sets = [mkset(0), mkset(1)]
        SFX[0] = ''
        Sf, r_Sf = T([128, 4, 64], F32, "gSf")
        ds_y = [K.new_dma_sem() for _ in range(2)]
        K.op("dve", lambda: nc.vector.memset(Sf[:].rearrange("p a b -> p (a b)"), 0.0), W=[r_Sf])
        K.op("dve", lambda: nc.vector.memset(sets[0]["Sbf"][0][0][:], 0.0), W=[sets[0]["Sbf"][0][1]])
        ident_f = C["ident_f"]
        fl = lambda t: t[:].rearrange("p a b -> p (a b)")

        def genA(g0, S_):
            cs = [slice((g0 + i) * 128, (g0 + i + 1) * 128) for i in range(G)]
            sz, r_sz, dtx, r_dtx, ax, r_ax, ex, r_ex, lx, r_lx, dt, r_dt, aa, r_aa, acs, r_acs, nacs, r_nacs, el, r_el, cd, r_cd, dd, r_dd, dec, r_dec, dtdec, r_dtdec, rseg, segT, xdt, xdd, xD, Btm, Gm, r_Gm, scT, t0, Sbf, gg, junk, r_junk, ssq, r_ssq, nscr, yb, r_yb = [S_[n_] for n_ in ['sz', 'r_sz', 'dtx', 'r_dtx', 'ax', 'r_ax', 'ex', 'r_ex', 'lx', 'r_lx', 'dt', 'r_dt', 'aa', 'r_aa', 'acs', 'r_acs', 'nacs', 'r_nacs', 'el', 'r_el', 'cd', 'r_cd', 'dd', 'r_dd', 'dec', 'r_dec', 'dtdec', 'r_dtdec', 'rseg', 'segT', 'xdt', 'xdd', 'xD', 'Btm', 'Gm', 'r_Gm', 'scT', 't0', 'Sbf', 'gg', 'junk', 'r_junk', 'ssq', 'r_ssq', 'nscr', 'yb', 'r_yb']]
            cs = [slice((g0 + i) * 128, (g0 + i + 1) * 128) for i in range(G)]
            pz = [X[0][:, 0:256], X[0][:, 256:512], X[1][:, 0:256], X[1][:, 256:512]]
            rpz = [rX[0], rX[0], rX[1], rX[1]]
            pdt = X[2][:, 0:4 * G].rearrange("p (g h) -> p g h", h=4)
            yield
            for i in range(G):
                for k in range(8):
                    K.op("pe", lambda k=k, i=i: nc.tensor.matmul(pz[i], lhsT=hnT[:, k, cs[i]], rhs=wtm[:, k, 0:256],
                                                                 start=(k == 0), stop=(k == 7)), R=[r_hnT, r_wtm], W=[rpz[i]])
                yield
                for k in range(8):
                    K.op("pe", lambda k=k, i=i: nc.tensor.matmul(pdt[:, i, :], lhsT=hnT[:, k, cs[i]], rhs=wtm[:, k, 256:260],
                                                                 start=(k == 0), stop=(k == 7)), R=[r_hnT, r_wtm], W=[rX[2]])
            yield
            for i in range(0, G, 2):
                K.op("act", lambda i=i: nc.scalar.activation(out=sz[:, i:i + 2, :].rearrange("p a b -> p (a b)"),
                                                             in_=X[i // 2], func=AF.Silu), R=[rpz[i]], W=[r_sz])
            yield
            K.op("dve", lambda: nc.vector.tensor_tensor(out=dtx[:], in0=pdt, in1=dtb[:].unsqueeze(1).to_broadcast([128, G, 4]),
                                                        op=ALU.add), R=[rX[2], r_c], W=[r_dtx])
            yield
            K.op("dve", lambda: nc.vector.scalar_tensor_tensor(out=ax[:], in0=dtx[:], scalar=-1.0, in1=dtx[:],
                                                               op0=ALU.mult, op1=ALU.min), R=[r_dtx], W=[r_ax])
            yield
            K.op("act", lambda: nc.scalar.activation(out=ex[:], in_=ax[:], func=AF.Exp), R=[r_ax], W=[r_ex])
            yield
            K.op("dve", lambda: nc.vector.tensor_scalar(out=ex[:], in0=ex[:], scalar1=1.0, scalar2=None, op0=ALU.add),
                 R=[r_ex], W=[r_ex])
            yield
            K.op("act", lambda: nc.scalar.activation(out=lx[:], in_=ex[:], func=AF.Ln), R=[r_ex], W=[r_lx])
            yield
            K.op("dve", lambda: nc.vector.scalar_tensor_tensor(out=dt[:], in0=dtx[:], scalar=0.0, in1=lx[:],
                                                               op0=ALU.max, op1=ALU.add), R=[r_dtx, r_lx], W=[r_dt])
            yield
            K.op("dve", lambda: nc.vector.tensor_tensor(out=aa[:], in0=dt[:], in1=Aneg[:].unsqueeze(1).to_broadcast([128, G, 4]),
                                                        op=ALU.mult), R=[r_dt, r_A], W=[r_aa])
            pacs = X[2][:, 64:64 + 4 * G].rearrange("p (g h) -> p g h", h=4)
            plast = X[2][:, 128:128 + 4 * G].rearrange("p (g h) -> p g h", h=4)
            yield
            K.op("pe", lambda: nc.tensor.matmul(X[2][:, 64:64 + 4 * G], lhsT=C["tri"], rhs=fl(aa), start=True, stop=True),
                 R=[r_aa, C["r"]], W=[rX[2]])
            yield
            K.op("pe", lambda: nc.tensor.matmul(X[2][:, 128:128 + 4 * G], lhsT=C["ones"], rhs=fl(aa), start=True, stop=True),
                 R=[r_aa, C["r"]], W=[rX[2]])
            yield
            K.op("dve", lambda: nc.vector.tensor_copy(out=acs[:], in_=pacs), R=[rX[2]], W=[r_acs])
            yield
            K.op("dve", lambda: nc.vector.tensor_scalar(out=nacs[:], in0=pacs, scalar1=-1.0, scalar2=None, op0=ALU.mult),
                 R=[rX[2]], W=[r_nacs])
            yield
            K.op("dve", lambda: nc.vector.tensor_tensor(out=dd[:], in0=plast, in1=acs[:], op=ALU.subtract),
                 R=[rX[2], r_acs], W=[r_dd])
            yield
            K.op("act", lambda: nc.scalar.activation(out=el[:], in_=acs[:], func=AF.Exp), R=[r_acs], W=[r_el])
            yield
            K.op("act", lambda: nc.scalar.activation(out=cd[:], in_=plast, func=AF.Exp), R=[rX[2]], W=[r_cd])
            yield
            K.op("act", lambda: nc.scalar.activation(out=dec[:], in_=dd[:], func=AF.Exp), R=[r_dd], W=[r_dec])
            yield
            K.op("dve", lambda: nc.vector.tensor_tensor(out=dtdec[:], in0=dt[:], in1=dec[:], op=ALU.mult),
                 R=[r_dt, r_dec], W=[r_dtdec])
            yield
            for i in range(G):
                rs_, rrs = rseg[i % 2]
                ps_ = X[3]; rps = rX[3]
                K.op("dve", lambda i=i: nc.vector.tensor_tensor(out=rs_[:], in0=ident_f[:].unsqueeze(1).to_broadcast([128, 4, 128]),
                                                                in1=acs[:, i, :].unsqueeze(2).to_broadcast([128, 4, 128]), op=ALU.mult),
                     R=[C["r"], r_acs], W=[rrs])
                yield
                K.op("pe", lambda: nc.tensor.matmul(ps_, lhsT=C["ones"], rhs=fl(rs_), start=True, stop=False),
                     R=[rrs, C["r"]], W=[rps])
                yield
                K.op("pe", lambda: nc.tensor.matmul(ps_, lhsT=ident_f[:], rhs=C["nm1"], start=False, stop=True),
                     R=[C["r"]], W=[rps])
                yield
                for h in range(4):
                    K.op("act", lambda i=i, h=h: nc.scalar.activation(out=segT[i][0][:, h, :], in_=ps_[:, h * 128:(h + 1) * 128],
                                                                      func=AF.Exp, bias=nacs[:, i, h:h + 1]),
                         R=[rps, r_nacs], W=[segT[i][1]])
            yield
            for i in range(G):
                pt_ = X[5][:, 0:192].bitcast(BF16).rearrange("p (a b) -> p a b", b=128); rpt = rX[5]
                for a in range(3):
                    K.op("pe", lambda a=a, i=i: nc.tensor.transpose(out=pt_[:, a, :], in_=xc[a][:, cs[i]], identity=C["ident_bf"][:]),
                         R=[r_xc[a], C["r"]], W=[rpt])
                xs_v = pt_[:, 0:2, :].rearrange("p a (h d) -> p (a h) d", d=64)
                yield
                K.op("dve", lambda i=i: nc.vector.tensor_tensor(out=xdt[i][0][:], in0=xs_v, in1=dt[:, i, :].unsqueeze(2).to_broadcast([128, 4, 64]),
                                                                op=ALU.mult), R=[rpt, r_dt], W=[xdt[i][1]])
                yield
                K.op("dve", lambda i=i: nc.vector.tensor_tensor(out=xdd[i][0][:], in0=xs_v, in1=dtdec[:, i, :].unsqueeze(2).to_broadcast([128, 4, 64]),
                                                                op=ALU.mult), R=[rpt, r_dtdec], W=[xdd[i][1]])
                yield
                K.op("dve", lambda i=i: nc.vector.tensor_tensor(out=xD[i][0][:], in0=xs_v, in1=dsk[:].unsqueeze(2).to_broadcast([128, 4, 64]),
                                                                op=ALU.mult), R=[rpt, r_c], W=[xD[i][1]])
                yield
                K.op("dve", lambda i=i: nc.vector.tensor_copy(out=Btm[i][0][:], in_=pt_[:, 2, :]), R=[rpt], W=[Btm[i][1]])
            yield
            for i in range(G):
                K.op("pe", lambda i=i: nc.tensor.matmul(X[7][:, i * 128:(i + 1) * 128], lhsT=xc[2][:, cs[i]], rhs=xc[3][:, cs[i]],
                                                        start=True, stop=True), R=[r_xc[2], r_xc[3]], W=[rX[7]])
            yield
            K.op("dve", lambda: nc.vector.tensor_tensor(out=Gm[:], in0=X[7][:, 0:G * 128].rearrange("p (g l) -> p g l", l=128),
                                                        in1=C["tri"].unsqueeze(1).to_broadcast([128, G, 128]), op=ALU.mult),
                 R=[rX[7], C["r"]], W=[r_Gm])
            yield
            for i in range(G):
                K.op("dve", lambda i=i: nc.vector.tensor_tensor(out=scT[i][0][:], in0=Gm[:, i, :].unsqueeze(1).to_broadcast([128, 4, 128]),
                                                                in1=segT[i][0][:], op=ALU.mult), R=[r_Gm, segT[i][1]], W=[scT[i][1]])
            yield
            for i in range(G):
                py_ = X[i % 2][:, (i // 2 % 2) * 256:(i // 2 % 2) * 256 + 256]; rpy = rX[i % 2]
                for h in range(4):
                    K.op("pe", lambda i=i, h=h: nc.tensor.matmul(py_[:, h * 64:(h + 1) * 64], lhsT=scT[i][0][:, h, :], rhs=xdt[i][0][:, h, :],
                                                                 start=True, stop=True), R=[scT[i][1], xdt[i][1]], W=[rpy])
                yield
                K.op("dve", lambda i=i: nc.vector.tensor_tensor(out=t0[i][0][:], in0=py_, in1=fl(xD[i][0]), op=ALU.add),
                     R=[rpy, xD[i][1]], W=[t0[i][1]])
            yield
            yield

        def genB(g0, S_, O_):
            cs = [slice((g0 + i) * 128, (g0 + i + 1) * 128) for i in range(G)]
            sz, r_sz, dtx, r_dtx, ax, r_ax, ex, r_ex, lx, r_lx, dt, r_dt, aa, r_aa, acs, r_acs, nacs, r_nacs, el, r_el, cd, r_cd, dd, r_dd, dec, r_dec, dtdec, r_dtdec, rseg, segT, xdt, xdd, xD, Btm, Gm, r_Gm, scT, t0, Sbf, gg, junk, r_junk, ssq, r_ssq, nscr, yb, r_yb = [S_[n_] for n_ in ['sz', 'r_sz', 'dtx', 'r_dtx', 'ax', 'r_ax', 'ex', 'r_ex', 'lx', 'r_lx', 'dt', 'r_dt', 'aa', 'r_aa', 'acs', 'r_acs', 'nacs', 'r_nacs', 'el', 'r_el', 'cd', 'r_cd', 'dd', 'r_dd', 'dec', 'r_dec', 'dtdec', 'r_dtdec', 'rseg', 'segT', 'xdt', 'xdd', 'xD', 'Btm', 'Gm', 'r_Gm', 'scT', 't0', 'Sbf', 'gg', 'junk', 'r_junk', 'ssq', 'r_ssq', 'nscr', 'yb', 'r_yb']]
            for i in range(G):
                pst_ = X[4][:, (i % 2) * 256:(i % 2) * 256 + 256]; rpst = rX[4]
                K.op("pe", lambda i=i: nc.tensor.matmul(pst_, lhsT=Btm[i][0][:], rhs=fl(xdd[i][0]), start=True, stop=True),
                     R=[Btm[i][1], xdd[i][1]], W=[rpst])
                yield
                K.op("dve", lambda i=i: nc.vector.tensor_tensor(out=Sf[:], in0=Sf[:], in1=cd[:, i, :].unsqueeze(2).to_broadcast([128, 4, 64]),
                                                                op=ALU.mult), R=[r_Sf, r_cd], W=[r_Sf])
                yield
                K.op("dve", lambda: nc.vector.tensor_tensor(out=fl(Sf), in0=fl(Sf), in1=pst_, op=ALU.add),
                     R=[r_Sf, rpst], W=[r_Sf])
                yield
                K.op("act", lambda i=i: nc.scalar.copy(out=Sbf[i + 1][0][:], in_=fl(Sf)), R=[r_Sf], W=[Sbf[i + 1][1]])
            yield
            for i in range(G):
                pyo_ = X[6][:, (i % 2) * 256:(i % 2) * 256 + 256]; rpyo = rX[6]
                K.op("pe", lambda i=i: nc.tensor.matmul(pyo_, lhsT=xc[3][:, cs[i]], rhs=Sbf[i][0][:], start=True, stop=True),
                     R=[r_xc[3], Sbf[i][1]], W=[rpyo])
                yield
                K.op("dve", lambda i=i: nc.vector.tensor_tensor(out=gg[i][0][:].rearrange("p (a b) -> p a b", b=64),
                                                                in0=pyo_.rearrange("p (a b) -> p a b", b=64),
                                                                in1=el[:, i, :].unsqueeze(2).to_broadcast([128, 4, 64]), op=ALU.mult),
                     R=[rpyo, r_el], W=[gg[i][1]])
                yield
                K.op("dve", lambda i=i: nc.vector.tensor_tensor(out=gg[i][0][:], in0=gg[i][0][:], in1=t0[i][0][:], op=ALU.add),
                     R=[gg[i][1], t0[i][1]], W=[gg[i][1]])
                yield
                K.op("dve", lambda i=i: nc.vector.tensor_tensor(out=gg[i][0][:], in0=gg[i][0][:], in1=sz[:, i, :], op=ALU.mult),
                     R=[gg[i][1], r_sz], W=[gg[i][1]])
                yield
                K.op("act", lambda i=i: nc.scalar.activation(out=junk[:], in_=gg[i][0][:], func=AF.Square, accum_out=ssq[:, i:i + 1]),
                     R=[gg[i][1]], W=[r_junk, r_ssq])
            yield
            rstd_from_ssq(K, ssq[:], r_ssq, G, nscr, 1.0 / 256)
            yield
            for i in range(G):
                K.op("dve", lambda i=i: nc.vector.scalar_tensor_tensor(out=yb[i][:], in0=gg[i][0][:], scalar=nscr["rstd"][:, i:i + 1],
                                                                       in1=snw[:], op0=ALU.mult, op1=ALU.mult),
                     R=[gg[i][1], nscr["r_rstd"], r_c], W=[r_yb[i]])
                K.dma("sp", y_d[cs[i], 128:384], yb[i][:], ds_y[i % 2], R=[r_yb[i]])
            yield
            K.op("act", lambda: nc.scalar.copy(out=O_["Sbf"][0][0][:], in_=Sbf[G][0][:]), R=[Sbf[G][1]], W=[O_["Sbf"][0][1]])
            yield

            yield

        groups = list(range(0, nT, G))
        run_gen(genA(groups[0], sets[0]))
        for gi, g0 in enumerate(groups):
            gB = genB(g0, sets[gi % 2], sets[1 - gi % 2])
            gA = genA(groups[gi + 1], sets[1 - gi % 2]) if gi + 1 < len(groups) else None
            while gA is not None or gB is not None:
                for _ in range(AW):
                    if gA is not None:
                        try:
                            next(gA)
                        except StopIteration:
                            gA = None
                if gB is not None:
                    try:
                        next(gB)
                    except StopIteration:
                        gB = None
        K.barrier()
        K.end_phase()
        K.stack = outer


def emit_mlstm2(K, C, hnT, r_hnT, P, y_d, S, W=None):
    nc = K.nc
    nB = S // 512
    outer = K.stack
    with ExitStack() as st:
        K.stack = st
        K.begin_phase()
        wfm, r_wfm = W["wM_fm"] if W is not None else load_w(K, P["wM_fm"], 256, "wMf")
        wg, r_wg = W["wM_g"] if W is not None else load_w(K, P["wM_g"], 4, "wMg")
        wtm, r_wtm = W["wM_tm"] if W is not None else load_w(K, P["wM_tm"], 384, "wMt")
        dsm = K.new_dma_sem()
        r_c = Reg("mconst")
        gbias = K.sb([2, 2], F32, "gbias"); mnw = K.sb([128, 128], F32, "mnw")
        K.dma("sp", gbias[:], P["gbias"], dsm, W=[r_c])
        K.dma("sp", mnw[:], P["mnw"], dsm, W=[r_c])
        ident_f = C["ident_f"]
        Y = [psbank(K, f"my{i}") for i in range(7)]
        rY = [Reg(f"my{i}", excl=True) for i in range(7)]
        pq, pk, pgi, pgf = Y[0], Y[1], Y[2][0:2, :], Y[3][0:2, :]
        ptl = Y[4][:, 0:32].rearrange("p (q i h) -> p q i h", q=4, i=4)
        pdec = Y[4][:, 32:40]
        pC = Y[4][:, 64:129]
        pD = [Y[2].rearrange("p (i t) -> p i t", t=128), Y[3].rearrange("p (i t) -> p i t", t=128)]
        pS = [Y[0].rearrange("p (i t) -> p i t", t=128), Y[1].rearrange("p (i t) -> p i t", t=128)]
        pQ = [Y[0][:, 0:260].rearrange("p (i v) -> p i v", v=65), Y[1][:, 0:260].rearrange("p (i v) -> p i v", v=65)]
        pN = [Y[5][:, 0:260].rearrange("p (i v) -> p i v", v=65), Y[6][:, 0:260].rearrange("p (i v) -> p i v", v=65)]
        ptm = [Y[5][:, 0:384], Y[6][:, 0:384]]

        def T(shape, dt, name):
            return K.sb(shape, dt, name), Reg(name)
        qTb, r_qTb = T([128, 512], BF16, "mqTb")
        kTb, r_kTb = T([128, 512], BF16, "mkTb")
        rows = {}
        for n_ in ["ipre", "yv", "e", "b", "al", "cma", "mu", "nmu", "wrow", "inter", "en", "tmp"]:
            rows[n_] = T([2, 512], F32, "mr_" + n_)
        rows["nab"] = rows["e"]; rows["l"] = rows["e"]
        rows["logf"] = rows["yv"]
        mnew, r_mnew = T([2, 8], F32, "mnew")
        mprev, r_mprev = T([2, 8], F32, "mprev")
        mcar, r_mcar = T([2, 1], F32, "mcar")
        decay, r_decay = T([2, 8], F32, "mdecay")
        tl, r_tl = T([128, 4, 4, 2], F32, "mtl")
        decr, r_decr = T([128, 8], F32, "mdecr")
        ktm, r_ktm = T([128, 4, 128], F32, "mktm")
        vaug, r_vaug = T([128, 4, 2, 65], BF16, "mvaug")
        og, r_og = T([128, 4, 128], F32, "mog")
        dT, r_dT = T([128, 2, 4, 128], F32, "mdT")
        sdT, r_sdT = T([128, 2, 4, 128], BF16, "msdT")
        nmv, r_nmv = T([128, 2, 4, 65], F32, "mnmv")
        kw, r_kw = T([128, 2, 4, 64], BF16, "mkw")
        Cst, r_Cst = T([128, 65], F32, "mCst")
        Cbf, r_Cbf = T([128, 9, 65], BF16, "mCbf")
        r_Cb = [Reg(f"Cb{i}") for i in range(9)]
        tq, r_tq = T([128, 2, 4, 65], F32, "mtq")
        dn, r_dn = T([128, 2, 4], F32, "mdn")
        rn, r_rn = T([128, 2, 4], F32, "mrn")
        hm, r_hm = T([128, 2, 4, 64], F32, "mhm")
        sqv, r_sqv = T([128, 2, 4, 64], F32, "msqv")
        ssq, r_ssq = T([128, 8], F32, "mssq")
        nscr = mk_scr(K, [128, 8], "mn")
        hn2, r_hn2 = T([128, 2, 4, 64], F32, "mhn2")
        yb = [K.sb([128, 4, 128], BF16, f"myb{i}") for i in range(2)]; r_yb = [Reg() for _ in range(2)]
        ds_y = [K.new_dma_sem() for _ in range(2)]
        K.op("dve", lambda: nc.vector.memset(Cst[:], 0.0), W=[r_Cst])
        K.op("dve", lambda: nc.vector.memset(Cbf[:].rearrange("p a b -> p (a b)"), 0.0), W=r_Cb)
        K.op("dve", lambda: nc.vector.memset(mcar[:], 0.0), W=[r_mcar])
        K.op("dve", lambda: nc.vector.memset(vaug[:].rearrange("p a b c -> p (a b c)"), 1.0), W=[r_vaug])

        def R_(n_):
            return rows[n_][0]

        def rr(n_):
            return rows[n_][1]
        rowc = C["rowc"]
        for b in range(nB):
            bs = slice(b * 512, (b + 1) * 512)
            for (pp_, rp_, c0, dst, rdst, sc) in [(pq, rY[0], 0, qTb, r_qTb, 1.0), (pk, rY[1], 128, kTb, r_kTb, 0.125)]:
                for k in range(8):
                    K.op("pe", lambda k=k: nc.tensor.matmul(pp_, lhsT=wfm[:, k, c0:c0 + 128], rhs=hnT[:, k, bs],
                                                            start=(k == 0), stop=(k == 7)), R=[r_wfm, r_hnT], W=[rp_])
                K.op("act", lambda: nc.scalar.mul(out=dst[:], in_=pp_, mul=sc), R=[rp_], W=[rdst])
            for (pp_, rp_, c0) in [(pgi, rY[2], 0), (pgf, rY[3], 2)]:
                for k in range(8):
                    K.op("pe", lambda k=k: nc.tensor.matmul(pp_, lhsT=wg[:, k, c0:c0 + 2], rhs=hnT[:, k, bs],
                                                            start=(k == 0), stop=(k == 7)), R=[r_wg, r_hnT], W=[rp_])
            K.op("dve", lambda: nc.vector.tensor_scalar(out=R_("ipre")[:], in0=pgi, scalar1=gbias[:, 0:1], scalar2=None,
                                                        op0=ALU.add), R=[rY[2], r_c], W=[rr("ipre")])
            K.op("dve", lambda: nc.vector.tensor_scalar(out=R_("yv")[:], in0=pgf, scalar1=gbias[:, 1:2], scalar2=-1.0,
                                                        op0=ALU.add, op1=ALU.mult), R=[rY[3], r_c], W=[rr("yv")])
            K.op("dve", lambda: nc.vector.scalar_tensor_tensor(out=R_("nab")[:], in0=R_("yv")[:], scalar=-1.0, in1=R_("yv")[:],
                                                               op0=ALU.mult, op1=ALU.min), R=[rr("yv")], W=[rr("nab")])
            K.op("act", lambda: nc.scalar.activation(out=R_("e")[:], in_=R_("nab")[:], func=AF.Exp), R=[rr("nab")], W=[rr("e")])
            K.op("dve", lambda: nc.vector.tensor_scalar(out=R_("e")[:], in0=R_("e")[:], scalar1=1.0, scalar2=None, op0=ALU.add),
                 R=[rr("e")], W=[rr("e")])
            K.op("act", lambda: nc.scalar.activation(out=R_("l")[:], in_=R_("e")[:], func=AF.Ln), R=[rr("e")], W=[rr("l")])
            K.op("dve", lambda: nc.vector.scalar_tensor_tensor(out=R_("logf")[:], in0=R_("yv")[:], scalar=0.0, in1=R_("l")[:],
                                                               op0=ALU.max, op1=ALU.add), R=[rr("yv"), rr("l")], W=[rr("logf")])
            K.op("dve", lambda: nc.vector.tensor_scalar(out=R_("logf")[:], in0=R_("logf")[:], scalar1=-1.0, scalar2=None,
                                                        op0=ALU.mult), R=[rr("logf")], W=[rr("logf")])
            K.op("dve", lambda: nc.vector.tensor_tensor_scan(out=R_("b")[:], data0=rowc[:, 0, :], data1=R_("logf")[:],
                                                             initial=0.0, op0=ALU.mult, op1=ALU.add),
                 R=[rr("logf"), C["r"]], W=[rr("b")])
            K.op("dve", lambda: nc.vector.tensor_tensor(out=R_("al")[:], in0=R_("ipre")[:], in1=R_("b")[:], op=ALU.subtract),
                 R=[rr("ipre"), rr("b")], W=[rr("al")])
            K.op("dve", lambda: nc.vector.tensor_tensor_scan(out=R_("cma")[:], data0=rowc[:, 1, :], data1=R_("al")[:],
                                                             initial=0.0, op0=ALU.add, op1=ALU.max),
                 R=[rr("al"), C["r"]], W=[rr("cma")])
            cma3 = R_("cma")[:].rearrange("p (c l) -> p c l", l=64)
            b3 = R_("b")[:].rearrange("p (c l) -> p c l", l=64)
            al3 = R_("al")[:].rearrange("p (c l) -> p c l", l=64)
            mu3 = R_("mu")[:].rearrange("p (c l) -> p c l", l=64)
            tmp3 = R_("tmp")[:].rearrange("p (c l) -> p c l", l=64)
            K.op("dve", lambda: nc.vector.tensor_tensor_scan(out=mnew[:], data0=cma3[:, :, 63], data1=b3[:, :, 63],
                                                             initial=mcar[:, 0:1], op0=ALU.max, op1=ALU.add),
                 R=[rr("cma"), rr("b"), r_mcar], W=[r_mnew])
            K.op("dve", lambda: nc.vector.tensor_copy(out=mprev[:, 0:1], in_=mcar[:]), R=[r_mcar], W=[r_mprev])
            K.op("dve", lambda: nc.vector.tensor_copy(out=mprev[:, 1:8], in_=mnew[:, 0:7]), R=[r_mnew], W=[r_mprev])
            K.op("dve", lambda: nc.vector.tensor_copy(out=mcar[:], in_=mnew[:, 7:8]), R=[r_mnew, r_mprev], W=[r_mcar])
            mpb = mprev[:].unsqueeze(2).to_broadcast([2, 8, 64])
            K.op("dve", lambda: nc.vector.tensor_tensor(out=mu3, in0=cma3, in1=mpb, op=ALU.max),
                 R=[rr("cma"), r_mprev], W=[rr("mu")])
            K.op("dve", lambda: nc.vector.tensor_scalar(out=R_("nmu")[:], in0=R_("mu")[:], scalar1=-1.0, scalar2=None,
                                                        op0=ALU.mult), R=[rr("mu")], W=[rr("nmu")])
            mcb = mu3[:, :, 63].unsqueeze(2).to_broadcast([2, 8, 64])
            K.op("dve", lambda: nc.vector.tensor_tensor(out=tmp3, in0=al3, in1=mcb, op=ALU.subtract),
                 R=[rr("al"), rr("mu")], W=[rr("tmp")])
            K.op("act", lambda: nc.scalar.activation(out=R_("wrow")[:], in_=R_("tmp")[:], func=AF.Exp), R=[rr("tmp")], W=[rr("wrow")])
            K.op("dve", lambda: nc.vector.tensor_tensor(out=decay[:], in0=mprev[:], in1=mu3[:, :, 63], op=ALU.subtract),
                 R=[r_mprev, rr("mu")], W=[r_decay])
            K.op("act", lambda: nc.scalar.activation(out=decay[:], in_=decay[:], func=AF.Exp), R=[r_decay], W=[r_decay])
            K.op("dve", lambda: nc.vector.tensor_tensor(out=tmp3, in0=mu3, in1=mpb, op=ALU.subtract),
                 R=[rr("mu"), r_mprev, rr("wrow")], W=[rr("tmp")])
            K.op("act", lambda: nc.scalar.activation(out=R_("inter")[:], in_=R_("tmp")[:], func=AF.Exp, scale=-1.0),
                 R=[rr("tmp")], W=[rr("inter")])
            K.op("dve", lambda: nc.vector.tensor_tensor(out=R_("tmp")[:], in0=R_("b")[:], in1=R_("mu")[:], op=ALU.add),
                 R=[rr("b"), rr("mu"), rr("inter")], W=[rr("tmp")])
            K.op("act", lambda: nc.scalar.activation(out=R_("en")[:], in_=R_("tmp")[:], func=AF.Exp, scale=-1.0),
                 R=[rr("tmp")], W=[rr("en")])
            for qi, qn_ in enumerate(["al", "wrow", "inter", "en"]):
                for i in range(4):
                    K.op("pe", lambda qi=qi, i=i, qn_=qn_: nc.tensor.transpose(
                        out=ptl[:, qi, i, :], in_=R_(qn_)[0:2, i * 128:(i + 1) * 128], identity=ident_f[0:2, 0:2]),
                        R=[rr(qn_), C["r"]], W=[rY[4]])
            K.op("pe", lambda: nc.tensor.matmul(pdec, lhsT=C["hsel"][:], rhs=decay[:], start=True, stop=True),
                 R=[r_decay, C["r"]], W=[rY[4]])
            K.op("dve", lambda: nc.vector.tensor_copy(out=tl[:], in_=ptl), R=[rY[4]], W=[r_tl])
            K.op("dve", lambda: nc.vector.tensor_copy(out=decr[:], in_=pdec), R=[rY[4]], W=[r_decr])
            for i in range(4):
                ts = slice((b * 4 + i) * 128, (b * 4 + i + 1) * 128)
                pt_ = ptm[i % 2]; rpt = rY[5 + i % 2]
                for k in range(8):
                    K.op("pe", lambda k=k: nc.tensor.matmul(pt_, lhsT=hnT[:, k, ts], rhs=wtm[:, k, :],
                                                            start=(k == 0), stop=(k == 7)), R=[r_hnT, r_wtm], W=[rpt])
                K.op("act", lambda i=i: nc.scalar.mul(out=ktm[:, i, :], in_=pt_[:, 0:128], mul=0.125), R=[rpt], W=[r_ktm])
                K.op("dve", lambda i=i: nc.vector.tensor_copy(out=vaug[:, i, :, 0:64],
                                                              in_=pt_[:, 128:256].rearrange("p (h d) -> p h d", d=64)),
                     R=[rpt], W=[r_vaug])
                K.op("act", lambda i=i: nc.scalar.activation(out=og[:, i, :], in_=pt_[:, 256:384], func=AF.Sigmoid),
                     R=[rpt], W=[r_og])
            for h in range(2):
                K.op("pe", lambda h=h: nc.tensor.matmul(pD[h].rearrange("p i t -> p (i t)"), lhsT=C["sel"][:, h, :],
                                                        rhs=R_("nmu")[:], start=True, stop=False),
                     R=[rr("nmu"), C["r"]], W=[rY[2 + h]])
                K.op("pe", lambda h=h: nc.tensor.matmul(pD[h].rearrange("p i t -> p (i t)"), lhsT=ident_f[:],
                                                        rhs=C["nm2"], start=False, stop=True),
                     R=[C["r"]], W=[rY[2 + h]])
            for h in range(2):
                for i in range(4):
                    K.op("act", lambda h=h, i=i: nc.scalar.activation(out=dT[:, h, i, :], in_=pD[h][:, i, :], func=AF.Exp,
                                                                      bias=tl[:, 0, i, h:h + 1]),
                         R=[rY[2 + h], r_tl], W=[r_dT])
            for h in range(2):
                hp = slice(64 * h, 64 * h + 64)
                for i in range(4):
                    tb = slice(i * 128, (i + 1) * 128)
                    K.op("pe", lambda h=h, i=i: nc.tensor.matmul(pS[h][:, i, :], lhsT=kTb[hp, tb], rhs=qTb[hp, tb],
                                                                 start=True, stop=True), R=[r_kTb, r_qTb], W=[rY[h]])
                K.op("dve", lambda h=h: nc.vector.tensor_tensor(out=sdT[:, h, :, :], in0=pS[h], in1=dT[:, h, :, :], op=ALU.mult),
                     R=[rY[h], r_dT], W=[r_sdT])
            for h in range(2):
                for i in range(4):
                    K.op("pe", lambda h=h, i=i: nc.tensor.matmul(pN[h][:, i, :], lhsT=sdT[:, h, i, :], rhs=vaug[:, i, h, :],
                                                                 start=True, stop=True), R=[r_sdT, r_vaug], W=[rY[5 + h]])
                K.op("dve", lambda h=h: nc.vector.tensor_copy(out=nmv[:, h, :, :], in_=pN[h]), R=[rY[5 + h]], W=[r_nmv])
            for h in range(2):
                hc = slice(64 * h, 64 * h + 64)
                K.op("dve", lambda h=h: nc.vector.tensor_tensor(out=kw[:, h, :, :], in0=ktm[:, :, hc],
                                                                in1=tl[:, 1, :, h].unsqueeze(2).to_broadcast([128, 4, 64]),
                                                                op=ALU.mult), R=[r_ktm, r_tl], W=[r_kw])
            for ce in range(8):
                i, half = ce // 2, ce % 2
                rs_ = slice(64 * half, 64 * half + 64)
                for h in range(2):
                    hp = slice(64 * h, 64 * h + 64)
                    K.op("pe", lambda h=h: nc.tensor.matmul(pC[hp, :], lhsT=kw[rs_, h, i, :], rhs=vaug[rs_, i, h, :],
                                                            start=True, stop=True), R=[r_kw, r_vaug], W=[rY[4]])
                K.op("dve", lambda: nc.vector.scalar_tensor_tensor(out=Cst[:], in0=Cst[:], scalar=decr[:, ce:ce + 1], in1=pC,
                                                                   op0=ALU.mult, op1=ALU.add), R=[r_Cst, r_decr, rY[4]], W=[r_Cst])
                K.op("act", lambda: nc.scalar.copy(out=Cbf[:, ce + 1, :], in_=Cst[:]), R=[r_Cst], W=[r_Cb[ce + 1]])
            for h in range(2):
                hp = slice(64 * h, 64 * h + 64)
                for ce in range(8):
                    i, half = ce // 2, ce % 2
                    rs_ = slice(64 * half, 64 * half + 64)
                    tc = slice(i * 128 + 64 * half, i * 128 + 64 * half + 64)
                    K.op("pe", lambda h=h: nc.tensor.matmul(pQ[h][rs_, i, :], lhsT=qTb[hp, tc], rhs=Cbf[hp, ce, :],
                                                            start=True, stop=True), R=[r_qTb, r_Cb[ce]], W=[rY[h]])
            ybt = yb[b % 2]
            for h in range(2):
                hc = slice(64 * h, 64 * h + 64)
                K.op("dve", lambda h=h: nc.vector.tensor_tensor(out=tq[:, h, :, :], in0=pQ[h],
                                                                in1=tl[:, 2, :, h].unsqueeze(2).to_broadcast([128, 4, 65]),
                                                                op=ALU.mult), R=[rY[h], r_tl], W=[r_tq])
                K.op("dve", lambda h=h: nc.vector.tensor_tensor(out=nmv[:, h, :, :], in0=nmv[:, h, :, :], in1=tq[:, h, :, :],
                                                                op=ALU.add), R=[r_nmv, r_tq], W=[r_nmv])
                K.op("dve", lambda h=h: nc.vector.scalar_tensor_tensor(out=dn[:, h, :], in0=nmv[:, h, :, 64], scalar=-1.0,
                                                                       in1=nmv[:, h, :, 64], op0=ALU.mult, op1=ALU.max),
                     R=[r_nmv], W=[r_dn])
                K.op("dve", lambda h=h: nc.vector.tensor_tensor(out=dn[:, h, :], in0=dn[:, h, :], in1=tl[:, 3, :, h], op=ALU.max),
                     R=[r_dn, r_tl], W=[r_dn])
                K.op("dve", lambda h=h: nc.vector.reciprocal(out=rn[:, h, :], in_=dn[:, h, :]), R=[r_dn], W=[r_rn])
                K.op("dve", lambda h=h: nc.vector.tensor_tensor(out=hm[:, h, :, :], in0=nmv[:, h, :, 0:64],
                                                                in1=rn[:, h, :].unsqueeze(2).to_broadcast([128, 4, 64]),
                                                                op=ALU.mult), R=[r_nmv, r_rn], W=[r_hm])
                K.op("act", lambda h=h: nc.scalar.activation(out=sqv[:, h, :, :], in_=hm[:, h, :, :], func=AF.Square),
                     R=[r_hm], W=[r_sqv])
            K.op("dve", lambda: nc.vector.tensor_reduce(out=ssq[:], in_=sqv[:].rearrange("p h i d -> p (h i) d"),
                                                        axis=AX.X, op=ALU.add), R=[r_sqv], W=[r_ssq])
            rstd_from_ssq(K, ssq[:], r_ssq, 8, nscr, 1.0 / 64)
            rs3 = nscr["rstd"].rearrange("p (h i) -> p h i", i=4)
            for h in range(2):
                hc = slice(64 * h, 64 * h + 64)
                K.op("dve", lambda h=h: nc.vector.tensor_tensor(out=hn2[:, h, :, :], in0=hm[:, h, :, :],
                                                                in1=rs3[:, h, :].unsqueeze(2).to_broadcast([128, 4, 64]),
                                                                op=ALU.mult), R=[r_hm, nscr["r_rstd"]], W=[r_hn2])
                K.op("dve", lambda h=h: nc.vector.tensor_tensor(out=hn2[:, h, :, :], in0=hn2[:, h, :, :],
                                                                in1=mnw[:, hc].unsqueeze(1).to_broadcast([128, 4, 64]),
                                                                op=ALU.mult), R=[r_hn2, r_c], W=[r_hn2])
                K.op("dve", lambda h=h: nc.vector.tensor_tensor(out=ybt[:, :, hc], in0=hn2[:, h, :, :], in1=og[:, :, hc],
                                                                op=ALU.mult), R=[r_hn2, r_og], W=[r_yb[b % 2]])
            K.dma("sp", y_d[b * 512:(b + 1) * 512, 0:128].rearrange("(i p) c -> p i c", p=128), ybt[:], ds_y[b % 2],
                  R=[r_yb[b % 2]])
            K.op("act", lambda: nc.scalar.copy(out=Cbf[:, 0, :], in_=Cbf[:, 8, :]), R=[r_Cb[8]], W=[r_Cb[0]])
        K.barrier()
        K.end_phase()
        K.stack = outer


NEGV = np.float32(-30000.0)
OFF = {}
_names = ["mq", "mk", "mv", "mo", "mi", "mf", "z", "xbc", "dt", "dq", "dk", "dv"]
_sizes = [256, 256, 256, 256, 4, 4, 512, 1024, 8, 256, 256, 256]
_o = 0
for n, s in zip(_names, _sizes):
    OFF[n] = _o
    _o += s


def t5_bucket_np(rel):
    n = np.maximum(rel, 0)
    nf = np.maximum(n, 1).astype(np.float32)
    large = 16 + (np.log(nf / np.float32(16)) / np.float32(math.log(128 / 16)) * np.float32(16)).astype(np.int32)
    large = np.minimum(large, 31)
    return np.where(n < 16, n, large)


def tile_w(w):
    n = w.shape[1]
    return np.ascontiguousarray(w.reshape(8, 128, n).transpose(1, 0, 2))


def rep(v, n=128):
    v = np.asarray(v, dtype=np.float32).reshape(1, -1)
    return np.ascontiguousarray(np.broadcast_to(v, (n, v.shape[1])))


def cols(w, name, a, b):
    return w[:, OFF[name] + a: OFF[name] + b]


def mixer_params(inp, l, h):
    w = np.asarray(inp["w_in"][l])
    P = {}
    P["wD"] = tile_w(np.concatenate([cols(w, "dq", 128 * h, 128 * h + 128), cols(w, "dk", 128 * h, 128 * h + 128),
                                     cols(w, "dv", 128 * h, 128 * h + 128)], axis=1))
    qn = np.asarray(inp["diff_q_norm_w"][l]).reshape(64)
    kn = np.asarray(inp["diff_k_norm_w"][l]).reshape(64)
    P["qkw"] = rep(np.concatenate([qn, qn, kn, kn]))
    P["sw"] = rep(np.asarray(inp["diff_subln_w"][l]))
    P["lamb"] = rep(np.asarray(inp["diff_lambda"][l]).reshape(-1)).reshape(128, 4, 32)
    rb = np.asarray(inp["rel_bias"])
    kl = np.arange(128)[:, None]
    c = np.arange(1024)[None, :]
    rel = c - kl - 384
    bk = t5_bucket_np(rel)
    Bt = np.zeros((128, 2, 1024), np.float32)
    for hl in range(2):
        Bt[:, hl, :] = np.where(rel >= 0, rb[bk, 2 * h + hl], NEGV)
    P["Bt"] = Bt
    P["c31"] = rep(rb[31, 2 * h: 2 * h + 2])
    xb = OFF["xbc"]
    ch = np.concatenate([np.arange(256 * h, 256 * h + 256), 512 + np.arange(128 * h, 128 * h + 128),
                         768 + np.arange(128 * h, 128 * h + 128)])
    P["wS_fm"] = tile_w(w[:, xb + ch])
    P["wS_tm"] = tile_w(np.concatenate([cols(w, "z", 256 * h, 256 * h + 256), cols(w, "dt", 4 * h, 4 * h + 4)], axis=1))
    cw = np.asarray(inp["ssm_conv_w"][l])[:, ch]
    P["cw"] = np.ascontiguousarray(cw.reshape(4, 4, 128).transpose(2, 1, 0))
    P["cb"] = np.ascontiguousarray(np.asarray(inp["ssm_conv_b"][l])[ch].reshape(4, 128).T)
    P["dtb"] = rep(np.asarray(inp["ssm_dt_bias"][l])[4 * h:4 * h + 4])
    P["alog"] = rep(np.asarray(inp["ssm_A_log"][l])[4 * h:4 * h + 4])
    P["dsk"] = rep(np.asarray(inp["ssm_D"][l])[4 * h:4 * h + 4])
    P["snw"] = rep(np.asarray(inp["ssm_norm_w"][l])[256 * h:256 * h + 256])
    P["wM_fm"] = tile_w(np.concatenate([cols(w, "mq", 128 * h, 128 * h + 128), cols(w, "mk", 128 * h, 128 * h + 128)], axis=1))
    P["wM_g"] = tile_w(np.concatenate([cols(w, "mi", 2 * h, 2 * h + 2), cols(w, "mf", 2 * h, 2 * h + 2)], axis=1))
    P["wM_tm"] = tile_w(np.concatenate([cols(w, "mk", 128 * h, 128 * h + 128), cols(w, "mv", 128 * h, 128 * h + 128),
                                        cols(w, "mo", 128 * h, 128 * h + 128)], axis=1))
    gb = np.asarray(inp["mlstm_gate_bias"][l])
    P["gbias"] = np.ascontiguousarray(gb[:, 2 * h:2 * h + 2].T)
    P["mnw"] = rep(np.asarray(inp["mlstm_norm_w"][l])[128 * h:128 * h + 128])
    return {k: np.ascontiguousarray(v, dtype=np.float32) for k, v in P.items()}


def const_arrays():
    Cn = {}
    Cn["ident_bf"] = np.eye(128, dtype=np.float32).astype(ml_dtypes.bfloat16)
    Cn["ident_f"] = np.eye(128, dtype=np.float32)
    j = np.arange(128)
    tri = (j[:, None] <= j[None, :]).astype(np.float32)
    same = (j[:, None] // 64) == (j[None, :] // 64)
    nm2 = np.where(same & (j[:, None] <= j[None, :]), 0.0, NEGV).astype(np.float32)
    sel = np.zeros((2, 2, 128), np.float32)
    sel[0, 0, :] = 1
    sel[1, 1, :] = 1
    hs = np.zeros((2, 128), np.float32)
    hs[0, :64] = 1
    hs[1, 64:] = 1
    nm1 = np.where(j[:, None] <= j[None, :], 0.0, NEGV).astype(np.float32)
    Cn["cf"] = np.ascontiguousarray(np.concatenate([np.ones((128, 128), np.float32), tri, np.tile(nm2, (1, 4)),
                                                    np.tile(nm1, (1, 4))], axis=1))
    Cn["sel"] = sel
    Cn["hsel"] = hs
    t = np.arange(512)
    rm = (t % 64 != 0).astype(np.float32)
    Cn["rowc"] = np.ascontiguousarray(np.stack([np.stack([rm, rm]), np.stack([(1 - rm) * np.float32(-1e30)] * 2)], axis=1))
    return Cn


import math as _math
from concourse.bass_utils import run_bass_kernel_spmd

NCORES = 8
SEQ = 4096
TOK = 2048
DEPTH = 2
PAIRS = [[0, 1], [2, 3], [4, 5], [6, 7]]


def _tile_gu(w):
    return np.ascontiguousarray(np.asarray(w).reshape(8, 128, 22, 128).transpose(2, 1, 0, 3).reshape(22, 128, 1024))


def _tile_nw(w):
    return np.ascontiguousarray(np.asarray(w).reshape(8, 128).T)


def ffn_host(inp, which, l, tag):
    return {
        tag + "nw": _tile_nw(inp[which + "_norm_w"][l]),
        tag + "wg": _tile_gu(inp[which + "_w_gate"][l]),
        tag + "wu": _tile_gu(inp[which + "_w_up"][l]),
        tag + "wd": np.ascontiguousarray(np.asarray(inp[which + "_w_down"][l]).reshape(22, 128, 1024)),
    }


def ffn_decl(nc, tag):
    d = {}
    d["nw"] = nc.dram_tensor(tag + "nw", [128, 8], F32, kind="ExternalInput").ap()
    d["wg"] = nc.dram_tensor(tag + "wg", [22, 128, 1024], F32, kind="ExternalInput").ap()
    d["wu"] = nc.dram_tensor(tag + "wu", [22, 128, 1024], F32, kind="ExternalInput").ap()
    d["wd"] = nc.dram_tensor(tag + "wd", [22, 128, 1024], F32, kind="ExternalInput").ap()
    return d


def wout_host(inp, l):
    perm = []
    for h in range(2):
        perm += list(range(128 * h, 128 * h + 128))
        perm += list(range(256 + 256 * h, 256 + 256 * h + 256))
        perm += list(range(768 + 128 * h, 768 + 128 * h + 128))
    w = np.asarray(inp["w_out"][l])[np.array(perm), :]
    return np.ascontiguousarray(w.reshape(8, 128, 1024))


def lam_init_of(l):
    return 0.8 - 0.6 * _math.exp(-0.3 * l)


def build_fused(Pshapes, Cn):
    nc = bass.Bass("TRN2", target_bir_lowering=False, num_devices=NCORES)
    Dc = {"ident_bf": nc.dram_tensor("ident_bf", [128, 128], BF16, kind="ExternalInput").ap(),
          "ident_f": nc.dram_tensor("ident_f", [128, 128], F32, kind="ExternalInput").ap()}
    Dk = {k: nc.dram_tensor(k, list(Cn[k].shape), F32, kind="ExternalInput").ap() for k in ["cf", "sel", "hsel", "rowc"]}
    x_d = nc.dram_tensor("x", [TOK, 1024], F32, kind="ExternalInput").ap()
    mh_d = nc.dram_tensor("mh", [128, 2], F32, kind="ExternalInput").ap()
    out_d = nc.dram_tensor("out", [TOK, 1024], F32, kind="ExternalOutput").ap()
    L = []
    for l in range(DEPTH):
        d = {"f1": ffn_decl(nc, f"a{l}_"), "f2": ffn_decl(nc, f"c{l}_")}
        d["mixnw"] = nc.dram_tensor(f"mixnw{l}", [128, 8], F32, kind="ExternalInput").ap()
        d["wo"] = nc.dram_tensor(f"wo{l}", [8, 128, 1024], F32, kind="ExternalInput").ap()
        d["P"] = {k: nc.dram_tensor(f"m{l}_{k}", list(shp), F32, kind="ExternalInput").ap() for k, shp in Pshapes.items()}
        d["P"].update(Dk)
        d["x1"] = nc.dram_tensor(f"x1_{l}", [TOK, 1024], F32, kind="Internal").ap()
        d["hn_own"] = [nc.dram_tensor(f"hn_own{l}_{i}", [1024, 1024], BF16, kind="Internal").ap() for i in range(2)]
        d["hn_all"] = [nc.dram_tensor(f"hn_all{l}_{i}", [2 * 1024, 1024], BF16, kind="Internal").ap() for i in range(2)]
        d["y_half"] = [nc.dram_tensor(f"y_half{l}_{i}", [TOK, 512], BF16, kind="Internal").ap() for i in range(2)]
        d["y_all"] = [nc.dram_tensor(f"y_all{l}_{i}", [2 * TOK, 512], BF16, kind="Internal").ap() for i in range(2)]
        d["r_hn_all"] = [Reg(f"hn_all{l}_{i}") for i in range(2)]
        d["r_y_all"] = [Reg(f"y_all{l}_{i}") for i in range(2)]
        d["x2"] = nc.dram_tensor(f"x2_{l}", [TOK, 1024], F32, kind="Internal").ap()
        d["x3"] = out_d if l == DEPTH - 1 else nc.dram_tensor(f"x3_{l}", [TOK, 1024], F32, kind="Internal").ap()
        L.append(d)
    with ExitStack() as st:
        K = KB(nc, st)
        C = load_consts(K, Dc["ident_bf"], Dc["ident_f"])
        load_mixer_consts(K, C, Dk)
        csem = K.new_dma_sem()
        xin = x_d
        for l in range(DEPTH):
            d = L[l]
            f1, f2 = d["f1"], d["f2"]
            def hn_block_done(b, reg, d=d):
                K.collective("AllGather", d["hn_own"][b], d["hn_all"][b], PAIRS, csem, R=[reg], W=[d["r_hn_all"][b]])
            emit_ffn(K, C, xin, d["x1"], f1["nw"], f1["wg"], f1["wu"], f1["wd"], TOK,
                     hn_out=[ho.rearrange("(k p) t -> k p t", p=128) for ho in d["hn_own"]], nw2_d=d["mixnw"],
                     on_block=hn_block_done)
            with ExitStack() as st2:
                outer = K.stack
                K.stack = st2
                K.begin_phase()
                Wm, stg_stack = load_w_all(K, [("wM_fm", d["P"]["wM_fm"], 256), ("wM_g", d["P"]["wM_g"], 4),
                                               ("wM_tm", d["P"]["wM_tm"], 384), ("wS_fm", d["P"]["wS_fm"], 512),
                                               ("wS_tm", d["P"]["wS_tm"], 260), ("wD", d["P"]["wD"], 384)])
                hnT, r_hnT = load_hnT_pair(K, d["hn_all"], SEQ, regs=d["r_hn_all"])
                ysp = YSplit(d["y_half"][0], d["y_half"][1], TOK)

                def y_hook(t, d=d, ysp=ysp):
                    if t == SEQ // 1024 - 1:
                        K.collective("AllGather", d["y_half"][0], d["y_all"][0], PAIRS, csem, R=[ysp.regs[0]], W=[d["r_y_all"][0]])
                ysp.hook = y_hook
                emit_mlstm2(K, C, hnT, r_hnT, d["P"], ysp, SEQ, W=Wm)
                emit_ssd2(K, C, hnT, r_hnT, d["P"], ysp, SEQ, W=Wm)
                emit_diff(K, C, hnT, r_hnT, d["P"], ysp, SEQ, lam_init_of(l), W=Wm)
                K.end_phase()
                K.stack = outer
            K.collective("AllGather", d["y_half"][1], d["y_all"][1], PAIRS, csem, W=[d["r_y_all"][1]])
            emit_outproj_sel(K, C, d["x1"], d["y_all"], mh_d, d["wo"], d["x2"], TOK, SEQ, yregs=d["r_y_all"])
            emit_ffn(K, C, d["x2"], d["x3"], f2["nw"], f2["wg"], f2["wu"], f2["wd"], TOK)
            xin = d["x3"]
        K.barrier(full=True)
    return nc


def kernel(**inputs):
    inp = {k: np.asarray(v) for k, v in inputs.items()}
    x = inp["x"]
    cores = list(range(NCORES))
    Cn = const_arrays()
    shared = {"ident_bf": Cn["ident_bf"], "ident_f": Cn["ident_f"]}
    for k in ["cf", "sel", "hsel", "rowc"]:
        shared[k] = Cn[k]
    Ps = [[mixer_params(inp, l, h) for h in range(2)] for l in range(DEPTH)]
    for l in range(DEPTH):
        shared.update(ffn_host(inp, "ffn1", l, f"a{l}_"))
        shared.update(ffn_host(inp, "ffn2", l, f"c{l}_"))
        shared[f"mixnw{l}"] = _tile_nw(inp["mix_norm_w"][l])
        shared[f"wo{l}"] = wout_host(inp, l)
    nc = build_fused({k: v.shape for k, v in Ps[0][0].items()}, Cn)
    maps = []
    for c in cores:
        b, h = c // 2, c % 2
        m = dict(shared)
        m["x"] = np.ascontiguousarray(x[b, h * TOK:(h + 1) * TOK])
        mh = np.zeros((128, 2), np.float32)
        mh[:, h] = 1.0
        m["mh"] = mh
        for l in range(DEPTH):
            for k, v in Ps[l][h].items():
                m[f"m{l}_{k}"] = v
        maps.append(m)
    res = run_bass_kernel_spmd(nc, maps, core_ids=cores).results
    out = np.zeros((4, SEQ, 1024), np.float32)
    for c in cores:
        b, h = c // 2, c % 2
        out[b, h * TOK:(h + 1) * TOK] = np.asarray(res[c]["out"])
    return out
```

```python
import math
import numpy as np
import ml_dtypes
from contextlib import ExitStack
import concourse.bass as bass
import concourse.mybir as mybir


F32 = mybir.dt.float32
BF16 = mybir.dt.bfloat16
AF = mybir.ActivationFunctionType
ALU = mybir.AluOpType
AX = mybir.AxisListType


class Reg:
    __slots__ = ("lw", "rd", "name", "excl")

    def __init__(self, name="", excl=False):
        self.excl = excl
        self.lw = None
        self.rd = {}
        self.name = name


class KB:
    def __init__(self, nc, stack):
        self.nc = nc
        self.stack = stack
        self.eng = {"pe": nc.tensor, "dve": nc.vector, "act": nc.scalar,
                    "pool": nc.gpsimd, "sp": nc.sync}
        self.sem = {}
        self.cnt = {}
        self.waited = {}
        for n in self.eng:
            self.sem[n] = stack.enter_context(nc.semaphore("s_" + n))
            self.cnt[n] = 0
            self.waited[n] = {}
        self.top_stack = stack
        self.free_sems = []
        self.phase_sems = []
        self.lazy_keys = set()
        self.dma_sems = []
        self.dma_by_key = {}
        self.n_dma_sem = 0
        self.uid = 0

    def sb(self, shape, dt, name=None):
        self.uid += 1
        return self.stack.enter_context(
            self.nc.sbuf_tensor(f"sb{self.uid}_{name or ""}", list(shape), dt))

    def ps(self, shape, dt, name=None):
        self.uid += 1
        full = 512 if dt == F32 else 1024
        t = self.stack.enter_context(
            self.nc.psum_tensor(f"ps{self.uid}_{name or ""}", [128, full], dt))
        n = 1
        for d in shape[1:]:
            n *= d
        assert n <= full
        v = t[0:shape[0], 0:n]
        if len(shape) == 3:
            v = v.rearrange("p (a b) -> p a b", b=shape[2])
        elif len(shape) == 4:
            v = v.rearrange("p (a b c) -> p a b c", b=shape[2], c=shape[3])
        return v

    def new_dma_sem(self):
        if self.free_sems:
            ent = self.free_sems.pop()
        else:
            s = self.top_stack.enter_context(self.nc.semaphore(f"d{self.n_dma_sem}"))
            self.n_dma_sem += 1
            ent = [s, 0, f"d{self.n_dma_sem}"]
            self.dma_sems.append(ent)
            self.dma_by_key[ent[2]] = ent
        if self.phase_sems:
            self.phase_sems[-1].append(ent)
        return ent

    def begin_phase(self):
        self.phase_sems.append([])

    def end_phase(self):
        for ent in self.phase_sems.pop():
            self.free_sems.append(ent)

    def collective(self, kind, src, dst, groups, csem, R=(), W=()):
        self._deps("pool", R, W, is_dma=True)
        ins = self.nc.gpsimd.collective_compute(kind, mybir.AluOpType.bypass, replica_groups=groups,
                                                ins=[src], outs=[dst])
        csem[1] += 1
        ins.then_inc(csem[0], 1)
        tag = (csem[2], csem[0], csem[1])
        for w in W:
            w.lw = tag
            w.rd = {}
        self.lazy_keys.add(csem[2])
        return ins

    def _wait(self, e, dep):
        key, sem, val = dep
        if key in self.dma_by_key:
            val = self.dma_by_key[key][1]
        w = self.waited[e]
        if w.get(key, 0) >= val:
            return
        self.eng[e].wait_ge(sem, val)
        w[key] = val

    def _deps(self, e, R, W, is_dma=False):
        deps = []
        for r in R:
            if r.lw is not None:
                deps.append(r.lw)
            if r.excl:
                for d in r.rd.values():
                    if d[0] != e:
                        deps.append(d)
        for w in W:
            if w.lw is not None:
                deps.append(w.lw)
            for d in w.rd.values():
                if d[0] != e or is_dma:
                    deps.append(d)
        for d in deps:
            if d[0] == e and e == "pe" and not is_dma:
                continue
            self._wait(e, d)

    def op(self, e, fn, R=(), W=()):
        self._deps(e, R, W)
        ins = fn()
        self.cnt[e] += 1
        ins.then_inc(self.sem[e], 1)
        tag = (e, self.sem[e], self.cnt[e])
        for w in W:
            w.lw = tag
            w.rd = {}
        for r in R:
            if r not in W:
                r.rd[e] = tag
        return ins

    def dma(self, q, out, in_, dsem, R=(), W=(), **kw):
        self._deps(q, R, W, is_dma=True)
        ins = self.eng[q].dma_start(out=out, in_=in_, **kw)
        dsem[1] += 16
        ins.then_inc(dsem[0], 16)
        tag = (dsem[2], dsem[0], dsem[1])
        for w in W:
            w.lw = tag
            w.rd = {}
        for r in R:
            if r not in W:
                r.rd[dsem[2]] = tag
        return ins

    def barrier(self, full=False):
        deps = [(n, self.sem[n], self.cnt[n]) for n in self.eng if self.cnt[n] > 0]
        deps += [(d[2], d[0], d[1]) for d in self.dma_sems if d[1] > 0 and (full or d[2] not in self.lazy_keys)]
        for e in self.eng:
            for d in deps:
                if d[0] != e:
                    self._wait(e, d)


def psbank(K, name):
    K.uid += 1
    t = K.stack.enter_context(K.nc.psum_tensor(f"pb{K.uid}_{name}", [128, 512], F32))
    return t[:, :]


D = 1024
DFF = 2816
NF = DFF // 128
EPS = 1e-6


def load_consts(K, ident_bf_d, ident_f_d):
    C = {}
    C["dsem"] = K.new_dma_sem()
    C["ident_bf"] = K.sb([128, 128], BF16, "ident_bf")
    C["ident_f"] = K.sb([128, 128], F32, "ident_f")
    C["r"] = Reg("consts")
    K.dma("sp", C["ident_bf"][:], ident_bf_d, C["dsem"], W=[C["r"]])
    K.dma("sp", C["ident_f"][:], ident_f_d, C["dsem"], W=[C["r"]])
    return C


def emit_norm_stats(K, C, xt, r_xt, S):
    nc = K.nc
    K.op("act", lambda: nc.scalar.activation(out=S["junk"][:], in_=xt, func=AF.Square,
                                             accum_out=S["ssq"][:]),
         R=[r_xt], W=[S["r_junk"], S["r_ssq"]])
    K.op("dve", lambda: nc.vector.tensor_scalar(out=S["ms"][:], in0=S["ssq"][:], scalar1=1.0 / D,
                                                scalar2=EPS, op0=ALU.mult, op1=ALU.add),
         R=[S["r_ssq"]], W=[S["r_ms"]])
    K.op("act", lambda: nc.scalar.activation(out=S["sd"][:], in_=S["ms"][:], func=AF.Sqrt),
         R=[S["r_ms"]], W=[S["r_sd"]])
    K.op("dve", lambda: nc.vector.reciprocal(out=S["rstd"][:], in_=S["sd"][:]),
         R=[S["r_sd"]], W=[S["r_rstd"]])
    K.op("dve", lambda: nc.vector.tensor_scalar(out=S["xn"][:], in0=xt, scalar1=S["rstd"][:],
                                                scalar2=None, op0=ALU.mult),
         R=[r_xt, S["r_rstd"]], W=[S["r_xn"]])


def emit_norm_tr(K, C, nw, r_nw, dst_fn, r_dst, S):
    nc = K.nc
    for k in range(8):
        K.op("pe", lambda k=k: nc.tensor.transpose(out=S["ptr"][:, k, :],
                                                   in_=S["xn"][:, k * 128:(k + 1) * 128],
                                                   identity=C["ident_bf"][:]),
             R=[S["r_xn"], C["r"]], W=[S["r_ptr"]])
    K.op("dve", lambda: nc.vector.tensor_tensor(out=dst_fn, in0=S["ptr"][:],
                                                in1=nw.unsqueeze(2).to_broadcast([128, 8, 128]),
                                                op=ALU.mult),
         R=[S["r_ptr"], r_nw], W=[r_dst])


def emit_norm_T(K, C, xt, r_xt, nw, r_nw, dst_fn, r_dst, S):
    emit_norm_stats(K, C, xt, r_xt, S)
    emit_norm_tr(K, C, nw, r_nw, dst_fn, r_dst, S)


def norm_scratch(K, tag, share=0, ptr=None, r_ptr=None):
    S = {}
    if share is not None:
        S["junk"] = K.sb([128, D], BF16, tag + "junk")
    S["ssq"] = K.sb([128, 1], F32, tag + "ssq")
    S["ms"] = K.sb([128, 1], F32, tag + "ms")
    S["sd"] = K.sb([128, 1], F32, tag + "sd")
    S["rstd"] = K.sb([128, 1], F32, tag + "rstd")
    S["xn"] = K.sb([128, D], BF16, tag + "xn")
    S["ptr"] = K.ps([128, 8, 128], BF16, tag + "ptr") if ptr is None else ptr
    for n in ["junk", "ssq", "ms", "sd", "rstd", "xn", "ptr"]:
        S["r_" + n] = Reg(tag + n)
    if r_ptr is not None:
        S["r_ptr"] = r_ptr
    return S


def emit_ffn(K, C, x_in, x_out, nw_d, wg_d, wu_d, wd_d, ntok, hn_out=None, nw2_d=None,
             blk=1024, on_block=None):
    nc = K.nc
    outer = K.stack
    with ExitStack() as st:
        K.stack = st
        K.begin_phase()
        nblk = ntok // blk
        nsub = blk // 128
        ntt = blk // 512
        nw = K.sb([128, 8], F32, "nw")
        r_nw = Reg("nw")
        ds_misc = K.new_dma_sem()
        K.dma("sp", nw[:], nw_d, ds_misc, W=[r_nw])
        if hn_out is not None:
            nw2 = K.sb([128, 8], F32, "nw2")
            r_nw2 = Reg("nw2")
            K.dma("sp", nw2[:], nw2_d, ds_misc, W=[r_nw2])
            hnb = K.sb([128, 8, blk], BF16, "hnb")
            r_hnb = Reg("hnb")
            ds_hn = K.new_dma_sem()
        hT = K.sb([128, 8, blk], BF16, "hT")
        r_hT = [Reg(f"hT{j}") for j in range(nsub)]
        aT = K.sb([128, NF, blk], BF16, "aT")
        r_aT = [[Reg(f"aT{f}_{t}") for t in range(ntt)] for f in range(NF)]
        NX = 3
        xt = [K.sb([128, D], F32, f"xt{i}") for i in range(NX)]
        r_xt = [Reg(f"xt{i}") for i in range(NX)]
        ds_xt = [K.new_dma_sem() for i in range(NX)]
        Ss = [norm_scratch(K, "n1"), norm_scratch(K, "n2", share=None)]
        Ss[1]["junk"] = Ss[0]["junk"]; Ss[1]["r_junk"] = Ss[0]["r_junk"]
        NW = 3
        wst = [[K.sb([128, 8 * 128], F32, f"wst{i}_{g}") for g in range(2)] for i in range(NW)]
        r_wst = [[Reg() for g in range(2)] for i in range(NW)]
        ds_w = [K.new_dma_sem() for i in range(NW)]
        wb = [[K.sb([128, 8, 128], BF16, f"wb{i}_{g}") for g in range(2)] for i in range(NW)]
        r_wb = [[Reg() for g in range(2)] for i in range(NW)]
        wdb = K.sb([128, NF, D], BF16, "wdb")
        r_wdb = [Reg(f"wdb{f}") for f in range(NF)]
        NDS = 2
        wdst = [K.sb([128, D], F32, f"wdst{i}") for i in range(NDS)]
        r_wdst = [Reg() for i in range(NDS)]
        ds_wd = [K.new_dma_sem() for i in range(NDS)]
        pg = [K.ps([128, 512], F32, f"pg{i}") for i in range(2)]
        pu = [K.ps([128, 512], F32, f"pu{i}") for i in range(2)]
        r_pg = [Reg() for i in range(2)]
        r_pu = [Reg() for i in range(2)]
        sg = [K.sb([128, 512], F32, f"sg{i}") for i in range(2)]
        r_sg = [Reg() for i in range(2)]
        po = [K.ps([128, 512], F32, f"po{i}") for i in range(2)]
        r_po = [Reg() for i in range(2)]
        ot = [K.sb([128, D], F32, f"ot{i}") for i in range(2)]
        r_ot = [Reg() for i in range(2)]
        ds_ot = [K.new_dma_sem() for i in range(2)]

        Sn = norm_scratch(K, "n3", share=None, ptr=pg[0].bitcast(BF16).rearrange("p (a b) -> p a b", b=128),
                          r_ptr=r_pg[0])
        Sn["junk"] = Ss[0]["junk"]; Sn["r_junk"] = Ss[0]["r_junk"]
        xs = [0]

        def ld_row(row0):
            slot = xs[0] % NX
            xs[0] += 1
            K.dma("sp", xt[slot][:], x_in[row0: row0 + 128, :], ds_xt[slot], W=[r_xt[slot]])
            return slot
        for b in range(nblk):
            t0 = b * blk
            if b == 0:
                nslot = ld_row(t0)
                for j in range(nsub):
                    slot = nslot
                    if j + 1 < nsub:
                        nslot = ld_row(t0 + (j + 1) * 128)
                    emit_norm_T(K, C, xt[slot][:], r_xt[slot], nw[:], r_nw,
                                hT[:, :, j * 128:(j + 1) * 128], r_hT[j], Ss[j % 2])

            def ld_w(f):
                s = f % NW
                K.dma("sp", wst[s][0][:], wg_d[f], ds_w[s], W=[r_wst[s][0]])
                K.dma("sp", wst[s][1][:], wu_d[f], ds_w[s], W=[r_wst[s][1]])

            def ld_wd(f):
                s = f % NDS
                K.dma("act", wdst[s][:], wd_d[f], ds_wd[s], W=[r_wdst[s]])

            def cast_wd(f):
                s = f % NDS
                K.op("act", lambda: nc.scalar.copy(out=wdb[:, f, :], in_=wdst[s][:]),
                     R=[r_wdst[s]], W=[r_wdb[f]])

            def cast_w(f):
                s = f % NW
                K.op("dve", lambda: nc.vector.tensor_copy(
                    out=wb[s][0][:].rearrange("p k c -> p (k c)"), in_=wst[s][0][:]),
                    R=[r_wst[s][0]], W=[r_wb[s][0]])
                K.op("act", lambda: nc.scalar.copy(
                    out=wb[s][1][:].rearrange("p k c -> p (k c)"), in_=wst[s][1][:]),
                    R=[r_wst[s][1]], W=[r_wb[s][1]])

            ld_w(0)
            ld_w(1)
            ld_wd(0)
            ld_wd(1)
            cast_w(0)
            for f in range(NF):
                s = f % NW
                if f + 2 < NF:
                    ld_w(f + 2)
                if f + 1 < NF:
                    cast_w(f + 1)
                cast_wd(f)
                if f + 2 < NF:
                    ld_wd(f + 2)
                for tt in range(ntt):
                    pb = (f * ntt + tt) % 2
                    rr = [r_hT[j] for j in range(tt * 4, tt * 4 + 4)]
                    for k in range(8):
                        K.op("pe", lambda k=k: nc.tensor.matmul(
                            pg[pb][:], lhsT=wb[s][0][:, k, :], rhs=hT[:, k, tt * 512:(tt + 1) * 512],
                            start=(k == 0), stop=(k == 7)),
                            R=[r_wb[s][0]] + rr, W=[r_pg[pb]])
                    for k in range(8):
                        K.op("pe", lambda k=k: nc.tensor.matmul(
                            pu[pb][:], lhsT=wb[s][1][:, k, :], rhs=hT[:, k, tt * 512:(tt + 1) * 512],
                            start=(k == 0), stop=(k == 7)),
                            R=[r_wb[s][1]] + rr, W=[r_pu[pb]])
                    K.op("act", lambda: nc.scalar.activation(out=sg[pb][:], in_=pg[pb][:], func=AF.Silu),
                         R=[r_pg[pb]], W=[r_sg[pb]])
                    K.op("dve", lambda: nc.vector.tensor_tensor(
                        out=aT[:, f, tt * 512:(tt + 1) * 512], in0=sg[pb][:], in1=pu[pb][:], op=ALU.mult),
                        R=[r_sg[pb], r_pu[pb]], W=[r_aT[f][tt]])

            pend = None
            pend2 = None
            nxtb = (b + 1 < nblk)
            nslot = ld_row(t0)
            for j in range(nsub):
                slot = nslot
                if j + 1 < nsub:
                    nslot = ld_row(t0 + (j + 1) * 128)
                os_ = j % 2
                for half in range(2):
                    pb = (j * 2 + half) % 2
                    for f in range(NF):
                        K.op("pe", lambda f=f: nc.tensor.matmul(
                            po[pb][:], lhsT=aT[:, f, j * 128:(j + 1) * 128],
                            rhs=wdb[:, f, half * 512:(half + 1) * 512],
                            start=(f == 0), stop=(f == NF - 1)),
                            R=[r_aT[f][j // 4], r_wdb[f]], W=[r_po[pb]])
                    K.op("dve", lambda: nc.vector.scalar_tensor_tensor(
                        out=ot[os_][:, half * 512:(half + 1) * 512], in0=po[pb][:], scalar=0.5,
                        in1=xt[slot][:, half * 512:(half + 1) * 512], op0=ALU.mult, op1=ALU.add),
                        R=[r_po[pb], r_xt[slot]], W=[r_ot[os_]])
                K.dma("sp", x_out[t0 + j * 128: t0 + (j + 1) * 128, :], ot[os_][:], ds_ot[os_],
                      R=[r_ot[os_]])
                if hn_out is not None:
                    if pend is not None:
                        emit_norm_tr(K, C, nw2[:], r_nw2, *pend)
                    emit_norm_stats(K, C, ot[os_][:], r_ot[os_], Ss[j % 2])
                    pend = (hnb[:, :, j * 128:(j + 1) * 128], r_hnb, Ss[j % 2])
                if nxtb:
                    if pend2 is not None:
                        emit_norm_tr(K, C, nw[:], r_nw, *pend2)
                    s2 = ld_row(t0 + blk + j * 128)
                    emit_norm_stats(K, C, xt[s2][:], r_xt[s2], Sn)
                    pend2 = (hT[:, :, j * 128:(j + 1) * 128], r_hT[j], Sn)
            if hn_out is not None and pend is not None:
                emit_norm_tr(K, C, nw2[:], r_nw2, *pend)
                pend = None
            if pend2 is not None:
                emit_norm_tr(K, C, nw[:], r_nw, *pend2)
                pend2 = None
            if hn_out is not None:
                if isinstance(hn_out, (list, tuple)):
                    rd_ = Reg(f"hn_dram{b}")
                    K.dma("sp", hn_out[b].rearrange("k p t -> p k t"), hnb[:], ds_hn, R=[r_hnb], W=[rd_])
                    if on_block is not None:
                        on_block(b, rd_)
                else:
                    K.dma("sp", hn_out[:, :, t0:t0 + blk].rearrange("k p t -> p k t"), hnb[:], ds_hn,
                          R=[r_hnb])
        K.barrier()
        K.end_phase()
        K.stack = outer


def emit_outproj(K, C, x_in, y_in, wo_d, x_out, ntok):
    nc = K.nc
    outer = K.stack
    with ExitStack() as st:
        K.stack = st
        K.begin_phase()
        wob = K.sb([128, 8, D], BF16, "wob")
        r_wob = Reg()
        wst = [K.sb([128, D], F32, f"wost{i}") for i in range(2)]
        r_wst = [Reg() for i in range(2)]
        ds_w = [K.new_dma_sem() for i in range(2)]
        for k in range(8):
            K.dma("sp", wst[k % 2][:], wo_d[k], ds_w[k % 2], W=[r_wst[k % 2]])
            K.op("pool", lambda k=k: nc.gpsimd.tensor_copy(out=wob[:, k, :], in_=wst[k % 2][:]),
                 R=[r_wst[k % 2]], W=[r_wob])
        NS = 3
        yt = [K.sb([128, D], BF16, f"yt{i}") for i in range(NS)]
        r_yt = [Reg() for i in range(NS)]
        xt = [K.sb([128, D], F32, f"oxt{i}") for i in range(NS)]
        r_xt = [Reg() for i in range(NS)]
        ds_in = [K.new_dma_sem() for i in range(NS)]
        ptr = [K.ps([128, 8, 128], BF16, f"optr{i}") for i in range(2)]
        r_ptr = [Reg() for i in range(2)]
        yT = [K.sb([128, 8, 128], BF16, f"yT{i}") for i in range(2)]
        r_yT = [Reg() for i in range(2)]
        po = [K.ps([128, 512], F32, f"opo{i}") for i in range(2)]
        r_po = [Reg() for i in range(2)]
        ot = [K.sb([128, D], F32, f"oot{i}") for i in range(2)]
        r_ot = [Reg() for i in range(2)]
        ds_ot = [K.new_dma_sem() for i in range(2)]
        nsub = ntok // 128

        def ld(j):
            s = j % NS
            K.dma("sp", yt[s][:], y_in[j * 128:(j + 1) * 128, :], ds_in[s], W=[r_yt[s]])
            K.dma("sp", xt[s][:], x_in[j * 128:(j + 1) * 128, :], ds_in[s], W=[r_xt[s]])
        ld(0)
        for j in range(nsub):
            s = j % NS
            p2 = j % 2
            if j + 1 < nsub:
                ld(j + 1)
            for k in range(8):
                K.op("pe", lambda k=k: nc.tensor.transpose(out=ptr[p2][:, k, :],
                                                           in_=yt[s][:, k * 128:(k + 1) * 128],
                                                           identity=C["ident_bf"][:]),
                     R=[r_yt[s], C["r"]], W=[r_ptr[p2]])
            K.op("act", lambda: nc.scalar.copy(out=yT[p2][:], in_=ptr[p2][:]),
                 R=[r_ptr[p2]], W=[r_yT[p2]])
            for half in range(2):
                pb = half
                for k in range(8):
                    K.op("pe", lambda k=k: nc.tensor.matmul(
                        po[pb][:], lhsT=yT[p2][:, k, :], rhs=wob[:, k, half * 512:(half + 1) * 512],
                        start=(k == 0), stop=(k == 7)),
                        R=[r_yT[p2], r_wob], W=[r_po[pb]])
                K.op("dve", lambda: nc.vector.tensor_tensor(
                    out=ot[p2][:, half * 512:(half + 1) * 512], in0=po[pb][:],
                    in1=xt[s][:, half * 512:(half + 1) * 512], op=ALU.add),
                    R=[r_po[pb], r_xt[s]], W=[r_ot[p2]])
            K.dma("sp", x_out[j * 128:(j + 1) * 128, :], ot[p2][:], ds_ot[p2], R=[r_ot[p2]])
        K.barrier()
        K.end_phase()
        K.stack = outer


def emit_outproj_sel(K, C, x_in, yall, mh_d, wo_d, x_out, ntok, seq, yregs=None):
    nc = K.nc
    outer = K.stack
    with ExitStack() as st:
        K.stack = st
        K.begin_phase()
        wob = K.sb([128, 8, D], BF16, "wob")
        r_wob = Reg()
        wst = [K.sb([128, D], F32, f"wost{i}") for i in range(2)]
        r_wst = [Reg() for i in range(2)]
        ds_w = [K.new_dma_sem() for i in range(2)]
        mh = K.sb([128, 2], F32, "mh"); r_mh = Reg()
        K.dma("sp", mh[:], mh_d, ds_w[0], W=[r_mh])
        for k in range(8):
            K.dma("sp", wst[k % 2][:], wo_d[k], ds_w[k % 2], W=[r_wst[k % 2]])
            if k % 2 == 0:
                K.op("dve", lambda k=k: nc.vector.tensor_copy(out=wob[:, k, :], in_=wst[k % 2][:]),
                     R=[r_wst[k % 2]], W=[r_wob])
            else:
                K.op("act", lambda k=k: nc.scalar.copy(out=wob[:, k, :], in_=wst[k % 2][:]),
                     R=[r_wst[k % 2]], W=[r_wob])
        NS = 3
        ya = [[K.sb([128, 2, 512], BF16, f"ya{i}_{c}") for c in range(2)] for i in range(NS)]
        r_ya = [[Reg(), Reg()] for i in range(NS)]
        yt = [K.sb([128, D], BF16, f"yt{i}") for i in range(2)]
        r_yt = [Reg() for i in range(2)]
        xt = [K.sb([128, D], F32, f"oxt{i}") for i in range(NS)]
        r_xt = [Reg() for i in range(NS)]
        ds_in = [K.new_dma_sem() for i in range(NS)]
        ptr = [K.ps([128, 8, 128], BF16, f"optr{i}") for i in range(2)]
        r_ptr = [Reg() for i in range(2)]
        yT = [K.sb([128, 8, 128], BF16, f"yT{i}") for i in range(2)]
        r_yT = [Reg() for i in range(2)]
        po = [K.ps([128, 512], F32, f"opo{i}") for i in range(2)]
        r_po = [Reg() for i in range(2)]
        ot = [K.sb([128, D], F32, f"oot{i}") for i in range(2)]
        r_ot = [Reg() for i in range(2)]
        ds_ot = [K.new_dma_sem() for i in range(2)]
        nsub = ntok // 128
        yv = [ya_.rearrange("(r t) c -> t r c", r=2) for ya_ in yall]

        def ld(j):
            s = j % NS
            for c in range(2):
                K.dma("sp" if c == 0 else "act", ya[s][c][:], yv[c][j * 128:(j + 1) * 128, :, :],
                      ds_in[s], W=[r_ya[s][c]], R=([yregs[c]] if yregs is not None else []))
            K.dma("sp", xt[s][:], x_in[j * 128:(j + 1) * 128, :], ds_in[s], W=[r_xt[s]])
        def front(j):
            s = j % NS
            p2 = j % 2
            K.op("dve", lambda: nc.vector.tensor_scalar(out=yt[p2][:], in0=ya[s][0][:].rearrange("p r c -> p (r c)"),
                                                        scalar1=mh[:, 0:1], scalar2=None, op0=ALU.mult),
                 R=[r_ya[s][0], r_mh], W=[r_yt[p2]])
            K.op("dve", lambda: nc.vector.scalar_tensor_tensor(out=yt[p2][:], in0=ya[s][1][:].rearrange("p r c -> p (r c)"),
                                                               scalar=mh[:, 1:2], in1=yt[p2][:], op0=ALU.mult, op1=ALU.add),
                 R=[r_ya[s][1], r_mh, r_yt[p2]], W=[r_yt[p2]])
            for k in range(8):
                K.op("pe", lambda k=k: nc.tensor.transpose(out=ptr[p2][:, k, :],
                                                           in_=yt[p2][:, k * 128:(k + 1) * 128],
                                                           identity=C["ident_bf"][:]),
                     R=[r_yt[p2], C["r"]], W=[r_ptr[p2]])
            K.op("act", lambda: nc.scalar.copy(out=yT[p2][:], in_=ptr[p2][:]),
                 R=[r_ptr[p2]], W=[r_yT[p2]])

        ld(0)
        if nsub > 1:
            ld(1)
        front(0)
        for j in range(nsub):
            s = j % NS
            p2 = j % 2
            if j + 2 < nsub:
                ld(j + 2)
            if j + 1 < nsub:
                front(j + 1)
            for half in range(2):
                pb = half
                for k in range(8):
                    K.op("pe", lambda k=k: nc.tensor.matmul(
                        po[pb][:], lhsT=yT[p2][:, k, :], rhs=wob[:, k, half * 512:(half + 1) * 512],
                        start=(k == 0), stop=(k == 7)),
                        R=[r_yT[p2], r_wob], W=[r_po[pb]])
                K.op("dve", lambda: nc.vector.tensor_tensor(
                    out=ot[p2][:, half * 512:(half + 1) * 512], in0=po[pb][:],
                    in1=xt[s][:, half * 512:(half + 1) * 512], op=ALU.add),
                    R=[r_po[pb], r_xt[s]], W=[r_ot[p2]])
            K.dma("sp", x_out[j * 128:(j + 1) * 128, :], ot[p2][:], ds_ot[p2], R=[r_ot[p2]])
        K.barrier()
        K.end_phase()
        K.stack = outer


EPS = 1e-6
NEG = -30000.0


def load_w(K, w_d, ncols, name):
    nc = K.nc
    wb = K.sb([128, 8, ncols], BF16, name)
    r_wb = Reg(name)
    with ExitStack() as st:
        outer = K.stack
        K.stack = st
        K.begin_phase()
        stg = [K.sb([128, ncols], F32, f"{name}st{i}") for i in range(2)]
        r_st = [Reg() for i in range(2)]
        ds = [K.new_dma_sem() for i in range(2)]
        for k in range(8):
            K.dma("sp", stg[k % 2][:], w_d[:, k, :], ds[k % 2], W=[r_st[k % 2]])
            K.op("pool", lambda k=k: nc.gpsimd.tensor_copy(out=wb[:, k, :], in_=stg[k % 2][:]),
                 R=[r_st[k % 2]], W=[r_wb])
        K.barrier()
        K.end_phase()
        K.stack = outer
    return wb, r_wb


def load_w_all(K, specs, nslots=4):
    nc = K.nc
    out = {}
    for key, w_d, ncols in specs:
        out[key] = (K.sb([128, 8, ncols], BF16, "w_" + key), Reg("w_" + key))
    stg_stack = None
    stg = [K.sb([128, 768], F32, f"wstg{i}") for i in range(nslots)]
    r_st = [Reg() for _ in range(nslots)]
    ds = [K.new_dma_sem() for _ in range(nslots)]
    n = 0
    for key, w_d, ncols in specs:
        wb, r_wb = out[key]
        for k in range(8):
            s = n % nslots
            K.dma("sp" if n % 2 == 0 else "act", stg[s][:, 0:ncols], w_d[:, k, :], ds[s], W=[r_st[s]])
            if n % 2 == 0:
                K.op("dve", lambda k=k: nc.vector.tensor_copy(out=wb[:, k, :], in_=stg[s][:, 0:ncols]),
                     R=[r_st[s]], W=[r_wb])
            else:
                K.op("act", lambda k=k: nc.scalar.copy(out=wb[:, k, :], in_=stg[s][:, 0:ncols]),
                     R=[r_st[s]], W=[r_wb])
            n += 1
    return out, stg_stack


def rstd_from_ssq(K, ssq, r_ssq, n, scr, inv_n):
    nc = K.nc
    K.op("dve", lambda: nc.vector.tensor_scalar(out=scr["ms"], in0=ssq, scalar1=inv_n, scalar2=EPS,
                                                op0=ALU.mult, op1=ALU.add),
         R=[r_ssq], W=[scr["r_ms"]])
    K.op("act", lambda: nc.scalar.activation(out=scr["sd"], in_=scr["ms"], func=AF.Sqrt),
         R=[scr["r_ms"]], W=[scr["r_sd"]])
    K.op("dve", lambda: nc.vector.reciprocal(out=scr["rstd"], in_=scr["sd"]),
         R=[scr["r_sd"]], W=[scr["r_rstd"]])


def rstd_explog(K, ssq, r_ssq, scr, inv_n):
    nc = K.nc
    K.op("dve", lambda: nc.vector.tensor_scalar(out=scr["ms"], in0=ssq, scalar1=inv_n, scalar2=EPS,
                                                op0=ALU.mult, op1=ALU.add),
         R=[r_ssq], W=[scr["r_ms"]])
    K.op("act", lambda: nc.scalar.activation(out=scr["sd"], in_=scr["ms"], func=AF.Ln),
         R=[scr["r_ms"]], W=[scr["r_sd"]])
    K.op("act", lambda: nc.scalar.activation(out=scr["rstd"], in_=scr["sd"], func=AF.Exp, scale=-0.5),
         R=[scr["r_sd"]], W=[scr["r_rstd"]])


def mk_scr(K, shape, tag):
    d = {}
    for n in ["ms", "sd", "rstd"]:
        t = K.sb(shape, F32, tag + n)
        d[n] = t[:]
        d["r_" + n] = Reg(tag + n)
    return d


def emit_diff(K, C, hnT, r_hnT, P, y_d, S, lam_init, W=None):
    nc = K.nc
    nT = S // 128
    nQ = S // 512
    outer = K.stack
    with ExitStack() as st:
        K.stack = st
        K.begin_phase()
        wb, r_wb = W["wD"] if W is not None else load_w(K, P["wD"], 384, "wD")
        dsm = K.new_dma_sem()
        qkw = K.sb([128, 256], F32, "qkw"); r_c = Reg("dconst")
        sw = K.sb([128, 64], F32, "sw")
        lamb = K.sb([128, 4, 32], F32, "lamb")
        Bt = K.sb([128, 2, 1024], F32, "Bt")
        c31 = K.sb([128, 2], F32, "c31")
        K.dma("sp", qkw[:], P["qkw"], dsm, W=[r_c])
        K.dma("sp", sw[:], P["sw"], dsm, W=[r_c])
        K.dma("sp", lamb[:], P["lamb"], dsm, W=[r_c])
        K.dma("sp", Bt[:], P["Bt"], dsm, W=[r_c])
        K.dma("sp", c31[:], P["c31"], dsm, W=[r_c])
        qT = K.sb([128, S], BF16, "dqT"); r_qT = [Reg() for _ in range(nT)]
        kT = K.sb([128, S], BF16, "dkT"); r_kT = [Reg() for _ in range(nT)]
        qTb = K.sb([32, S], BF16, "dqTb")
        kTb = K.sb([32, S], BF16, "dkTb")
        vaug = K.sb([128, nT, 2, 65], BF16, "dvaug"); r_v = [Reg() for _ in range(nT)]
        zer = K.sb([128, 260], BF16, "zer"); r_z = Reg()
        K.op("dve", lambda: nc.vector.memset(zer[:], 0.0), W=[r_z])
        K.op("dve", lambda: nc.vector.memset(vaug[:].rearrange("p a b c -> p (a b c)"), 1.0), W=r_v)
        lt = K.sb([128, 2, 32], F32, "lt"); r_lt = Reg()
        ls = K.sb([128, 2], F32, "ls"); r_ls = Reg()
        le = K.sb([128, 2], F32, "le"); r_le = Reg()
        nlam = K.sb([128, 1], F32, "nlam"); r_nlam = Reg()
        swl = K.sb([128, 64], F32, "swl"); r_swl = Reg()
        K.op("dve", lambda: nc.vector.tensor_tensor(out=lt[:, 0, :], in0=lamb[:, 0, :], in1=lamb[:, 1, :], op=ALU.mult),
             R=[r_c], W=[r_lt])
        K.op("dve", lambda: nc.vector.tensor_tensor(out=lt[:, 1, :], in0=lamb[:, 2, :], in1=lamb[:, 3, :], op=ALU.mult),
             R=[r_c], W=[r_lt])
        K.op("dve", lambda: nc.vector.tensor_reduce(out=ls[:], in_=lt[:], axis=AX.X, op=ALU.add), R=[r_lt], W=[r_ls])
        K.op("act", lambda: nc.scalar.activation(out=le[:], in_=ls[:], func=AF.Exp), R=[r_ls], W=[r_le])
        K.op("dve", lambda: nc.vector.tensor_tensor(out=nlam[:], in0=le[:, 1:2], in1=le[:, 0:1], op=ALU.subtract),
             R=[r_le], W=[r_nlam])
        K.op("dve", lambda: nc.vector.tensor_scalar(out=nlam[:], in0=nlam[:], scalar1=-lam_init, scalar2=None, op0=ALU.add),
             R=[r_nlam], W=[r_nlam])
        K.op("dve", lambda: nc.vector.tensor_scalar(out=swl[:], in0=sw[:], scalar1=1.0 - lam_init, scalar2=None, op0=ALU.mult),
             R=[r_c], W=[r_swl])

        GD = 4
        st_d2 = ExitStack()
        K.stack = st_d2
        ppb = [psbank(K, f"dpp{i}") for i in range(GD)]; r_ppb = [Reg(excl=True) for _ in range(GD)]
        ptrb = [K.ps([128, 8, 128], BF16, f"dptr{i}") for i in range(GD // 2)]; r_ptrb = [Reg() for _ in range(GD // 2)]
        sq = K.sb([128, GD, 256], F32, "dsq"); r_sq = Reg()
        ssq = K.sb([128, GD * 8], F32, "dssq"); r_ssq = Reg()
        scr = mk_scr(K, [128, GD * 8], "dq")
        qn = K.sb([128, GD, 256], F32, "dqn"); r_qn = Reg()
        qnb = K.sb([128, GD, 256], BF16, "dqnb"); r_qnb = Reg()
        K.stack = st
        scale = 32 ** -0.5
        for i0 in range(0, nT, GD):
            tss = [slice((i0 + i) * 128, (i0 + i + 1) * 128) for i in range(GD)]
            for i in range(GD):
                for k in range(8):
                    K.op("pe", lambda k=k, i=i: nc.tensor.matmul(ppb[i][:, 0:384], lhsT=hnT[:, k, tss[i]], rhs=wb[:, k, :],
                                                                 start=(k == 0), stop=(k == 7)),
                         R=[r_hnT, r_wb], W=[r_ppb[i]])
            for i in range(GD):
                K.op("act", lambda i=i: nc.scalar.activation(out=sq[:, i, :], in_=ppb[i][:, 0:256], func=AF.Square),
                     R=[r_ppb[i]], W=[r_sq])
            K.op("dve", lambda: nc.vector.tensor_reduce(out=ssq[:], in_=sq[:].rearrange("p t (g d) -> p (t g) d", d=32),
                                                        axis=AX.X, op=ALU.add), R=[r_sq], W=[r_ssq])
            rstd_explog(K, ssq[:], r_ssq, scr, 1.0 / 32)
            rs3 = scr["rstd"].rearrange("p (t g) -> p t g", g=8)
            K.op("dve", lambda: nc.vector.tensor_scalar(out=rs3[:, :, 0:4], in0=rs3[:, :, 0:4], scalar1=scale,
                                                        scalar2=None, op0=ALU.mult), R=[scr["r_rstd"]], W=[scr["r_rstd"]])
            for i in range(GD):
                K.op("dve", lambda i=i: nc.vector.tensor_tensor(
                    out=qn[:, i, :].rearrange("p (g d) -> p g d", d=32), in0=ppb[i][:, 0:256].rearrange("p (g d) -> p g d", d=32),
                    in1=rs3[:, i, :].unsqueeze(2).to_broadcast([128, 8, 32]), op=ALU.mult),
                    R=[r_ppb[i], scr["r_rstd"]], W=[r_qn])
                K.op("dve", lambda i=i: nc.vector.tensor_copy(out=vaug[:, i0 + i, :, 0:64],
                                                              in_=ppb[i][:, 256:384].rearrange("p (h d) -> p h d", d=64)),
                     R=[r_ppb[i]], W=[r_v[i0 + i]])
            K.op("dve", lambda: nc.vector.tensor_tensor(out=qnb[:], in0=qn[:], in1=qkw[:].unsqueeze(1).to_broadcast([128, GD, 256]),
                                                        op=ALU.mult), R=[r_qn, r_c], W=[r_qnb])
            for i in range(GD):
                ptr = ptrb[i // 2][:, (i % 2) * 4:(i % 2) * 4 + 4, :]; r_ptr = r_ptrb[i // 2]
                for a in range(2):
                    K.op("pe", lambda a=a, i=i: nc.tensor.transpose(out=ptr[0:96, 2 * a, :], in_=qnb[:, i, a * 128:a * 128 + 96],
                                                                    identity=C["ident_bf"][:]),
                         R=[r_qnb, C["r"]], W=[r_ptr])
                    K.op("pe", lambda a=a, i=i: nc.tensor.transpose(out=ptr[0:32, 2 * a + 1, :], in_=qnb[:, i, a * 128 + 96:a * 128 + 128],
                                                                    identity=C["ident_bf"][:]),
                         R=[r_qnb, C["r"]], W=[r_ptr])
            for i in range(GD):
                ptr = ptrb[i // 2][:, (i % 2) * 4:(i % 2) * 4 + 4, :]; r_ptr = r_ptrb[i // 2]
                ti = i0 + i
                K.op("act", lambda: nc.scalar.copy(out=qT[0:96, tss[i]], in_=ptr[0:96, 0, :]), R=[r_ptr], W=[r_qT[ti]])
                K.op("act", lambda: nc.scalar.copy(out=qTb[0:32, tss[i]], in_=ptr[0:32, 1, :]), R=[r_ptr], W=[r_qT[ti]])
                K.op("act", lambda: nc.scalar.copy(out=kT[0:96, tss[i]], in_=ptr[0:96, 2, :]), R=[r_ptr], W=[r_kT[ti]])
                K.op("act", lambda: nc.scalar.copy(out=kTb[0:32, tss[i]], in_=ptr[0:32, 3, :]), R=[r_ptr], W=[r_kT[ti]])

        K.barrier()
        st_d2.close()
        NSB = 4
        sbk = [K.ps([128, 512], F32, f"dsb{i}") for i in range(NSB)]; r_sb = [Reg() for _ in range(NSB)]
        acc = [K.ps([128, 4, 65], F32, f"dacc{i}") for i in range(4)]; r_acc = [Reg() for _ in range(4)]
        NP = 8
        pT = [K.sb([128, 512], BF16, f"dpT{i}") for i in range(NP)]; r_pT = [Reg() for _ in range(NP)]
        tmp = [K.sb([128, 512], F32, f"dtmp{i}") for i in range(2)]; r_tmp = [Reg() for _ in range(2)]
        rd = K.sb([128, 2, 4], F32, "drd"); r_rd = Reg()
        o1 = K.sb([128, 4, 64], F32, "do1"); r_o1 = Reg()
        o2 = K.sb([128, 4, 64], F32, "do2"); r_o2 = Reg()
        osq = K.sb([128, 4, 64], F32, "dosq"); r_osq = Reg()
        oss = K.sb([128, 4], F32, "doss"); r_oss = Reg()
        oscr = mk_scr(K, [128, 4], "do")
        yb = [K.sb([128, 4, 128], BF16, f"dyb{i}") for i in range(2)]; r_yb = [Reg() for _ in range(2)]
        ds_y = [K.new_dma_sem() for _ in range(2)]
        cnt = 0
        ntmp = 0
        ai = 0
        for t in range(nQ):
            ybt = yb[t % 2]
            for hl in range(2):
                accs = []
                items = []
                for c in range(2):
                    g = hl * 2 + c
                    A = acc[ai % 4]; rA = r_acc[ai % 4]; ai += 1
                    accs.append((A, rA))
                    K.op("pe", lambda: nc.tensor.matmul(A[:].rearrange("p a b -> p (a b)"), lhsT=zer[:, 0:128],
                                                        rhs=zer[:, 0:260], start=True, stop=False),
                         R=[r_z], W=[rA])
                for j in range(4 * t + 4):
                    for c in range(2):
                        items.append((c, hl * 2 + c, accs[c][0], accs[c][1], j))

                def emit_S(it):
                    nonlocal cnt, ntmp
                    c, g, A, rA, j = it
                    pr = slice(32 * g, 32 * g + 32) if g < 3 else slice(0, 32)
                    qTg = qT if g < 3 else qTb
                    kTg = kT if g < 3 else kTb
                    m = 4 * t - j
                    c0 = max(0, -m) * 128
                    sb_ = sbk[cnt % NSB]; rs = r_sb[cnt % NSB]
                    p_ = pT[cnt % NP]; rp = r_pT[cnt % NP]
                    cnt += 1
                    K.op("pe", lambda: nc.tensor.matmul(sb_[:, c0:512], lhsT=kTg[pr, j * 128:(j + 1) * 128],
                                                        rhs=qTg[pr, t * 512 + c0:(t + 1) * 512], start=True, stop=True),
                         R=[r_kT[j]] + [r_qT[4 * t + x] for x in range(c0 // 128, 4)], W=[rs])
                    if m >= 2:
                        K.op("act", lambda: nc.scalar.activation(out=p_[:, c0:512], in_=sb_[:, c0:512], func=AF.Exp,
                                                                 bias=c31[:, hl:hl + 1]),
                             R=[rs, r_c], W=[rp])
                    else:
                        tm_ = tmp[ntmp % 2]; rt = r_tmp[ntmp % 2]; ntmp += 1
                        b0 = 128 * m + 384
                        K.op("dve", lambda: nc.vector.tensor_tensor(out=tm_[:, c0:512], in0=sb_[:, c0:512],
                                                                    in1=Bt[:, hl, b0 + c0:b0 + 512], op=ALU.add),
                             R=[rs, r_c], W=[rt])
                        K.op("act", lambda: nc.scalar.activation(out=p_[:, c0:512], in_=tm_[:, c0:512], func=AF.Exp),
                             R=[rt], W=[rp])
                    return (p_, rp, c0)

                def emit_AV(it, pinfo):
                    c, g, A, rA, j = it
                    p_, rp, c0 = pinfo
                    for sub in range(c0 // 128, 4):
                        K.op("pe", lambda sub=sub: nc.tensor.matmul(
                            A[:, sub, :], lhsT=p_[:, sub * 128:(sub + 1) * 128], rhs=vaug[:, j, hl, :],
                            start=False, stop=(j == 4 * t + 3 and sub == 3)),
                            R=[rp, r_v[j]], W=[rA])

                LOOK = 4
                pend = []
                for idx in range(0, len(items), 4):
                    for it in items[idx:idx + 4]:
                        pend.append((it, emit_S(it)))
                    while len(pend) > LOOK:
                        emit_AV(*pend.pop(0))
                while pend:
                    emit_AV(*pend.pop(0))
                (A1, rA1), (A2, rA2) = accs
                K.op("dve", lambda: nc.vector.reciprocal(out=rd[:, 0, :], in_=A1[:, :, 64]), R=[rA1], W=[r_rd])
                K.op("dve", lambda: nc.vector.reciprocal(out=rd[:, 1, :], in_=A2[:, :, 64]), R=[rA2], W=[r_rd])
                K.op("dve", lambda: nc.vector.tensor_scalar(out=rd[:, 1, :], in0=rd[:, 1, :], scalar1=nlam[:, 0:1],
                                                            scalar2=None, op0=ALU.mult), R=[r_rd, r_nlam], W=[r_rd])
                K.op("dve", lambda: nc.vector.tensor_tensor(out=o1[:], in0=A1[:, :, 0:64],
                                                            in1=rd[:, 0, :].unsqueeze(2).to_broadcast([128, 4, 64]),
                                                            op=ALU.mult), R=[rA1, r_rd], W=[r_o1])
                K.op("dve", lambda: nc.vector.tensor_tensor(out=o2[:], in0=A2[:, :, 0:64],
                                                            in1=rd[:, 1, :].unsqueeze(2).to_broadcast([128, 4, 64]),
                                                            op=ALU.mult), R=[rA2, r_rd], W=[r_o2])
                K.op("pool", lambda: nc.gpsimd.tensor_tensor(out=o1[:], in0=o1[:], in1=o2[:], op=ALU.add),
                     R=[r_o1, r_o2], W=[r_o1])
                K.op("dve", lambda: nc.vector.tensor_tensor(out=osq[:], in0=o1[:], in1=o1[:], op=ALU.mult), R=[r_o1], W=[r_osq])
                K.op("dve", lambda: nc.vector.tensor_reduce(out=oss[:], in_=osq[:], axis=AX.X, op=ALU.add),
                     R=[r_osq], W=[r_oss])
                rstd_explog(K, oss[:], r_oss, oscr, 1.0 / 64)
                K.op("dve", lambda: nc.vector.tensor_tensor(out=o2[:], in0=o1[:],
                                                            in1=oscr["rstd"].unsqueeze(2).to_broadcast([128, 4, 64]),
                                                            op=ALU.mult), R=[r_o1, oscr["r_rstd"]], W=[r_o2])
                K.op("dve", lambda: nc.vector.tensor_tensor(out=ybt[:, :, hl * 64:(hl + 1) * 64], in0=o2[:],
                                                            in1=swl[:].unsqueeze(1).to_broadcast([128, 4, 64]),
                                                            op=ALU.mult), R=[r_o2, r_swl], W=[r_yb[t % 2]])
            K.dma("sp", y_d[t * 512:(t + 1) * 512, 384:512].rearrange("(s p) c -> p s c", p=128), ybt[:],
                  ds_y[t % 2], R=[r_yb[t % 2]], W=([y_d.reg(t * 512)] if hasattr(y_d, "reg") else []))
            if getattr(y_d, "hook", None) is not None:
                y_d.hook(t)
        K.barrier()
        K.end_phase()
        K.stack = outer


def load_mixer_consts(K, C, D):
    ds = C["dsem"]
    C["cf"] = K.sb([128, 1280], F32, "cf")
    C["sel"] = K.sb([2, 2, 128], F32, "sel")
    C["hsel"] = K.sb([2, 128], F32, "hsel")
    C["rowc"] = K.sb([2, 2, 512], F32, "rowc")
    K.dma("sp", C["cf"][:], D["cf"], ds, W=[C["r"]])
    K.dma("sp", C["sel"][:], D["sel"], ds, W=[C["r"]])
    K.dma("sp", C["hsel"][:], D["hsel"], ds, W=[C["r"]])
    K.dma("sp", C["rowc"][:], D["rowc"], ds, W=[C["r"]])
    C["ones"] = C["cf"][:, 0:128]
    C["tri"] = C["cf"][:, 128:256]
    C["nm2"] = C["cf"][:, 256:768]
    C["nm1"] = C["cf"][:, 768:1280]


def load_hnT(K, hn_d, S):
    hnT = K.sb([128, 8, S], BF16, "hnT")
    r = Reg("hnT")
    ds = K.new_dma_sem()
    for k in range(8):
        K.dma("sp" if k % 2 == 0 else "act", hnT[:, k, :], hn_d[k], ds, W=[r])
    return hnT, r


def emit_ssd_gen(K, C, hnT, r_hnT, P, y_d, S, W, nsets=2):
    STOP = 99
    nc = K.nc
    nT = S // 128
    nTT = S // 512
    if True:
        wfm, r_wfm = W["wS_fm"]
        wtm, r_wtm = W["wS_tm"]
        dsm = K.new_dma_sem()
        r_c = Reg("sconst")
        cw = K.sb([128, 4, 4], F32, "cw"); cb = K.sb([128, 4], F32, "cb")
        dtb = K.sb([128, 4], F32, "dtb"); alog = K.sb([128, 4], F32, "alog")
        dsk = K.sb([128, 4], F32, "dsk"); snw = K.sb([128, 256], F32, "snw")
        for t_, n_ in [(cw, "cw"), (cb, "cb"), (dtb, "dtb"), (alog, "alog"), (dsk, "dsk"), (snw, "snw")]:
            K.dma("sp", t_[:], P[n_], dsm, W=[r_c])
        Aneg = K.sb([128, 4], F32, "Aneg"); r_A = Reg()
        K.op("act", lambda: nc.scalar.activation(out=Aneg[:], in_=alog[:], func=AF.Exp), R=[r_c], W=[r_A])
        K.op("dve", lambda: nc.vector.tensor_scalar(out=Aneg[:], in0=Aneg[:], scalar1=-1.0, scalar2=None, op0=ALU.mult),
             R=[r_A], W=[r_A])
        xc = [K.sb([128, S], BF16, f"xc{i}") for i in range(4)]
        r_xc = [Reg() for _ in range(4)]
        def T(shape, dt, name):
            return K.sb(shape, dt, name), Reg(name)
        names = [("sz", [128, 256], F32), ("dtx", [128, 4], F32), ("ax", [128, 4], F32), ("ex", [128, 4], F32),
                 ("lx", [128, 4], F32), ("dt", [128, 4], F32), ("aa", [128, 4], F32), ("acs", [128, 4], F32),
                 ("nacs", [128, 4], F32), ("el", [128, 4], F32), ("cd", [128, 4], F32), ("dd", [128, 4], F32),
                 ("dec", [128, 4], F32), ("dtdec", [128, 4], F32), ("rseg", [128, 4, 128], F32),
                 ("segT", [128, 4, 128], F32), ("xdt", [128, 4, 64], BF16), ("xdd", [128, 4, 64], BF16),
                 ("xD", [128, 4, 64], F32), ("Btm", [128, 128], BF16), ("Gm", [128, 128], F32),
                 ("scT", [128, 4, 128], BF16), ("t1", [128, 4, 64], F32), ("gg", [128, 256], F32),
                 ("junk", [128, 256], F32), ("ssq", [128, 1], F32)]
        sets = []
        for par in range(nsets):
            d_ = {}
            for (n_, shp, dt_) in names:
                d_[n_] = T(shp, dt_, f"s{par}{n_}")
            d_["nscr"] = mk_scr(K, [128, 1], f"sn{par}")
            bA = psbank(K, f"sA{par}"); bB = psbank(K, f"sB{par}"); bC = psbank(K, f"sC{par}"); bD = psbank(K, f"sD{par}")
            d_["bA"] = bA; d_["bB"] = bB
            d_["rA"] = Reg(excl=True); d_["rB"] = Reg(excl=True); d_["rC"] = Reg(excl=True); d_["rD"] = Reg(excl=True)
            d_["pz"] = bA[:, 0:256]; d_["pst"] = bA[:, 256:512]
            d_["pseg"] = bB.rearrange("p (a b) -> p a b", b=128)
            d_["ptr"] = bC[:, 0:192].bitcast(BF16).rearrange("p (a b) -> p a b", b=128)
            d_["pG"] = bC[:, 192:320]; d_["pa"] = bC[:, 320:328]; d_["pdt"] = bC[:, 328:332]
            d_["py"] = bD[:, 0:256]; d_["pyo"] = bD[:, 256:512]
            sets.append(d_)
        Sf, r_Sf = T([128, 4, 64], F32, "sSf")
        Sbf, r_Sbf = T([128, 256], BF16, "sSbf")
        yb = [K.sb([128, 256], BF16, f"syb{i}") for i in range(2)]; r_yb = [Reg() for _ in range(2)]
        ds_y = [K.new_dma_sem() for _ in range(2)]
        K.op("dve", lambda: nc.vector.memset(Sf[:].rearrange("p a b -> p (a b)"), 0.0), W=[r_Sf])
        K.op("dve", lambda: nc.vector.memset(Sbf[:], 0.0), W=[r_Sbf])
        ident_f = C["ident_f"]
        if True:
            SH = S // 2
            xpre = K.sb([128, SH + 3], F32, "xpre"); r_xpre = Reg()
            cacc = K.sb([128, SH], F32, "cacc"); r_cacc = Reg()
            pf = [sets[0]["bA"], sets[0]["bB"]]; r_pf = [sets[0]["rA"], sets[0]["rB"]]
            n = 0
            for ct in range(4):
                for hf in range(2):
                    if hf == 0:
                        K.op("dve", lambda: nc.vector.memset(xpre[:, 0:3], 0.0), W=[r_xpre])
                    else:
                        K.op("dve", lambda: nc.vector.tensor_copy(out=xpre[:, 0:3], in_=xpre[:, SH:SH + 3]),
                             R=[r_xpre], W=[r_xpre])
                    for tt in range(nTT // 2):
                        tg_ = hf * (nTT // 2) + tt
                        p_ = pf[n % 2]; rp = r_pf[n % 2]; n += 1
                        for k in range(8):
                            K.op("pe", lambda k=k: nc.tensor.matmul(p_[:], lhsT=wfm[:, k, ct * 128:(ct + 1) * 128],
                                                                    rhs=hnT[:, k, tg_ * 512:(tg_ + 1) * 512],
                                                                    start=(k == 0), stop=(k == 7)),
                                 R=[r_wfm, r_hnT], W=[rp])
                        K.op("act", lambda: nc.scalar.copy(out=xpre[:, 3 + tt * 512:3 + (tt + 1) * 512], in_=p_[:]),
                             R=[rp], W=[r_xpre])
                        yield
                    K.op("dve", lambda: nc.vector.tensor_scalar(out=cacc[:], in0=xpre[:, 0:SH], scalar1=cw[:, ct, 0:1],
                                                                scalar2=None, op0=ALU.mult), R=[r_xpre, r_c], W=[r_cacc])
                    for j in range(1, 4):
                        K.op("dve", lambda j=j: nc.vector.scalar_tensor_tensor(
                            out=cacc[:], in0=xpre[:, j:SH + j], scalar=cw[:, ct, j:j + 1], in1=cacc[:],
                            op0=ALU.mult, op1=ALU.add), R=[r_xpre, r_c, r_cacc], W=[r_cacc])
                        yield
                    K.op("act", lambda: nc.scalar.activation(out=xc[ct][:, hf * SH:(hf + 1) * SH], in_=cacc[:], func=AF.Silu,
                                                             bias=cb[:, ct:ct + 1]),
                         R=[r_cacc, r_c], W=[r_xc[ct]])
                    yield
        state_done = [-1]

        def chunk_flow(c):
                cs = slice(c * 128, (c + 1) * 128)
                S_ = sets[c % nsets]
                (sz, r_sz), (dtx, r_dtx), (ax, r_ax), (ex, r_ex), (lx, r_lx), (dt, r_dt), (aa, r_aa), (acs, r_acs), \
                    (nacs, r_nacs), (el, r_el), (cd, r_cd), (dd, r_dd), (dec, r_dec), (dtdec, r_dtdec), (rseg, r_rseg), \
                    (segT, r_segT), (xdt, r_xdt), (xdd, r_xdd), (xD, r_xD), (Btm, r_Btm), (Gm, r_Gm), (scT, r_scT), \
                    (t1, r_t1), (gg, r_gg), (junk, r_junk), (ssq, r_ssq) = [S_[n_[0]] for n_ in names]
                nscr = S_["nscr"]
                pz, pst, pseg, ptr, pG, pa, pdt, py, pyo = [S_[n_] for n_ in ["pz", "pst", "pseg", "ptr", "pG", "pa", "pdt", "py", "pyo"]]
                r_pz = r_pst = S_["rA"]; r_pseg = S_["rB"]; r_ptr = r_pG = r_pa = r_pdt = S_["rC"]; r_py = r_pyo = S_["rD"]
                for k in range(8):
                    K.op("pe", lambda k=k: nc.tensor.matmul(pz[:], lhsT=hnT[:, k, cs], rhs=wtm[:, k, 0:256],
                                                            start=(k == 0), stop=(k == 7)), R=[r_hnT, r_wtm], W=[r_pz])
                for k in range(8):
                    K.op("pe", lambda k=k: nc.tensor.matmul(pdt[:], lhsT=hnT[:, k, cs], rhs=wtm[:, k, 256:260],
                                                            start=(k == 0), stop=(k == 7)), R=[r_hnT, r_wtm], W=[r_pdt])
                yield
                K.op("act", lambda: nc.scalar.activation(out=sz[:], in_=pz[:, 0:256], func=AF.Silu), R=[r_pz], W=[r_sz])
                yield
                K.op("dve", lambda: nc.vector.tensor_tensor(out=dtx[:], in0=pdt[:], in1=dtb[:], op=ALU.add),
                     R=[r_pdt, r_c], W=[r_dtx])
                yield
                K.op("dve", lambda: nc.vector.scalar_tensor_tensor(out=ax[:], in0=dtx[:], scalar=-1.0, in1=dtx[:],
                                                                   op0=ALU.mult, op1=ALU.min), R=[r_dtx], W=[r_ax])
                yield
                K.op("act", lambda: nc.scalar.activation(out=ex[:], in_=ax[:], func=AF.Exp), R=[r_ax], W=[r_ex])
                yield
                K.op("dve", lambda: nc.vector.tensor_scalar(out=ex[:], in0=ex[:], scalar1=1.0, scalar2=None, op0=ALU.add),
                     R=[r_ex], W=[r_ex])
                yield
                K.op("act", lambda: nc.scalar.activation(out=lx[:], in_=ex[:], func=AF.Ln), R=[r_ex], W=[r_lx])
                yield
                K.op("dve", lambda: nc.vector.scalar_tensor_tensor(out=dt[:], in0=dtx[:], scalar=0.0, in1=lx[:],
                                                                   op0=ALU.max, op1=ALU.add), R=[r_dtx, r_lx], W=[r_dt])
                yield
                K.op("dve", lambda: nc.vector.tensor_tensor(out=aa[:], in0=dt[:], in1=Aneg[:], op=ALU.mult),
                     R=[r_dt, r_A], W=[r_aa])
                yield
                K.op("pe", lambda: nc.tensor.matmul(pa[:, 0:4], lhsT=C["tri"], rhs=aa[:], start=True, stop=True),
                     R=[r_aa, C["r"]], W=[r_pa])
                yield
                K.op("pe", lambda: nc.tensor.matmul(pa[:, 4:8], lhsT=C["ones"], rhs=aa[:], start=True, stop=True),
                     R=[r_aa, C["r"]], W=[r_pa])
                yield
                K.op("dve", lambda: nc.vector.tensor_copy(out=acs[:], in_=pa[:, 0:4]), R=[r_pa], W=[r_acs])
                yield
                K.op("dve", lambda: nc.vector.tensor_scalar(out=nacs[:], in0=pa[:, 0:4], scalar1=-1.0, scalar2=None,
                                                            op0=ALU.mult), R=[r_pa], W=[r_nacs])
                yield
                K.op("act", lambda: nc.scalar.activation(out=el[:], in_=pa[:, 0:4], func=AF.Exp), R=[r_pa], W=[r_el])
                yield
                K.op("act", lambda: nc.scalar.activation(out=cd[:], in_=pa[:, 4:8], func=AF.Exp), R=[r_pa], W=[r_cd])
                yield
                K.op("dve", lambda: nc.vector.tensor_tensor(out=dd[:], in0=pa[:, 4:8], in1=acs[:], op=ALU.subtract),
                     R=[r_pa, r_acs], W=[r_dd])
                yield
                K.op("act", lambda: nc.scalar.activation(out=dec[:], in_=dd[:], func=AF.Exp), R=[r_dd], W=[r_dec])
                yield
                K.op("dve", lambda: nc.vector.tensor_tensor(out=dtdec[:], in0=dt[:], in1=dec[:], op=ALU.mult),
                     R=[r_dt, r_dec], W=[r_dtdec])
                yield
                K.op("dve", lambda: nc.vector.tensor_tensor(out=rseg[:], in0=ident_f[:].unsqueeze(1).to_broadcast([128, 4, 128]),
                                                            in1=acs[:].unsqueeze(2).to_broadcast([128, 4, 128]), op=ALU.mult),
                     R=[C["r"], r_acs], W=[r_rseg])
                yield
                K.op("pe", lambda: nc.tensor.matmul(pseg[:].rearrange("p a b -> p (a b)"), lhsT=C["ones"],
                                                    rhs=rseg[:].rearrange("p a b -> p (a b)"), start=True, stop=False),
                     R=[r_rseg, C["r"]], W=[r_pseg])
                yield
                K.op("pe", lambda: nc.tensor.matmul(pseg[:].rearrange("p a b -> p (a b)"), lhsT=ident_f[:],
                                                    rhs=C["nm1"], start=False, stop=True),
                     R=[C["r"]], W=[r_pseg])
                for h in range(4):
                    K.op("act", lambda h=h: nc.scalar.activation(out=segT[:, h, :], in_=pseg[:, h, :], func=AF.Exp,
                                                                 bias=nacs[:, h:h + 1]), R=[r_pseg, r_nacs], W=[r_segT])
                yield
                for a in range(3):
                    K.op("pe", lambda a=a: nc.tensor.transpose(out=ptr[:, a, :], in_=xc[a][:, cs], identity=C["ident_bf"][:]),
                         R=[r_xc[a], C["r"]], W=[r_ptr])
                xs_v = ptr[:, 0:2, :].rearrange("p a (h d) -> p (a h) d", d=64)
                yield
                K.op("dve", lambda: nc.vector.tensor_tensor(out=xdt[:], in0=xs_v, in1=dt[:].unsqueeze(2).to_broadcast([128, 4, 64]),
                                                            op=ALU.mult), R=[r_ptr, r_dt], W=[r_xdt])
                yield
                K.op("dve", lambda: nc.vector.tensor_tensor(out=xdd[:], in0=xs_v, in1=dtdec[:].unsqueeze(2).to_broadcast([128, 4, 64]),
                                                            op=ALU.mult), R=[r_ptr, r_dtdec], W=[r_xdd])
                yield
                K.op("dve", lambda: nc.vector.tensor_tensor(out=xD[:], in0=xs_v, in1=dsk[:].unsqueeze(2).to_broadcast([128, 4, 64]),
                                                            op=ALU.mult), R=[r_ptr, r_c], W=[r_xD])
                yield
                K.op("act", lambda: nc.scalar.copy(out=Btm[:], in_=ptr[:, 2, :]), R=[r_ptr], W=[r_Btm])
                yield
                K.op("pe", lambda: nc.tensor.matmul(pG[:], lhsT=xc[2][:, cs], rhs=xc[3][:, cs], start=True, stop=True),
                     R=[r_xc[2], r_xc[3]], W=[r_pG])
                yield
                K.op("dve", lambda: nc.vector.tensor_tensor(out=Gm[:], in0=pG[:], in1=C["tri"], op=ALU.mult),
                     R=[r_pG, C["r"]], W=[r_Gm])
                yield
                K.op("dve", lambda: nc.vector.tensor_tensor(out=scT[:], in0=Gm[:].unsqueeze(1).to_broadcast([128, 4, 128]),
                                                            in1=segT[:], op=ALU.mult), R=[r_Gm, r_segT], W=[r_scT])
                for h in range(4):
                    K.op("pe", lambda h=h: nc.tensor.matmul(py[:, h * 64:(h + 1) * 64], lhsT=scT[:, h, :], rhs=xdt[:, h, :],
                                                            start=True, stop=True), R=[r_scT, r_xdt], W=[r_py])
                yield
                while state_done[0] < c - 1:
                    yield
                K.op("pe", lambda: nc.tensor.matmul(pyo[:], lhsT=xc[3][:, cs], rhs=Sbf[:], start=True, stop=True),
                     R=[r_xc[3], r_Sbf], W=[r_pyo])
                yield
                K.op("pe", lambda: nc.tensor.matmul(pst[:], lhsT=Btm[:], rhs=xdd[:].rearrange("p a b -> p (a b)"),
                                                    start=True, stop=True), R=[r_Btm, r_xdd], W=[r_pst])
                yield
                K.op("pool", lambda: nc.gpsimd.tensor_tensor(out=Sf[:], in0=Sf[:], in1=cd[:].unsqueeze(2).to_broadcast([128, 4, 64]),
                                                             op=ALU.mult), R=[r_Sf, r_cd], W=[r_Sf])
                yield
                K.op("dve", lambda: nc.vector.tensor_tensor(out=Sf[:].rearrange("p a b -> p (a b)"),
                                                            in0=Sf[:].rearrange("p a b -> p (a b)"), in1=pst[:], op=ALU.add),
                     R=[r_Sf, r_pst], W=[r_Sf])
                yield
                K.op("act", lambda: nc.scalar.copy(out=Sbf[:], in_=Sf[:].rearrange("p a b -> p (a b)")), R=[r_Sf], W=[r_Sbf])
                state_done[0] = c
                yield
                K.op("dve", lambda: nc.vector.tensor_tensor(out=t1[:], in0=pyo[:].rearrange("p (a b) -> p a b", b=64),
                                                            in1=el[:].unsqueeze(2).to_broadcast([128, 4, 64]), op=ALU.mult),
                     R=[r_pyo, r_el], W=[r_t1])
                yield
                K.op("dve", lambda: nc.vector.tensor_tensor(out=t1[:].rearrange("p a b -> p (a b)"),
                                                            in0=t1[:].rearrange("p a b -> p (a b)"), in1=py[:], op=ALU.add),
                     R=[r_t1, r_py], W=[r_t1])
                yield
                K.op("pool", lambda: nc.gpsimd.tensor_tensor(out=t1[:], in0=t1[:], in1=xD[:], op=ALU.add),
                     R=[r_t1, r_xD], W=[r_t1])
                yield
                K.op("pool", lambda: nc.gpsimd.tensor_tensor(out=gg[:], in0=t1[:].rearrange("p a b -> p (a b)"), in1=sz[:],
                                                             op=ALU.mult), R=[r_t1, r_sz], W=[r_gg])
                yield
                K.op("act", lambda: nc.scalar.activation(out=junk[:], in_=gg[:], func=AF.Square, accum_out=ssq[:]),
                     R=[r_gg], W=[r_junk, r_ssq])
                rstd_from_ssq(K, ssq[:], r_ssq, 1, nscr, 1.0 / 256)
                yb_ = yb[c % 2]
                yield
                K.op("dve", lambda: nc.vector.scalar_tensor_tensor(out=yb_[:], in0=gg[:], scalar=nscr["rstd"], in1=snw[:],
                                                                   op0=ALU.mult, op1=ALU.mult),
                     R=[r_gg, nscr["r_rstd"], r_c], W=[r_yb[c % 2]])
                K.dma("sp", y_d[cs, 128:384], yb_[:], ds_y[c % 2], R=[r_yb[c % 2]])
        nrun = nT if STOP > 1 else 0
        active = []
        nxt_c = 0
        while nxt_c < nrun or active:
            while len(active) < nsets and nxt_c < nrun:
                active.append(chunk_flow(nxt_c))
                nxt_c += 1
            for g_ in list(active):
                try:
                    next(g_)
                except StopIteration:
                    active.remove(g_)
            yield


def run_gen(g):
    for _ in g:
        pass


def emit_ssd(K, C, hnT, r_hnT, P, y_d, S):
    outer = K.stack
    with ExitStack() as st:
        K.stack = st
        K.begin_phase()
        W = {"wS_fm": load_w(K, P["wS_fm"], 512, "wSf"), "wS_tm": load_w(K, P["wS_tm"], 260, "wSt")}
        run_gen(emit_ssd_gen(K, C, hnT, r_hnT, P, y_d, S, W, nsets=2))
        K.barrier()
        K.end_phase()
        K.stack = outer


def emit_mlstm_gen(K, C, hnT, r_hnT, P, y_d, S, W):
    nc = K.nc
    nB = S // 512
    if True:
        wfm, r_wfm = W["wM_fm"]
        wg, r_wg = W["wM_g"]
        wtm, r_wtm = W["wM_tm"]
        dsm = K.new_dma_sem()
        r_c = Reg("mconst")
        gbias = K.sb([2, 2], F32, "gbias"); mnw = K.sb([128, 128], F32, "mnw")
        K.dma("sp", gbias[:], P["gbias"], dsm, W=[r_c])
        K.dma("sp", mnw[:], P["mnw"], dsm, W=[r_c])
        ident_f = C["ident_f"]
        B = [psbank(K, f"mb{i}") for i in range(4)]
        rB = [Reg(f"mb{i}", excl=True) for i in range(4)]
        pq, pk, pgi, pgf = B[2], B[3], B[0][0:2, :], B[1][0:2, :]
        ptl = B[2][:, 0:32].rearrange("p (q i h) -> p q i h", q=4, i=4)
        pdec = B[2][:, 32:40]
        pDt = B[2][:, 0:256].rearrange("p (h t) -> p h t", t=128)
        ptm = B[3][:, 0:384]

        def T(shape, dt, name):
            return K.sb(shape, dt, name), Reg(name)
        qTb, r_qTb = T([128, 512], BF16, "mqTb")
        kTb, r_kTb = T([128, 512], BF16, "mkTb")
        rows = {}
        for n_ in ["ipre", "yv", "e", "b", "al", "cma", "mu", "nmu", "wrow", "inter", "en", "tmp"]:
            rows[n_] = T([2, 512], F32, "mr_" + n_)
        rows["nab"] = rows["e"]; rows["l"] = rows["e"]
        rows["logf"] = rows["yv"]
        mnew, r_mnew = T([2, 8], F32, "mnew")
        mprev, r_mprev = T([2, 8], F32, "mprev")
        mcar, r_mcar = T([2, 1], F32, "mcar")
        decay, r_decay = T([2, 8], F32, "mdecay")
        tl, r_tl = T([128, 4, 4, 2], F32, "mtl")
        decr, r_decr = T([128, 8], F32, "mdecr")
        ktm, r_ktm = T([128, 128], F32, "mktm")
        vaug, r_vaug = T([128, 2, 65], BF16, "mvaug")
        og, r_og = T([128, 128], F32, "mog")
        dT, r_dT = T([128, 128], F32, "mdT")
        sdT, r_sdT = T([128, 128], BF16, "msdT")
        kw, r_kw = T([128, 64], BF16, "mkw")
        Cst, r_Cst = T([128, 65], F32, "mCst")
        Cbf, r_Cbf = T([128, 65], BF16, "mCbf")
        nmv, r_nmv = T([128, 65], F32, "mnmv")
        dn, r_dn = T([128, 1], F32, "mdn")
        rn, r_rn = T([128, 1], F32, "mrn")
        hm, r_hm = T([128, 64], F32, "mhm")
        junk, r_junk = T([128, 64], F32, "mjunk")
        ssq, r_ssq = T([128, 1], F32, "mssq")
        nscr = mk_scr(K, [128, 1], "mn")
        hn2, r_hn2 = T([128, 64], F32, "mhn2")
        yb = [K.sb([128, 128], BF16, f"myb{i}") for i in range(2)]; r_yb = [Reg() for _ in range(2)]
        ds_y = [K.new_dma_sem() for _ in range(2)]
        r_Cst = [Reg("Cst0"), Reg("Cst1")]
        r_Cbf = [Reg("Cbf0"), Reg("Cbf1")]
        K.op("dve", lambda: nc.vector.memset(Cst[:], 0.0), W=r_Cst)
        K.op("dve", lambda: nc.vector.memset(Cbf[:], 0.0), W=r_Cbf)
        K.op("dve", lambda: nc.vector.memset(mcar[:], 0.0), W=[r_mcar])
        ktm2 = [T([128, 128], F32, f"mktm{i}") for i in range(2)]
        vaug2 = [T([128, 2, 65], BF16, f"mvaug{i}") for i in range(2)]
        og2 = [T([128, 128], F32, f"mog{i}") for i in range(2)]
        for i_ in range(2):
            K.op("dve", lambda i_=i_: nc.vector.memset(vaug2[i_][0][:].rearrange("p a b -> p (a b)"), 1.0), W=[vaug2[i_][1]])
        r_yb2 = [[Reg(), Reg()] for _ in range(2)]
        HT = []
        for h_ in range(2):
            d_ = {}
            d_["dT"] = T([128, 128], F32, f"mdT{h_}")
            d_["sdT"] = T([128, 128], BF16, f"msdT{h_}")
            d_["kw"] = T([128, 64], BF16, f"mkw{h_}")
            d_["nmv"] = T([128, 65], F32, f"mnmv{h_}")
            d_["dn"] = T([128, 1], F32, f"mdn{h_}")
            d_["rn"] = T([128, 1], F32, f"mrn{h_}")
            d_["hm"] = T([128, 64], F32, f"mhm{h_}")
            d_["junk"] = T([128, 64], F32, f"mjunk{h_}")
            d_["ssq"] = T([128, 1], F32, f"mssq{h_}")
            d_["hn2"] = T([128, 64], F32, f"mhn2{h_}")
            d_["nscr"] = mk_scr(K, [128, 1], f"mn{h_}")
            HT.append(d_)

        def R_(n_):
            return rows[n_][0]

        def rr(n_):
            return rows[n_][1]
        rowc = C["rowc"]
        ntile = 0
        for b in range(nB):
            bs = slice(b * 512, (b + 1) * 512)
            for (pp_, rp_, c0, dst, rdst, sc) in [(pq, rB[2], 0, qTb, r_qTb, 1.0), (pk, rB[3], 128, kTb, r_kTb, 0.125)]:
                for k in range(8):
                    K.op("pe", lambda k=k: nc.tensor.matmul(pp_, lhsT=wfm[:, k, c0:c0 + 128], rhs=hnT[:, k, bs],
                                                            start=(k == 0), stop=(k == 7)), R=[r_wfm, r_hnT], W=[rp_])
                K.op("act", lambda: nc.scalar.mul(out=dst[:], in_=pp_, mul=sc), R=[rp_], W=[rdst])
            for (pp_, rp_, c0) in [(pgi, rB[0], 0), (pgf, rB[1], 2)]:
                for k in range(8):
                    K.op("pe", lambda k=k: nc.tensor.matmul(pp_, lhsT=wg[:, k, c0:c0 + 2], rhs=hnT[:, k, bs],
                                                            start=(k == 0), stop=(k == 7)), R=[r_wg, r_hnT], W=[rp_])
            yield
            K.op("dve", lambda: nc.vector.tensor_scalar(out=R_("ipre")[:], in0=pgi, scalar1=gbias[:, 0:1], scalar2=None,
                                                        op0=ALU.add), R=[rB[0], r_c], W=[rr("ipre")])
            yield
            K.op("dve", lambda: nc.vector.tensor_scalar(out=R_("yv")[:], in0=pgf, scalar1=gbias[:, 1:2], scalar2=-1.0,
                                                        op0=ALU.add, op1=ALU.mult), R=[rB[1], r_c], W=[rr("yv")])
            yield
            K.op("dve", lambda: nc.vector.scalar_tensor_tensor(out=R_("nab")[:], in0=R_("yv")[:], scalar=-1.0, in1=R_("yv")[:],
                                                               op0=ALU.mult, op1=ALU.min), R=[rr("yv")], W=[rr("nab")])
            yield
            K.op("act", lambda: nc.scalar.activation(out=R_("e")[:], in_=R_("nab")[:], func=AF.Exp), R=[rr("nab")], W=[rr("e")])
            yield
            K.op("dve", lambda: nc.vector.tensor_scalar(out=R_("e")[:], in0=R_("e")[:], scalar1=1.0, scalar2=None, op0=ALU.add),
                 R=[rr("e")], W=[rr("e")])
            yield
            K.op("act", lambda: nc.scalar.activation(out=R_("l")[:], in_=R_("e")[:], func=AF.Ln), R=[rr("e")], W=[rr("l")])
            yield
            K.op("dve", lambda: nc.vector.scalar_tensor_tensor(out=R_("logf")[:], in0=R_("yv")[:], scalar=0.0, in1=R_("l")[:],
                                                               op0=ALU.max, op1=ALU.add), R=[rr("yv"), rr("l")], W=[rr("logf")])
            yield
            K.op("dve", lambda: nc.vector.tensor_scalar(out=R_("logf")[:], in0=R_("logf")[:], scalar1=-1.0, scalar2=None,
                                                        op0=ALU.mult), R=[rr("logf")], W=[rr("logf")])
            yield
            K.op("dve", lambda: nc.vector.tensor_tensor_scan(out=R_("b")[:], data0=rowc[:, 0, :], data1=R_("logf")[:],
                                                             initial=0.0, op0=ALU.mult, op1=ALU.add),
                 R=[rr("logf"), C["r"]], W=[rr("b")])
            yield
            K.op("dve", lambda: nc.vector.tensor_tensor(out=R_("al")[:], in0=R_("ipre")[:], in1=R_("b")[:], op=ALU.subtract),
                 R=[rr("ipre"), rr("b")], W=[rr("al")])
            yield
            K.op("dve", lambda: nc.vector.tensor_tensor_scan(out=R_("cma")[:], data0=rowc[:, 1, :], data1=R_("al")[:],
                                                             initial=0.0, op0=ALU.add, op1=ALU.max),
                 R=[rr("al"), C["r"]], W=[rr("cma")])
            cma3 = R_("cma")[:].rearrange("p (c l) -> p c l", l=64)
            b3 = R_("b")[:].rearrange("p (c l) -> p c l", l=64)
            al3 = R_("al")[:].rearrange("p (c l) -> p c l", l=64)
            mu3 = R_("mu")[:].rearrange("p (c l) -> p c l", l=64)
            tmp3 = R_("tmp")[:].rearrange("p (c l) -> p c l", l=64)
            yield
            K.op("dve", lambda: nc.vector.tensor_tensor_scan(out=mnew[:], data0=cma3[:, :, 63], data1=b3[:, :, 63],
                                                             initial=mcar[:, 0:1], op0=ALU.max, op1=ALU.add),
                 R=[rr("cma"), rr("b"), r_mcar], W=[r_mnew])
            yield
            K.op("dve", lambda: nc.vector.tensor_copy(out=mprev[:, 0:1], in_=mcar[:]), R=[r_mcar], W=[r_mprev])
            yield
            K.op("dve", lambda: nc.vector.tensor_copy(out=mprev[:, 1:8], in_=mnew[:, 0:7]), R=[r_mnew], W=[r_mprev])
            yield
            K.op("dve", lambda: nc.vector.tensor_copy(out=mcar[:], in_=mnew[:, 7:8]), R=[r_mnew, r_mprev], W=[r_mcar])
            mpb = mprev[:].unsqueeze(2).to_broadcast([2, 8, 64])
            yield
            K.op("dve", lambda: nc.vector.tensor_tensor(out=mu3, in0=cma3, in1=mpb, op=ALU.max),
                 R=[rr("cma"), r_mprev], W=[rr("mu")])
            yield
            K.op("dve", lambda: nc.vector.tensor_scalar(out=R_("nmu")[:], in0=R_("mu")[:], scalar1=-1.0, scalar2=None,
                                                        op0=ALU.mult), R=[rr("mu")], W=[rr("nmu")])
            mcb = mu3[:, :, 63].unsqueeze(2).to_broadcast([2, 8, 64])
            yield
            K.op("dve", lambda: nc.vector.tensor_tensor(out=tmp3, in0=al3, in1=mcb, op=ALU.subtract),
                 R=[rr("al"), rr("mu")], W=[rr("tmp")])
            yield
            K.op("act", lambda: nc.scalar.activation(out=R_("wrow")[:], in_=R_("tmp")[:], func=AF.Exp), R=[rr("tmp")], W=[rr("wrow")])
            yield
            K.op("dve", lambda: nc.vector.tensor_tensor(out=decay[:], in0=mprev[:], in1=mu3[:, :, 63], op=ALU.subtract),
                 R=[r_mprev, rr("mu")], W=[r_decay])
            yield
            K.op("act", lambda: nc.scalar.activation(out=decay[:], in_=decay[:], func=AF.Exp), R=[r_decay], W=[r_decay])
            yield
            K.op("dve", lambda: nc.vector.tensor_tensor(out=tmp3, in0=mu3, in1=mpb, op=ALU.subtract),
                 R=[rr("mu"), r_mprev, rr("wrow")], W=[rr("tmp")])
            yield
            K.op("act", lambda: nc.scalar.activation(out=R_("inter")[:], in_=R_("tmp")[:], func=AF.Exp, scale=-1.0),
                 R=[rr("tmp")], W=[rr("inter")])
            yield
            K.op("dve", lambda: nc.vector.tensor_tensor(out=R_("tmp")[:], in0=R_("b")[:], in1=R_("mu")[:], op=ALU.add),
                 R=[rr("b"), rr("mu"), rr("inter")], W=[rr("tmp")])
            yield
            K.op("act", lambda: nc.scalar.activation(out=R_("en")[:], in_=R_("tmp")[:], func=AF.Exp, scale=-1.0),
                 R=[rr("tmp")], W=[rr("en")])
            for qi, qn_ in enumerate(["al", "wrow", "inter", "en"]):
                for i in range(4):
                    K.op("pe", lambda qi=qi, i=i, qn_=qn_: nc.tensor.transpose(
                        out=ptl[:, qi, i, :], in_=R_(qn_)[0:2, i * 128:(i + 1) * 128], identity=ident_f[0:2, 0:2]),
                        R=[rr(qn_), C["r"]], W=[rB[2]])
            yield
            K.op("pe", lambda: nc.tensor.matmul(pdec, lhsT=C["hsel"][:], rhs=decay[:], start=True, stop=True),
                 R=[r_decay, C["r"]], W=[rB[2]])
            yield
            K.op("dve", lambda: nc.vector.tensor_copy(out=tl[:], in_=ptl), R=[rB[2]], W=[r_tl])
            yield
            K.op("dve", lambda: nc.vector.tensor_copy(out=decr[:], in_=pdec), R=[rB[2]], W=[r_decr])
            for i in range(4):
                tg = b * 4 + i
                ts = slice(tg * 128, (tg + 1) * 128)
                tb = slice(i * 128, (i + 1) * 128)
                par = ntile % 2
                ktm_, r_ktm_ = ktm2[par]
                vaug_, r_vaug_ = vaug2[par]
                og_, r_og_ = og2[par]
                for k in range(8):
                    K.op("pe", lambda k=k: nc.tensor.matmul(ptm, lhsT=hnT[:, k, ts], rhs=wtm[:, k, :],
                                                            start=(k == 0), stop=(k == 7)), R=[r_hnT, r_wtm], W=[rB[3]])
                K.op("act", lambda: nc.scalar.mul(out=ktm_[:], in_=ptm[:, 0:128], mul=0.125), R=[rB[3]], W=[r_ktm_])
                K.op("dve", lambda: nc.vector.tensor_copy(out=vaug_[:, :, 0:64],
                                                          in_=ptm[:, 128:256].rearrange("p (h d) -> p h d", d=64)),
                     R=[rB[3]], W=[r_vaug_])
                K.op("act", lambda: nc.scalar.activation(out=og_[:], in_=ptm[:, 256:384], func=AF.Sigmoid), R=[rB[3]], W=[r_og_])
                yb_ = yb[par]
                ryb2 = r_yb2[par]
                for h in range(2):
                    K.op("pe", lambda h=h: nc.tensor.matmul(pDt[:, h, :], lhsT=C["sel"][:, h, :],
                                                            rhs=R_("nmu")[0:2, tb], start=True, stop=False),
                         R=[rr("nmu"), C["r"]], W=[rB[2]])
                    K.op("pe", lambda h=h: nc.tensor.matmul(pDt[:, h, :], lhsT=ident_f[:],
                                                            rhs=C["nm2"][:, 0:128], start=False, stop=True),
                         R=[C["r"]], W=[rB[2]])
                yield

                def head_flow(h):
                    hp = slice(64 * h, 64 * h + 64)
                    hc = slice(64 * h, 64 * h + 64)
                    PB = B[h]; rPB = rB[h]
                    pS = PB[:, 0:128]; pN = PB[:, 128:193]
                    pQs = [PB[:, 200:265], PB[:, 272:337]]
                    pC = PB[:, 344:409]
                    Hh = HT[h]
                    dT, r_dT = Hh["dT"]; sdT, r_sdT = Hh["sdT"]; kw, r_kw = Hh["kw"]; nmv, r_nmv = Hh["nmv"]
                    dn, r_dn = Hh["dn"]; rn, r_rn = Hh["rn"]; hm, r_hm = Hh["hm"]; junk, r_junk = Hh["junk"]
                    ssq, r_ssq = Hh["ssq"]; hn2, r_hn2 = Hh["hn2"]; nscr = Hh["nscr"]
                    K.op("pe", lambda: nc.tensor.matmul(pS, lhsT=kTb[hp, tb], rhs=qTb[hp, tb], start=True, stop=True),
                         R=[r_kTb, r_qTb], W=[rPB])
                    K.op("act", lambda: nc.scalar.activation(out=dT[:], in_=pDt[:, h, :], func=AF.Exp,
                                                             bias=tl[:, 0, i, h:h + 1]), R=[rB[2], r_tl], W=[r_dT])
                    yield
                    K.op("dve", lambda: nc.vector.tensor_tensor(out=sdT[:], in0=pS, in1=dT[:], op=ALU.mult),
                         R=[rPB, r_dT], W=[r_sdT])
                    K.op("pe", lambda: nc.tensor.matmul(pN, lhsT=sdT[:], rhs=vaug_[:, h, :], start=True, stop=True),
                         R=[r_sdT, r_vaug_], W=[rPB])
                    yield
                    K.op("dve", lambda: nc.vector.tensor_copy(out=nmv[:], in_=pN), R=[rPB], W=[r_nmv])
                    for half in range(2):
                        ce = 2 * i + half
                        rs_ = slice(64 * half, 64 * half + 64)
                        pQ = pQs[half]
                        K.op("pe", lambda: nc.tensor.matmul(pQ, lhsT=qTb[hp, tb], rhs=Cbf[hp, :], start=True, stop=True),
                             R=[r_qTb, r_Cbf[h]], W=[rPB])
                        K.op("dve", lambda: nc.vector.tensor_scalar(out=kw[rs_, :], in0=ktm_[rs_, hc], scalar1=tl[rs_, 1, i, h:h + 1],
                                                                    scalar2=None, op0=ALU.mult), R=[r_ktm_, r_tl], W=[r_kw])
                        yield
                        K.op("dve", lambda: nc.vector.scalar_tensor_tensor(
                            out=nmv[rs_, :], in0=pQ[rs_, :], scalar=tl[rs_, 2, i, h:h + 1], in1=nmv[rs_, :],
                            op0=ALU.mult, op1=ALU.add), R=[rPB, r_tl, r_nmv], W=[r_nmv])
                        K.op("pe", lambda: nc.tensor.matmul(pC[hp, :], lhsT=kw[rs_, :], rhs=vaug_[rs_, h, :], start=True, stop=True),
                             R=[r_kw, r_vaug_], W=[rPB])
                        yield
                        K.op("dve", lambda: nc.vector.scalar_tensor_tensor(
                            out=Cst[hp, :], in0=Cst[hp, :], scalar=decr[hp, ce:ce + 1], in1=pC[hp, :],
                            op0=ALU.mult, op1=ALU.add), R=[r_Cst[h], r_decr, rPB], W=[r_Cst[h]])
                        K.op("act", lambda: nc.scalar.copy(out=Cbf[hp, :], in_=Cst[hp, :]), R=[r_Cst[h]], W=[r_Cbf[h]])
                        yield
                    K.op("dve", lambda: nc.vector.scalar_tensor_tensor(out=dn[:], in0=nmv[:, 64:65], scalar=-1.0, in1=nmv[:, 64:65],
                                                                       op0=ALU.mult, op1=ALU.max), R=[r_nmv], W=[r_dn])
                    K.op("dve", lambda: nc.vector.tensor_tensor(out=dn[:], in0=dn[:], in1=tl[:, 3, i, h:h + 1], op=ALU.max),
                         R=[r_dn, r_tl], W=[r_dn])
                    yield
                    K.op("dve", lambda: nc.vector.reciprocal(out=rn[:], in_=dn[:]), R=[r_dn], W=[r_rn])
                    K.op("dve", lambda: nc.vector.tensor_scalar(out=hm[:], in0=nmv[:, 0:64], scalar1=rn[:, 0:1], scalar2=None,
                                                                op0=ALU.mult), R=[r_nmv, r_rn], W=[r_hm])
                    yield
                    K.op("act", lambda: nc.scalar.activation(out=junk[:], in_=hm[:], func=AF.Square, accum_out=ssq[:]),
                         R=[r_hm], W=[r_junk, r_ssq])
                    yield
                    rstd_from_ssq(K, ssq[:], r_ssq, 1, nscr, 1.0 / 64)
                    yield
                    K.op("dve", lambda: nc.vector.scalar_tensor_tensor(out=hn2[:], in0=hm[:], scalar=nscr["rstd"], in1=mnw[:, hc],
                                                                       op0=ALU.mult, op1=ALU.mult),
                         R=[r_hm, nscr["r_rstd"], r_c], W=[r_hn2])
                    K.op("dve", lambda: nc.vector.tensor_tensor(out=yb_[:, hc], in0=hn2[:], in1=og_[:, hc], op=ALU.mult),
                         R=[r_hn2, r_og_], W=[ryb2[h]])

                gens = [head_flow(0), head_flow(1)]
                while gens:
                    for g_ in list(gens):
                        try:
                            next(g_)
                        except StopIteration:
                            gens.remove(g_)
                    yield
                K.dma("sp", y_d[ts, 0:128], yb_[:], ds_y[par], R=ryb2)
                ntile += 1


def emit_mlstm(K, C, hnT, r_hnT, P, y_d, S):
    outer = K.stack
    with ExitStack() as st:
        K.stack = st
        K.begin_phase()
        W = {"wM_fm": load_w(K, P["wM_fm"], 256, "wMf"), "wM_g": load_w(K, P["wM_g"], 4, "wMg"),
             "wM_tm": load_w(K, P["wM_tm"], 384, "wMt")}
        run_gen(emit_mlstm_gen(K, C, hnT, r_hnT, P, y_d, S, W))
        K.barrier()
        K.end_phase()
        K.stack = outer


def load_hnT_pair(K, hn_all, S, blk=1024, regs=None):
    hnT = K.sb([128, 8, S], BF16, "hnT")
    r = Reg("hnT")
    ds = K.new_dma_sem()
    half = S // 2
    n = 0
    for b, ha in enumerate(hn_all):
        for rk in range(2):
            for k in range(8):
                c0 = rk * half + b * blk
                K.dma("sp" if n % 2 == 0 else "act", hnT[:, k, c0:c0 + blk],
                      ha[rk * 1024 + k * 128: rk * 1024 + (k + 1) * 128, :], ds, W=[r],
                      R=([regs[b]] if regs is not None else []))
                n += 1
    return hnT, r


class YSplit:
    def __init__(self, a, b, half):
        self.a, self.b, self.half = a, b, half
        self.regs = [Reg("yhalf0"), Reg("yhalf1")]
        self.hook = None

    def reg(self, row0):
        return self.regs[0 if row0 < self.half else 1]

    def __getitem__(self, key):
        rs, cs = key
        if rs.start < self.half:
            assert rs.stop <= self.half
            return self.a[rs, cs]
        return self.b[slice(rs.start - self.half, rs.stop - self.half), cs]


def emit_ms_concurrent(K, C, hnT, r_hnT, P, y_d, S):
    outer = K.stack
    with ExitStack() as st:
        K.stack = st
        K.begin_phase()
        W = {"wM_fm": load_w(K, P["wM_fm"], 256, "wMf"), "wM_g": load_w(K, P["wM_g"], 4, "wMg"),
             "wM_tm": load_w(K, P["wM_tm"], 384, "wMt"),
             "wS_fm": load_w(K, P["wS_fm"], 512, "wSf"), "wS_tm": load_w(K, P["wS_tm"], 260, "wSt")}
        gens = [emit_mlstm_gen(K, C, hnT, r_hnT, P, y_d, S, W),
                emit_ssd_gen(K, C, hnT, r_hnT, P, y_d, S, W, nsets=1)]
        while gens:
            for g_ in list(gens):
                try:
                    next(g_)
                except StopIteration:
                    gens.remove(g_)
        K.barrier()
        K.end_phase()
        K.stack = outer


def emit_ssd2(K, C, hnT, r_hnT, P, y_d, S, G=4, W=None):
    nc = K.nc
    nT = S // 128
    nTT = S // 512
    outer = K.stack
    with ExitStack() as st:
        K.stack = st
        K.begin_phase()
        wfm, r_wfm = W["wS_fm"] if W is not None else load_w(K, P["wS_fm"], 512, "wSf")
        wtm, r_wtm = W["wS_tm"] if W is not None else load_w(K, P["wS_tm"], 260, "wSt")
        dsm = K.new_dma_sem()
        r_c = Reg("sconst")
        cw = K.sb([128, 4, 4], F32, "cw"); cb = K.sb([128, 4], F32, "cb")
        dtb = K.sb([128, 4], F32, "dtb"); alog = K.sb([128, 4], F32, "alog")
        dsk = K.sb([128, 4], F32, "dsk"); snw = K.sb([128, 256], F32, "snw")
        for t_, n_ in [(cw, "cw"), (cb, "cb"), (dtb, "dtb"), (alog, "alog"), (dsk, "dsk"), (snw, "snw")]:
            K.dma("sp", t_[:], P[n_], dsm, W=[r_c])
        Aneg = K.sb([128, 4], F32, "Aneg"); r_A = Reg()
        K.op("act", lambda: nc.scalar.activation(out=Aneg[:], in_=alog[:], func=AF.Exp), R=[r_c], W=[r_A])
        K.op("dve", lambda: nc.vector.tensor_scalar(out=Aneg[:], in0=Aneg[:], scalar1=-1.0, scalar2=None, op0=ALU.mult),
             R=[r_A], W=[r_A])
        xc = [K.sb([128, S], BF16, f"xc{i}") for i in range(4)]
        r_xc = [Reg() for _ in range(4)]
        X = [psbank(K, f"sx{i}") for i in range(8)]
        rX = [Reg(f"sx{i}", excl=True) for i in range(8)]
        with ExitStack() as st2:
            K.stack = st2
            SH = S // 2
            xpre = K.sb([128, SH + 3], F32, "xpre"); r_xpre = Reg()
            cacc = K.sb([128, SH], F32, "cacc"); r_cacc = Reg()
            n = 0
            for ct in range(4):
                for hf in range(2):
                    if hf == 0:
                        K.op("dve", lambda: nc.vector.memset(xpre[:, 0:3], 0.0), W=[r_xpre])
                    else:
                        K.op("dve", lambda: nc.vector.tensor_copy(out=xpre[:, 0:3], in_=xpre[:, SH:SH + 3]),
                             R=[r_xpre], W=[r_xpre])
                    for tt in range(nTT // 2):
                        tg_ = hf * (nTT // 2) + tt
                        p_ = X[n % 4]; rp = rX[n % 4]; n += 1
                        for k in range(8):
                            K.op("pe", lambda k=k: nc.tensor.matmul(p_, lhsT=wfm[:, k, ct * 128:(ct + 1) * 128],
                                                                    rhs=hnT[:, k, tg_ * 512:(tg_ + 1) * 512],
                                                                    start=(k == 0), stop=(k == 7)),
                                 R=[r_wfm, r_hnT], W=[rp])
                        K.op("act", lambda: nc.scalar.copy(out=xpre[:, 3 + tt * 512:3 + (tt + 1) * 512], in_=p_),
                             R=[rp], W=[r_xpre])
                    K.op("dve", lambda: nc.vector.tensor_scalar(out=cacc[:], in0=xpre[:, 0:SH], scalar1=cw[:, ct, 0:1],
                                                                scalar2=None, op0=ALU.mult), R=[r_xpre, r_c], W=[r_cacc])
                    for j in range(1, 4):
                        K.op("dve", lambda j=j: nc.vector.scalar_tensor_tensor(
                            out=cacc[:], in0=xpre[:, j:SH + j], scalar=cw[:, ct, j:j + 1], in1=cacc[:],
                            op0=ALU.mult, op1=ALU.add), R=[r_xpre, r_c, r_cacc], W=[r_cacc])
                    K.op("act", lambda: nc.scalar.activation(out=xc[ct][:, hf * SH:(hf + 1) * SH], in_=cacc[:], func=AF.Silu,
                                                             bias=cb[:, ct:ct + 1]),
                         R=[r_cacc, r_c], W=[r_xc[ct]])
            K.barrier()
            K.stack = st

        def T(shape, dt, name):
            return K.sb(shape, dt, name), Reg(name)
        sz, r_sz = T([128, G, 256], BF16, "gsz")
        dtx, r_dtx = T([128, G, 4], F32, "gdtx"); ax, r_ax = T([128, G, 4], F32, "gax")
        ex, r_ex = T([128, G, 4], F32, "gex"); lx, r_lx = T([128, G, 4], F32, "glx")
        dt, r_dt = T([128, G, 4], F32, "gdt"); aa, r_aa = T([128, G, 4], F32, "gaa")
        acs, r_acs = T([128, G, 4], F32, "gacs"); nacs, r_nacs = T([128, G, 4], F32, "gnacs")
        el, r_el = T([128, G, 4], F32, "gel"); cd, r_cd = T([128, G, 4], F32, "gcd")
        dd, r_dd = T([128, G, 4], F32, "gdd"); dec, r_dec = T([128, G, 4], F32, "gdec")
        dtdec, r_dtdec = T([128, G, 4], F32, "gdtdec")
        rseg = [T([128, 4, 128], F32, f"grseg{i}") for i in range(2)]
        segT = [T([128, 4, 128], F32, f"gsegT{i}") for i in range(G)]
        xdt = [T([128, 4, 64], BF16, f"gxdt{i}") for i in range(G)]
        xdd = [T([128, 4, 64], BF16, f"gxdd{i}") for i in range(G)]
        xD = [T([128, 4, 64], F32, f"gxD{i}") for i in range(G)]
        Btm = [T([128, 128], BF16, f"gBtm{i}") for i in range(G)]
        Gm, r_Gm = T([128, G, 128], F32, "gGm")
        scT = [T([128, 4, 128], BF16, f"gscT{i}") for i in range(G)]
        t0 = [T([128, 256], F32, f"gt0{i}") for i in range(G)]
        Sf, r_Sf = T([128, 4, 64], F32, "gSf")
        Sbf = [T([128, 256], BF16, f"gSbf{i}") for i in range(G + 1)]
        gg = [T([128, 256], F32, f"ggg{i}") for i in range(G)]
        junk, r_junk = T([128, 256], BF16, "gjunk")
        ssq, r_ssq = T([128, G], F32, "gssq")
        nscr = mk_scr(K, [128, G], "gn")
        yb = [K.sb([128, 256], BF16, f"gyb{i}") for i in range(G)]; r_yb = [Reg() for _ in range(G)]
        ds_y = [K.new_dma_sem() for _ in range(G)]
        K.op("dve", lambda: nc.vector.memset(Sf[:].rearrange("p a b -> p (a b)"), 0.0), W=[r_Sf])
        K.op("dve", lambda: nc.vector.memset(Sbf[0][0][:], 0.0), W=[Sbf[0][1]])
        ident_f = C["ident_f"]
        fl = lambda t: t[:].rearrange("p a b -> p (a b)")
        for g0 in range(0, nT, G):
            cs = [slice((g0 + i) * 128, (g0 + i + 1) * 128) for i in range(G)]
            pz = [X[0][:, 0:256], X[0][:, 256:512], X[1][:, 0:256], X[1][:, 256:512]]
            rpz = [rX[0], rX[0], rX[1], rX[1]]
            pdt = X[2][:, 0:4 * G].rearrange("p (g h) -> p g h", h=4)
            for i in range(G):
                for k in range(8):
                    K.op("pe", lambda k=k, i=i: nc.tensor.matmul(pz[i], lhsT=hnT[:, k, cs[i]], rhs=wtm[:, k, 0:256],
                                                                 start=(k == 0), stop=(k == 7)), R=[r_hnT, r_wtm], W=[rpz[i]])
                for k in range(8):
                    K.op("pe", lambda k=k, i=i: nc.tensor.matmul(pdt[:, i, :], lhsT=hnT[:, k, cs[i]], rhs=wtm[:, k, 256:260],
                                                                 start=(k == 0), stop=(k == 7)), R=[r_hnT, r_wtm], W=[rX[2]])
            for i in range(0, G, 2):
                K.op("act", lambda i=i: nc.scalar.activation(out=sz[:, i:i + 2, :].rearrange("p a b -> p (a b)"),
                                                             in_=X[i // 2], func=AF.Silu), R=[rpz[i]], W=[r_sz])
            K.op("dve", lambda: nc.vector.tensor_tensor(out=dtx[:], in0=pdt, in1=dtb[:].unsqueeze(1).to_broadcast([128, G, 4]),
                                                        op=ALU.add), R=[rX[2], r_c], W=[r_dtx])
            K.op("dve", lambda: nc.vector.scalar_tensor_tensor(out=ax[:], in0=dtx[:], scalar=-1.0, in1=dtx[:],
                                                               op0=ALU.mult, op1=ALU.min), R=[r_dtx], W=[r_ax])
            K.op("act", lambda: nc.scalar.activation(out=ex[:], in_=ax[:], func=AF.Exp), R=[r_ax], W=[r_ex])
            K.op("dve", lambda: nc.vector.tensor_scalar(out=ex[:], in0=ex[:], scalar1=1.0, scalar2=None, op0=ALU.add),
                 R=[r_ex], W=[r_ex])
            K.op("act", lambda: nc.scalar.activation(out=lx[:], in_=ex[:], func=AF.Ln), R=[r_ex], W=[r_lx])
            K.op("dve", lambda: nc.vector.scalar_tensor_tensor(out=dt[:], in0=dtx[:], scalar=0.0, in1=lx[:],
                                                               op0=ALU.max, op1=ALU.add), R=[r_dtx, r_lx], W=[r_dt])
            K.op("dve", lambda: nc.vector.tensor_tensor(out=aa[:], in0=dt[:], in1=Aneg[:].unsqueeze(1).to_broadcast([128, G, 4]),
                                                        op=ALU.mult), R=[r_dt, r_A], W=[r_aa])
            pacs = X[2][:, 64:64 + 4 * G].rearrange("p (g h) -> p g h", h=4)
            plast = X[2][:, 128:128 + 4 * G].rearrange("p (g h) -> p g h", h=4)
            K.op("pe", lambda: nc.tensor.matmul(X[2][:, 64:64 + 4 * G], lhsT=C["tri"], rhs=fl(aa), start=True, stop=True),
                 R=[r_aa, C["r"]], W=[rX[2]])
            K.op("pe", lambda: nc.tensor.matmul(X[2][:, 128:128 + 4 * G], lhsT=C["ones"], rhs=fl(aa), start=True, stop=True),
                 R=[r_aa, C["r"]], W=[rX[2]])
            K.op("dve", lambda: nc.vector.tensor_copy(out=acs[:], in_=pacs), R=[rX[2]], W=[r_acs])
            K.op("dve", lambda: nc.vector.tensor_scalar(out=nacs[:], in0=pacs, scalar1=-1.0, scalar2=None, op0=ALU.mult),
                 R=[rX[2]], W=[r_nacs])
            K.op("dve", lambda: nc.vector.tensor_tensor(out=dd[:], in0=plast, in1=acs[:], op=ALU.subtract),
                 R=[rX[2], r_acs], W=[r_dd])
            K.op("act", lambda: nc.scalar.activation(out=el[:], in_=acs[:], func=AF.Exp), R=[r_acs], W=[r_el])
            K.op("act", lambda: nc.scalar.activation(out=cd[:], in_=plast, func=AF.Exp), R=[rX[2]], W=[r_cd])
            K.op("act", lambda: nc.scalar.activation(out=dec[:], in_=dd[:], func=AF.Exp), R=[r_dd], W=[r_dec])
            K.op("dve", lambda: nc.vector.tensor_tensor(out=dtdec[:], in0=dt[:], in1=dec[:], op=ALU.mult),
                 R=[r_dt, r_dec], W=[r_dtdec])
            for i in range(G):
                rs_, rrs = rseg[i % 2]
                ps_ = X[3 + i % 2]; rps = rX[3 + i % 2]
                K.op("dve", lambda i=i: nc.vector.tensor_tensor(out=rs_[:], in0=ident_f[:].unsqueeze(1).to_broadcast([128, 4, 128]),
                                                                in1=acs[:, i, :].unsqueeze(2).to_broadcast([128, 4, 128]), op=ALU.mult),
                     R=[C["r"], r_acs], W=[rrs])
                K.op("pe", lambda: nc.tensor.matmul(ps_, lhsT=C["ones"], rhs=fl(rs_), start=True, stop=False),
                     R=[rrs, C["r"]], W=[rps])
                K.op("pe", lambda: nc.tensor.matmul(ps_, lhsT=ident_f[:], rhs=C["nm1"], start=False, stop=True),
                     R=[C["r"]], W=[rps])
                for h in range(4):
                    K.op("act", lambda i=i, h=h: nc.scalar.activation(out=segT[i][0][:, h, :], in_=ps_[:, h * 128:(h + 1) * 128],
                                                                      func=AF.Exp, bias=nacs[:, i, h:h + 1]),
                         R=[rps, r_nacs], W=[segT[i][1]])
            for i in range(G):
                pt_ = X[5 + i % 2][:, 0:192].bitcast(BF16).rearrange("p (a b) -> p a b", b=128); rpt = rX[5 + i % 2]
                for a in range(3):
                    K.op("pe", lambda a=a, i=i: nc.tensor.transpose(out=pt_[:, a, :], in_=xc[a][:, cs[i]], identity=C["ident_bf"][:]),
                         R=[r_xc[a], C["r"]], W=[rpt])
                xs_v = pt_[:, 0:2, :].rearrange("p a (h d) -> p (a h) d", d=64)
                K.op("dve", lambda i=i: nc.vector.tensor_tensor(out=xdt[i][0][:], in0=xs_v, in1=dt[:, i, :].unsqueeze(2).to_broadcast([128, 4, 64]),
                                                                op=ALU.mult), R=[rpt, r_dt], W=[xdt[i][1]])
                K.op("dve", lambda i=i: nc.vector.tensor_tensor(out=xdd[i][0][:], in0=xs_v, in1=dtdec[:, i, :].unsqueeze(2).to_broadcast([128, 4, 64]),
                                                                op=ALU.mult), R=[rpt, r_dtdec], W=[xdd[i][1]])
                K.op("dve", lambda i=i: nc.vector.tensor_tensor(out=xD[i][0][:], in0=xs_v, in1=dsk[:].unsqueeze(2).to_broadcast([128, 4, 64]),
                                                                op=ALU.mult), R=[rpt, r_c], W=[xD[i][1]])
                K.op("dve", lambda i=i: nc.vector.tensor_copy(out=Btm[i][0][:], in_=pt_[:, 2, :]), R=[rpt], W=[Btm[i][1]])
            for i in range(G):
                K.op("pe", lambda i=i: nc.tensor.matmul(X[7][:, i * 128:(i + 1) * 128], lhsT=xc[2][:, cs[i]], rhs=xc[3][:, cs[i]],
                                                        start=True, stop=True), R=[r_xc[2], r_xc[3]], W=[rX[7]])
            K.op("dve", lambda: nc.vector.tensor_tensor(out=Gm[:], in0=X[7][:, 0:G * 128].rearrange("p (g l) -> p g l", l=128),
                                                        in1=C["tri"].unsqueeze(1).to_broadcast([128, G, 128]), op=ALU.mult),
                 R=[rX[7], C["r"]], W=[r_Gm])
            for i in range(G):
                K.op("dve", lambda i=i: nc.vector.tensor_tensor(out=scT[i][0][:], in0=Gm[:, i, :].unsqueeze(1).to_broadcast([128, 4, 128]),
                                                                in1=segT[i][0][:], op=ALU.mult), R=[r_Gm, segT[i][1]], W=[scT[i][1]])
            for i in range(G):
                py_ = X[i % 2][:, (i // 2 % 2) * 256:(i // 2 % 2) * 256 + 256]; rpy = rX[i % 2]
                for h in range(4):
                    K.op("pe", lambda i=i, h=h: nc.tensor.matmul(py_[:, h * 64:(h + 1) * 64], lhsT=scT[i][0][:, h, :], rhs=xdt[i][0][:, h, :],
                                                                 start=True, stop=True), R=[scT[i][1], xdt[i][1]], W=[rpy])
                K.op("dve", lambda i=i: nc.vector.tensor_tensor(out=t0[i][0][:], in0=py_, in1=fl(xD[i][0]), op=ALU.add),
                     R=[rpy, xD[i][1]], W=[t0[i][1]])
            for i in range(G):
                pst_ = X[3 + i % 2][:, 0:256]; rpst = rX[3 + i % 2]
                K.op("pe", lambda i=i: nc.tensor.matmul(pst_, lhsT=Btm[i][0][:], rhs=fl(xdd[i][0]), start=True, stop=True),
                     R=[Btm[i][1], xdd[i][1]], W=[rpst])
                K.op("dve", lambda i=i: nc.vector.tensor_tensor(out=Sf[:], in0=Sf[:], in1=cd[:, i, :].unsqueeze(2).to_broadcast([128, 4, 64]),
                                                                op=ALU.mult), R=[r_Sf, r_cd], W=[r_Sf])
                K.op("dve", lambda: nc.vector.tensor_tensor(out=fl(Sf), in0=fl(Sf), in1=pst_, op=ALU.add),
                     R=[r_Sf, rpst], W=[r_Sf])
                K.op("act", lambda i=i: nc.scalar.copy(out=Sbf[i + 1][0][:], in_=fl(Sf)), R=[r_Sf], W=[Sbf[i + 1][1]])
            for i in range(G):
                pyo_ = X[5 + i % 2][:, 256:512]; rpyo = rX[5 + i % 2]
                K.op("pe", lambda i=i: nc.tensor.matmul(pyo_, lhsT=xc[3][:, cs[i]], rhs=Sbf[i][0][:], start=True, stop=True),
                     R=[r_xc[3], Sbf[i][1]], W=[rpyo])
                K.op("dve", lambda i=i: nc.vector.tensor_tensor(out=gg[i][0][:].rearrange("p (a b) -> p a b", b=64),
                                                                in0=pyo_.rearrange("p (a b) -> p a b", b=64),
                                                                in1=el[:, i, :].unsqueeze(2).to_broadcast([128, 4, 64]), op=ALU.mult),
                     R=[rpyo, r_el], W=[gg[i][1]])
                K.op("dve", lambda i=i: nc.vector.tensor_tensor(out=gg[i][0][:], in0=gg[i][0][:], in1=t0[i][0][:], op=ALU.add),
                     R=[gg[i][1], t0[i][1]], W=[gg[i][1]])
                K.op("dve", lambda i=i: nc.vector.tensor_tensor(out=gg[i][0][:], in0=gg[i][0][:], in1=sz[:, i, :], op=ALU.mult),
                     R=[gg[i][1], r_sz], W=[gg[i][1]])
                K.op("act", lambda i=i: nc.scalar.activation(out=junk[:], in_=gg[i][0][:], func=AF.Square, accum_out=ssq[:, i:i + 1]),
                     R=[gg[i][1]], W=[r_junk, r_ssq])
            rstd_from_ssq(K, ssq[:], r_ssq, G, nscr, 1.0 / 256)
            for i in range(G):
                K.op("dve", lambda i=i: nc.vector.scalar_tensor_tensor(out=yb[i][:], in0=gg[i][0][:], scalar=nscr["rstd"][:, i:i + 1],
                                                                       in1=snw[:], op0=ALU.mult, op1=ALU.mult),
                     R=[gg[i][1], nscr["r_rstd"], r_c], W=[r_yb[i]])
                K.dma("sp", y_d[cs[i], 128:384], yb[i][:], ds_y[i], R=[r_yb[i]])
            K.op("act", lambda: nc.scalar.copy(out=Sbf[0][0][:], in_=Sbf[G][0][:]), R=[Sbf[G][1]], W=[Sbf[0][1]])
        K.barrier()
        K.end_phase()
        K.stack = outer


def emit_ssd3(K, C, hnT, r_hnT, P, y_d, S, G=4, AW=3):
    nc = K.nc
    nT = S // 128
    nTT = S // 512
    outer = K.stack
    with ExitStack() as st:
        K.stack = st
        K.begin_phase()
        wfm, r_wfm = load_w(K, P["wS_fm"], 512, "wSf")
        wtm, r_wtm = load_w(K, P["wS_tm"], 260, "wSt")
        dsm = K.new_dma_sem()
        r_c = Reg("sconst")
        cw = K.sb([128, 4, 4], F32, "cw"); cb = K.sb([128, 4], F32, "cb")
        dtb = K.sb([128, 4], F32, "dtb"); alog = K.sb([128, 4], F32, "alog")
        dsk = K.sb([128, 4], F32, "dsk"); snw = K.sb([128, 256], F32, "snw")
        for t_, n_ in [(cw, "cw"), (cb, "cb"), (dtb, "dtb"), (alog, "alog"), (dsk, "dsk"), (snw, "snw")]:
            K.dma("sp", t_[:], P[n_], dsm, W=[r_c])
        Aneg = K.sb([128, 4], F32, "Aneg"); r_A = Reg()
        K.op("act", lambda: nc.scalar.activation(out=Aneg[:], in_=alog[:], func=AF.Exp), R=[r_c], W=[r_A])
        K.op("dve", lambda: nc.vector.tensor_scalar(out=Aneg[:], in0=Aneg[:], scalar1=-1.0, scalar2=None, op0=ALU.mult),
             R=[r_A], W=[r_A])
        xc = [K.sb([128, S], BF16, f"xc{i}") for i in range(4)]
        r_xc = [Reg() for _ in range(4)]
        X = [psbank(K, f"sx{i}") for i in range(8)]
        rX = [Reg(f"sx{i}", excl=True) for i in range(8)]
        with ExitStack() as st2:
            K.stack = st2
            SH = S // 2
            xpre = K.sb([128, SH + 3], F32, "xpre"); r_xpre = Reg()
            cacc = K.sb([128, SH], F32, "cacc"); r_cacc = Reg()
            n = 0
            for ct in range(4):
                for hf in range(2):
                    if hf == 0:
                        K.op("dve", lambda: nc.vector.memset(xpre[:, 0:3], 0.0), W=[r_xpre])
                    else:
                        K.op("dve", lambda: nc.vector.tensor_copy(out=xpre[:, 0:3], in_=xpre[:, SH:SH + 3]),
                             R=[r_xpre], W=[r_xpre])
                    for tt in range(nTT // 2):
                        tg_ = hf * (nTT // 2) + tt
                        p_ = X[n % 4]; rp = rX[n % 4]; n += 1
                        for k in range(8):
                            K.op("pe", lambda k=k: nc.tensor.matmul(p_, lhsT=wfm[:, k, ct * 128:(ct + 1) * 128],
                                                                    rhs=hnT[:, k, tg_ * 512:(tg_ + 1) * 512],
                                                                    start=(k == 0), stop=(k == 7)),
                                 R=[r_wfm, r_hnT], W=[rp])
                        K.op("act", lambda: nc.scalar.copy(out=xpre[:, 3 + tt * 512:3 + (tt + 1) * 512], in_=p_),
                             R=[rp], W=[r_xpre])
                    K.op("dve", lambda: nc.vector.tensor_scalar(out=cacc[:], in0=xpre[:, 0:SH], scalar1=cw[:, ct, 0:1],
                                                                scalar2=None, op0=ALU.mult), R=[r_xpre, r_c], W=[r_cacc])
                    for j in range(1, 4):
                        K.op("dve", lambda j=j: nc.vector.scalar_tensor_tensor(
                            out=cacc[:], in0=xpre[:, j:SH + j], scalar=cw[:, ct, j:j + 1], in1=cacc[:],
                            op0=ALU.mult, op1=ALU.add), R=[r_xpre, r_c, r_cacc], W=[r_cacc])
                    K.op("act", lambda: nc.scalar.activation(out=xc[ct][:, hf * SH:(hf + 1) * SH], in_=cacc[:], func=AF.Silu,
                                                             bias=cb[:, ct:ct + 1]),
                         R=[r_cacc, r_c], W=[r_xc[ct]])
            K.barrier()
            K.stack = st

        def T(shape, dt, name):
            return K.sb(shape, dt, name + SFX[0]), Reg(name)
        SFX = ['']

        def mkset(par):
            SFX[0] = f'_{par}'
            sz, r_sz = T([128, G, 256], BF16, "gsz")
            dtx, r_dtx = T([128, G, 4], F32, "gdtx"); ax, r_ax = T([128, G, 4], F32, "gax")
            ex, r_ex = T([128, G, 4], F32, "gex"); lx, r_lx = T([128, G, 4], F32, "glx")
            dt, r_dt = T([128, G, 4], F32, "gdt"); aa, r_aa = T([128, G, 4], F32, "gaa")
            acs, r_acs = T([128, G, 4], F32, "gacs"); nacs, r_nacs = T([128, G, 4], F32, "gnacs")
            el, r_el = T([128, G, 4], F32, "gel"); cd, r_cd = T([128, G, 4], F32, "gcd")
            dd, r_dd = T([128, G, 4], F32, "gdd"); dec, r_dec = T([128, G, 4], F32, "gdec")
            dtdec, r_dtdec = T([128, G, 4], F32, "gdtdec")
            rseg = [T([128, 4, 128], F32, f"grseg{i}") for i in range(2)]
            segT = [T([128, 4, 128], F32, f"gsegT{i}") for i in range(G)]
            xdt = [T([128, 4, 64], BF16, f"gxdt{i}") for i in range(G)]
            xdd = [T([128, 4, 64], BF16, f"gxdd{i}") for i in range(G)]
            xD = [T([128, 4, 64], F32, f"gxD{i}") for i in range(G)]
            Btm = [T([128, 128], BF16, f"gBtm{i}") for i in range(G)]
            Gm, r_Gm = T([128, G, 128], F32, "gGm")
            scT = [T([128, 4, 128], BF16, f"gscT{i}") for i in range(G)]
            t0 = [T([128, 256], F32, f"gt0{i}") for i in range(G)]
            Sbf = [T([128, 256], BF16, f"gSbf{i}") for i in range(G + 1)]
            gg = [T([128, 256], F32, f"ggg{i}") for i in range(G)]
            junk, r_junk = T([128, 256], BF16, "gjunk")
            ssq, r_ssq = T([128, G], F32, "gssq")
            nscr = mk_scr(K, [128, G], "gn")
            yb = [K.sb([128, 256], BF16, f"gyb{i}") for i in range(G)]; r_yb = [Reg() for _ in range(G)]
            return dict(locals())
        sets = [mkset(0), mkset(1)]
        SFX[0] = ''
        Sf, r_Sf = T([128, 4, 64], F32, "gSf")
        ds_y = [K.new_dma_sem() for _ in range(2)]
        K.op("dve", lambda: nc.vector.memset(Sf[:].rearrange("p a b -> p (a b)"), 0.0), W=[r_Sf])
        K.op("dve", lambda: nc.vector.memset(sets[0]["Sbf"][0][0][:], 0.0), W=[sets[0]["Sbf"][0][1]])
        ident_f = C["ident_f"]
        fl = lambda t: t[:].rearrange("p a b -> p (a b)")

        def genA(g0, S_):
            cs = [slice((g0 + i) * 128, (g0 + i + 1) * 128) for i in range(G)]
            sz, r_sz, dtx, r_dtx, ax, r_ax, ex, r_ex, lx, r_lx, dt, r_dt, aa, r_aa, acs, r_acs, nacs, r_nacs, el, r_el, cd, r_cd, dd, r_dd, dec, r_dec, dtdec, r_dtdec, rseg, segT, xdt, xdd, xD, Btm, Gm, r_Gm, scT, t0, Sbf, gg, junk, r_junk, ssq, r_ssq, nscr, yb, r_yb = [S_[n_] for n_ in ['sz', 'r_sz', 'dtx', 'r_dtx', 'ax', 'r_ax', 'ex', 'r_ex', 'lx', 'r_lx', 'dt', 'r_dt', 'aa', 'r_aa', 'acs', 'r_acs', 'nacs', 'r_nacs', 'el', 'r_el', 'cd', 'r_cd', 'dd', 'r_dd', 'dec', 'r_dec', 'dtdec', 'r_dtdec', 'rseg', 'segT', 'xdt', 'xdd', 'xD', 'Btm', 'Gm', 'r_Gm', 'scT', 't0', 'Sbf', 'gg', 'junk', 'r_junk', 'ssq', 'r_ssq', 'nscr', 'yb', 'r_yb']]
            cs = [slice((g0 + i) * 128, (g0 + i + 1) * 128) for i in range(G)]
            pz = [X[0][:, 0:256], X[0][:, 256:512], X[1][:, 0:256], X[1][:, 256:512]]
            rpz = [rX[0], rX[0], rX[1], rX[1]]
            pdt = X[2][:, 0:4 * G].rearrange("p (g h) -> p g h", h=4)
            yield
            for i in range(G):
                for k in range(8):
                    K.op("pe", lambda k=k, i=i: nc.tensor.matmul(pz[i], lhsT=hnT[:, k, cs[i]], rhs=wtm[:, k, 0:256],
                                                                 start=(k == 0), stop=(k == 7)), R=[r_hnT, r_wtm], W=[rpz[i]])
                yield
                for k in range(8):
                    K.op("pe", lambda k=k, i=i: nc.tensor.matmul(pdt[:, i, :], lhsT=hnT[:, k, cs[i]], rhs=wtm[:, k, 256:260],
                                                                 start=(k == 0), stop=(k == 7)), R=[r_hnT, r_wtm], W=[rX[2]])
            yield
            for i in range(0, G, 2):
                K.op("act", lambda i=i: nc.scalar.activation(out=sz[:, i:i + 2, :].rearrange("p a b -> p (a b)"),
                                                             in_=X[i // 2], func=AF.Silu), R=[rpz[i]], W=[r_sz])
            yield
            K.op("dve", lambda: nc.vector.tensor_tensor(out=dtx[:], in0=pdt, in1=dtb[:].unsqueeze(1).to_broadcast([128, G, 4]),
                                                        op=ALU.add), R=[rX[2], r_c], W=[r_dtx])
            yield
            K.op("dve", lambda: nc.vector.scalar_tensor_tensor(out=ax[:], in0=dtx[:], scalar=-1.0, in1=dtx[:],
                                                               op0=ALU.mult, op1=ALU.min), R=[r_dtx], W=[r_ax])
            yield
            K.op("act", lambda: nc.scalar.activation(out=ex[:], in_=ax[:], func=AF.Exp), R=[r_ax], W=[r_ex])
            yield
            K.op("dve", lambda: nc.vector.tensor_scalar(out=ex[:], in0=ex[:], scalar1=1.0, scalar2=None, op0=ALU.add),
                 R=[r_ex], W=[r_ex])
            yield
            K.op("act", lambda: nc.scalar.activation(out=lx[:], in_=ex[:], func=AF.Ln), R=[r_ex], W=[r_lx])
            yield
            K.op("dve", lambda: nc.vector.scalar_tensor_tensor(out=dt[:], in0=dtx[:], scalar=0.0, in1=lx[:],
                                                               op0=ALU.max, op1=ALU.add), R=[r_dtx, r_lx], W=[r_dt])
            yield
            K.op("dve", lambda: nc.vector.tensor_tensor(out=aa[:], in0=dt[:], in1=Aneg[:].unsqueeze(1).to_broadcast([128, G, 4]),
                                                        op=ALU.mult), R=[r_dt, r_A], W=[r_aa])
            pacs = X[2][:, 64:64 + 4 * G].rearrange("p (g h) -> p g h", h=4)
            plast = X[2][:, 128:128 + 4 * G].rearrange("p (g h) -> p g h", h=4)
            yield
            K.op("pe", lambda: nc.tensor.matmul(X[2][:, 64:64 + 4 * G], lhsT=C["tri"], rhs=fl(aa), start=True, stop=True),
                 R=[r_aa, C["r"]], W=[rX[2]])
            yield
            K.op("pe", lambda: nc.tensor.matmul(X[2][:, 128:128 + 4 * G], lhsT=C["ones"], rhs=fl(aa), start=True, stop=True),
                 R=[r_aa, C["r"]], W=[rX[2]])
            yield
            K.op("dve", lambda: nc.vector.tensor_copy(out=acs[:], in_=pacs), R=[rX[2]], W=[r_acs])
            yield
            K.op("dve", lambda: nc.vector.tensor_scalar(out=nacs[:], in0=pacs, scalar1=-1.0, scalar2=None, op0=ALU.mult),
                 R=[rX[2]], W=[r_nacs])
            yield
            K.op("dve", lambda: nc.vector.tensor_tensor(out=dd[:], in0=plast, in1=acs[:], op=ALU.subtract),
                 R=[rX[2], r_acs], W=[r_dd])
            yield
            K.op("act", lambda: nc.scalar.activation(out=el[:], in_=acs[:], func=AF.Exp), R=[r_acs], W=[r_el])
            yield
            K.op("act", lambda: nc.scalar.activation(out=cd[:], in_=plast, func=AF.Exp), R=[rX[2]], W=[r_cd])
            yield
            K.op("act", lambda: nc.scalar.activation(out=dec[:], in_=dd[:], func=AF.Exp), R=[r_dd], W=[r_dec])
            yield
            K.op("dve", lambda: nc.vector.tensor_tensor(out=dtdec[:], in0=dt[:], in1=dec[:], op=ALU.mult),
                 R=[r_dt, r_dec], W=[r_dtdec])
            yield
            for i in range(G):
                rs_, rrs = rseg[i % 2]
                ps_ = X[3]; rps = rX[3]
                K.op("dve", lambda i=i: nc.vector.tensor_tensor(out=rs_[:], in0=ident_f[:].unsqueeze(1).to_broadcast([128, 4, 128]),
                                                                in1=acs[:, i, :].unsqueeze(2).to_broadcast([128, 4, 128]), op=ALU.mult),
                     R=[C["r"], r_acs], W=[rrs])
                yield
                K.op("pe", lambda: nc.tensor.matmul(ps_, lhsT=C["ones"], rhs=fl(rs_), start=True, stop=False),
                     R=[rrs, C["r"]], W=[rps])
                yield
                K.op("pe", lambda: nc.tensor.matmul(ps_, lhsT=ident_f[:], rhs=C["nm1"], start=False, stop=True),
                     R=[C["r"]], W=[rps])
                yield
                for h in range(4):
                    K.op("act", lambda i=i, h=h: nc.scalar.activation(out=segT[i][0][:, h, :], in_=ps_[:, h * 128:(h + 1) * 128],
                                                                      func=AF.Exp, bias=nacs[:, i, h:h + 1]),
                         R=[rps, r_nacs], W=[segT[i][1]])
            yield
            for i in range(G):
                pt_ = X[5][:, 0:192].bitcast(BF16).rearrange("p (a b) -> p a b", b=128); rpt = rX[5]
                for a in range(3):
                    K.op("pe", lambda a=a, i=i: nc.tensor.transpose(out=pt_[:, a, :], in_=xc[a][:, cs[i]], identity=C["ident_bf"][:]),
                         R=[r_xc[a], C["r"]], W=[rpt])
                xs_v = pt_[:, 0:2, :].rearrange("p a (h d) -> p (a h) d", d=64)
                yield
                K.op("dve", lambda i=i: nc.vector.tensor_tensor(out=xdt[i][0][:], in0=xs_v, in1=dt[:, i, :].unsqueeze(2).to_broadcast([128, 4, 64]),
                                                                op=ALU.mult), R=[rpt, r_dt], W=[xdt[i][1]])
                yield
                K.op("dve", lambda i=i: nc.vector.tensor_tensor(out=xdd[i][0][:], in0=xs_v, in1=dtdec[:, i, :].unsqueeze(2).to_broadcast([128, 4, 64]),
                                                                op=ALU.mult), R=[rpt, r_dtdec], W=[xdd[i][1]])
                yield
                K.op("dve", lambda i=i: nc.vector.tensor_tensor(out=xD[i][0][:], in0=xs_v, in1=dsk[:].unsqueeze(2).to_broadcast([128, 4, 64]),
                                                                op=ALU.mult), R=[rpt, r_c], W=[xD[i][1]])
                yield
                K.op("dve", lambda i=i: nc.vector.tensor_copy(out=Btm[i][0][:], in_=pt_[:, 2, :]), R=[rpt], W=[Btm[i][1]])
            yield
            for i in range(G):
                K.op("pe", lambda i=i: nc.tensor.matmul(X[7][:, i * 128:(i + 1) * 128], lhsT=xc[2][:, cs[i]], rhs=xc[3][:, cs[i]],
                                                        start=True, stop=True), R=[r_xc[2], r_xc[3]], W=[rX[7]])
            yield
            K.op("dve", lambda: nc.vector.tensor_tensor(out=Gm[:], in0=X[7][:, 0:G * 128].rearrange("p (g l) -> p g l", l=128),
                                                        in1=C["tri"].unsqueeze(1).to_broadcast([128, G, 128]), op=ALU.mult),
                 R=[rX[7], C["r"]], W=[r_Gm])
            yield
            for i in range(G):
                K.op("dve", lambda i=i: nc.vector.tensor_tensor(out=scT[i][0][:], in0=Gm[:, i, :].unsqueeze(1).to_broadcast([128, 4, 128]),
                                                                in1=segT[i][0][:], op=ALU.mult), R=[r_Gm, segT[i][1]], W=[scT[i][1]])
            yield
            for i in range(G):
                py_ = X[i % 2][:, (i // 2 % 2) * 256:(i // 2 % 2) * 256 + 256]; rpy = rX[i % 2]
                for h in range(4):
                    K.op("pe", lambda i=i, h=h: nc.tensor.matmul(py_[:, h * 64:(h + 1) * 64], lhsT=scT[i][0][:, h, :], rhs=xdt[i][0][:, h, :],
                                                                 start=True, stop=True), R=[scT[i][1], xdt[i][1]], W=[rpy])
                yield
                K.op("dve", lambda i=i: nc.vector.tensor_tensor(out=t0[i][0][:], in0=py_, in1=fl(xD[i][0]), op=ALU.add),
                     R=[rpy, xD[i][1]], W=[t0[i][1]])
            yield
            yield

        def genB(g0, S_, O_):
            cs = [slice((g0 + i) * 128, (g0 + i + 1) * 128) for i in range(G)]
            sz, r_sz, dtx, r_dtx, ax, r_ax, ex, r_ex, lx, r_lx, dt, r_dt, aa, r_aa, acs, r_acs, nacs, r_nacs, el, r_el, cd, r_cd, dd, r_dd, dec, r_dec, dtdec, r_dtdec, rseg, segT, xdt, xdd, xD, Btm, Gm, r_Gm, scT, t0, Sbf, gg, junk, r_junk, ssq, r_ssq, nscr, yb, r_yb = [S_[n_] for n_ in ['sz', 'r_sz', 'dtx', 'r_dtx', 'ax', 'r_ax', 'ex', 'r_ex', 'lx', 'r_lx', 'dt', 'r_dt', 'aa', 'r_aa', 'acs', 'r_acs', 'nacs', 'r_nacs', 'el', 'r_el', 'cd', 'r_cd', 'dd', 'r_dd', 'dec', 'r_dec', 'dtdec', 'r_dtdec', 'rseg', 'segT', 'xdt', 'xdd', 'xD', 'Btm', 'Gm', 'r_Gm', 'scT', 't0', 'Sbf', 'gg', 'junk', 'r_junk', 'ssq', 'r_ssq', 'nscr', 'yb', 'r_yb']]
            for i in range(G):
                pst_ = X[4][:, (i % 2) * 256:(i % 2) * 256 + 256]; rpst = rX[4]
                K.op("pe", lambda i=i: nc.tensor.matmul(pst_, lhsT=Btm[i][0][:], rhs=fl(xdd[i][0]), start=True, stop=True),
                     R=[Btm[i][1], xdd[i][1]], W=[rpst])
                yield
                K.op("dve", lambda i=i: nc.vector.tensor_tensor(out=Sf[:], in0=Sf[:], in1=cd[:, i, :].unsqueeze(2).to_broadcast([128, 4, 64]),
                                                                op=ALU.mult), R=[r_Sf, r_cd], W=[r_Sf])
                yield
                K.op("dve", lambda: nc.vector.tensor_tensor(out=fl(Sf), in0=fl(Sf), in1=pst_, op=ALU.add),
                     R=[r_Sf, rpst], W=[r_Sf])
                yield
                K.op("act", lambda i=i: nc.scalar.copy(out=Sbf[i + 1][0][:], in_=fl(Sf)), R=[r_Sf], W=[Sbf[i + 1][1]])
            yield
            for i in range(G):
                pyo_ = X[6][:, (i % 2) * 256:(i % 2) * 256 + 256]; rpyo = rX[6]
                K.op("pe", lambda i=i: nc.tensor.matmul(pyo_, lhsT=xc[3][:, cs[i]], rhs=Sbf[i][0][:], start=True, stop=True),
                     R=[r_xc[3], Sbf[i][1]], W=[rpyo])
                yield
                K.op("dve", lambda i=i: nc.vector.tensor_tensor(out=gg[i][0][:].rearrange("p (a b) -> p a b", b=64),
                                                                in0=pyo_.rearrange("p (a b) -> p a b", b=64),
                                                                in1=el[:, i, :].unsqueeze(2).to_broadcast([128, 4, 64]), op=ALU.mult),
                     R=[rpyo, r_el], W=[gg[i][1]])
                yield
                K.op("dve", lambda i=i: nc.vector.tensor_tensor(out=gg[i][0][:], in0=gg[i][0][:], in1=t0[i][0][:], op=ALU.add),
                     R=[gg[i][1], t0[i][1]], W=[gg[i][1]])
                yield
                K.op("dve", lambda i=i: nc.vector.tensor_tensor(out=gg[i][0][:], in0=gg[i][0][:], in1=sz[:, i, :], op=ALU.mult),
                     R=[gg[i][1], r_sz], W=[gg[i][1]])
                yield
                K.op("act", lambda i=i: nc.scalar.activation(out=junk[:], in_=gg[i][0][:], func=AF.Square, accum_out=ssq[:, i:i + 1]),
                     R=[gg[i][1]], W=[r_junk, r_ssq])
            yield
            rstd_from_ssq(K, ssq[:], r_ssq, G, nscr, 1.0 / 256)
            yield
            for i in range(G):
                K.op("dve", lambda i=i: nc.vector.scalar_tensor_tensor(out=yb[i][:], in0=gg[i][0][:], scalar=nscr["rstd"][:, i:i + 1],
                                                                       in1=snw[:], op0=ALU.mult, op1=ALU.mult),
                     R=[gg[i][1], nscr["r_rstd"], r_c], W=[r_yb[i]])
                K.dma("sp", y_d[cs[i], 128:384], yb[i][:], ds_y[i % 2], R=[r_yb[i]])
            yield
            K.op("act", lambda: nc.scalar.copy(out=O_["Sbf"][0][0][:], in_=Sbf[G][0][:]), R=[Sbf[G][1]], W=[O_["Sbf"][0][1]])
            yield

            yield

        groups = list(range(0, nT, G))
        run_gen(genA(groups[0], sets[0]))
        for gi, g0 in enumerate(groups):
            gB = genB(g0, sets[gi % 2], sets[1 - gi % 2])
            gA = genA(groups[gi + 1], sets[1 - gi % 2]) if gi + 1 < len(groups) else None
            while gA is not None or gB is not None:
                for _ in range(AW):
                    if gA is not None:
                        try:
                            next(gA)
                        except StopIteration:
                            gA = None
                if gB is not None:
                    try:
                        next(gB)
                    except StopIteration:
                        gB = None
        K.barrier()
        K.end_phase()
        K.stack = outer


def emit_mlstm2(K, C, hnT, r_hnT, P, y_d, S, W=None):
    nc = K.nc
    nB = S // 512
    outer = K.stack
    with ExitStack() as st:
        K.stack = st
        K.begin_phase()
        wfm, r_wfm = W["wM_fm"] if W is not None else load_w(K, P["wM_fm"], 256, "wMf")
        wg, r_wg = W["wM_g"] if W is not None else load_w(K, P["wM_g"], 4, "wMg")
        wtm, r_wtm = W["wM_tm"] if W is not None else load_w(K, P["wM_tm"], 384, "wMt")
        dsm = K.new_dma_sem()
        r_c = Reg("mconst")
        gbias = K.sb([2, 2], F32, "gbias"); mnw = K.sb([128, 128], F32, "mnw")
        K.dma("sp", gbias[:], P["gbias"], dsm, W=[r_c])
        K.dma("sp", mnw[:], P["mnw"], dsm, W=[r_c])
        ident_f = C["ident_f"]
        Y = [psbank(K, f"my{i}") for i in range(7)]
        rY = [Reg(f"my{i}", excl=True) for i in range(7)]
        pq, pk, pgi, pgf = Y[0], Y[1], Y[2][0:2, :], Y[3][0:2, :]
        ptl = Y[4][:, 0:32].rearrange("p (q i h) -> p q i h", q=4, i=4)
        pdec = Y[4][:, 32:40]
        pC = Y[4][:, 64:129]
        pD = [Y[2].rearrange("p (i t) -> p i t", t=128), Y[3].rearrange("p (i t) -> p i t", t=128)]
        pS = [Y[0].rearrange("p (i t) -> p i t", t=128), Y[1].rearrange("p (i t) -> p i t", t=128)]
        pQ = [Y[0][:, 0:260].rearrange("p (i v) -> p i v", v=65), Y[1][:, 0:260].rearrange("p (i v) -> p i v", v=65)]
        pN = [Y[5][:, 0:260].rearrange("p (i v) -> p i v", v=65), Y[6][:, 0:260].rearrange("p (i v) -> p i v", v=65)]
        ptm = [Y[5][:, 0:384], Y[6][:, 0:384]]

        def T(shape, dt, name):
            return K.sb(shape, dt, name), Reg(name)
        qTb, r_qTb = T([128, 512], BF16, "mqTb")
        kTb, r_kTb = T([128, 512], BF16, "mkTb")
        rows = {}
        for n_ in ["ipre", "yv", "e", "b", "al", "cma", "mu", "nmu", "wrow", "inter", "en", "tmp"]:
            rows[n_] = T([2, 512], F32, "mr_" + n_)
        rows["nab"] = rows["e"]; rows["l"] = rows["e"]
        rows["logf"] = rows["yv"]
        mnew, r_mnew = T([2, 8], F32, "mnew")
        mprev, r_mprev = T([2, 8], F32, "mprev")
        mcar, r_mcar = T([2, 1], F32, "mcar")
        decay, r_decay = T([2, 8], F32, "mdecay")
        tl, r_tl = T([128, 4, 4, 2], F32, "mtl")
        decr, r_decr = T([128, 8], F32, "mdecr")
        ktm, r_ktm = T([128, 4, 128], F32, "mktm")
        vaug, r_vaug = T([128, 4, 2, 65], BF16, "mvaug")
        og, r_og = T([128, 4, 128], F32, "mog")
        dT, r_dT = T([128, 2, 4, 128], F32, "mdT")
        sdT, r_sdT = T([128, 2, 4, 128], BF16, "msdT")
        nmv, r_nmv = T([128, 2, 4, 65], F32, "mnmv")
        kw, r_kw = T([128, 2, 4, 64], BF16, "mkw")
        Cst, r_Cst = T([128, 65], F32, "mCst")
        Cbf, r_Cbf = T([128, 9, 65], BF16, "mCbf")
        r_Cb = [Reg(f"Cb{i}") for i in range(9)]
        tq, r_tq = T([128, 2, 4, 65], F32, "mtq")
        dn, r_dn = T([128, 2, 4], F32, "mdn")
        rn, r_rn = T([128, 2, 4], F32, "mrn")
        hm, r_hm = T([128, 2, 4, 64], F32, "mhm")
        sqv, r_sqv = T([128, 2, 4, 64], F32, "msqv")
        ssq, r_ssq = T([128, 8], F32, "mssq")
        nscr = mk_scr(K, [128, 8], "mn")
        hn2, r_hn2 = T([128, 2, 4, 64], F32, "mhn2")
        yb = [K.sb([128, 4, 128], BF16, f"myb{i}") for i in range(2)]; r_yb = [Reg() for _ in range(2)]
        ds_y = [K.new_dma_sem() for _ in range(2)]
        K.op("dve", lambda: nc.vector.memset(Cst[:], 0.0), W=[r_Cst])
        K.op("dve", lambda: nc.vector.memset(Cbf[:].rearrange("p a b -> p (a b)"), 0.0), W=r_Cb)
        K.op("dve", lambda: nc.vector.memset(mcar[:], 0.0), W=[r_mcar])
        K.op("dve", lambda: nc.vector.memset(vaug[:].rearrange("p a b c -> p (a b c)"), 1.0), W=[r_vaug])

        def R_(n_):
            return rows[n_][0]

        def rr(n_):
            return rows[n_][1]
        rowc = C["rowc"]
        for b in range(nB):
            bs = slice(b * 512, (b + 1) * 512)
            for (pp_, rp_, c0, dst, rdst, sc) in [(pq, rY[0], 0, qTb, r_qTb, 1.0), (pk, rY[1], 128, kTb, r_kTb, 0.125)]:
                for k in range(8):
                    K.op("pe", lambda k=k: nc.tensor.matmul(pp_, lhsT=wfm[:, k, c0:c0 + 128], rhs=hnT[:, k, bs],
                                                            start=(k == 0), stop=(k == 7)), R=[r_wfm, r_hnT], W=[rp_])
                K.op("act", lambda: nc.scalar.mul(out=dst[:], in_=pp_, mul=sc), R=[rp_], W=[rdst])
            for (pp_, rp_, c0) in [(pgi, rY[2], 0), (pgf, rY[3], 2)]:
                for k in range(8):
                    K.op("pe", lambda k=k: nc.tensor.matmul(pp_, lhsT=wg[:, k, c0:c0 + 2], rhs=hnT[:, k, bs],
                                                            start=(k == 0), stop=(k == 7)), R=[r_wg, r_hnT], W=[rp_])
            K.op("dve", lambda: nc.vector.tensor_scalar(out=R_("ipre")[:], in0=pgi, scalar1=gbias[:, 0:1], scalar2=None,
                                                        op0=ALU.add), R=[rY[2], r_c], W=[rr("ipre")])
            K.op("dve", lambda: nc.vector.tensor_scalar(out=R_("yv")[:], in0=pgf, scalar1=gbias[:, 1:2], scalar2=-1.0,
                                                        op0=ALU.add, op1=ALU.mult), R=[rY[3], r_c], W=[rr("yv")])
            K.op("dve", lambda: nc.vector.scalar_tensor_tensor(out=R_("nab")[:], in0=R_("yv")[:], scalar=-1.0, in1=R_("yv")[:],
                                                               op0=ALU.mult, op1=ALU.min), R=[rr("yv")], W=[rr("nab")])
            K.op("act", lambda: nc.scalar.activation(out=R_("e")[:], in_=R_("nab")[:], func=AF.Exp), R=[rr("nab")], W=[rr("e")])
            K.op("dve", lambda: nc.vector.tensor_scalar(out=R_("e")[:], in0=R_("e")[:], scalar1=1.0, scalar2=None, op0=ALU.add),
                 R=[rr("e")], W=[rr("e")])
            K.op("act", lambda: nc.scalar.activation(out=R_("l")[:], in_=R_("e")[:], func=AF.Ln), R=[rr("e")], W=[rr("l")])
            K.op("dve", lambda: nc.vector.scalar_tensor_tensor(out=R_("logf")[:], in0=R_("yv")[:], scalar=0.0, in1=R_("l")[:],
                                                               op0=ALU.max, op1=ALU.add), R=[rr("yv"), rr("l")], W=[rr("logf")])
            K.op("dve", lambda: nc.vector.tensor_scalar(out=R_("logf")[:], in0=R_("logf")[:], scalar1=-1.0, scalar2=None,
                                                        op0=ALU.mult), R=[rr("logf")], W=[rr("logf")])
            K.op("dve", lambda: nc.vector.tensor_tensor_scan(out=R_("b")[:], data0=rowc[:, 0, :], data1=R_("logf")[:],
                                                             initial=0.0, op0=ALU.mult, op1=ALU.add),
                 R=[rr("logf"), C["r"]], W=[rr("b")])
            K.op("dve", lambda: nc.vector.tensor_tensor(out=R_("al")[:], in0=R_("ipre")[:], in1=R_("b")[:], op=ALU.subtract),
                 R=[rr("ipre"), rr("b")], W=[rr("al")])
            K.op("dve", lambda: nc.vector.tensor_tensor_scan(out=R_("cma")[:], data0=rowc[:, 1, :], data1=R_("al")[:],
                                                             initial=0.0, op0=ALU.add, op1=ALU.max),
                 R=[rr("al"), C["r"]], W=[rr("cma")])
            cma3 = R_("cma")[:].rearrange("p (c l) -> p c l", l=64)
            b3 = R_("b")[:].rearrange("p (c l) -> p c l", l=64)
            al3 = R_("al")[:].rearrange("p (c l) -> p c l", l=64)
            mu3 = R_("mu")[:].rearrange("p (c l) -> p c l", l=64)
            tmp3 = R_("tmp")[:].rearrange("p (c l) -> p c l", l=64)
            K.op("dve", lambda: nc.vector.tensor_tensor_scan(out=mnew[:], data0=cma3[:, :, 63], data1=b3[:, :, 63],
                                                             initial=mcar[:, 0:1], op0=ALU.max, op1=ALU.add),
                 R=[rr("cma"), rr("b"), r_mcar], W=[r_mnew])
            K.op("dve", lambda: nc.vector.tensor_copy(out=mprev[:, 0:1], in_=mcar[:]), R=[r_mcar], W=[r_mprev])
            K.op("dve", lambda: nc.vector.tensor_copy(out=mprev[:, 1:8], in_=mnew[:, 0:7]), R=[r_mnew], W=[r_mprev])
            K.op("dve", lambda: nc.vector.tensor_copy(out=mcar[:], in_=mnew[:, 7:8]), R=[r_mnew, r_mprev], W=[r_mcar])
            mpb = mprev[:].unsqueeze(2).to_broadcast([2, 8, 64])
            K.op("dve", lambda: nc.vector.tensor_tensor(out=mu3, in0=cma3, in1=mpb, op=ALU.max),
                 R=[rr("cma"), r_mprev], W=[rr("mu")])
            K.op("dve", lambda: nc.vector.tensor_scalar(out=R_("nmu")[:], in0=R_("mu")[:], scalar1=-1.0, scalar2=None,
                                                        op0=ALU.mult), R=[rr("mu")], W=[rr("nmu")])
            mcb = mu3[:, :, 63].unsqueeze(2).to_broadcast([2, 8, 64])
            K.op("dve", lambda: nc.vector.tensor_tensor(out=tmp3, in0=al3, in1=mcb, op=ALU.subtract),
                 R=[rr("al"), rr("mu")], W=[rr("tmp")])
            K.op("act", lambda: nc.scalar.activation(out=R_("wrow")[:], in_=R_("tmp")[:], func=AF.Exp), R=[rr("tmp")], W=[rr("wrow")])
            K.op("dve", lambda: nc.vector.tensor_tensor(out=decay[:], in0=mprev[:], in1=mu3[:, :, 63], op=ALU.subtract),
                 R=[r_mprev, rr("mu")], W=[r_decay])
            K.op("act", lambda: nc.scalar.activation(out=decay[:], in_=decay[:], func=AF.Exp), R=[r_decay], W=[r_decay])
            K.op("dve", lambda: nc.vector.tensor_tensor(out=tmp3, in0=mu3, in1=mpb, op=ALU.subtract),
                 R=[rr("mu"), r_mprev, rr("wrow")], W=[rr("tmp")])
            K.op("act", lambda: nc.scalar.activation(out=R_("inter")[:], in_=R_("tmp")[:], func=AF.Exp, scale=-1.0),
                 R=[rr("tmp")], W=[rr("inter")])
            K.op("dve", lambda: nc.vector.tensor_tensor(out=R_("tmp")[:], in0=R_("b")[:], in1=R_("mu")[:], op=ALU.add),
                 R=[rr("b"), rr("mu"), rr("inter")], W=[rr("tmp")])
            K.op("act", lambda: nc.scalar.activation(out=R_("en")[:], in_=R_("tmp")[:], func=AF.Exp, scale=-1.0),
                 R=[rr("tmp")], W=[rr("en")])
            for qi, qn_ in enumerate(["al", "wrow", "inter", "en"]):
                for i in range(4):
                    K.op("pe", lambda qi=qi, i=i, qn_=qn_: nc.tensor.transpose(
                        out=ptl[:, qi, i, :], in_=R_(qn_)[0:2, i * 128:(i + 1) * 128], identity=ident_f[0:2, 0:2]),
                        R=[rr(qn_), C["r"]], W=[rY[4]])
            K.op("pe", lambda: nc.tensor.matmul(pdec, lhsT=C["hsel"][:], rhs=decay[:], start=True, stop=True),
                 R=[r_decay, C["r"]], W=[rY[4]])
            K.op("dve", lambda: nc.vector.tensor_copy(out=tl[:], in_=ptl), R=[rY[4]], W=[r_tl])
            K.op("dve", lambda: nc.vector.tensor_copy(out=decr[:], in_=pdec), R=[rY[4]], W=[r_decr])
            for i in range(4):
                ts = slice((b * 4 + i) * 128, (b * 4 + i + 1) * 128)
                pt_ = ptm[i % 2]; rpt = rY[5 + i % 2]
                for k in range(8):
                    K.op("pe", lambda k=k: nc.tensor.matmul(pt_, lhsT=hnT[:, k, ts], rhs=wtm[:, k, :],
                                                            start=(k == 0), stop=(k == 7)), R=[r_hnT, r_wtm], W=[rpt])
                K.op("act", lambda i=i: nc.scalar.mul(out=ktm[:, i, :], in_=pt_[:, 0:128], mul=0.125), R=[rpt], W=[r_ktm])
                K.op("dve", lambda i=i: nc.vector.tensor_copy(out=vaug[:, i, :, 0:64],
                                                              in_=pt_[:, 128:256].rearrange("p (h d) -> p h d", d=64)),
                     R=[rpt], W=[r_vaug])
                K.op("act", lambda i=i: nc.scalar.activation(out=og[:, i, :], in_=pt_[:, 256:384], func=AF.Sigmoid),
                     R=[rpt], W=[r_og])
            for h in range(2):
                K.op("pe", lambda h=h: nc.tensor.matmul(pD[h].rearrange("p i t -> p (i t)"), lhsT=C["sel"][:, h, :],
                                                        rhs=R_("nmu")[:], start=True, stop=False),
                     R=[rr("nmu"), C["r"]], W=[rY[2 + h]])
                K.op("pe", lambda h=h: nc.tensor.matmul(pD[h].rearrange("p i t -> p (i t)"), lhsT=ident_f[:],
                                                        rhs=C["nm2"], start=False, stop=True),
                     R=[C["r"]], W=[rY[2 + h]])
            for h in range(2):
                for i in range(4):
                    K.op("act", lambda h=h, i=i: nc.scalar.activation(out=dT[:, h, i, :], in_=pD[h][:, i, :], func=AF.Exp,
                                                                      bias=tl[:, 0, i, h:h + 1]),
                         R=[rY[2 + h], r_tl], W=[r_dT])
            for h in range(2):
                hp = slice(64 * h, 64 * h + 64)
                for i in range(4):
                    tb = slice(i * 128, (i + 1) * 128)
                    K.op("pe", lambda h=h, i=i: nc.tensor.matmul(pS[h][:, i, :], lhsT=kTb[hp, tb], rhs=qTb[hp, tb],
                                                                 start=True, stop=True), R=[r_kTb, r_qTb], W=[rY[h]])
                K.op("dve", lambda h=h: nc.vector.tensor_tensor(out=sdT[:, h, :, :], in0=pS[h], in1=dT[:, h, :, :], op=ALU.mult),
                     R=[rY[h], r_dT], W=[r_sdT])
            for h in range(2):
                for i in range(4):
                    K.op("pe", lambda h=h, i=i: nc.tensor.matmul(pN[h][:, i, :], lhsT=sdT[:, h, i, :], rhs=vaug[:, i, h, :],
                                                                 start=True, stop=True), R=[r_sdT, r_vaug], W=[rY[5 + h]])
                K.op("dve", lambda h=h: nc.vector.tensor_copy(out=nmv[:, h, :, :], in_=pN[h]), R=[rY[5 + h]], W=[r_nmv])
            for h in range(2):
                hc = slice(64 * h, 64 * h + 64)
                K.op("dve", lambda h=h: nc.vector.tensor_tensor(out=kw[:, h, :, :], in0=ktm[:, :, hc],
                                                                in1=tl[:, 1, :, h].unsqueeze(2).to_broadcast([128, 4, 64]),
                                                                op=ALU.mult), R=[r_ktm, r_tl], W=[r_kw])
            for ce in range(8):
                i, half = ce // 2, ce % 2
                rs_ = slice(64 * half, 64 * half + 64)
                for h in range(2):
                    hp = slice(64 * h, 64 * h + 64)
                    K.op("pe", lambda h=h: nc.tensor.matmul(pC[hp, :], lhsT=kw[rs_, h, i, :], rhs=vaug[rs_, i, h, :],
                                                            start=True, stop=True), R=[r_kw, r_vaug], W=[rY[4]])
                K.op("dve", lambda: nc.vector.scalar_tensor_tensor(out=Cst[:], in0=Cst[:], scalar=decr[:, ce:ce + 1], in1=pC,
                                                                   op0=ALU.mult, op1=ALU.add), R=[r_Cst, r_decr, rY[4]], W=[r_Cst])
                K.op("act", lambda: nc.scalar.copy(out=Cbf[:, ce + 1, :], in_=Cst[:]), R=[r_Cst], W=[r_Cb[ce + 1]])
            for h in range(2):
                hp = slice(64 * h, 64 * h + 64)
                for ce in range(8):
                    i, half = ce // 2, ce % 2
                    rs_ = slice(64 * half, 64 * half + 64)
                    tc = slice(i * 128 + 64 * half, i * 128 + 64 * half + 64)
                    K.op("pe", lambda h=h: nc.tensor.matmul(pQ[h][rs_, i, :], lhsT=qTb[hp, tc], rhs=Cbf[hp, ce, :],
                                                            start=True, stop=True), R=[r_qTb, r_Cb[ce]], W=[rY[h]])
            ybt = yb[b % 2]
            for h in range(2):
                hc = slice(64 * h, 64 * h + 64)
                K.op("dve", lambda h=h: nc.vector.tensor_tensor(out=tq[:, h, :, :], in0=pQ[h],
                                                                in1=tl[:, 2, :, h].unsqueeze(2).to_broadcast([128, 4, 65]),
                                                                op=ALU.mult), R=[rY[h], r_tl], W=[r_tq])
                K.op("dve", lambda h=h: nc.vector.tensor_tensor(out=nmv[:, h, :, :], in0=nmv[:, h, :, :], in1=tq[:, h, :, :],
                                                                op=ALU.add), R=[r_nmv, r_tq], W=[r_nmv])
                K.op("dve", lambda h=h: nc.vector.scalar_tensor_tensor(out=dn[:, h, :], in0=nmv[:, h, :, 64], scalar=-1.0,
                                                                       in1=nmv[:, h, :, 64], op0=ALU.mult, op1=ALU.max),
                     R=[r_nmv], W=[r_dn])
                K.op("dve", lambda h=h: nc.vector.tensor_tensor(out=dn[:, h, :], in0=dn[:, h, :], in1=tl[:, 3, :, h], op=ALU.max),
                     R=[r_dn, r_tl], W=[r_dn])
                K.op("dve", lambda h=h: nc.vector.reciprocal(out=rn[:, h, :], in_=dn[:, h, :]), R=[r_dn], W=[r_rn])
                K.op("dve", lambda h=h: nc.vector.tensor_tensor(out=hm[:, h, :, :], in0=nmv[:, h, :, 0:64],
                                                                in1=rn[:, h, :].unsqueeze(2).to_broadcast([128, 4, 64]),
                                                                op=ALU.mult), R=[r_nmv, r_rn], W=[r_hm])
                K.op("act", lambda h=h: nc.scalar.activation(out=sqv[:, h, :, :], in_=hm[:, h, :, :], func=AF.Square),
                     R=[r_hm], W=[r_sqv])
            K.op("dve", lambda: nc.vector.tensor_reduce(out=ssq[:], in_=sqv[:].rearrange("p h i d -> p (h i) d"),
                                                        axis=AX.X, op=ALU.add), R=[r_sqv], W=[r_ssq])
            rstd_from_ssq(K, ssq[:], r_ssq, 8, nscr, 1.0 / 64)
            rs3 = nscr["rstd"].rearrange("p (h i) -> p h i", i=4)
            for h in range(2):
                hc = slice(64 * h, 64 * h + 64)
                K.op("dve", lambda h=h: nc.vector.tensor_tensor(out=hn2[:, h, :, :], in0=hm[:, h, :, :],
                                                                in1=rs3[:, h, :].unsqueeze(2).to_broadcast([128, 4, 64]),
                                                                op=ALU.mult), R=[r_hm, nscr["r_rstd"]], W=[r_hn2])
                K.op("dve", lambda h=h: nc.vector.tensor_tensor(out=hn2[:, h, :, :], in0=hn2[:, h, :, :],
                                                                in1=mnw[:, hc].unsqueeze(1).to_broadcast([128, 4, 64]),
                                                                op=ALU.mult), R=[r_hn2, r_c], W=[r_hn2])
                K.op("dve", lambda h=h: nc.vector.tensor_tensor(out=ybt[:, :, hc], in0=hn2[:, h, :, :], in1=og[:, :, hc],
                                                                op=ALU.mult), R=[r_hn2, r_og], W=[r_yb[b % 2]])
            K.dma("sp", y_d[b * 512:(b + 1) * 512, 0:128].rearrange("(i p) c -> p i c", p=128), ybt[:], ds_y[b % 2],
                  R=[r_yb[b % 2]])
            K.op("act", lambda: nc.scalar.copy(out=Cbf[:, 0, :], in_=Cbf[:, 8, :]), R=[r_Cb[8]], W=[r_Cb[0]])
        K.barrier()
        K.end_phase()
        K.stack = outer


NEGV = np.float32(-30000.0)
OFF = {}
_names = ["mq", "mk", "mv", "mo", "mi", "mf", "z", "xbc", "dt", "dq", "dk", "dv"]
_sizes = [256, 256, 256, 256, 4, 4, 512, 1024, 8, 256, 256, 256]
_o = 0
for n, s in zip(_names, _sizes):
    OFF[n] = _o
    _o += s


def t5_bucket_np(rel):
    n = np.maximum(rel, 0)
    nf = np.maximum(n, 1).astype(np.float32)
    large = 16 + (np.log(nf / np.float32(16)) / np.float32(math.log(128 / 16)) * np.float32(16)).astype(np.int32)
    large = np.minimum(large, 31)
    return np.where(n < 16, n, large)


def tile_w(w):
    n = w.shape[1]
    return np.ascontiguousarray(w.reshape(8, 128, n).transpose(1, 0, 2))


def rep(v, n=128):
    v = np.asarray(v, dtype=np.float32).reshape(1, -1)
    return np.ascontiguousarray(np.broadcast_to(v, (n, v.shape[1])))


def cols(w, name, a, b):
    return w[:, OFF[name] + a: OFF[name] + b]


def mixer_params(inp, l, h):
    w = np.asarray(inp["w_in"][l])
    P = {}
    P["wD"] = tile_w(np.concatenate([cols(w, "dq", 128 * h, 128 * h + 128), cols(w, "dk", 128 * h, 128 * h + 128),
                                     cols(w, "dv", 128 * h, 128 * h + 128)], axis=1))
    qn = np.asarray(inp["diff_q_norm_w"][l]).reshape(64)
    kn = np.asarray(inp["diff_k_norm_w"][l]).reshape(64)
    P["qkw"] = rep(np.concatenate([qn, qn, kn, kn]))
    P["sw"] = rep(np.asarray(inp["diff_subln_w"][l]))
    P["lamb"] = rep(np.asarray(inp["diff_lambda"][l]).reshape(-1)).reshape(128, 4, 32)
    rb = np.asarray(inp["rel_bias"])
    kl = np.arange(128)[:, None]
    c = np.arange(1024)[None, :]
    rel = c - kl - 384
    bk = t5_bucket_np(rel)
    Bt = np.zeros((128, 2, 1024), np.float32)
    for hl in range(2):
        Bt[:, hl, :] = np.where(rel >= 0, rb[bk, 2 * h + hl], NEGV)
    P["Bt"] = Bt
    P["c31"] = rep(rb[31, 2 * h: 2 * h + 2])
    xb = OFF["xbc"]
    ch = np.concatenate([np.arange(256 * h, 256 * h + 256), 512 + np.arange(128 * h, 128 * h + 128),
                         768 + np.arange(128 * h, 128 * h + 128)])
    P["wS_fm"] = tile_w(w[:, xb + ch])
    P["wS_tm"] = tile_w(np.concatenate([cols(w, "z", 256 * h, 256 * h + 256), cols(w, "dt", 4 * h, 4 * h + 4)], axis=1))
    cw = np.asarray(inp["ssm_conv_w"][l])[:, ch]
    P["cw"] = np.ascontiguousarray(cw.reshape(4, 4, 128).transpose(2, 1, 0))
    P["cb"] = np.ascontiguousarray(np.asarray(inp["ssm_conv_b"][l])[ch].reshape(4, 128).T)
    P["dtb"] = rep(np.asarray(inp["ssm_dt_bias"][l])[4 * h:4 * h + 4])
    P["alog"] = rep(np.asarray(inp["ssm_A_log"][l])[4 * h:4 * h + 4])
    P["dsk"] = rep(np.asarray(inp["ssm_D"][l])[4 * h:4 * h + 4])
    P["snw"] = rep(np.asarray(inp["ssm_norm_w"][l])[256 * h:256 * h + 256])
    P["wM_fm"] = tile_w(np.concatenate([cols(w, "mq", 128 * h, 128 * h + 128), cols(w, "mk", 128 * h, 128 * h + 128)], axis=1))
    P["wM_g"] = tile_w(np.concatenate([cols(w, "mi", 2 * h, 2 * h + 2), cols(w, "mf", 2 * h, 2 * h + 2)], axis=1))
    P["wM_tm"] = tile_w(np.concatenate([cols(w, "mk", 128 * h, 128 * h + 128), cols(w, "mv", 128 * h, 128 * h + 128),
                                        cols(w, "mo", 128 * h, 128 * h + 128)], axis=1))
    gb = np.asarray(inp["mlstm_gate_bias"][l])
    P["gbias"] = np.ascontiguousarray(gb[:, 2 * h:2 * h + 2].T)
    P["mnw"] = rep(np.asarray(inp["mlstm_norm_w"][l])[128 * h:128 * h + 128])
    return {k: np.ascontiguousarray(v, dtype=np.float32) for k, v in P.items()}


def const_arrays():
    Cn = {}
    Cn["ident_bf"] = np.eye(128, dtype=np.float32).astype(ml_dtypes.bfloat16)
    Cn["ident_f"] = np.eye(128, dtype=np.float32)
    j = np.arange(128)
    tri = (j[:, None] <= j[None, :]).astype(np.float32)
    same = (j[:, None] // 64) == (j[None, :] // 64)
    nm2 = np.where(same & (j[:, None] <= j[None, :]), 0.0, NEGV).astype(np.float32)
    sel = np.zeros((2, 2, 128), np.float32)
    sel[0, 0, :] = 1
    sel[1, 1, :] = 1
    hs = np.zeros((2, 128), np.float32)
    hs[0, :64] = 1
    hs[1, 64:] = 1
    nm1 = np.where(j[:, None] <= j[None, :], 0.0, NEGV).astype(np.float32)
    Cn["cf"] = np.ascontiguousarray(np.concatenate([np.ones((128, 128), np.float32), tri, np.tile(nm2, (1, 4)),
                                                    np.tile(nm1, (1, 4))], axis=1))
    Cn["sel"] = sel
    Cn["hsel"] = hs
    t = np.arange(512)
    rm = (t % 64 != 0).astype(np.float32)
    Cn["rowc"] = np.ascontiguousarray(np.stack([np.stack([rm, rm]), np.stack([(1 - rm) * np.float32(-1e30)] * 2)], axis=1))
    return Cn


import math as _math
from concourse.bass_utils import run_bass_kernel_spmd

NCORES = 8
SEQ = 4096
TOK = 2048
DEPTH = 2
PAIRS = [[0, 1], [2, 3], [4, 5], [6, 7]]


def _tile_gu(w):
    return np.ascontiguousarray(np.asarray(w).reshape(8, 128, 22, 128).transpose(2, 1, 0, 3).reshape(22, 128, 1024))


def _tile_nw(w):
    return np.ascontiguousarray(np.asarray(w).reshape(8, 128).T)


def ffn_host(inp, which, l, tag):
    return {
        tag + "nw": _tile_nw(inp[which + "_norm_w"][l]),
        tag + "wg": _tile_gu(inp[which + "_w_gate"][l]),
        tag + "wu": _tile_gu(inp[which + "_w_up"][l]),
        tag + "wd": np.ascontiguousarray(np.asarray(inp[which + "_w_down"][l]).reshape(22, 128, 1024)),
    }


def ffn_decl(nc, tag):
    d = {}
    d["nw"] = nc.dram_tensor(tag + "nw", [128, 8], F32, kind="ExternalInput").ap()
    d["wg"] = nc.dram_tensor(tag + "wg", [22, 128, 1024], F32, kind="ExternalInput").ap()
    d["wu"] = nc.dram_tensor(tag + "wu", [22, 128, 1024], F32, kind="ExternalInput").ap()
    d["wd"] = nc.dram_tensor(tag + "wd", [22, 128, 1024], F32, kind="ExternalInput").ap()
    return d


def wout_host(inp, l):
    perm = []
    for h in range(2):
        perm += list(range(128 * h, 128 * h + 128))
        perm += list(range(256 + 256 * h, 256 + 256 * h + 256))
        perm += list(range(768 + 128 * h, 768 + 128 * h + 128))
    w = np.asarray(inp["w_out"][l])[np.array(perm), :]
    return np.ascontiguousarray(w.reshape(8, 128, 1024))


def lam_init_of(l):
    return 0.8 - 0.6 * _math.exp(-0.3 * l)


def build_fused(Pshapes, Cn):
    nc = bass.Bass("TRN2", target_bir_lowering=False, num_devices=NCORES)
    Dc = {"ident_bf": nc.dram_tensor("ident_bf", [128, 128], BF16, kind="ExternalInput").ap(),
          "ident_f": nc.dram_tensor("ident_f", [128, 128], F32, kind="ExternalInput").ap()}
    Dk = {k: nc.dram_tensor(k, list(Cn[k].shape), F32, kind="ExternalInput").ap() for k in ["cf", "sel", "hsel", "rowc"]}
    x_d = nc.dram_tensor("x", [TOK, 1024], F32, kind="ExternalInput").ap()
    mh_d = nc.dram_tensor("mh", [128, 2], F32, kind="ExternalInput").ap()
    out_d = nc.dram_tensor("out", [TOK, 1024], F32, kind="ExternalOutput").ap()
    L = []
    for l in range(DEPTH):
        d = {"f1": ffn_decl(nc, f"a{l}_"), "f2": ffn_decl(nc, f"c{l}_")}
        d["mixnw"] = nc.dram_tensor(f"mixnw{l}", [128, 8], F32, kind="ExternalInput").ap()
        d["wo"] = nc.dram_tensor(f"wo{l}", [8, 128, 1024], F32, kind="ExternalInput").ap()
        d["P"] = {k: nc.dram_tensor(f"m{l}_{k}", list(shp), F32, kind="ExternalInput").ap() for k, shp in Pshapes.items()}
        d["P"].update(Dk)
        d["x1"] = nc.dram_tensor(f"x1_{l}", [TOK, 1024], F32, kind="Internal").ap()
        d["hn_own"] = [nc.dram_tensor(f"hn_own{l}_{i}", [1024, 1024], BF16, kind="Internal").ap() for i in range(2)]
        d["hn_all"] = [nc.dram_tensor(f"hn_all{l}_{i}", [2 * 1024, 1024], BF16, kind="Internal").ap() for i in range(2)]
        d["y_half"] = [nc.dram_tensor(f"y_half{l}_{i}", [TOK, 512], BF16, kind="Internal").ap() for i in range(2)]
        d["y_all"] = [nc.dram_tensor(f"y_all{l}_{i}", [2 * TOK, 512], BF16, kind="Internal").ap() for i in range(2)]
        d["r_hn_all"] = [Reg(f"hn_all{l}_{i}") for i in range(2)]
        d["r_y_all"] = [Reg(f"y_all{l}_{i}") for i in range(2)]
        d["x2"] = nc.dram_tensor(f"x2_{l}", [TOK, 1024], F32, kind="Internal").ap()
        d["x3"] = out_d if l == DEPTH - 1 else nc.dram_tensor(f"x3_{l}", [TOK, 1024], F32, kind="Internal").ap()
        L.append(d)
    with ExitStack() as st:
        K = KB(nc, st)
        C = load_consts(K, Dc["ident_bf"], Dc["ident_f"])
        load_mixer_consts(K, C, Dk)
        csem = K.new_dma_sem()
        xin = x_d
        for l in range(DEPTH):
            d = L[l]
            f1, f2 = d["f1"], d["f2"]
            def hn_block_done(b, reg, d=d):
                K.collective("AllGather", d["hn_own"][b], d["hn_all"][b], PAIRS, csem, R=[reg], W=[d["r_hn_all"][b]])
            emit_ffn(K, C, xin, d["x1"], f1["nw"], f1["wg"], f1["wu"], f1["wd"], TOK,
                     hn_out=[ho.rearrange("(k p) t -> k p t", p=128) for ho in d["hn_own"]], nw2_d=d["mixnw"],
                     on_block=hn_block_done)
            with ExitStack() as st2:
                outer = K.stack
                K.stack = st2
                K.begin_phase()
                Wm, stg_stack = load_w_all(K, [("wM_fm", d["P"]["wM_fm"], 256), ("wM_g", d["P"]["wM_g"], 4),
                                               ("wM_tm", d["P"]["wM_tm"], 384), ("wS_fm", d["P"]["wS_fm"], 512),
                                               ("wS_tm", d["P"]["wS_tm"], 260), ("wD", d["P"]["wD"], 384)])
                hnT, r_hnT = load_hnT_pair(K, d["hn_all"], SEQ, regs=d["r_hn_all"])
                ysp = YSplit(d["y_half"][0], d["y_half"][1], TOK)

                def y_hook(t, d=d, ysp=ysp):
                    if t == SEQ // 1024 - 1:
                        K.collective("AllGather", d["y_half"][0], d["y_all"][0], PAIRS, csem, R=[ysp.regs[0]], W=[d["r_y_all"][0]])
                ysp.hook = y_hook
                emit_mlstm2(K, C, hnT, r_hnT, d["P"], ysp, SEQ, W=Wm)
                emit_ssd2(K, C, hnT, r_hnT, d["P"], ysp, SEQ, W=Wm)
                emit_diff(K, C, hnT, r_hnT, d["P"], ysp, SEQ, lam_init_of(l), W=Wm)
                K.end_phase()
                K.stack = outer
            K.collective("AllGather", d["y_half"][1], d["y_all"][1], PAIRS, csem, W=[d["r_y_all"][1]])
            emit_outproj_sel(K, C, d["x1"], d["y_all"], mh_d, d["wo"], d["x2"], TOK, SEQ, yregs=d["r_y_all"])
            emit_ffn(K, C, d["x2"], d["x3"], f2["nw"], f2["wg"], f2["wu"], f2["wd"], TOK)
            xin = d["x3"]
        K.barrier(full=True)
    return nc


def kernel(**inputs):
    inp = {k: np.asarray(v) for k, v in inputs.items()}
    x = inp["x"]
    cores = list(range(NCORES))
    Cn = const_arrays()
    shared = {"ident_bf": Cn["ident_bf"], "ident_f": Cn["ident_f"]}
    for k in ["cf", "sel", "hsel", "rowc"]:
        shared[k] = Cn[k]
    Ps = [[mixer_params(inp, l, h) for h in range(2)] for l in range(DEPTH)]
    for l in range(DEPTH):
        shared.update(ffn_host(inp, "ffn1", l, f"a{l}_"))
        shared.update(ffn_host(inp, "ffn2", l, f"c{l}_"))
        shared[f"mixnw{l}"] = _tile_nw(inp["mix_norm_w"][l])
        shared[f"wo{l}"] = wout_host(inp, l)
    nc = build_fused({k: v.shape for k, v in Ps[0][0].items()}, Cn)
    maps = []
    for c in cores:
        b, h = c // 2, c % 2
        m = dict(shared)
        m["x"] = np.ascontiguousarray(x[b, h * TOK:(h + 1) * TOK])
        mh = np.zeros((128, 2), np.float32)
        mh[:, h] = 1.0
        m["mh"] = mh
        for l in range(DEPTH):
            for k, v in Ps[l][h].items():
                m[f"m{l}_{k}"] = v
        maps.append(m)
    res = run_bass_kernel_spmd(nc, maps, core_ids=cores).results
    out = np.zeros((4, SEQ, 1024), np.float32)
    for c in cores:
        b, h = c // 2, c % 2
        out[b, h * TOK:(h + 1) * TOK] = np.asarray(res[c]["out"])
    return out
```

```python
import math
import numpy as np
import ml_dtypes
from contextlib import ExitStack
import concourse.bass as bass
import concourse.mybir as mybir


F32 = mybir.dt.float32
BF16 = mybir.dt.bfloat16
AF = mybir.ActivationFunctionType
ALU = mybir.AluOpType
AX = mybir.AxisListType


class Reg:
    __slots__ = ("lw", "rd", "name", "excl")

    def __init__(self, name="", excl=False):
        self.excl = excl
        self.lw = None
        self.rd = {}
        self.name = name


class KB:
    def __init__(self, nc, stack):
        self.nc = nc
        self.stack = stack
        self.eng = {"pe": nc.tensor, "dve": nc.vector, "act": nc.scalar,
                    "pool": nc.gpsimd, "sp": nc.sync}
        self.sem = {}
        self.cnt = {}
        self.waited = {}
        for n in self.eng:
            self.sem[n] = stack.enter_context(nc.semaphore("s_" + n))
            self.cnt[n] = 0
            self.waited[n] = {}
        self.top_stack = stack
        self.free_sems = []
        self.phase_sems = []
        self.lazy_keys = set()
        self.dma_sems = []
        self.dma_by_key = {}
        self.n_dma_sem = 0
        self.uid = 0

    def sb(self, shape, dt, name=None):
        self.uid += 1
        return self.stack.enter_context(
            self.nc.sbuf_tensor(f"sb{self.uid}_{name or ""}", list(shape), dt))

    def ps(self, shape, dt, name=None):
        self.uid += 1
        full = 512 if dt == F32 else 1024
        t = self.stack.enter_context(
            self.nc.psum_tensor(f"ps{self.uid}_{name or ""}", [128, full], dt))
        n = 1
        for d in shape[1:]:
            n *= d
        assert n <= full
        v = t[0:shape[0], 0:n]
        if len(shape) == 3:
            v = v.rearrange("p (a b) -> p a b", b=shape[2])
        elif len(shape) == 4:
            v = v.rearrange("p (a b c) -> p a b c", b=shape[2], c=shape[3])
        return v

    def new_dma_sem(self):
        if self.free_sems:
            ent = self.free_sems.pop()
        else:
            s = self.top_stack.enter_context(self.nc.semaphore(f"d{self.n_dma_sem}"))
            self.n_dma_sem += 1
            ent = [s, 0, f"d{self.n_dma_sem}"]
            self.dma_sems.append(ent)
            self.dma_by_key[ent[2]] = ent
        if self.phase_sems:
            self.phase_sems[-1].append(ent)
        return ent

    def begin_phase(self):
        self.phase_sems.append([])

    def end_phase(self):
        for ent in self.phase_sems.pop():
            self.free_sems.append(ent)

    def collective(self, kind, src, dst, groups, csem, R=(), W=()):
        self._deps("pool", R, W, is_dma=True)
        ins = self.nc.gpsimd.collective_compute(kind, mybir.AluOpType.bypass, replica_groups=groups,
                                                ins=[src], outs=[dst])
        csem[1] += 1
        ins.then_inc(csem[0], 1)
        tag = (csem[2], csem[0], csem[1])
        for w in W:
            w.lw = tag
            w.rd = {}
        self.lazy_keys.add(csem[2])
        return ins

    def _wait(self, e, dep):
        key, sem, val = dep
        if key in self.dma_by_key:
            val = self.dma_by_key[key][1]
        w = self.waited[e]
        if w.get(key, 0) >= val:
            return
        self.eng[e].wait_ge(sem, val)
        w[key] = val

    def _deps(self, e, R, W, is_dma=False):
        deps = []
        for r in R:
            if r.lw is not None:
                deps.append(r.lw)
            if r.excl:
                for d in r.rd.values():
                    if d[0] != e:
                        deps.append(d)
        for w in W:
            if w.lw is not None:
                deps.append(w.lw)
            for d in w.rd.values():
                if d[0] != e or is_dma:
                    deps.append(d)
        for d in deps:
            if d[0] == e and e == "pe" and not is_dma:
                continue
            self._wait(e, d)

    def op(self, e, fn, R=(), W=()):
        self._deps(e, R, W)
        ins = fn()
        self.cnt[e] += 1
        ins.then_inc(self.sem[e], 1)
        tag = (e, self.sem[e], self.cnt[e])
        for w in W:
            w.lw = tag
            w.rd = {}
        for r in R:
            if r not in W:
                r.rd[e] = tag
        return ins

    def dma(self, q, out, in_, dsem, R=(), W=(), **kw):
        self._deps(q, R, W, is_dma=True)
        ins = self.eng[q].dma_start(out=out, in_=in_, **kw)
        dsem[1] += 16
        ins.then_inc(dsem[0], 16)
        tag = (dsem[2], dsem[0], dsem[1])
        for w in W:
            w.lw = tag
            w.rd = {}
        for r in R:
            if r not in W:
                r.rd[dsem[2]] = tag
        return ins

    def barrier(self, full=False):
        deps = [(n, self.sem[n], self.cnt[n]) for n in self.eng if self.cnt[n] > 0]
        deps += [(d[2], d[0], d[1]) for d in self.dma_sems if d[1] > 0 and (full or d[2] not in self.lazy_keys)]
        for e in self.eng:
            for d in deps:
                if d[0] != e:
                    self._wait(e, d)


def psbank(K, name):
    K.uid += 1
    t = K.stack.enter_context(K.nc.psum_tensor(f"pb{K.uid}_{name}", [128, 512], F32))
    return t[:, :]


D = 1024
DFF = 2816
NF = DFF // 128
EPS = 1e-6


def load_consts(K, ident_bf_d, ident_f_d):
    C = {}
    C["dsem"] = K.new_dma_sem()
    C["ident_bf"] = K.sb([128, 128], BF16, "ident_bf")
    C["ident_f"] = K.sb([128, 128], F32, "ident_f")
    C["r"] = Reg("consts")
    K.dma("sp", C["ident_bf"][:], ident_bf_d, C["dsem"], W=[C["r"]])
    K.dma("sp", C["ident_f"][:], ident_f_d, C["dsem"], W=[C["r"]])
    return C


def emit_norm_stats(K, C, xt, r_xt, S):
    nc = K.nc
    K.op("act", lambda: nc.scalar.activation(out=S["junk"][:], in_=xt, func=AF.Square,
                                             accum_out=S["ssq"][:]),
         R=[r_xt], W=[S["r_junk"], S["r_ssq"]])
    K.op("dve", lambda: nc.vector.tensor_scalar(out=S["ms"][:], in0=S["ssq"][:], scalar1=1.0 / D,
                                                scalar2=EPS, op0=ALU.mult, op1=ALU.add),
         R=[S["r_ssq"]], W=[S["r_ms"]])
    K.op("act", lambda: nc.scalar.activation(out=S["sd"][:], in_=S["ms"][:], func=AF.Sqrt),
         R=[S["r_ms"]], W=[S["r_sd"]])
    K.op("dve", lambda: nc.vector.reciprocal(out=S["rstd"][:], in_=S["sd"][:]),
         R=[S["r_sd"]], W=[S["r_rstd"]])
    K.op("dve", lambda: nc.vector.tensor_scalar(out=S["xn"][:], in0=xt, scalar1=S["rstd"][:],
                                                scalar2=None, op0=ALU.mult),
         R=[r_xt, S["r_rstd"]], W=[S["r_xn"]])


def emit_norm_tr(K, C, nw, r_nw, dst_fn, r_dst, S):
    nc = K.nc
    for k in range(8):
        K.op("pe", lambda k=k: nc.tensor.transpose(out=S["ptr"][:, k, :],
                                                   in_=S["xn"][:, k * 128:(k + 1) * 128],
                                                   identity=C["ident_bf"][:]),
             R=[S["r_xn"], C["r"]], W=[S["r_ptr"]])
    K.op("dve", lambda: nc.vector.tensor_tensor(out=dst_fn, in0=S["ptr"][:],
                                                in1=nw.unsqueeze(2).to_broadcast([128, 8, 128]),
                                                op=ALU.mult),
         R=[S["r_ptr"], r_nw], W=[r_dst])


def emit_norm_T(K, C, xt, r_xt, nw, r_nw, dst_fn, r_dst, S):
    emit_norm_stats(K, C, xt, r_xt, S)
    emit_norm_tr(K, C, nw, r_nw, dst_fn, r_dst, S)


def norm_scratch(K, tag, share=0, ptr=None, r_ptr=None):
    S = {}
    if share is not None:
        S["junk"] = K.sb([128, D], BF16, tag + "junk")
    S["ssq"] = K.sb([128, 1], F32, tag + "ssq")
    S["ms"] = K.sb([128, 1], F32, tag + "ms")
    S["sd"] = K.sb([128, 1], F32, tag + "sd")
    S["rstd"] = K.sb([128, 1], F32, tag + "rstd")
    S["xn"] = K.sb([128, D], BF16, tag + "xn")
    S["ptr"] = K.ps([128, 8, 128], BF16, tag + "ptr") if ptr is None else ptr
    for n in ["junk", "ssq", "ms", "sd", "rstd", "xn", "ptr"]:
        S["r_" + n] = Reg(tag + n)
    if r_ptr is not None:
        S["r_ptr"] = r_ptr
    return S


def emit_ffn(K, C, x_in, x_out, nw_d, wg_d, wu_d, wd_d, ntok, hn_out=None, nw2_d=None,
             blk=1024, on_block=None):
    nc = K.nc
    outer = K.stack
    with ExitStack() as st:
        K.stack = st
        K.begin_phase()
        nblk = ntok // blk
        nsub = blk // 128
        ntt = blk // 512
        nw = K.sb([128, 8], F32, "nw")
        r_nw = Reg("nw")
        ds_misc = K.new_dma_sem()
        K.dma("sp", nw[:], nw_d, ds_misc, W=[r_nw])
        if hn_out is not None:
            nw2 = K.sb([128, 8], F32, "nw2")
            r_nw2 = Reg("nw2")
            K.dma("sp", nw2[:], nw2_d, ds_misc, W=[r_nw2])
            hnb = K.sb([128, 8, blk], BF16, "hnb")
            r_hnb = Reg("hnb")
            ds_hn = K.new_dma_sem()
        hT = K.sb([128, 8, blk], BF16, "hT")
        r_hT = [Reg(f"hT{j}") for j in range(nsub)]
        aT = K.sb([128, NF, blk], BF16, "aT")
        r_aT = [[Reg(f"aT{f}_{t}") for t in range(ntt)] for f in range(NF)]
        NX = 3
        xt = [K.sb([128, D], F32, f"xt{i}") for i in range(NX)]
        r_xt = [Reg(f"xt{i}") for i in range(NX)]
        ds_xt = [K.new_dma_sem() for i in range(NX)]
        Ss = [norm_scratch(K, "n1"), norm_scratch(K, "n2", share=None)]
        Ss[1]["junk"] = Ss[0]["junk"]; Ss[1]["r_junk"] = Ss[0]["r_junk"]
        NW = 3
        wst = [[K.sb([128, 8 * 128], F32, f"wst{i}_{g}") for g in range(2)] for i in range(NW)]
        r_wst = [[Reg() for g in range(2)] for i in range(NW)]
        ds_w = [K.new_dma_sem() for i in range(NW)]
        wb = [[K.sb([128, 8, 128], BF16, f"wb{i}_{g}") for g in range(2)] for i in range(NW)]
        r_wb = [[Reg() for g in range(2)] for i in range(NW)]
        wdb = K.sb([128, NF, D], BF16, "wdb")
        r_wdb = [Reg(f"wdb{f}") for f in range(NF)]
        NDS = 2
        wdst = [K.sb([128, D], F32, f"wdst{i}") for i in range(NDS)]
        r_wdst = [Reg() for i in range(NDS)]
        ds_wd = [K.new_dma_sem() for i in range(NDS)]
        pg = [K.ps([128, 512], F32, f"pg{i}") for i in range(2)]
        pu = [K.ps([128, 512], F32, f"pu{i}") for i in range(2)]
        r_pg = [Reg() for i in range(2)]
        r_pu = [Reg() for i in range(2)]
        sg = [K.sb([128, 512], F32, f"sg{i}") for i in range(2)]
        r_sg = [Reg() for i in range(2)]
        po = [K.ps([128, 512], F32, f"po{i}") for i in range(2)]
        r_po = [Reg() for i in range(2)]
        ot = [K.sb([128, D], F32, f"ot{i}") for i in range(2)]
        r_ot = [Reg() for i in range(2)]
        ds_ot = [K.new_dma_sem() for i in range(2)]

        Sn = norm_scratch(K, "n3", share=None, ptr=pg[0].bitcast(BF16).rearrange("p (a b) -> p a b", b=128),
                          r_ptr=r_pg[0])
        Sn["junk"] = Ss[0]["junk"]; Sn["r_junk"] = Ss[0]["r_junk"]
        xs = [0]

        def ld_row(row0):
            slot = xs[0] % NX
            xs[0] += 1
            K.dma("sp", xt[slot][:], x_in[row0: row0 + 128, :], ds_xt[slot], W=[r_xt[slot]])
            return slot
        for b in range(nblk):
            t0 = b * blk
            if b == 0:
                nslot = ld_row(t0)
                for j in range(nsub):
                    slot = nslot
                    if j + 1 < nsub:
                        nslot = ld_row(t0 + (j + 1) * 128)
                    emit_norm_T(K, C, xt[slot][:], r_xt[slot], nw[:], r_nw,
                                hT[:, :, j * 128:(j + 1) * 128], r_hT[j], Ss[j % 2])

            def ld_w(f):
                s = f % NW
                K.dma("sp", wst[s][0][:], wg_d[f], ds_w[s], W=[r_wst[s][0]])
                K.dma("sp", wst[s][1][:], wu_d[f], ds_w[s], W=[r_wst[s][1]])

            def ld_wd(f):
                s = f % NDS
                K.dma("act", wdst[s][:], wd_d[f], ds_wd[s], W=[r_wdst[s]])

            def cast_wd(f):
                s = f % NDS
                K.op("act", lambda: nc.scalar.copy(out=wdb[:, f, :], in_=wdst[s][:]),
                     R=[r_wdst[s]], W=[r_wdb[f]])

            def cast_w(f):
                s = f % NW
                K.op("dve", lambda: nc.vector.tensor_copy(
                    out=wb[s][0][:].rearrange("p k c -> p (k c)"), in_=wst[s][0][:]),
                    R=[r_wst[s][0]], W=[r_wb[s][0]])
                K.op("act", lambda: nc.scalar.copy(
                    out=wb[s][1][:].rearrange("p k c -> p (k c)"), in_=wst[s][1][:]),
                    R=[r_wst[s][1]], W=[r_wb[s][1]])

            ld_w(0)
            ld_w(1)
            ld_wd(0)
            ld_wd(1)
            cast_w(0)
            for f in range(NF):
                s = f % NW
                if f + 2 < NF:
                    ld_w(f + 2)
                if f + 1 < NF:
                    cast_w(f + 1)
                cast_wd(f)
                if f + 2 < NF:
                    ld_wd(f + 2)
                for tt in range(ntt):
                    pb = (f * ntt + tt) % 2
                    rr = [r_hT[j] for j in range(tt * 4, tt * 4 + 4)]
                    for k in range(8):
                        K.op("pe", lambda k=k: nc.tensor.matmul(
                            pg[pb][:], lhsT=wb[s][0][:, k, :], rhs=hT[:, k, tt * 512:(tt + 1) * 512],
                            start=(k == 0), stop=(k == 7)),
                            R=[r_wb[s][0]] + rr, W=[r_pg[pb]])
                    for k in range(8):
                        K.op("pe", lambda k=k: nc.tensor.matmul(
                            pu[pb][:], lhsT=wb[s][1][:, k, :], rhs=hT[:, k, tt * 512:(tt + 1) * 512],
                            start=(k == 0), stop=(k == 7)),
                            R=[r_wb[s][1]] + rr, W=[r_pu[pb]])
                    K.op("act", lambda: nc.scalar.activation(out=sg[pb][:], in_=pg[pb][:], func=AF.Silu),
                         R=[r_pg[pb]], W=[r_sg[pb]])
                    K.op("dve", lambda: nc.vector.tensor_tensor(
                        out=aT[:, f, tt * 512:(tt + 1) * 512], in0=sg[pb][:], in1=pu[pb][:], op=ALU.mult),
                        R=[r_sg[pb], r_pu[pb]], W=[r_aT[f][tt]])

            pend = None
            pend2 = None
            nxtb = (b + 1 < nblk)
            nslot = ld_row(t0)
            for j in range(nsub):
                slot = nslot
                if j + 1 < nsub:
                    nslot = ld_row(t0 + (j + 1) * 128)
                os_ = j % 2
                for half in range(2):
                    pb = (j * 2 + half) % 2
                    for f in range(NF):
                        K.op("pe", lambda f=f: nc.tensor.matmul(
                            po[pb][:], lhsT=aT[:, f, j * 128:(j + 1) * 128],
                            rhs=wdb[:, f, half * 512:(half + 1) * 512],
                            start=(f == 0), stop=(f == NF - 1)),
                            R=[r_aT[f][j // 4], r_wdb[f]], W=[r_po[pb]])
                    K.op("dve", lambda: nc.vector.scalar_tensor_tensor(
                        out=ot[os_][:, half * 512:(half + 1) * 512], in0=po[pb][:], scalar=0.5,
                        in1=xt[slot][:, half * 512:(half + 1) * 512], op0=ALU.mult, op1=ALU.add),
                        R=[r_po[pb], r_xt[slot]], W=[r_ot[os_]])
                K.dma("sp", x_out[t0 + j * 128: t0 + (j + 1) * 128, :], ot[os_][:], ds_ot[os_],
                      R=[r_ot[os_]])
                if hn_out is not None:
                    if pend is not None:
                        emit_norm_tr(K, C, nw2[:], r_nw2, *pend)
                    emit_norm_stats(K, C, ot[os_][:], r_ot[os_], Ss[j % 2])
                    pend = (hnb[:, :, j * 128:(j + 1) * 128], r_hnb, Ss[j % 2])
                if nxtb:
                    if pend2 is not None:
                        emit_norm_tr(K, C, nw[:], r_nw, *pend2)
                    s2 = ld_row(t0 + blk + j * 128)
                    emit_norm_stats(K, C, xt[s2][:], r_xt[s2], Sn)
                    pend2 = (hT[:, :, j * 128:(j + 1) * 128], r_hT[j], Sn)
            if hn_out is not None and pend is not None:
                emit_norm_tr(K, C, nw2[:], r_nw2, *pend)
                pend = None
            if pend2 is not None:
                emit_norm_tr(K, C, nw[:], r_nw, *pend2)
                pend2 = None
            if hn_out is not None:
                if isinstance(hn_out, (list, tuple)):
                    rd_ = Reg(f"hn_dram{b}")
                    K.dma("sp", hn_out[b].rearrange("k p t -> p k t"), hnb[:], ds_hn, R=[r_hnb], W=[rd_])
                    if on_block is not None:
                        on_block(b, rd_)
                else:
                    K.dma("sp", hn_out[:, :, t0:t0 + blk].rearrange("k p t -> p k t"), hnb[:], ds_hn,
                          R=[r_hnb])
        K.barrier()
        K.end_phase()
        K.stack = outer


def emit_outproj(K, C, x_in, y_in, wo_d, x_out, ntok):
    nc = K.nc
    outer = K.stack
    with ExitStack() as st:
        K.stack = st
        K.begin_phase()
        wob = K.sb([128, 8, D], BF16, "wob")
        r_wob = Reg()
        wst = [K.sb([128, D], F32, f"wost{i}") for i in range(2)]
        r_wst = [Reg() for i in range(2)]
        ds_w = [K.new_dma_sem() for i in range(2)]
        for k in range(8):
            K.dma("sp", wst[k % 2][:], wo_d[k], ds_w[k % 2], W=[r_wst[k % 2]])
            K.op("pool", lambda k=k: nc.gpsimd.tensor_copy(out=wob[:, k, :], in_=wst[k % 2][:]),
                 R=[r_wst[k % 2]], W=[r_wob])
        NS = 3
        yt = [K.sb([128, D], BF16, f"yt{i}") for i in range(NS)]
        r_yt = [Reg() for i in range(NS)]
        xt = [K.sb([128, D], F32, f"oxt{i}") for i in range(NS)]
        r_xt = [Reg() for i in range(NS)]
        ds_in = [K.new_dma_sem() for i in range(NS)]
        ptr = [K.ps([128, 8, 128], BF16, f"optr{i}") for i in range(2)]
        r_ptr = [Reg() for i in range(2)]
        yT = [K.sb([128, 8, 128], BF16, f"yT{i}") for i in range(2)]
        r_yT = [Reg() for i in range(2)]
        po = [K.ps([128, 512], F32, f"opo{i}") for i in range(2)]
        r_po = [Reg() for i in range(2)]
        ot = [K.sb([128, D], F32, f"oot{i}") for i in range(2)]
        r_ot = [Reg() for i in range(2)]
        ds_ot = [K.new_dma_sem() for i in range(2)]
        nsub = ntok // 128

        def ld(j):
            s = j % NS
            K.dma("sp", yt[s][:], y_in[j * 128:(j + 1) * 128, :], ds_in[s], W=[r_yt[s]])
            K.dma("sp", xt[s][:], x_in[j * 128:(j + 1) * 128, :], ds_in[s], W=[r_xt[s]])
        ld(0)
        for j in range(nsub):
            s = j % NS
            p2 = j % 2
            if j + 1 < nsub:
                ld(j + 1)
            for k in range(8):
                K.op("pe", lambda k=k: nc.tensor.transpose(out=ptr[p2][:, k, :],
                                                           in_=yt[s][:, k * 128:(k + 1) * 128],
                                                           identity=C["ident_bf"][:]),
                     R=[r_yt[s], C["r"]], W=[r_ptr[p2]])
            K.op("act", lambda: nc.scalar.copy(out=yT[p2][:], in_=ptr[p2][:]),
                 R=[r_ptr[p2]], W=[r_yT[p2]])
            for half in range(2):
                pb = half
                for k in range(8):
                    K.op("pe", lambda k=k: nc.tensor.matmul(
                        po[pb][:], lhsT=yT[p2][:, k, :], rhs=wob[:, k, half * 512:(half + 1) * 512],
                        start=(k == 0), stop=(k == 7)),
                        R=[r_yT[p2], r_wob], W=[r_po[pb]])
                K.op("dve", lambda: nc.vector.tensor_tensor(
                    out=ot[p2][:, half * 512:(half + 1) * 512], in0=po[pb][:],
                    in1=xt[s][:, half * 512:(half + 1) * 512], op=ALU.add),
                    R=[r_po[pb], r_xt[s]], W=[r_ot[p2]])
            K.dma("sp", x_out[j * 128:(j + 1) * 128, :], ot[p2][:], ds_ot[p2], R=[r_ot[p2]])
        K.barrier()
        K.end_phase()
        K.stack = outer


def emit_outproj_sel(K, C, x_in, yall, mh_d, wo_d, x_out, ntok, seq, yregs=None):
    nc = K.nc
    outer = K.stack
    with ExitStack() as st:
        K.stack = st
        K.begin_phase()
        wob = K.sb([128, 8, D], BF16, "wob")
        r_wob = Reg()
        wst = [K.sb([128, D], F32, f"wost{i}") for i in range(2)]
        r_wst = [Reg() for i in range(2)]
        ds_w = [K.new_dma_sem() for i in range(2)]
        mh = K.sb([128, 2], F32, "mh"); r_mh = Reg()
        K.dma("sp", mh[:], mh_d, ds_w[0], W=[r_mh])
        for k in range(8):
            K.dma("sp", wst[k % 2][:], wo_d[k], ds_w[k % 2], W=[r_wst[k % 2]])
            if k % 2 == 0:
                K.op("dve", lambda k=k: nc.vector.tensor_copy(out=wob[:, k, :], in_=wst[k % 2][:]),
                     R=[r_wst[k % 2]], W=[r_wob])
            else:
                K.op("act", lambda k=k: nc.scalar.copy(out=wob[:, k, :], in_=wst[k % 2][:]),
                     R=[r_wst[k % 2]], W=[r_wob])
        NS = 3
        ya = [[K.sb([128, 2, 512], BF16, f"ya{i}_{c}") for c in range(2)] for i in range(NS)]
        r_ya = [[Reg(), Reg()] for i in range(NS)]
        yt = [K.sb([128, D], BF16, f"yt{i}") for i in range(2)]
        r_yt = [Reg() for i in range(2)]
        xt = [K.sb([128, D], F32, f"oxt{i}") for i in range(NS)]
        r_xt = [Reg() for i in range(NS)]
        ds_in = [K.new_dma_sem() for i in range(NS)]
        ptr = [K.ps([128, 8, 128], BF16, f"optr{i}") for i in range(2)]
        r_ptr = [Reg() for i in range(2)]
        yT = [K.sb([128, 8, 128], BF16, f"yT{i}") for i in range(2)]
        r_yT = [Reg() for i in range(2)]
        po = [K.ps([128, 512], F32, f"opo{i}") for i in range(2)]
        r_po = [Reg() for i in range(2)]
        ot = [K.sb([128, D], F32, f"oot{i}") for i in range(2)]
        r_ot = [Reg() for i in range(2)]
        ds_ot = [K.new_dma_sem() for i in range(2)]
        nsub = ntok // 128
        yv = [ya_.rearrange("(r t) c -> t r c", r=2) for ya_ in yall]

        def ld(j):
            s = j % NS
            for c in range(2):
                K.dma("sp" if c == 0 else "act", ya[s][c][:], yv[c][j * 128:(j + 1) * 128, :, :],
                      ds_in[s], W=[r_ya[s][c]], R=([yregs[c]] if yregs is not None else []))
            K.dma("sp", xt[s][:], x_in[j * 128:(j + 1) * 128, :], ds_in[s], W=[r_xt[s]])
        def front(j):
            s = j % NS
            p2 = j % 2
            K.op("dve", lambda: nc.vector.tensor_scalar(out=yt[p2][:], in0=ya[s][0][:].rearrange("p r c -> p (r c)"),
                                                        scalar1=mh[:, 0:1], scalar2=None, op0=ALU.mult),
                 R=[r_ya[s][0], r_mh], W=[r_yt[p2]])
            K.op("dve", lambda: nc.vector.scalar_tensor_tensor(out=yt[p2][:], in0=ya[s][1][:].rearrange("p r c -> p (r c)"),
                                                               scalar=mh[:, 1:2], in1=yt[p2][:], op0=ALU.mult, op1=ALU.add),
                 R=[r_ya[s][1], r_mh, r_yt[p2]], W=[r_yt[p2]])
            for k in range(8):
                K.op("pe", lambda k=k: nc.tensor.transpose(out=ptr[p2][:, k, :],
                                                           in_=yt[p2][:, k * 128:(k + 1) * 128],
                                                           identity=C["ident_bf"][:]),
                     R=[r_yt[p2], C["r"]], W=[r_ptr[p2]])
            K.op("act", lambda: nc.scalar.copy(out=yT[p2][:], in_=ptr[p2][:]),
                 R=[r_ptr[p2]], W=[r_yT[p2]])

        ld(0)
        if nsub > 1:
            ld(1)
        front(0)
        for j in range(nsub):
            s = j % NS
            p2 = j % 2
            if j + 2 < nsub:
                ld(j + 2)
            if j + 1 < nsub:
                front(j + 1)
            for half in range(2):
                pb = half
                for k in range(8):
                    K.op("pe", lambda k=k: nc.tensor.matmul(
                        po[pb][:], lhsT=yT[p2][:, k, :], rhs=wob[:, k, half * 512:(half + 1) * 512],
                        start=(k == 0), stop=(k == 7)),
                        R=[r_yT[p2], r_wob], W=[r_po[pb]])
                K.op("dve", lambda: nc.vector.tensor_tensor(
                    out=ot[p2][:, half * 512:(half + 1) * 512], in0=po[pb][:],
                    in1=xt[s][:, half * 512:(half + 1) * 512], op=ALU.add),
                    R=[r_po[pb], r_xt[s]], W=[r_ot[p2]])
            K.dma("sp", x_out[j * 128:(j + 1) * 128, :], ot[p2][:], ds_ot[p2], R=[r_ot[p2]])
        K.barrier()
        K.end_phase()
        K.stack = outer


EPS = 1e-6
NEG = -30000.0


def load_w(K, w_d, ncols, name):
    nc = K.nc
    wb = K.sb([128, 8, ncols], BF16, name)
    r_wb = Reg(name)
    with ExitStack() as st:
        outer = K.stack
        K.stack = st
        K.begin_phase()
        stg = [K.sb([128, ncols], F32, f"{name}st{i}") for i in range(2)]
        r_st = [Reg() for i in range(2)]
        ds = [K.new_dma_sem() for i in range(2)]
        for k in range(8):
            K.dma("sp", stg[k % 2][:], w_d[:, k, :], ds[k % 2], W=[r_st[k % 2]])
            K.op("pool", lambda k=k: nc.gpsimd.tensor_copy(out=wb[:, k, :], in_=stg[k % 2][:]),
                 R=[r_st[k % 2]], W=[r_wb])
        K.barrier()
        K.end_phase()
        K.stack = outer
    return wb, r_wb


def load_w_all(K, specs, nslots=4):
    nc = K.nc
    out = {}
    for key, w_d, ncols in specs:
        out[key] = (K.sb([128, 8, ncols], BF16, "w_" + key), Reg("w_" + key))
    stg_stack = None
    stg = [K.sb([128, 768], F32, f"wstg{i}") for i in range(nslots)]
    r_st = [Reg() for _ in range(nslots)]
    ds = [K.new_dma_sem() for _ in range(nslots)]
    n = 0
    for key, w_d, ncols in specs:
        wb, r_wb = out[key]
        for k in range(8):
            s = n % nslots
            K.dma("sp" if n % 2 == 0 else "act", stg[s][:, 0:ncols], w_d[:, k, :], ds[s], W=[r_st[s]])
            if n % 2 == 0:
                K.op("dve", lambda k=k: nc.vector.tensor_copy(out=wb[:, k, :], in_=stg[s][:, 0:ncols]),
                     R=[r_st[s]], W=[r_wb])
            else:
                K.op("act", lambda k=k: nc.scalar.copy(out=wb[:, k, :], in_=stg[s][:, 0:ncols]),
                     R=[r_st[s]], W=[r_wb])
            n += 1
    return out, stg_stack


def rstd_from_ssq(K, ssq, r_ssq, n, scr, inv_n):
    nc = K.nc
    K.op("dve", lambda: nc.vector.tensor_scalar(out=scr["ms"], in0=ssq, scalar1=inv_n, scalar2=EPS,
                                                op0=ALU.mult, op1=ALU.add),
         R=[r_ssq], W=[scr["r_ms"]])
    K.op("act", lambda: nc.scalar.activation(out=scr["sd"], in_=scr["ms"], func=AF.Sqrt),
         R=[scr["r_ms"]], W=[scr["r_sd"]])
    K.op("dve", lambda: nc.vector.reciprocal(out=scr["rstd"], in_=scr["sd"]),
         R=[scr["r_sd"]], W=[scr["r_rstd"]])


def rstd_explog(K, ssq, r_ssq, scr, inv_n):
    nc = K.nc
    K.op("dve", lambda: nc.vector.tensor_scalar(out=scr["ms"], in0=ssq, scalar1=inv_n, scalar2=EPS,
                                                op0=ALU.mult, op1=ALU.add),
         R=[r_ssq], W=[scr["r_ms"]])
    K.op("act", lambda: nc.scalar.activation(out=scr["sd"], in_=scr["ms"], func=AF.Ln),
         R=[scr["r_ms"]], W=[scr["r_sd"]])
    K.op("act", lambda: nc.scalar.activation(out=scr["rstd"], in_=scr["sd"], func=AF.Exp, scale=-0.5),
         R=[scr["r_sd"]], W=[scr["r_rstd"]])


def mk_scr(K, shape, tag):
    d = {}
    for n in ["ms", "sd", "rstd"]:
        t = K.sb(shape, F32, tag + n)
        d[n] = t[:]
        d["r_" + n] = Reg(tag + n)
    return d


def emit_diff(K, C, hnT, r_hnT, P, y_d, S, lam_init, W=None):
    nc = K.nc
    nT = S // 128
    nQ = S // 512
    outer = K.stack
    with ExitStack() as st:
        K.stack = st
        K.begin_phase()
        wb, r_wb = W["wD"] if W is not None else load_w(K, P["wD"], 384, "wD")
        dsm = K.new_dma_sem()
        qkw = K.sb([128, 256], F32, "qkw"); r_c = Reg("dconst")
        sw = K.sb([128, 64], F32, "sw")
        lamb = K.sb([128, 4, 32], F32, "lamb")
        Bt = K.sb([128, 2, 1024], F32, "Bt")
        c31 = K.sb([128, 2], F32, "c31")
        K.dma("sp", qkw[:], P["qkw"], dsm, W=[r_c])
        K.dma("sp", sw[:], P["sw"], dsm, W=[r_c])
        K.dma("sp", lamb[:], P["lamb"], dsm, W=[r_c])
        K.dma("sp", Bt[:], P["Bt"], dsm, W=[r_c])
        K.dma("sp", c31[:], P["c31"], dsm, W=[r_c])
        qT = K.sb([128, S], BF16, "dqT"); r_qT = [Reg() for _ in range(nT)]
        kT = K.sb([128, S], BF16, "dkT"); r_kT = [Reg() for _ in range(nT)]
        qTb = K.sb([32, S], BF16, "dqTb")
        kTb = K.sb([32, S], BF16, "dkTb")
        vaug = K.sb([128, nT, 2, 65], BF16, "dvaug"); r_v = [Reg() for _ in range(nT)]
        zer = K.sb([128, 260], BF16, "zer"); r_z = Reg()
        K.op("dve", lambda: nc.vector.memset(zer[:], 0.0), W=[r_z])
        K.op("dve", lambda: nc.vector.memset(vaug[:].rearrange("p a b c -> p (a b c)"), 1.0), W=r_v)
        lt = K.sb([128, 2, 32], F32, "lt"); r_lt = Reg()
        ls = K.sb([128, 2], F32, "ls"); r_ls = Reg()
        le = K.sb([128, 2], F32, "le"); r_le = Reg()
        nlam = K.sb([128, 1], F32, "nlam"); r_nlam = Reg()
        swl = K.sb([128, 64], F32, "swl"); r_swl = Reg()
        K.op("dve", lambda: nc.vector.tensor_tensor(out=lt[:, 0, :], in0=lamb[:, 0, :], in1=lamb[:, 1, :], op=ALU.mult),
             R=[r_c], W=[r_lt])
        K.op("dve", lambda: nc.vector.tensor_tensor(out=lt[:, 1, :], in0=lamb[:, 2, :], in1=lamb[:, 3, :], op=ALU.mult),
             R=[r_c], W=[r_lt])
        K.op("dve", lambda: nc.vector.tensor_reduce(out=ls[:], in_=lt[:], axis=AX.X, op=ALU.add), R=[r_lt], W=[r_ls])
        K.op("act", lambda: nc.scalar.activation(out=le[:], in_=ls[:], func=AF.Exp), R=[r_ls], W=[r_le])
        K.op("dve", lambda: nc.vector.tensor_tensor(out=nlam[:], in0=le[:, 1:2], in1=le[:, 0:1], op=ALU.subtract),
             R=[r_le], W=[r_nlam])
        K.op("dve", lambda: nc.vector.tensor_scalar(out=nlam[:], in0=nlam[:], scalar1=-lam_init, scalar2=None, op0=ALU.add),
             R=[r_nlam], W=[r_nlam])
        K.op("dve", lambda: nc.vector.tensor_scalar(out=swl[:], in0=sw[:], scalar1=1.0 - lam_init, scalar2=None, op0=ALU.mult),
             R=[r_c], W=[r_swl])

        GD = 4
        st_d2 = ExitStack()
        K.stack = st_d2
        ppb = [psbank(K, f"dpp{i}") for i in range(GD)]; r_ppb = [Reg(excl=True) for _ in range(GD)]
        ptrb = [K.ps([128, 8, 128], BF16, f"dptr{i}") for i in range(GD // 2)]; r_ptrb = [Reg() for _ in range(GD // 2)]
        sq = K.sb([128, GD, 256], F32, "dsq"); r_sq = Reg()
        ssq = K.sb([128, GD * 8], F32, "dssq"); r_ssq = Reg()
        scr = mk_scr(K, [128, GD * 8], "dq")
        qn = K.sb([128, GD, 256], F32, "dqn"); r_qn = Reg()
        qnb = K.sb([128, GD, 256], BF16, "dqnb"); r_qnb = Reg()
        K.stack = st
        scale = 32 ** -0.5
        for i0 in range(0, nT, GD):
            tss = [slice((i0 + i) * 128, (i0 + i + 1) * 128) for i in range(GD)]
            for i in range(GD):
                for k in range(8):
                    K.op("pe", lambda k=k, i=i: nc.tensor.matmul(ppb[i][:, 0:384], lhsT=hnT[:, k, tss[i]], rhs=wb[:, k, :],
                                                                 start=(k == 0), stop=(k == 7)),
                         R=[r_hnT, r_wb], W=[r_ppb[i]])
            for i in range(GD):
                K.op("act", lambda i=i: nc.scalar.activation(out=sq[:, i, :], in_=ppb[i][:, 0:256], func=AF.Square),
                     R=[r_ppb[i]], W=[r_sq])
            K.op("dve", lambda: nc.vector.tensor_reduce(out=ssq[:], in_=sq[:].rearrange("p t (g d) -> p (t g) d", d=32),
                                                        axis=AX.X, op=ALU.add), R=[r_sq], W=[r_ssq])
            rstd_explog(K, ssq[:], r_ssq, scr, 1.0 / 32)
            rs3 = scr["rstd"].rearrange("p (t g) -> p t g", g=8)
            K.op("dve", lambda: nc.vector.tensor_scalar(out=rs3[:, :, 0:4], in0=rs3[:, :, 0:4], scalar1=scale,
                                                        scalar2=None, op0=ALU.mult), R=[scr["r_rstd"]], W=[scr["r_rstd"]])
            for i in range(GD):
                K.op("dve", lambda i=i: nc.vector.tensor_tensor(
                    out=qn[:, i, :].rearrange("p (g d) -> p g d", d=32), in0=ppb[i][:, 0:256].rearrange("p (g d) -> p g d", d=32),
                    in1=rs3[:, i, :].unsqueeze(2).to_broadcast([128, 8, 32]), op=ALU.mult),
                    R=[r_ppb[i], scr["r_rstd"]], W=[r_qn])
                K.op("dve", lambda i=i: nc.vector.tensor_copy(out=vaug[:, i0 + i, :, 0:64],
                                                              in_=ppb[i][:, 256:384].rearrange("p (h d) -> p h d", d=64)),
                     R=[r_ppb[i]], W=[r_v[i0 + i]])
            K.op("dve", lambda: nc.vector.tensor_tensor(out=qnb[:], in0=qn[:], in1=qkw[:].unsqueeze(1).to_broadcast([128, GD, 256]),
                                                        op=ALU.mult), R=[r_qn, r_c], W=[r_qnb])
            for i in range(GD):
                ptr = ptrb[i // 2][:, (i % 2) * 4:(i % 2) * 4 + 4, :]; r_ptr = r_ptrb[i // 2]
                for a in range(2):
                    K.op("pe", lambda a=a, i=i: nc.tensor.transpose(out=ptr[0:96, 2 * a, :], in_=qnb[:, i, a * 128:a * 128 + 96],
                                                                    identity=C["ident_bf"][:]),
                         R=[r_qnb, C["r"]], W=[r_ptr])
                    K.op("pe", lambda a=a, i=i: nc.tensor.transpose(out=ptr[0:32, 2 * a + 1, :], in_=qnb[:, i, a * 128 + 96:a * 128 + 128],
                                                                    identity=C["ident_bf"][:]),
                         R=[r_qnb, C["r"]], W=[r_ptr])
            for i in range(GD):
                ptr = ptrb[i // 2][:, (i % 2) * 4:(i % 2) * 4 + 4, :]; r_ptr = r_ptrb[i // 2]
                ti = i0 + i
                K.op("act", lambda: nc.scalar.copy(out=qT[0:96, tss[i]], in_=ptr[0:96, 0, :]), R=[r_ptr], W=[r_qT[ti]])
                K.op("act", lambda: nc.scalar.copy(out=qTb[0:32, tss[i]], in_=ptr[0:32, 1, :]), R=[r_ptr], W=[r_qT[ti]])
                K.op("act", lambda: nc.scalar.copy(out=kT[0:96, tss[i]], in_=ptr[0:96, 2, :]), R=[r_ptr], W=[r_kT[ti]])
                K.op("act", lambda: nc.scalar.copy(out=kTb[0:32, tss[i]], in_=ptr[0:32, 3, :]), R=[r_ptr], W=[r_kT[ti]])

        K.barrier()
        st_d2.close()
        NSB = 4
        sbk = [K.ps([128, 512], F32, f"dsb{i}") for i in range(NSB)]; r_sb = [Reg() for _ in range(NSB)]
        acc = [K.ps([128, 4, 65], F32, f"dacc{i}") for i in range(4)]; r_acc = [Reg() for _ in range(4)]
        NP = 8
        pT = [K.sb([128, 512], BF16, f"dpT{i}") for i in range(NP)]; r_pT = [Reg() for _ in range(NP)]
        tmp = [K.sb([128, 512], F32, f"dtmp{i}") for i in range(2)]; r_tmp = [Reg() for _ in range(2)]
        rd = K.sb([128, 2, 4], F32, "drd"); r_rd = Reg()
        o1 = K.sb([128, 4, 64], F32, "do1"); r_o1 = Reg()
        o2 = K.sb([128, 4, 64], F32, "do2"); r_o2 = Reg()
        osq = K.sb([128, 4, 64], F32, "dosq"); r_osq = Reg()
        oss = K.sb([128, 4], F32, "doss"); r_oss = Reg()
        oscr = mk_scr(K, [128, 4], "do")
        yb = [K.sb([128, 4, 128], BF16, f"dyb{i}") for i in range(2)]; r_yb = [Reg() for _ in range(2)]
        ds_y = [K.new_dma_sem() for _ in range(2)]
        cnt = 0
        ntmp = 0
        ai = 0
        for t in range(nQ):
            ybt = yb[t % 2]
            for hl in range(2):
                accs = []
                items = []
                for c in range(2):
                    g = hl * 2 + c
                    A = acc[ai % 4]; rA = r_acc[ai % 4]; ai += 1
                    accs.append((A, rA))
                    K.op("pe", lambda: nc.tensor.matmul(A[:].rearrange("p a b -> p (a b)"), lhsT=zer[:, 0:128],
                                                        rhs=zer[:, 0:260], start=True, stop=False),
                         R=[r_z], W=[rA])
                for j in range(4 * t + 4):
                    for c in range(2):
                        items.append((c, hl * 2 + c, accs[c][0], accs[c][1], j))

                def emit_S(it):
                    nonlocal cnt, ntmp
                    c, g, A, rA, j = it
                    pr = slice(32 * g, 32 * g + 32) if g < 3 else slice(0, 32)
                    qTg = qT if g < 3 else qTb
                    kTg = kT if g < 3 else kTb
                    m = 4 * t - j
                    c0 = max(0, -m) * 128
                    sb_ = sbk[cnt % NSB]; rs = r_sb[cnt % NSB]
                    p_ = pT[cnt % NP]; rp = r_pT[cnt % NP]
                    cnt += 1
                    K.op("pe", lambda: nc.tensor.matmul(sb_[:, c0:512], lhsT=kTg[pr, j * 128:(j + 1) * 128],
                                                        rhs=qTg[pr, t * 512 + c0:(t + 1) * 512], start=True, stop=True),
                         R=[r_kT[j]] + [r_qT[4 * t + x] for x in range(c0 // 128, 4)], W=[rs])
                    if m >= 2:
                        K.op("act", lambda: nc.scalar.activation(out=p_[:, c0:512], in_=sb_[:, c0:512], func=AF.Exp,
                                                                 bias=c31[:, hl:hl + 1]),
                             R=[rs, r_c], W=[rp])
                    else:
                        tm_ = tmp[ntmp % 2]; rt = r_tmp[ntmp % 2]; ntmp += 1
                        b0 = 128 * m + 384
                        K.op("dve", lambda: nc.vector.tensor_tensor(out=tm_[:, c0:512], in0=sb_[:, c0:512],
                                                                    in1=Bt[:, hl, b0 + c0:b0 + 512], op=ALU.add),
                             R=[rs, r_c], W=[rt])
                        K.op("act", lambda: nc.scalar.activation(out=p_[:, c0:512], in_=tm_[:, c0:512], func=AF.Exp),
                             R=[rt], W=[rp])
                    return (p_, rp, c0)

                def emit_AV(it, pinfo):
                    c, g, A, rA, j = it
                    p_, rp, c0 = pinfo
                    for sub in range(c0 // 128, 4):
                        K.op("pe", lambda sub=sub: nc.tensor.matmul(
                            A[:, sub, :], lhsT=p_[:, sub * 128:(sub + 1) * 128], rhs=vaug[:, j, hl, :],
                            start=False, stop=(j == 4 * t + 3 and sub == 3)),
                            R=[rp, r_v[j]], W=[rA])

                LOOK = 4
                pend = []
                for idx in range(0, len(items), 4):
                    for it in items[idx:idx + 4]:
                        pend.append((it, emit_S(it)))
                    while len(pend) > LOOK:
                        emit_AV(*pend.pop(0))
                while pend:
                    emit_AV(*pend.pop(0))
                (A1, rA1), (A2, rA2) = accs
                K.op("dve", lambda: nc.vector.reciprocal(out=rd[:, 0, :], in_=A1[:, :, 64]), R=[rA1], W=[r_rd])
                K.op("dve", lambda: nc.vector.reciprocal(out=rd[:, 1, :], in_=A2[:, :, 64]), R=[rA2], W=[r_rd])
                K.op("dve", lambda: nc.vector.tensor_scalar(out=rd[:, 1, :], in0=rd[:, 1, :], scalar1=nlam[:, 0:1],
                                                            scalar2=None, op0=ALU.mult), R=[r_rd, r_nlam], W=[r_rd])
                K.op("dve", lambda: nc.vector.tensor_tensor(out=o1[:], in0=A1[:, :, 0:64],
                                                            in1=rd[:, 0, :].unsqueeze(2).to_broadcast([128, 4, 64]),
                                                            op=ALU.mult), R=[rA1, r_rd], W=[r_o1])
                K.op("dve", lambda: nc.vector.tensor_tensor(out=o2[:], in0=A2[:, :, 0:64],
                                                            in1=rd[:, 1, :].unsqueeze(2).to_broadcast([128, 4, 64]),
                                                            op=ALU.mult), R=[rA2, r_rd], W=[r_o2])
                K.op("pool", lambda: nc.gpsimd.tensor_tensor(out=o1[:], in0=o1[:], in1=o2[:], op=ALU.add),
                     R=[r_o1, r_o2], W=[r_o1])
                K.op("dve", lambda: nc.vector.tensor_tensor(out=osq[:], in0=o1[:], in1=o1[:], op=ALU.mult), R=[r_o1], W=[r_osq])
                K.op("dve", lambda: nc.vector.tensor_reduce(out=oss[:], in_=osq[:], axis=AX.X, op=ALU.add),
                     R=[r_osq], W=[r_oss])
                rstd_explog(K, oss[:], r_oss, oscr, 1.0 / 64)
                K.op("dve", lambda: nc.vector.tensor_tensor(out=o2[:], in0=o1[:],
                                                            in1=oscr["rstd"].unsqueeze(2).to_broadcast([128, 4, 64]),
                                                            op=ALU.mult), R=[r_o1, oscr["r_rstd"]], W=[r_o2])
                K.op("dve", lambda: nc.vector.tensor_tensor(out=ybt[:, :, hl * 64:(hl + 1) * 64], in0=o2[:],
                                                            in1=swl[:].unsqueeze(1).to_broadcast([128, 4, 64]),
                                                            op=ALU.mult), R=[r_o2, r_swl], W=[r_yb[t % 2]])
            K.dma("sp", y_d[t * 512:(t + 1) * 512, 384:512].rearrange("(s p) c -> p s c", p=128), ybt[:],
                  ds_y[t % 2], R=[r_yb[t % 2]], W=([y_d.reg(t * 512)] if hasattr(y_d, "reg") else []))
            if getattr(y_d, "hook", None) is not None:
                y_d.hook(t)
        K.barrier()
        K.end_phase()
        K.stack = outer


def load_mixer_consts(K, C, D):
    ds = C["dsem"]
    C["cf"] = K.sb([128, 1280], F32, "cf")
    C["sel"] = K.sb([2, 2, 128], F32, "sel")
    C["hsel"] = K.sb([2, 128], F32, "hsel")
    C["rowc"] = K.sb([2, 2, 512], F32, "rowc")
    K.dma("sp", C["cf"][:], D["cf"], ds, W=[C["r"]])
    K.dma("sp", C["sel"][:], D["sel"], ds, W=[C["r"]])
    K.dma("sp", C["hsel"][:], D["hsel"], ds, W=[C["r"]])
    K.dma("sp", C["rowc"][:], D["rowc"], ds, W=[C["r"]])
    C["ones"] = C["cf"][:, 0:128]
    C["tri"] = C["cf"][:, 128:256]
    C["nm2"] = C["cf"][:, 256:768]
    C["nm1"] = C["cf"][:, 768:1280]


def load_hnT(K, hn_d, S):
    hnT = K.sb([128, 8, S], BF16, "hnT")
    r = Reg("hnT")
    ds = K.new_dma_sem()
    for k in range(8):
        K.dma("sp" if k % 2 == 0 else "act", hnT[:, k, :], hn_d[k], ds, W=[r])
    return hnT, r


def emit_ssd_gen(K, C, hnT, r_hnT, P, y_d, S, W, nsets=2):
    STOP = 99
    nc = K.nc
    nT = S // 128
    nTT = S // 512
    if True:
        wfm, r_wfm = W["wS_fm"]
        wtm, r_wtm = W["wS_tm"]
        dsm = K.new_dma_sem()
        r_c = Reg("sconst")
        cw = K.sb([128, 4, 4], F32, "cw"); cb = K.sb([128, 4], F32, "cb")
        dtb = K.sb([128, 4], F32, "dtb"); alog = K.sb([128, 4], F32, "alog")
        dsk = K.sb([128, 4], F32, "dsk"); snw = K.sb([128, 256], F32, "snw")
        for t_, n_ in [(cw, "cw"), (cb, "cb"), (dtb, "dtb"), (alog, "alog"), (dsk, "dsk"), (snw, "snw")]:
            K.dma("sp", t_[:], P[n_], dsm, W=[r_c])
        Aneg = K.sb([128, 4], F32, "Aneg"); r_A = Reg()
        K.op("act", lambda: nc.scalar.activation(out=Aneg[:], in_=alog[:], func=AF.Exp), R=[r_c], W=[r_A])
        K.op("dve", lambda: nc.vector.tensor_scalar(out=Aneg[:], in0=Aneg[:], scalar1=-1.0, scalar2=None, op0=ALU.mult),
             R=[r_A], W=[r_A])
        xc = [K.sb([128, S], BF16, f"xc{i}") for i in range(4)]
        r_xc = [Reg() for _ in range(4)]
        def T(shape, dt, name):
            return K.sb(shape, dt, name), Reg(name)
        names = [("sz", [128, 256], F32), ("dtx", [128, 4], F32), ("ax", [128, 4], F32), ("ex", [128, 4], F32),
                 ("lx", [128, 4], F32), ("dt", [128, 4], F32), ("aa", [128, 4], F32), ("acs", [128, 4], F32),
                 ("nacs", [128, 4], F32), ("el", [128, 4], F32), ("cd", [128, 4], F32), ("dd", [128, 4], F32),
                 ("dec", [128, 4], F32), ("dtdec", [128, 4], F32), ("rseg", [128, 4, 128], F32),
                 ("segT", [128, 4, 128], F32), ("xdt", [128, 4, 64], BF16), ("xdd", [128, 4, 64], BF16),
                 ("xD", [128, 4, 64], F32), ("Btm", [128, 128], BF16), ("Gm", [128, 128], F32),
                 ("scT", [128, 4, 128], BF16), ("t1", [128, 4, 64], F32), ("gg", [128, 256], F32),
                 ("junk", [128, 256], F32), ("ssq", [128, 1], F32)]
        sets = []
        for par in range(nsets):
            d_ = {}
            for (n_, shp, dt_) in names:
                d_[n_] = T(shp, dt_, f"s{par}{n_}")
            d_["nscr"] = mk_scr(K, [128, 1], f"sn{par}")
            bA = psbank(K, f"sA{par}"); bB = psbank(K, f"sB{par}"); bC = psbank(K, f"sC{par}"); bD = psbank(K, f"sD{par}")
            d_["bA"] = bA; d_["bB"] = bB
            d_["rA"] = Reg(excl=True); d_["rB"] = Reg(excl=True); d_["rC"] = Reg(excl=True); d_["rD"] = Reg(excl=True)
            d_["pz"] = bA[:, 0:256]; d_["pst"] = bA[:, 256:512]
            d_["pseg"] = bB.rearrange("p (a b) -> p a b", b=128)
            d_["ptr"] = bC[:, 0:192].bitcast(BF16).rearrange("p (a b) -> p a b", b=128)
            d_["pG"] = bC[:, 192:320]; d_["pa"] = bC[:, 320:328]; d_["pdt"] = bC[:, 328:332]
            d_["py"] = bD[:, 0:256]; d_["pyo"] = bD[:, 256:512]
            sets.append(d_)
        Sf, r_Sf = T([128, 4, 64], F32, "sSf")
        Sbf, r_Sbf = T([128, 256], BF16, "sSbf")
        yb = [K.sb([128, 256], BF16, f"syb{i}") for i in range(2)]; r_yb = [Reg() for _ in range(2)]
        ds_y = [K.new_dma_sem() for _ in range(2)]
        K.op("dve", lambda: nc.vector.memset(Sf[:].rearrange("p a b -> p (a b)"), 0.0), W=[r_Sf])
        K.op("dve", lambda: nc.vector.memset(Sbf[:], 0.0), W=[r_Sbf])
        ident_f = C["ident_f"]
        if True:
            SH = S // 2
            xpre = K.sb([128, SH + 3], F32, "xpre"); r_xpre = Reg()
            cacc = K.sb([128, SH], F32, "cacc"); r_cacc = Reg()
            pf = [sets[0]["bA"], sets[0]["bB"]]; r_pf = [sets[0]["rA"], sets[0]["rB"]]
            n = 0
            for ct in range(4):
                for hf in range(2):
                    if hf == 0:
                        K.op("dve", lambda: nc.vector.memset(xpre[:, 0:3], 0.0), W=[r_xpre])
                    else:
                        K.op("dve", lambda: nc.vector.tensor_copy(out=xpre[:, 0:3], in_=xpre[:, SH:SH + 3]),
                             R=[r_xpre], W=[r_xpre])
                    for tt in range(nTT // 2):
                        tg_ = hf * (nTT // 2) + tt
                        p_ = pf[n % 2]; rp = r_pf[n % 2]; n += 1
                        for k in range(8):
                            K.op("pe", lambda k=k: nc.tensor.matmul(p_[:], lhsT=wfm[:, k, ct * 128:(ct + 1) * 128],
                                                                    rhs=hnT[:, k, tg_ * 512:(tg_ + 1) * 512],
                                                                    start=(k == 0), stop=(k == 7)),
                                 R=[r_wfm, r_hnT], W=[rp])
                        K.op("act", lambda: nc.scalar.copy(out=xpre[:, 3 + tt * 512:3 + (tt + 1) * 512], in_=p_[:]),
                             R=[rp], W=[r_xpre])
                        yield
                    K.op("dve", lambda: nc.vector.tensor_scalar(out=cacc[:], in0=xpre[:, 0:SH], scalar1=cw[:, ct, 0:1],
                                                                scalar2=None, op0=ALU.mult), R=[r_xpre, r_c], W=[r_cacc])
                    for j in range(1, 4):
                        K.op("dve", lambda j=j: nc.vector.scalar_tensor_tensor(
                            out=cacc[:], in0=xpre[:, j:SH + j], scalar=cw[:, ct, j:j + 1], in1=cacc[:],
                            op0=ALU.mult, op1=ALU.add), R=[r_xpre, r_c, r_cacc], W=[r_cacc])
                        yield
                    K.op("act", lambda: nc.scalar.activation(out=xc[ct][:, hf * SH:(hf + 1) * SH], in_=cacc[:], func=AF.Silu,
                                                             bias=cb[:, ct:ct + 1]),
                         R=[r_cacc, r_c], W=[r_xc[ct]])
                    yield
        state_done = [-1]

        def chunk_flow(c):
                cs = slice(c * 128, (c + 1) * 128)
                S_ = sets[c % nsets]
                (sz, r_sz), (dtx, r_dtx), (ax, r_ax), (ex, r_ex), (lx, r_lx), (dt, r_dt), (aa, r_aa), (acs, r_acs), \
                    (nacs, r_nacs), (el, r_el), (cd, r_cd), (dd, r_dd), (dec, r_dec), (dtdec, r_dtdec), (rseg, r_rseg), \
                    (segT, r_segT), (xdt, r_xdt), (xdd, r_xdd), (xD, r_xD), (Btm, r_Btm), (Gm, r_Gm), (scT, r_scT), \
                    (t1, r_t1), (gg, r_gg), (junk, r_junk), (ssq, r_ssq) = [S_[n_[0]] for n_ in names]
                nscr = S_["nscr"]
                pz, pst, pseg, ptr, pG, pa, pdt, py, pyo = [S_[n_] for n_ in ["pz", "pst", "pseg", "ptr", "pG", "pa", "pdt", "py", "pyo"]]
                r_pz = r_pst = S_["rA"]; r_pseg = S_["rB"]; r_ptr = r_pG = r_pa = r_pdt = S_["rC"]; r_py = r_pyo = S_["rD"]
                for k in range(8):
                    K.op("pe", lambda k=k: nc.tensor.matmul(pz[:], lhsT=hnT[:, k, cs], rhs=wtm[:, k, 0:256],
                                                            start=(k == 0), stop=(k == 7)), R=[r_hnT, r_wtm], W=[r_pz])
                for k in range(8):
                    K.op("pe", lambda k=k: nc.tensor.matmul(pdt[:], lhsT=hnT[:, k, cs], rhs=wtm[:, k, 256:260],
                                                            start=(k == 0), stop=(k == 7)), R=[r_hnT, r_wtm], W=[r_pdt])
                yield
                K.op("act", lambda: nc.scalar.activation(out=sz[:], in_=pz[:, 0:256], func=AF.Silu), R=[r_pz], W=[r_sz])
                yield
                K.op("dve", lambda: nc.vector.tensor_tensor(out=dtx[:], in0=pdt[:], in1=dtb[:], op=ALU.add),
                     R=[r_pdt, r_c], W=[r_dtx])
                yield
                K.op("dve", lambda: nc.vector.scalar_tensor_tensor(out=ax[:], in0=dtx[:], scalar=-1.0, in1=dtx[:],
                                                                   op0=ALU.mult, op1=ALU.min), R=[r_dtx], W=[r_ax])
                yield
                K.op("act", lambda: nc.scalar.activation(out=ex[:], in_=ax[:], func=AF.Exp), R=[r_ax], W=[r_ex])
                yield
                K.op("dve", lambda: nc.vector.tensor_scalar(out=ex[:], in0=ex[:], scalar1=1.0, scalar2=None, op0=ALU.add),
                     R=[r_ex], W=[r_ex])
                yield
                K.op("act", lambda: nc.scalar.activation(out=lx[:], in_=ex[:], func=AF.Ln), R=[r_ex], W=[r_lx])
                yield
                K.op("dve", lambda: nc.vector.scalar_tensor_tensor(out=dt[:], in0=dtx[:], scalar=0.0, in1=lx[:],
                                                                   op0=ALU.max, op1=ALU.add), R=[r_dtx, r_lx], W=[r_dt])
                yield
                K.op("dve", lambda: nc.vector.tensor_tensor(out=aa[:], in0=dt[:], in1=Aneg[:], op=ALU.mult),
                     R=[r_dt, r_A], W=[r_aa])
                yield
                K.op("pe", lambda: nc.tensor.matmul(pa[:, 0:4], lhsT=C["tri"], rhs=aa[:], start=True, stop=True),
                     R=[r_aa, C["r"]], W=[r_pa])
                yield
                K.op("pe", lambda: nc.tensor.matmul(pa[:, 4:8], lhsT=C["ones"], rhs=aa[:], start=True, stop=True),
                     R=[r_aa, C["r"]], W=[r_pa])
                yield
                K.op("dve", lambda: nc.vector.tensor_copy(out=acs[:], in_=pa[:, 0:4]), R=[r_pa], W=[r_acs])
                yield
                K.op("dve", lambda: nc.vector.tensor_scalar(out=nacs[:], in0=pa[:, 0:4], scalar1=-1.0, scalar2=None,
                                                            op0=ALU.mult), R=[r_pa], W=[r_nacs])
                yield
                K.op("act", lambda: nc.scalar.activation(out=el[:], in_=pa[:, 0:4], func=AF.Exp), R=[r_pa], W=[r_el])
                yield
                K.op("act", lambda: nc.scalar.activation(out=cd[:], in_=pa[:, 4:8], func=AF.Exp), R=[r_pa], W=[r_cd])
                yield
                K.op("dve", lambda: nc.vector.tensor_tensor(out=dd[:], in0=pa[:, 4:8], in1=acs[:], op=ALU.subtract),
                     R=[r_pa, r_acs], W=[r_dd])
                yield
                K.op("act", lambda: nc.scalar.activation(out=dec[:], in_=dd[:], func=AF.Exp), R=[r_dd], W=[r_dec])
                yield
                K.op("dve", lambda: nc.vector.tensor_tensor(out=dtdec[:], in0=dt[:], in1=dec[:], op=ALU.mult),
                     R=[r_dt, r_dec], W=[r_dtdec])
                yield
                K.op("dve", lambda: nc.vector.tensor_tensor(out=rseg[:], in0=ident_f[:].unsqueeze(1).to_broadcast([128, 4, 128]),
                                                            in1=acs[:].unsqueeze(2).to_broadcast([128, 4, 128]), op=ALU.mult),
                     R=[C["r"], r_acs], W=[r_rseg])
                yield
                K.op("pe", lambda: nc.tensor.matmul(pseg[:].rearrange("p a b -> p (a b)"), lhsT=C["ones"],
                                                    rhs=rseg[:].rearrange("p a b -> p (a b)"), start=True, stop=False),
                     R=[r_rseg, C["r"]], W=[r_pseg])
                yield
                K.op("pe", lambda: nc.tensor.matmul(pseg[:].rearrange("p a b -> p (a b)"), lhsT=ident_f[:],
                                                    rhs=C["nm1"], start=False, stop=True),
                     R=[C["r"]], W=[r_pseg])
                for h in range(4):
                    K.op("act", lambda h=h: nc.scalar.activation(out=segT[:, h, :], in_=pseg[:, h, :], func=AF.Exp,
                                                                 bias=nacs[:, h:h + 1]), R=[r_pseg, r_nacs], W=[r_segT])
                yield
                for a in range(3):
                    K.op("pe", lambda a=a: nc.tensor.transpose(out=ptr[:, a, :], in_=xc[a][:, cs], identity=C["ident_bf"][:]),
                         R=[r_xc[a], C["r"]], W=[r_ptr])
                xs_v = ptr[:, 0:2, :].rearrange("p a (h d) -> p (a h) d", d=64)
                yield
                K.op("dve", lambda: nc.vector.tensor_tensor(out=xdt[:], in0=xs_v, in1=dt[:].unsqueeze(2).to_broadcast([128, 4, 64]),
                                                            op=ALU.mult), R=[r_ptr, r_dt], W=[r_xdt])
                yield
                K.op("dve", lambda: nc.vector.tensor_tensor(out=xdd[:], in0=xs_v, in1=dtdec[:].unsqueeze(2).to_broadcast([128, 4, 64]),
                                                            op=ALU.mult), R=[r_ptr, r_dtdec], W=[r_xdd])
                yield
                K.op("dve", lambda: nc.vector.tensor_tensor(out=xD[:], in0=xs_v, in1=dsk[:].unsqueeze(2).to_broadcast([128, 4, 64]),
                                                            op=ALU.mult), R=[r_ptr, r_c], W=[r_xD])
                yield
                K.op("act", lambda: nc.scalar.copy(out=Btm[:], in_=ptr[:, 2, :]), R=[r_ptr], W=[r_Btm])
                yield
                K.op("pe", lambda: nc.tensor.matmul(pG[:], lhsT=xc[2][:, cs], rhs=xc[3][:, cs], start=True, stop=True),
                     R=[r_xc[2], r_xc[3]], W=[r_pG])
                yield
                K.op("dve", lambda: nc.vector.tensor_tensor(out=Gm[:], in0=pG[:], in1=C["tri"], op=ALU.mult),
                     R=[r_pG, C["r"]], W=[r_Gm])
                yield
                K.op("dve", lambda: nc.vector.tensor_tensor(out=scT[:], in0=Gm[:].unsqueeze(1).to_broadcast([128, 4, 128]),
                                                            in1=segT[:], op=ALU.mult), R=[r_Gm, r_segT], W=[r_scT])
                for h in range(4):
                    K.op("pe", lambda h=h: nc.tensor.matmul(py[:, h * 64:(h + 1) * 64], lhsT=scT[:, h, :], rhs=xdt[:, h, :],
                                                            start=True, stop=True), R=[r_scT, r_xdt], W=[r_py])
                yield
                while state_done[0] < c - 1:
                    yield
                K.op("pe", lambda: nc.tensor.matmul(pyo[:], lhsT=xc[3][:, cs], rhs=Sbf[:], start=True, stop=True),
                     R=[r_xc[3], r_Sbf], W=[r_pyo])
                yield
                K.op("pe", lambda: nc.tensor.matmul(pst[:], lhsT=Btm[:], rhs=xdd[:].rearrange("p a b -> p (a b)"),
                                                    start=True, stop=True), R=[r_Btm, r_xdd], W=[r_pst])
                yield
                K.op("pool", lambda: nc.gpsimd.tensor_tensor(out=Sf[:], in0=Sf[:], in1=cd[:].unsqueeze(2).to_broadcast([128, 4, 64]),
                                                             op=ALU.mult), R=[r_Sf, r_cd], W=[r_Sf])
                yield
                K.op("dve", lambda: nc.vector.tensor_tensor(out=Sf[:].rearrange("p a b -> p (a b)"),
                                                            in0=Sf[:].rearrange("p a b -> p (a b)"), in1=pst[:], op=ALU.add),
                     R=[r_Sf, r_pst], W=[r_Sf])
                yield
                K.op("act", lambda: nc.scalar.copy(out=Sbf[:], in_=Sf[:].rearrange("p a b -> p (a b)")), R=[r_Sf], W=[r_Sbf])
                state_done[0] = c
                yield
                K.op("dve", lambda: nc.vector.tensor_tensor(out=t1[:], in0=pyo[:].rearrange("p (a b) -> p a b", b=64),
                                                            in1=el[:].unsqueeze(2).to_broadcast([128, 4, 64]), op=ALU.mult),
                     R=[r_pyo, r_el], W=[r_t1])
                yield
                K.op("dve", lambda: nc.vector.tensor_tensor(out=t1[:].rearrange("p a b -> p (a b)"),
                                                            in0=t1[:].rearrange("p a b -> p (a b)"), in1=py[:], op=ALU.add),
                     R=[r_t1, r_py], W=[r_t1])
                yield
                K.op("pool", lambda: nc.gpsimd.tensor_tensor(out=t1[:], in0=t1[:], in1=xD[:], op=ALU.add),
                     R=[r_t1, r_xD], W=[r_t1])
                yield
                K.op("pool", lambda: nc.gpsimd.tensor_tensor(out=gg[:], in0=t1[:].rearrange("p a b -> p (a b)"), in1=sz[:],
                                                             op=ALU.mult), R=[r_t1, r_sz], W=[r_gg])
                yield
                K.op("act", lambda: nc.scalar.activation(out=junk[:], in_=gg[:], func=AF.Square, accum_out=ssq[:]),
                     R=[r_gg], W=[r_junk, r_ssq])
                rstd_from_ssq(K, ssq[:], r_ssq, 1, nscr, 1.0 / 256)
                yb_ = yb[c % 2]
                yield
                K.op("dve", lambda: nc.vector.scalar_tensor_tensor(out=yb_[:], in0=gg[:], scalar=nscr["rstd"], in1=snw[:],
                                                                   op0=ALU.mult, op1=ALU.mult),
                     R=[r_gg, nscr["r_rstd"], r_c], W=[r_yb[c % 2]])
                K.dma("sp", y_d[cs, 128:384], yb_[:], ds_y[c % 2], R=[r_yb[c % 2]])
        nrun = nT if STOP > 1 else 0
        active = []
        nxt_c = 0
        while nxt_c < nrun or active:
            while len(active) < nsets and nxt_c < nrun:
                active.append(chunk_flow(nxt_c))
                nxt_c += 1
            for g_ in list(active):
                try:
                    next(g_)
                except StopIteration:
                    active.remove(g_)
            yield


def run_gen(g):
    for _ in g:
        pass


def emit_ssd(K, C, hnT, r_hnT, P, y_d, S):
    outer = K.stack
    with ExitStack() as st:
        K.stack = st
        K.begin_phase()
        W = {"wS_fm": load_w(K, P["wS_fm"], 512, "wSf"), "wS_tm": load_w(K, P["wS_tm"], 260, "wSt")}
        run_gen(emit_ssd_gen(K, C, hnT, r_hnT, P, y_d, S, W, nsets=2))
        K.barrier()
        K.end_phase()
        K.stack = outer


def emit_mlstm_gen(K, C, hnT, r_hnT, P, y_d, S, W):
    nc = K.nc
    nB = S // 512
    if True:
        wfm, r_wfm = W["wM_fm"]
        wg, r_wg = W["wM_g"]
        wtm, r_wtm = W["wM_tm"]
        dsm = K.new_dma_sem()
        r_c = Reg("mconst")
        gbias = K.sb([2, 2], F32, "gbias"); mnw = K.sb([128, 128], F32, "mnw")
        K.dma("sp", gbias[:], P["gbias"], dsm, W=[r_c])
        K.dma("sp", mnw[:], P["mnw"], dsm, W=[r_c])
        ident_f = C["ident_f"]
        B = [psbank(K, f"mb{i}") for i in range(4)]
        rB = [Reg(f"mb{i}", excl=True) for i in range(4)]
        pq, pk, pgi, pgf = B[2], B[3], B[0][0:2, :], B[1][0:2, :]
        ptl = B[2][:, 0:32].rearrange("p (q i h) -> p q i h", q=4, i=4)
        pdec = B[2][:, 32:40]
        pDt = B[2][:, 0:256].rearrange("p (h t) -> p h t", t=128)
        ptm = B[3][:, 0:384]

        def T(shape, dt, name):
            return K.sb(shape, dt, name), Reg(name)
        qTb, r_qTb = T([128, 512], BF16, "mqTb")
        kTb, r_kTb = T([128, 512], BF16, "mkTb")
        rows = {}
        for n_ in ["ipre", "yv", "e", "b", "al", "cma", "mu", "nmu", "wrow", "inter", "en", "tmp"]:
            rows[n_] = T([2, 512], F32, "mr_" + n_)
        rows["nab"] = rows["e"]; rows["l"] = rows["e"]
        rows["logf"] = rows["yv"]
        mnew, r_mnew = T([2, 8], F32, "mnew")
        mprev, r_mprev = T([2, 8], F32, "mprev")
        mcar, r_mcar = T([2, 1], F32, "mcar")
        decay, r_decay = T([2, 8], F32, "mdecay")
        tl, r_tl = T([128, 4, 4, 2], F32, "mtl")
        decr, r_decr = T([128, 8], F32, "mdecr")
        ktm, r_ktm = T([128, 128], F32, "mktm")
        vaug, r_vaug = T([128, 2, 65], BF16, "mvaug")
        og, r_og = T([128, 128], F32, "mog")
        dT, r_dT = T([128, 128], F32, "mdT")
        sdT, r_sdT = T([128, 128], BF16, "msdT")
        kw, r_kw = T([128, 64], BF16, "mkw")
        Cst, r_Cst = T([128, 65], F32, "mCst")
        Cbf, r_Cbf = T([128, 65], BF16, "mCbf")
        nmv, r_nmv = T([128, 65], F32, "mnmv")
        dn, r_dn = T([128, 1], F32, "mdn")
        rn, r_rn = T([128, 1], F32, "mrn")
        hm, r_hm = T([128, 64], F32, "mhm")
        junk, r_junk = T([128, 64], F32, "mjunk")
        ssq, r_ssq = T([128, 1], F32, "mssq")
        nscr = mk_scr(K, [128, 1], "mn")
        hn2, r_hn2 = T([128, 64], F32, "mhn2")
        yb = [K.sb([128, 128], BF16, f"myb{i}") for i in range(2)]; r_yb = [Reg() for _ in range(2)]
        ds_y = [K.new_dma_sem() for _ in range(2)]
        r_Cst = [Reg("Cst0"), Reg("Cst1")]
        r_Cbf = [Reg("Cbf0"), Reg("Cbf1")]
        K.op("dve", lambda: nc.vector.memset(Cst[:], 0.0), W=r_Cst)
        K.op("dve", lambda: nc.vector.memset(Cbf[:], 0.0), W=r_Cbf)
        K.op("dve", lambda: nc.vector.memset(mcar[:], 0.0), W=[r_mcar])
        ktm2 = [T([128, 128], F32, f"mktm{i}") for i in range(2)]
        vaug2 = [T([128, 2, 65], BF16, f"mvaug{i}") for i in range(2)]
        og2 = [T([128, 128], F32, f"mog{i}") for i in range(2)]
        for i_ in range(2):
            K.op("dve", lambda i_=i_: nc.vector.memset(vaug2[i_][0][:].rearrange("p a b -> p (a b)"), 1.0), W=[vaug2[i_][1]])
        r_yb2 = [[Reg(), Reg()] for _ in range(2)]
        HT = []
        for h_ in range(2):
            d_ = {}
            d_["dT"] = T([128, 128], F32, f"mdT{h_}")
            d_["sdT"] = T([128, 128], BF16, f"msdT{h_}")
            d_["kw"] = T([128, 64], BF16, f"mkw{h_}")
            d_["nmv"] = T([128, 65], F32, f"mnmv{h_}")
            d_["dn"] = T([128, 1], F32, f"mdn{h_}")
            d_["rn"] = T([128, 1], F32, f"mrn{h_}")
            d_["hm"] = T([128, 64], F32, f"mhm{h_}")
            d_["junk"] = T([128, 64], F32, f"mjunk{h_}")
            d_["ssq"] = T([128, 1], F32, f"mssq{h_}")
            d_["hn2"] = T([128, 64], F32, f"mhn2{h_}")
            d_["nscr"] = mk_scr(K, [128, 1], f"mn{h_}")
            HT.append(d_)

        def R_(n_):
            return rows[n_][0]

        def rr(n_):
            return rows[n_][1]
        rowc = C["rowc"]
        ntile = 0
        for b in range(nB):
            bs = slice(b * 512, (b + 1) * 512)
            for (pp_, rp_, c0, dst, rdst, sc) in [(pq, rB[2], 0, qTb, r_qTb, 1.0), (pk, rB[3], 128, kTb, r_kTb, 0.125)]:
                for k in range(8):
                    K.op("pe", lambda k=k: nc.tensor.matmul(pp_, lhsT=wfm[:, k, c0:c0 + 128], rhs=hnT[:, k, bs],
                                                            start=(k == 0), stop=(k == 7)), R=[r_wfm, r_hnT], W=[rp_])
                K.op("act", lambda: nc.scalar.mul(out=dst[:], in_=pp_, mul=sc), R=[rp_], W=[rdst])
            for (pp_, rp_, c0) in [(pgi, rB[0], 0), (pgf, rB[1], 2)]:
                for k in range(8):
                    K.op("pe", lambda k=k: nc.tensor.matmul(pp_, lhsT=wg[:, k, c0:c0 + 2], rhs=hnT[:, k, bs],
                                                            start=(k == 0), stop=(k == 7)), R=[r_wg, r_hnT], W=[rp_])
            yield
            K.op("dve", lambda: nc.vector.tensor_scalar(out=R_("ipre")[:], in0=pgi, scalar1=gbias[:, 0:1], scalar2=None,
                                                        op0=ALU.add), R=[rB[0], r_c], W=[rr("ipre")])
            yield
            K.op("dve", lambda: nc.vector.tensor_scalar(out=R_("yv")[:], in0=pgf, scalar1=gbias[:, 1:2], scalar2=-1.0,
                                                        op0=ALU.add, op1=ALU.mult), R=[rB[1], r_c], W=[rr("yv")])
            yield
            K.op("dve", lambda: nc.vector.scalar_tensor_tensor(out=R_("nab")[:], in0=R_("yv")[:], scalar=-1.0, in1=R_("yv")[:],
                                                               op0=ALU.mult, op1=ALU.min), R=[rr("yv")], W=[rr("nab")])
            yield
            K.op("act", lambda: nc.scalar.activation(out=R_("e")[:], in_=R_("nab")[:], func=AF.Exp), R=[rr("nab")], W=[rr("e")])
            yield
            K.op("dve", lambda: nc.vector.tensor_scalar(out=R_("e")[:], in0=R_("e")[:], scalar1=1.0, scalar2=None, op0=ALU.add),
                 R=[rr("e")], W=[rr("e")])
            yield
            K.op("act", lambda: nc.scalar.activation(out=R_("l")[:], in_=R_("e")[:], func=AF.Ln), R=[rr("e")], W=[rr("l")])
            yield
            K.op("dve", lambda: nc.vector.scalar_tensor_tensor(out=R_("logf")[:], in0=R_("yv")[:], scalar=0.0, in1=R_("l")[:],
                                                               op0=ALU.max, op1=ALU.add), R=[rr("yv"), rr("l")], W=[rr("logf")])
            yield
            K.op("dve", lambda: nc.vector.tensor_scalar(out=R_("logf")[:], in0=R_("logf")[:], scalar1=-1.0, scalar2=None,
                                                        op0=ALU.mult), R=[rr("logf")], W=[rr("logf")])
            yield
            K.op("dve", lambda: nc.vector.tensor_tensor_scan(out=R_("b")[:], data0=rowc[:, 0, :], data1=R_("logf")[:],
                                                             initial=0.0, op0=ALU.mult, op1=ALU.add),
                 R=[rr("logf"), C["r"]], W=[rr("b")])
            yield
            K.op("dve", lambda: nc.vector.tensor_tensor(out=R_("al")[:], in0=R_("ipre")[:], in1=R_("b")[:], op=ALU.subtract),
                 R=[rr("ipre"), rr("b")], W=[rr("al")])
            yield
            K.op("dve", lambda: nc.vector.tensor_tensor_scan(out=R_("cma")[:], data0=rowc[:, 1, :], data1=R_("al")[:],
                                                             initial=0.0, op0=ALU.add, op1=ALU.max),
                 R=[rr("al"), C["r"]], W=[rr("cma")])
            cma3 = R_("cma")[:].rearrange("p (c l) -> p c l", l=64)
            b3 = R_("b")[:].rearrange("p (c l) -> p c l", l=64)
            al3 = R_("al")[:].rearrange("p (c l) -> p c l", l=64)
            mu3 = R_("mu")[:].rearrange("p (c l) -> p c l", l=64)
            tmp3 = R_("tmp")[:].rearrange("p (c l) -> p c l", l=64)
            yield
            K.op("dve", lambda: nc.vector.tensor_tensor_scan(out=mnew[:], data0=cma3[:, :, 63], data1=b3[:, :, 63],
                                                             initial=mcar[:, 0:1], op0=ALU.max, op1=ALU.add),
                 R=[rr("cma"), rr("b"), r_mcar], W=[r_mnew])
            yield
            K.op("dve", lambda: nc.vector.tensor_copy(out=mprev[:, 0:1], in_=mcar[:]), R=[r_mcar], W=[r_mprev])
            yield
            K.op("dve", lambda: nc.vector.tensor_copy(out=mprev[:, 1:8], in_=mnew[:, 0:7]), R=[r_mnew], W=[r_mprev])
            yield
            K.op("dve", lambda: nc.vector.tensor_copy(out=mcar[:], in_=mnew[:, 7:8]), R=[r_mnew, r_mprev], W=[r_mcar])
            mpb = mprev[:].unsqueeze(2).to_broadcast([2, 8, 64])
            yield
            K.op("dve", lambda: nc.vector.tensor_tensor(out=mu3, in0=cma3, in1=mpb, op=ALU.max),
                 R=[rr("cma"), r_mprev], W=[rr("mu")])
            yield
            K.op("dve", lambda: nc.vector.tensor_scalar(out=R_("nmu")[:], in0=R_("mu")[:], scalar1=-1.0, scalar2=None,
                                                        op0=ALU.mult), R=[rr("mu")], W=[rr("nmu")])
            mcb = mu3[:, :, 63].unsqueeze(2).to_broadcast([2, 8, 64])
            yield
            K.op("dve", lambda: nc.vector.tensor_tensor(out=tmp3, in0=al3, in1=mcb, op=ALU.subtract),
                 R=[rr("al"), rr("mu")], W=[rr("tmp")])
            yield
            K.op("act", lambda: nc.scalar.activation(out=R_("wrow")[:], in_=R_("tmp")[:], func=AF.Exp), R=[rr("tmp")], W=[rr("wrow")])
            yield
            K.op("dve", lambda: nc.vector.tensor_tensor(out=decay[:], in0=mprev[:], in1=mu3[:, :, 63], op=ALU.subtract),
                 R=[r_mprev, rr("mu")], W=[r_decay])
            yield
            K.op("act", lambda: nc.scalar.activation(out=decay[:], in_=decay[:], func=AF.Exp), R=[r_decay], W=[r_decay])
            yield
            K.op("dve", lambda: nc.vector.tensor_tensor(out=tmp3, in0=mu3, in1=mpb, op=ALU.subtract),
                 R=[rr("mu"), r_mprev, rr("wrow")], W=[rr("tmp")])
            yield
            K.op("act", lambda: nc.scalar.activation(out=R_("inter")[:], in_=R_("tmp")[:], func=AF.Exp, scale=-1.0),
                 R=[rr("tmp")], W=[rr("inter")])
            yield
            K.op("dve", lambda: nc.vector.tensor_tensor(out=R_("tmp")[:], in0=R_("b")[:], in1=R_("mu")[:], op=ALU.add),
                 R=[rr("b"), rr("mu"), rr("inter")], W=[rr("tmp")])
            yield
            K.op("act", lambda: nc.scalar.activation(out=R_("en")[:], in_=R_("tmp")[:], func=AF.Exp, scale=-1.0),
                 R=[rr("tmp")], W=[rr("en")])
            for qi, qn_ in enumerate(["al", "wrow", "inter", "en"]):
                for i in range(4):
                    K.op("pe", lambda qi=qi, i=i, qn_=qn_: nc.tensor.transpose(
                        out=ptl[:, qi, i, :], in_=R_(qn_)[0:2, i * 128:(i + 1) * 128], identity=ident_f[0:2, 0:2]),
                        R=[rr(qn_), C["r"]], W=[rB[2]])
            yield
            K.op("pe", lambda: nc.tensor.matmul(pdec, lhsT=C["hsel"][:], rhs=decay[:], start=True, stop=True),
                 R=[r_decay, C["r"]], W=[rB[2]])
            yield
            K.op("dve", lambda: nc.vector.tensor_copy(out=tl[:], in_=ptl), R=[rB[2]], W=[r_tl])
            yield
            K.op("dve", lambda: nc.vector.tensor_copy(out=decr[:], in_=pdec), R=[rB[2]], W=[r_decr])
            for i in range(4):
                tg = b * 4 + i
                ts = slice(tg * 128, (tg + 1) * 128)
                tb = slice(i * 128, (i + 1) * 128)
                par = ntile % 2
                ktm_, r_ktm_ = ktm2[par]
                vaug_, r_vaug_ = vaug2[par]
                og_, r_og_ = og2[par]
                for k in range(8):
                    K.op("pe", lambda k=k: nc.tensor.matmul(ptm, lhsT=hnT[:, k, ts], rhs=wtm[:, k, :],
                                                            start=(k == 0), stop=(k == 7)), R=[r_hnT, r_wtm], W=[rB[3]])
                K.op("act", lambda: nc.scalar.mul(out=ktm_[:], in_=ptm[:, 0:128], mul=0.125), R=[rB[3]], W=[r_ktm_])
                K.op("dve", lambda: nc.vector.tensor_copy(out=vaug_[:, :, 0:64],
                                                          in_=ptm[:, 128:256].rearrange("p (h d) -> p h d", d=64)),
                     R=[rB[3]], W=[r_vaug_])
                K.op("act", lambda: nc.scalar.activation(out=og_[:], in_=ptm[:, 256:384], func=AF.Sigmoid), R=[rB[3]], W=[r_og_])
                yb_ = yb[par]
                ryb2 = r_yb2[par]
                for h in range(2):
                    K.op("pe", lambda h=h: nc.tensor.matmul(pDt[:, h, :], lhsT=C["sel"][:, h, :],
                                                            rhs=R_("nmu")[0:2, tb], start=True, stop=False),
                         R=[rr("nmu"), C["r"]], W=[rB[2]])
                    K.op("pe", lambda h=h: nc.tensor.matmul(pDt[:, h, :], lhsT=ident_f[:],
                                                            rhs=C["nm2"][:, 0:128], start=False, stop=True),
                         R=[C["r"]], W=[rB[2]])
                yield

                def head_flow(h):
                    hp = slice(64 * h, 64 * h + 64)
                    hc = slice(64 * h, 64 * h + 64)
                    PB = B[h]; rPB = rB[h]
                    pS = PB[:, 0:128]; pN = PB[:, 128:193]
                    pQs = [PB[:, 200:265], PB[:, 272:337]]
                    pC = PB[:, 344:409]
                    Hh = HT[h]
                    dT, r_dT = Hh["dT"]; sdT, r_sdT = Hh["sdT"]; kw, r_kw = Hh["kw"]; nmv, r_nmv = Hh["nmv"]
                    dn, r_dn = Hh["dn"]; rn, r_rn = Hh["rn"]; hm, r_hm = Hh["hm"]; junk, r_junk = Hh["junk"]
                    ssq, r_ssq = Hh["ssq"]; hn2, r_hn2 = Hh["hn2"]; nscr = Hh["nscr"]
                    K.op("pe", lambda: nc.tensor.matmul(pS, lhsT=kTb[hp, tb], rhs=qTb[hp, tb], start=True, stop=True),
                         R=[r_kTb, r_qTb], W=[rPB])
                    K.op("act", lambda: nc.scalar.activation(out=dT[:], in_=pDt[:, h, :], func=AF.Exp,
                                                             bias=tl[:, 0, i, h:h + 1]), R=[rB[2], r_tl], W=[r_dT])
                    yield
                    K.op("dve", lambda: nc.vector.tensor_tensor(out=sdT[:], in0=pS, in1=dT[:], op=ALU.mult),
                         R=[rPB, r_dT], W=[r_sdT])
                    K.op("pe", lambda: nc.tensor.matmul(pN, lhsT=sdT[:], rhs=vaug_[:, h, :], start=True, stop=True),
                         R=[r_sdT, r_vaug_], W=[rPB])
                    yield
                    K.op("dve", lambda: nc.vector.tensor_copy(out=nmv[:], in_=pN), R=[rPB], W=[r_nmv])
                    for half in range(2):
                        ce = 2 * i + half
                        rs_ = slice(64 * half, 64 * half + 64)
                        pQ = pQs[half]
                        K.op("pe", lambda: nc.tensor.matmul(pQ, lhsT=qTb[hp, tb], rhs=Cbf[hp, :], start=True, stop=True),
                             R=[r_qTb, r_Cbf[h]], W=[rPB])
                        K.op("dve", lambda: nc.vector.tensor_scalar(out=kw[rs_, :], in0=ktm_[rs_, hc], scalar1=tl[rs_, 1, i, h:h + 1],
                                                                    scalar2=None, op0=ALU.mult), R=[r_ktm_, r_tl], W=[r_kw])
                        yield
                        K.op("dve", lambda: nc.vector.scalar_tensor_tensor(
                            out=nmv[rs_, :], in0=pQ[rs_, :], scalar=tl[rs_, 2, i, h:h + 1], in1=nmv[rs_, :],
                            op0=ALU.mult, op1=ALU.add), R=[rPB, r_tl, r_nmv], W=[r_nmv])
                        K.op("pe", lambda: nc.tensor.matmul(pC[hp, :], lhsT=kw[rs_, :], rhs=vaug_[rs_, h, :], start=True, stop=True),
                             R=[r_kw, r_vaug_], W=[rPB])
                        yield
                        K.op("dve", lambda: nc.vector.scalar_tensor_tensor(
                            out=Cst[hp, :], in0=Cst[hp, :], scalar=decr[hp, ce:ce + 1], in1=pC[hp, :],
                            op0=ALU.mult, op1=ALU.add), R=[r_Cst[h], r_decr, rPB], W=[r_Cst[h]])
                        K.op("act", lambda: nc.scalar.copy(out=Cbf[hp, :], in_=Cst[hp, :]), R=[r_Cst[h]], W=[r_Cbf[h]])
                        yield
                    K.op("dve", lambda: nc.vector.scalar_tensor_tensor(out=dn[:], in0=nmv[:, 64:65], scalar=-1.0, in1=nmv[:, 64:65],
                                                                       op0=ALU.mult, op1=ALU.max), R=[r_nmv], W=[r_dn])
                    K.op("dve", lambda: nc.vector.tensor_tensor(out=dn[:], in0=dn[:], in1=tl[:, 3, i, h:h + 1], op=ALU.max),
                         R=[r_dn, r_tl], W=[r_dn])
                    yield
                    K.op("dve", lambda: nc.vector.reciprocal(out=rn[:], in_=dn[:]), R=[r_dn], W=[r_rn])
                    K.op("dve", lambda: nc.vector.tensor_scalar(out=hm[:], in0=nmv[:, 0:64], scalar1=rn[:, 0:1], scalar2=None,
                                                                op0=ALU.mult), R=[r_nmv, r_rn], W=[r_hm])
                    yield
                    K.op("act", lambda: nc.scalar.activation(out=junk[:], in_=hm[:], func=AF.Square, accum_out=ssq[:]),
                         R=[r_hm], W=[r_junk, r_ssq])
                    yield
                    rstd_from_ssq(K, ssq[:], r_ssq, 1, nscr, 1.0 / 64)
                    yield
                    K.op("dve", lambda: nc.vector.scalar_tensor_tensor(out=hn2[:], in0=hm[:], scalar=nscr["rstd"], in1=mnw[:, hc],
                                                                       op0=ALU.mult, op1=ALU.mult),
                         R=[r_hm, nscr["r_rstd"], r_c], W=[r_hn2])
                    K.op("dve", lambda: nc.vector.tensor_tensor(out=yb_[:, hc], in0=hn2[:], in1=og_[:, hc], op=ALU.mult),
                         R=[r_hn2, r_og_], W=[ryb2[h]])

                gens = [head_flow(0), head_flow(1)]
                while gens:
                    for g_ in list(gens):
                        try:
                            next(g_)
                        except StopIteration:
                            gens.remove(g_)
                    yield
                K.dma("sp", y_d[ts, 0:128], yb_[:], ds_y[par], R=ryb2)
                ntile += 1


def emit_mlstm(K, C, hnT, r_hnT, P, y_d, S):
    outer = K.stack
    with ExitStack() as st:
        K.stack = st
        K.begin_phase()
        W = {"wM_fm": load_w(K, P["wM_fm"], 256, "wMf"), "wM_g": load_w(K, P["wM_g"], 4, "wMg"),
             "wM_tm": load_w(K, P["wM_tm"], 384, "wMt")}
        run_gen(emit_mlstm_gen(K, C, hnT, r_hnT, P, y_d, S, W))
        K.barrier()
        K.end_phase()
        K.stack = outer


def load_hnT_pair(K, hn_all, S, blk=1024, regs=None):
    hnT = K.sb([128, 8, S], BF16, "hnT")
    half = S // 2
    nq = S // blk
    rq = [Reg(f"hnT_q{i}") for i in range(nq)]
    dsq = [K.new_dma_sem() for i in range(nq)]
    n = 0
    for b, ha in enumerate(hn_all):
        for rk in range(2):
            c0 = rk * half + b * blk
            q = c0 // blk
            for k in range(8):
                eng = "sp" if (b > 0 or n % 2 == 0) else "act"
                K.dma(eng, hnT[:, k, c0:c0 + blk],
                      ha[rk * 1024 + k * 128: rk * 1024 + (k + 1) * 128, :], dsq[q], W=[rq[q]],
                      R=([regs[b]] if regs is not None else []))
                n += 1
    return hnT, rq


class YSplit:
    def __init__(self, a, b, half):
        self.a, self.b, self.half = a, b, half
        self.regs = [Reg("yhalf0"), Reg("yhalf1")]
        self.hook = None

    def reg(self, row0):
        return self.regs[0 if row0 < self.half else 1]

    def __getitem__(self, key):
        rs, cs = key
        if rs.start < self.half:
            assert rs.stop <= self.half
            return self.a[rs, cs]
        return self.b[slice(rs.start - self.half, rs.stop - self.half), cs]


def emit_ms_concurrent(K, C, hnT, r_hnT, P, y_d, S):
    outer = K.stack
    with ExitStack() as st:
        K.stack = st
        K.begin_phase()
        W = {"wM_fm": load_w(K, P["wM_fm"], 256, "wMf"), "wM_g": load_w(K, P["wM_g"], 4, "wMg"),
             "wM_tm": load_w(K, P["wM_tm"], 384, "wMt"),
             "wS_fm": load_w(K, P["wS_fm"], 512, "wSf"), "wS_tm": load_w(K, P["wS_tm"], 260, "wSt")}
        gens = [emit_mlstm_gen(K, C, hnT, r_hnT, P, y_d, S, W),
                emit_ssd_gen(K, C, hnT, r_hnT, P, y_d, S, W, nsets=1)]
        while gens:
            for g_ in list(gens):
                try:
                    next(g_)
                except StopIteration:
                    gens.remove(g_)
        K.barrier()
        K.end_phase()
        K.stack = outer


def emit_ssd2(K, C, hnT, r_hnT, P, y_d, S, G=4, W=None):
    nc = K.nc
    nT = S // 128
    nTT = S // 512
    outer = K.stack
    with ExitStack() as st:
        K.stack = st
        K.begin_phase()
        wfm, r_wfm = W["wS_fm"] if W is not None else load_w(K, P["wS_fm"], 512, "wSf")
        wtm, r_wtm = W["wS_tm"] if W is not None else load_w(K, P["wS_tm"], 260, "wSt")
        dsm = K.new_dma_sem()
        r_c = Reg("sconst")
        cw = K.sb([128, 4, 4], F32, "cw"); cb = K.sb([128, 4], F32, "cb")
        dtb = K.sb([128, 4], F32, "dtb"); alog = K.sb([128, 4], F32, "alog")
        dsk = K.sb([128, 4], F32, "dsk"); snw = K.sb([128, 256], F32, "snw")
        for t_, n_ in [(cw, "cw"), (cb, "cb"), (dtb, "dtb"), (alog, "alog"), (dsk, "dsk"), (snw, "snw")]:
            K.dma("sp", t_[:], P[n_], dsm, W=[r_c])
        Aneg = K.sb([128, 4], F32, "Aneg"); r_A = Reg()
        K.op("act", lambda: nc.scalar.activation(out=Aneg[:], in_=alog[:], func=AF.Exp), R=[r_c], W=[r_A])
        K.op("dve", lambda: nc.vector.tensor_scalar(out=Aneg[:], in0=Aneg[:], scalar1=-1.0, scalar2=None, op0=ALU.mult),
             R=[r_A], W=[r_A])
        xc = [K.sb([128, S], BF16, f"xc{i}") for i in range(4)]
        r_xc = [Reg() for _ in range(4)]
        X = [psbank(K, f"sx{i}") for i in range(8)]
        rX = [Reg(f"sx{i}", excl=True) for i in range(8)]
        with ExitStack() as st2:
            K.stack = st2
            SH = S // 2
            xpre = K.sb([128, SH + 3], F32, "xpre"); r_xpre = Reg()
            cacc = K.sb([128, SH], F32, "cacc"); r_cacc = Reg()
            n = 0
            for ct in range(4):
                for hf in range(2):
                    if hf == 0:
                        K.op("dve", lambda: nc.vector.memset(xpre[:, 0:3], 0.0), W=[r_xpre])
                    else:
                        K.op("dve", lambda: nc.vector.tensor_copy(out=xpre[:, 0:3], in_=xpre[:, SH:SH + 3]),
                             R=[r_xpre], W=[r_xpre])
                    for tt in range(nTT // 2):
                        tg_ = hf * (nTT // 2) + tt
                        p_ = X[n % 4]; rp = rX[n % 4]; n += 1
                        for k in range(8):
                            K.op("pe", lambda k=k: nc.tensor.matmul(p_, lhsT=wfm[:, k, ct * 128:(ct + 1) * 128],
                                                                    rhs=hnT[:, k, tg_ * 512:(tg_ + 1) * 512],
                                                                    start=(k == 0), stop=(k == 7)),
                                 R=[r_wfm, r_hnT], W=[rp])
                        K.op("act", lambda: nc.scalar.copy(out=xpre[:, 3 + tt * 512:3 + (tt + 1) * 512], in_=p_),
                             R=[rp], W=[r_xpre])
                    K.op("dve", lambda: nc.vector.tensor_scalar(out=cacc[:], in0=xpre[:, 0:SH], scalar1=cw[:, ct, 0:1],
                                                                scalar2=None, op0=ALU.mult), R=[r_xpre, r_c], W=[r_cacc])
                    for j in range(1, 4):
                        K.op("dve", lambda j=j: nc.vector.scalar_tensor_tensor(
                            out=cacc[:], in0=xpre[:, j:SH + j], scalar=cw[:, ct, j:j + 1], in1=cacc[:],
                            op0=ALU.mult, op1=ALU.add), R=[r_xpre, r_c, r_cacc], W=[r_cacc])
                    K.op("act", lambda: nc.scalar.activation(out=xc[ct][:, hf * SH:(hf + 1) * SH], in_=cacc[:], func=AF.Silu,
                                                             bias=cb[:, ct:ct + 1]),
                         R=[r_cacc, r_c], W=[r_xc[ct]])
            K.barrier()
            K.stack = st

        def T(shape, dt, name):
            return K.sb(shape, dt, name), Reg(name)
        sz, r_sz = T([128, G, 256], BF16, "gsz")
        dtx, r_dtx = T([128, G, 4], F32, "gdtx"); ax, r_ax = T([128, G, 4], F32, "gax")
        ex, r_ex = T([128, G, 4], F32, "gex"); lx, r_lx = T([128, G, 4], F32, "glx")
        dt, r_dt = T([128, G, 4], F32, "gdt"); aa, r_aa = T([128, G, 4], F32, "gaa")
        acs, r_acs = T([128, G, 4], F32, "gacs"); nacs, r_nacs = T([128, G, 4], F32, "gnacs")
        el, r_el = T([128, G, 4], F32, "gel"); cd, r_cd = T([128, G, 4], F32, "gcd")
        dd, r_dd = T([128, G, 4], F32, "gdd"); dec, r_dec = T([128, G, 4], F32, "gdec")
        dtdec, r_dtdec = T([128, G, 4], F32, "gdtdec")
        rseg = [T([128, 4, 128], F32, f"grseg{i}") for i in range(2)]
        segT = [T([128, 4, 128], F32, f"gsegT{i}") for i in range(G)]
        xdt = [T([128, 4, 64], BF16, f"gxdt{i}") for i in range(G)]
        xdd = [T([128, 4, 64], BF16, f"gxdd{i}") for i in range(G)]
        xD = [T([128, 4, 64], F32, f"gxD{i}") for i in range(G)]
        Btm = [T([128, 128], BF16, f"gBtm{i}") for i in range(G)]
        Gm, r_Gm = T([128, G, 128], F32, "gGm")
        scT = [T([128, 4, 128], BF16, f"gscT{i}") for i in range(G)]
        t0 = [T([128, 256], F32, f"gt0{i}") for i in range(G)]
        Sf, r_Sf = T([128, 4, 64], F32, "gSf")
        Sbf = [T([128, 256], BF16, f"gSbf{i}") for i in range(G + 1)]
        gg = [T([128, 256], F32, f"ggg{i}") for i in range(G)]
        junk, r_junk = T([128, 256], BF16, "gjunk")
        ssq, r_ssq = T([128, G], F32, "gssq")
        nscr = mk_scr(K, [128, G], "gn")
        yb = [K.sb([128, 256], BF16, f"gyb{i}") for i in range(G)]; r_yb = [Reg() for _ in range(G)]
        ds_y = [K.new_dma_sem() for _ in range(G)]
        K.op("dve", lambda: nc.vector.memset(Sf[:].rearrange("p a b -> p (a b)"), 0.0), W=[r_Sf])
        K.op("dve", lambda: nc.vector.memset(Sbf[0][0][:], 0.0), W=[Sbf[0][1]])
        ident_f = C["ident_f"]
        fl = lambda t: t[:].rearrange("p a b -> p (a b)")
        for g0 in range(0, nT, G):
            cs = [slice((g0 + i) * 128, (g0 + i + 1) * 128) for i in range(G)]
            pz = [X[0][:, 0:256], X[0][:, 256:512], X[1][:, 0:256], X[1][:, 256:512]]
            rpz = [rX[0], rX[0], rX[1], rX[1]]
            pdt = X[2][:, 0:4 * G].rearrange("p (g h) -> p g h", h=4)
            for i in range(G):
                for k in range(8):
                    K.op("pe", lambda k=k, i=i: nc.tensor.matmul(pz[i], lhsT=hnT[:, k, cs[i]], rhs=wtm[:, k, 0:256],
                                                                 start=(k == 0), stop=(k == 7)), R=[r_hnT, r_wtm], W=[rpz[i]])
                for k in range(8):
                    K.op("pe", lambda k=k, i=i: nc.tensor.matmul(pdt[:, i, :], lhsT=hnT[:, k, cs[i]], rhs=wtm[:, k, 256:260],
                                                                 start=(k == 0), stop=(k == 7)), R=[r_hnT, r_wtm], W=[rX[2]])
            for i in range(0, G, 2):
                K.op("act", lambda i=i: nc.scalar.activation(out=sz[:, i:i + 2, :].rearrange("p a b -> p (a b)"),
                                                             in_=X[i // 2], func=AF.Silu), R=[rpz[i]], W=[r_sz])
            K.op("dve", lambda: nc.vector.tensor_tensor(out=dtx[:], in0=pdt, in1=dtb[:].unsqueeze(1).to_broadcast([128, G, 4]),
                                                        op=ALU.add), R=[rX[2], r_c], W=[r_dtx])
            K.op("dve", lambda: nc.vector.scalar_tensor_tensor(out=ax[:], in0=dtx[:], scalar=-1.0, in1=dtx[:],
                                                               op0=ALU.mult, op1=ALU.min), R=[r_dtx], W=[r_ax])
            K.op("act", lambda: nc.scalar.activation(out=ex[:], in_=ax[:], func=AF.Exp), R=[r_ax], W=[r_ex])
            K.op("dve", lambda: nc.vector.tensor_scalar(out=ex[:], in0=ex[:], scalar1=1.0, scalar2=None, op0=ALU.add),
                 R=[r_ex], W=[r_ex])
            K.op("act", lambda: nc.scalar.activation(out=lx[:], in_=ex[:], func=AF.Ln), R=[r_ex], W=[r_lx])
            K.op("dve", lambda: nc.vector.scalar_tensor_tensor(out=dt[:], in0=dtx[:], scalar=0.0, in1=lx[:],
                                                               op0=ALU.max, op1=ALU.add), R=[r_dtx, r_lx], W=[r_dt])
            K.op("dve", lambda: nc.vector.tensor_tensor(out=aa[:], in0=dt[:], in1=Aneg[:].unsqueeze(1).to_broadcast([128, G, 4]),
                                                        op=ALU.mult), R=[r_dt, r_A], W=[r_aa])
            pacs = X[2][:, 64:64 + 4 * G].rearrange("p (g h) -> p g h", h=4)
            plast = X[2][:, 128:128 + 4 * G].rearrange("p (g h) -> p g h", h=4)
            K.op("pe", lambda: nc.tensor.matmul(X[2][:, 64:64 + 4 * G], lhsT=C["tri"], rhs=fl(aa), start=True, stop=True),
                 R=[r_aa, C["r"]], W=[rX[2]])
            K.op("pe", lambda: nc.tensor.matmul(X[2][:, 128:128 + 4 * G], lhsT=C["ones"], rhs=fl(aa), start=True, stop=True),
                 R=[r_aa, C["r"]], W=[rX[2]])
            K.op("dve", lambda: nc.vector.tensor_copy(out=acs[:], in_=pacs), R=[rX[2]], W=[r_acs])
            K.op("dve", lambda: nc.vector.tensor_scalar(out=nacs[:], in0=pacs, scalar1=-1.0, scalar2=None, op0=ALU.mult),
                 R=[rX[2]], W=[r_nacs])
            K.op("dve", lambda: nc.vector.tensor_tensor(out=dd[:], in0=plast, in1=acs[:], op=ALU.subtract),
                 R=[rX[2], r_acs], W=[r_dd])
            K.op("act", lambda: nc.scalar.activation(out=el[:], in_=acs[:], func=AF.Exp), R=[r_acs], W=[r_el])
            K.op("act", lambda: nc.scalar.activation(out=cd[:], in_=plast, func=AF.Exp), R=[rX[2]], W=[r_cd])
            K.op("act", lambda: nc.scalar.activation(out=dec[:], in_=dd[:], func=AF.Exp), R=[r_dd], W=[r_dec])
            K.op("dve", lambda: nc.vector.tensor_tensor(out=dtdec[:], in0=dt[:], in1=dec[:], op=ALU.mult),
                 R=[r_dt, r_dec], W=[r_dtdec])
            for i in range(G):
                rs_, rrs = rseg[i % 2]
                ps_ = X[3 + i % 2]; rps = rX[3 + i % 2]
                K.op("dve", lambda i=i: nc.vector.tensor_tensor(out=rs_[:], in0=ident_f[:].unsqueeze(1).to_broadcast([128, 4, 128]),
                                                                in1=acs[:, i, :].unsqueeze(2).to_broadcast([128, 4, 128]), op=ALU.mult),
                     R=[C["r"], r_acs], W=[rrs])
                K.op("pe", lambda: nc.tensor.matmul(ps_, lhsT=C["ones"], rhs=fl(rs_), start=True, stop=False),
                     R=[rrs, C["r"]], W=[rps])
                K.op("pe", lambda: nc.tensor.matmul(ps_, lhsT=ident_f[:], rhs=C["nm1"], start=False, stop=True),
                     R=[C["r"]], W=[rps])
                for h in range(4):
                    K.op("act", lambda i=i, h=h: nc.scalar.activation(out=segT[i][0][:, h, :], in_=ps_[:, h * 128:(h + 1) * 128],
                                                                      func=AF.Exp, bias=nacs[:, i, h:h + 1]),
                         R=[rps, r_nacs], W=[segT[i][1]])
            for i in range(G):
                pt_ = X[5 + i % 2][:, 0:192].bitcast(BF16).rearrange("p (a b) -> p a b", b=128); rpt = rX[5 + i % 2]
                for a in range(3):
                    K.op("pe", lambda a=a, i=i: nc.tensor.transpose(out=pt_[:, a, :], in_=xc[a][:, cs[i]], identity=C["ident_bf"][:]),
                         R=[r_xc[a], C["r"]], W=[rpt])
                xs_v = pt_[:, 0:2, :].rearrange("p a (h d) -> p (a h) d", d=64)
                K.op("dve", lambda i=i: nc.vector.tensor_tensor(out=xdt[i][0][:], in0=xs_v, in1=dt[:, i, :].unsqueeze(2).to_broadcast([128, 4, 64]),
                                                                op=ALU.mult), R=[rpt, r_dt], W=[xdt[i][1]])
                K.op("dve", lambda i=i: nc.vector.tensor_tensor(out=xdd[i][0][:], in0=xs_v, in1=dtdec[:, i, :].unsqueeze(2).to_broadcast([128, 4, 64]),
                                                                op=ALU.mult), R=[rpt, r_dtdec], W=[xdd[i][1]])
                K.op("dve", lambda i=i: nc.vector.tensor_tensor(out=xD[i][0][:], in0=xs_v, in1=dsk[:].unsqueeze(2).to_broadcast([128, 4, 64]),
                                                                op=ALU.mult), R=[rpt, r_c], W=[xD[i][1]])
                K.op("dve", lambda i=i: nc.vector.tensor_copy(out=Btm[i][0][:], in_=pt_[:, 2, :]), R=[rpt], W=[Btm[i][1]])
            for i in range(G):
                K.op("pe", lambda i=i: nc.tensor.matmul(X[7][:, i * 128:(i + 1) * 128], lhsT=xc[2][:, cs[i]], rhs=xc[3][:, cs[i]],
                                                        start=True, stop=True), R=[r_xc[2], r_xc[3]], W=[rX[7]])
            K.op("dve", lambda: nc.vector.tensor_tensor(out=Gm[:], in0=X[7][:, 0:G * 128].rearrange("p (g l) -> p g l", l=128),
                                                        in1=C["tri"].unsqueeze(1).to_broadcast([128, G, 128]), op=ALU.mult),
                 R=[rX[7], C["r"]], W=[r_Gm])
            for i in range(G):
                K.op("dve", lambda i=i: nc.vector.tensor_tensor(out=scT[i][0][:], in0=Gm[:, i, :].unsqueeze(1).to_broadcast([128, 4, 128]),
                                                                in1=segT[i][0][:], op=ALU.mult), R=[r_Gm, segT[i][1]], W=[scT[i][1]])
            for i in range(G):
                py_ = X[i % 2][:, (i // 2 % 2) * 256:(i // 2 % 2) * 256 + 256]; rpy = rX[i % 2]
                for h in range(4):
                    K.op("pe", lambda i=i, h=h: nc.tensor.matmul(py_[:, h * 64:(h + 1) * 64], lhsT=scT[i][0][:, h, :], rhs=xdt[i][0][:, h, :],
                                                                 start=True, stop=True), R=[scT[i][1], xdt[i][1]], W=[rpy])
                K.op("dve", lambda i=i: nc.vector.tensor_tensor(out=t0[i][0][:], in0=py_, in1=fl(xD[i][0]), op=ALU.add),
                     R=[rpy, xD[i][1]], W=[t0[i][1]])
            for i in range(G):
                pst_ = X[3 + i % 2][:, 0:256]; rpst = rX[3 + i % 2]
                K.op("pe", lambda i=i: nc.tensor.matmul(pst_, lhsT=Btm[i][0][:], rhs=fl(xdd[i][0]), start=True, stop=True),
                     R=[Btm[i][1], xdd[i][1]], W=[rpst])
                K.op("dve", lambda i=i: nc.vector.tensor_tensor(out=Sf[:], in0=Sf[:], in1=cd[:, i, :].unsqueeze(2).to_broadcast([128, 4, 64]),
                                                                op=ALU.mult), R=[r_Sf, r_cd], W=[r_Sf])
                K.op("dve", lambda: nc.vector.tensor_tensor(out=fl(Sf), in0=fl(Sf), in1=pst_, op=ALU.add),
                     R=[r_Sf, rpst], W=[r_Sf])
                K.op("act", lambda i=i: nc.scalar.copy(out=Sbf[i + 1][0][:], in_=fl(Sf)), R=[r_Sf], W=[Sbf[i + 1][1]])
            for i in range(G):
                pyo_ = X[5 + i % 2][:, 256:512]; rpyo = rX[5 + i % 2]
                K.op("pe", lambda i=i: nc.tensor.matmul(pyo_, lhsT=xc[3][:, cs[i]], rhs=Sbf[i][0][:], start=True, stop=True),
                     R=[r_xc[3], Sbf[i][1]], W=[rpyo])
                K.op("dve", lambda i=i: nc.vector.tensor_tensor(out=gg[i][0][:].rearrange("p (a b) -> p a b", b=64),
                                                                in0=pyo_.rearrange("p (a b) -> p a b", b=64),
                                                                in1=el[:, i, :].unsqueeze(2).to_broadcast([128, 4, 64]), op=ALU.mult),
                     R=[rpyo, r_el], W=[gg[i][1]])
                K.op("dve", lambda i=i: nc.vector.tensor_tensor(out=gg[i][0][:], in0=gg[i][0][:], in1=t0[i][0][:], op=ALU.add),
                     R=[gg[i][1], t0[i][1]], W=[gg[i][1]])
                K.op("dve", lambda i=i: nc.vector.tensor_tensor(out=gg[i][0][:], in0=gg[i][0][:], in1=sz[:, i, :], op=ALU.mult),
                     R=[gg[i][1], r_sz], W=[gg[i][1]])
                K.op("act", lambda i=i: nc.scalar.activation(out=junk[:], in_=gg[i][0][:], func=AF.Square, accum_out=ssq[:, i:i + 1]),
                     R=[gg[i][1]], W=[r_junk, r_ssq])
            rstd_from_ssq(K, ssq[:], r_ssq, G, nscr, 1.0 / 256)
            for i in range(G):
                K.op("dve", lambda i=i: nc.vector.scalar_tensor_tensor(out=yb[i][:], in0=gg[i][0][:], scalar=nscr["rstd"][:, i:i + 1],
                                                                       in1=snw[:], op0=ALU.mult, op1=ALU.mult),
                     R=[gg[i][1], nscr["r_rstd"], r_c], W=[r_yb[i]])
                K.dma("sp", y_d[cs[i], 128:384], yb[i][:], ds_y[i], R=[r_yb[i]])
            K.op("act", lambda: nc.scalar.copy(out=Sbf[0][0][:], in_=Sbf[G][0][:]), R=[Sbf[G][1]], W=[Sbf[0][1]])
        K.barrier()
        K.end_phase()
        K.stack = outer


def emit_ssd3(K, C, hnT, r_hnT, P, y_d, S, G=4, AW=3):
    nc = K.nc
    nT = S // 128
    nTT = S // 512
    outer = K.stack
    with ExitStack() as st:
        K.stack = st
        K.begin_phase()
        wfm, r_wfm = load_w(K, P["wS_fm"], 512, "wSf")
        wtm, r_wtm = load_w(K, P["wS_tm"], 260, "wSt")
        dsm = K.new_dma_sem()
        r_c = Reg("sconst")
        cw = K.sb([128, 4, 4], F32, "cw"); cb = K.sb([128, 4], F32, "cb")
        dtb = K.sb([128, 4], F32, "dtb"); alog = K.sb([128, 4], F32, "alog")
        dsk = K.sb([128, 4], F32, "dsk"); snw = K.sb([128, 256], F32, "snw")
        for t_, n_ in [(cw, "cw"), (cb, "cb"), (dtb, "dtb"), (alog, "alog"), (dsk, "dsk"), (snw, "snw")]:
            K.dma("sp", t_[:], P[n_], dsm, W=[r_c])
        Aneg = K.sb([128, 4], F32, "Aneg"); r_A = Reg()
        K.op("act", lambda: nc.scalar.activation(out=Aneg[:], in_=alog[:], func=AF.Exp), R=[r_c], W=[r_A])
        K.op("dve", lambda: nc.vector.tensor_scalar(out=Aneg[:], in0=Aneg[:], scalar1=-1.0, scalar2=None, op0=ALU.mult),
             R=[r_A], W=[r_A])
        xc = [K.sb([128, S], BF16, f"xc{i}") for i in range(4)]
        r_xc = [Reg() for _ in range(4)]
        X = [psbank(K, f"sx{i}") for i in range(8)]
        rX = [Reg(f"sx{i}", excl=True) for i in range(8)]
        with ExitStack() as st2:
            K.stack = st2
            SH = S // 2
            xpre = K.sb([128, SH + 3], F32, "xpre"); r_xpre = Reg()
            cacc = K.sb([128, SH], F32, "cacc"); r_cacc = Reg()
            n = 0
            for ct in range(4):
                for hf in range(2):
                    if hf == 0:
                        K.op("dve", lambda: nc.vector.memset(xpre[:, 0:3], 0.0), W=[r_xpre])
                    else:
                        K.op("dve", lambda: nc.vector.tensor_copy(out=xpre[:, 0:3], in_=xpre[:, SH:SH + 3]),
                             R=[r_xpre], W=[r_xpre])
                    for tt in range(nTT // 2):
                        tg_ = hf * (nTT // 2) + tt
                        p_ = X[n % 4]; rp = rX[n % 4]; n += 1
                        for k in range(8):
                            K.op("pe", lambda k=k: nc.tensor.matmul(p_, lhsT=wfm[:, k, ct * 128:(ct + 1) * 128],
                                                                    rhs=hnT[:, k, tg_ * 512:(tg_ + 1) * 512],
                                                                    start=(k == 0), stop=(k == 7)),
                                 R=[r_wfm, r_hnT], W=[rp])
                        K.op("act", lambda: nc.scalar.copy(out=xpre[:, 3 + tt * 512:3 + (tt + 1) * 512], in_=p_),
                             R=[rp], W=[r_xpre])
                    K.op("dve", lambda: nc.vector.tensor_scalar(out=cacc[:], in0=xpre[:, 0:SH], scalar1=cw[:, ct, 0:1],
                                                                scalar2=None, op0=ALU.mult), R=[r_xpre, r_c], W=[r_cacc])
                    for j in range(1, 4):
                        K.op("dve", lambda j=j: nc.vector.scalar_tensor_tensor(
                            out=cacc[:], in0=xpre[:, j:SH + j], scalar=cw[:, ct, j:j + 1], in1=cacc[:],
                            op0=ALU.mult, op1=ALU.add), R=[r_xpre, r_c, r_cacc], W=[r_cacc])
                    K.op("act", lambda: nc.scalar.activation(out=xc[ct][:, hf * SH:(hf + 1) * SH], in_=cacc[:], func=AF.Silu,
                                                             bias=cb[:, ct:ct + 1]),
                         R=[r_cacc, r_c], W=[r_xc[ct]])
            K.barrier()
            K.stack = st

        def T(shape, dt, name):
            return K.sb(shape, dt, name + SFX[0]), Reg(name)
        SFX = ['']

        def mkset(par):
            SFX[0] = f'_{par}'
            sz, r_sz = T([128, G, 256], BF16, "gsz")
            dtx, r_dtx = T([128, G, 4], F32, "gdtx"); ax, r_ax = T([128, G, 4], F32, "gax")
            ex, r_ex = T([128, G, 4], F32, "gex"); lx, r_lx = T([128, G, 4], F32, "glx")
            dt, r_dt = T([128, G, 4], F32, "gdt"); aa, r_aa = T([128, G, 4], F32, "gaa")
            acs, r_acs = T([128, G, 4], F32, "gacs"); nacs, r_nacs = T([128, G, 4], F32, "gnacs")
            el, r_el = T([128, G, 4], F32, "gel"); cd, r_cd = T([128, G, 4], F32, "gcd")
            dd, r_dd = T([128, G, 4], F32, "gdd"); dec, r_dec = T([128, G, 4], F32, "gdec")
            dtdec, r_dtdec = T([128, G, 4], F32, "gdtdec")
            rseg = [T([128, 4, 128], F32, f"grseg{i}") for i in range(2)]
            segT = [T([128, 4, 128], F32, f"gsegT{i}") for i in range(G)]
            xdt = [T([128, 4, 64], BF16, f"gxdt{i}") for i in range(G)]
            xdd = [T([128, 4, 64], BF16, f"gxdd{i}") for i in range(G)]
            xD = [T([128, 4, 64], F32, f"gxD{i}") for i in range(G)]
            Btm = [T([128, 128], BF16, f"gBtm{i}") for i in range(G)]
            Gm, r_Gm = T([128, G, 128], F32, "gGm")
            scT = [T([128, 4, 128], BF16, f"gscT{i}") for i in range(G)]
            t0 = [T([128, 256], F32, f"gt0{i}") for i in range(G)]
            Sbf = [T([128, 256], BF16, f"gSbf{i}") for i in range(G + 1)]
            gg = [T([128, 256], F32, f"ggg{i}") for i in range(G)]
            junk, r_junk = T([128, 256], BF16, "gjunk")
            ssq, r_ssq = T([128, G], F32, "gssq")
            nscr = mk_scr(K, [128, G], "gn")
            yb = [K.sb([128, 256], BF16, f"gyb{i}") for i in range(G)]; r_yb = [Reg() for _ in range(G)]
            return dict(locals())
        sets = [mkset(0), mkset(1)]
        SFX[0] = ''
        Sf, r_Sf = T([128, 4, 64], F32, "gSf")
        ds_y = [K.new_dma_sem() for _ in range(2)]
        K.op("dve", lambda: nc.vector.memset(Sf[:].rearrange("p a b -> p (a b)"), 0.0), W=[r_Sf])
        K.op("dve", lambda: nc.vector.memset(sets[0]["Sbf"][0][0][:], 0.0), W=[sets[0]["Sbf"][0][1]])
        ident_f = C["ident_f"]
        fl = lambda t: t[:].rearrange("p a b -> p (a b)")

        def genA(g0, S_):
            cs = [slice((g0 + i) * 128, (g0 + i + 1) * 128) for i in range(G)]
            sz, r_sz, dtx, r_dtx, ax, r_ax, ex, r_ex, lx, r_lx, dt, r_dt, aa, r_aa, acs, r_acs, nacs, r_nacs, el, r_el, cd, r_cd, dd, r_dd, dec, r_dec, dtdec, r_dtdec, rseg, segT, xdt, xdd, xD, Btm, Gm, r_Gm, scT, t0, Sbf, gg, junk, r_junk, ssq, r_ssq, nscr, yb, r_yb = [S_[n_] for n_ in ['sz', 'r_sz', 'dtx', 'r_dtx', 'ax', 'r_ax', 'ex', 'r_ex', 'lx', 'r_lx', 'dt', 'r_dt', 'aa', 'r_aa', 'acs', 'r_acs', 'nacs', 'r_nacs', 'el', 'r_el', 'cd', 'r_cd', 'dd', 'r_dd', 'dec', 'r_dec', 'dtdec', 'r_dtdec', 'rseg', 'segT', 'xdt', 'xdd', 'xD', 'Btm', 'Gm', 'r_Gm', 'scT', 't0', 'Sbf', 'gg', 'junk', 'r_junk', 'ssq', 'r_ssq', 'nscr', 'yb', 'r_yb']]
            cs = [slice((g0 + i) * 128, (g0 + i + 1) * 128) for i in range(G)]
            pz = [X[0][:, 0:256], X[0][:, 256:512], X[1][:, 0:256], X[1][:, 256:512]]
            rpz = [rX[0], rX[0], rX[1], rX[1]]
            pdt = X[2][:, 0:4 * G].rearrange("p (g h) -> p g h", h=4)
            yield
            for i in range(G):
                for k in range(8):
                    K.op("pe", lambda k=k, i=i: nc.tensor.matmul(pz[i], lhsT=hnT[:, k, cs[i]], rhs=wtm[:, k, 0:256],
                                                                 start=(k == 0), stop=(k == 7)), R=[r_hnT, r_wtm], W=[rpz[i]])
                yield
                for k in range(8):
                    K.op("pe", lambda k=k, i=i: nc.tensor.matmul(pdt[:, i, :], lhsT=hnT[:, k, cs[i]], rhs=wtm[:, k, 256:260],
                                                                 start=(k == 0), stop=(k == 7)), R=[r_hnT, r_wtm], W=[rX[2]])
            yield
            for i in range(0, G, 2):
                K.op("act", lambda i=i: nc.scalar.activation(out=sz[:, i:i + 2, :].rearrange("p a b -> p (a b)"),
                                                             in_=X[i // 2], func=AF.Silu), R=[rpz[i]], W=[r_sz])
            yield
            K.op("dve", lambda: nc.vector.tensor_tensor(out=dtx[:], in0=pdt, in1=dtb[:].unsqueeze(1).to_broadcast([128, G, 4]),
                                                        op=ALU.add), R=[rX[2], r_c], W=[r_dtx])
            yield
            K.op("dve", lambda: nc.vector.scalar_tensor_tensor(out=ax[:], in0=dtx[:], scalar=-1.0, in1=dtx[:],
                                                               op0=ALU.mult, op1=ALU.min), R=[r_dtx], W=[r_ax])
            yield
            K.op("act", lambda: nc.scalar.activation(out=ex[:], in_=ax[:], func=AF.Exp), R=[r_ax], W=[r_ex])
            yield
            K.op("dve", lambda: nc.vector.tensor_scalar(out=ex[:], in0=ex[:], scalar1=1.0, scalar2=None, op0=ALU.add),
                 R=[r_ex], W=[r_ex])
            yield
            K.op("act", lambda: nc.scalar.activation(out=lx[:], in_=ex[:], func=AF.Ln), R=[r_ex], W=[r_lx])
            yield
            K.op("dve", lambda: nc.vector.scalar_tensor_tensor(out=dt[:], in0=dtx[:], scalar=0.0, in1=lx[:],
                                                               op0=ALU.max, op1=ALU.add), R=[r_dtx, r_lx], W=[r_dt])
            yield
            K.op("dve", lambda: nc.vector.tensor_tensor(out=aa[:], in0=dt[:], in1=Aneg[:].unsqueeze(1).to_broadcast([128, G, 4]),
                                                        op=ALU.mult), R=[r_dt, r_A], W=[r_aa])
            pacs = X[2][:, 64:64 + 4 * G].rearrange("p (g h) -> p g h", h=4)
            plast = X[2][:, 128:128 + 4 * G].rearrange("p (g h) -> p g h", h=4)
            yield
            K.op("pe", lambda: nc.tensor.matmul(X[2][:, 64:64 + 4 * G], lhsT=C["tri"], rhs=fl(aa), start=True, stop=True),
                 R=[r_aa, C["r"]], W=[rX[2]])
            yield
            K.op("pe", lambda: nc.tensor.matmul(X[2][:, 128:128 + 4 * G], lhsT=C["ones"], rhs=fl(aa), start=True, stop=True),
                 R=[r_aa, C["r"]], W=[rX[2]])
            yield
            K.op("dve", lambda: nc.vector.tensor_copy(out=acs[:], in_=pacs), R=[rX[2]], W=[r_acs])
            yield
            K.op("dve", lambda: nc.vector.tensor_scalar(out=nacs[:], in0=pacs, scalar1=-1.0, scalar2=None, op0=ALU.mult),
                 R=[rX[2]], W=[r_nacs])
            yield
            K.op("dve", lambda: nc.vector.tensor_tensor(out=dd[:], in0=plast, in1=acs[:], op=ALU.subtract),
                 R=[rX[2], r_acs], W=[r_dd])
            yield
            K.op("act", lambda: nc.scalar.activation(out=el[:], in_=acs[:], func=AF.Exp), R=[r_acs], W=[r_el])
            yield
            K.op("act", lambda: nc.scalar.activation(out=cd[:], in_=plast, func=AF.Exp), R=[rX[2]], W=[r_cd])
            yield
            K.op("act", lambda: nc.scalar.activation(out=dec[:], in_=dd[:], func=AF.Exp), R=[r_dd], W=[r_dec])
            yield
            K.op("dve", lambda: nc.vector.tensor_tensor(out=dtdec[:], in0=dt[:], in1=dec[:], op=ALU.mult),
                 R=[r_dt, r_dec], W=[r_dtdec])
            yield
            for i in range(G):
                rs_, rrs = rseg[i % 2]
                ps_ = X[3]; rps = rX[3]
                K.op("dve", lambda i=i: nc.vector.tensor_tensor(out=rs_[:], in0=ident_f[:].unsqueeze(1).to_broadcast([128, 4, 128]),
                                                                in1=acs[:, i, :].unsqueeze(2).to_broadcast([128, 4, 128]), op=ALU.mult),
                     R=[C["r"], r_acs], W=[rrs])
                yield
                K.op("pe", lambda: nc.tensor.matmul(ps_, lhsT=C["ones"], rhs=fl(rs_), start=True, stop=False),
                     R=[rrs, C["r"]], W=[rps])
                yield
                K.op("pe", lambda: nc.tensor.matmul(ps_, lhsT=ident_f[:], rhs=C["nm1"], start=False, stop=True),
                     R=[C["r"]], W=[rps])
                yield
                for h in range(4):
                    K.op("act", lambda i=i, h=h: nc.scalar.activation(out=segT[i][0][:, h, :], in_=ps_[:, h * 128:(h + 1) * 128],
                                                                      func=AF.Exp, bias=nacs[:, i, h:h + 1]),
                         R=[rps, r_nacs], W=[segT[i][1]])
            yield
            for i in range(G):
                pt_ = X[5][:, 0:192].bitcast(BF16).rearrange("p (a b) -> p a b", b=128); rpt = rX[5]
                for a in range(3):
                    K.op("pe", lambda a=a, i=i: nc.tensor.transpose(out=pt_[:, a, :], in_=xc[a][:, cs[i]], identity=C["ident_bf"][:]),
                         R=[r_xc[a], C["r"]], W=[rpt])
                xs_v = pt_[:, 0:2, :].rearrange("p a (h d) -> p (a h) d", d=64)
                yield
                K.op("dve", lambda i=i: nc.vector.tensor_tensor(out=xdt[i][0][:], in0=xs_v, in1=dt[:, i, :].unsqueeze(2).to_broadcast([128, 4, 64]),
                                                                op=ALU.mult), R=[rpt, r_dt], W=[xdt[i][1]])
                yield
                K.op("dve", lambda i=i: nc.vector.tensor_tensor(out=xdd[i][0][:], in0=xs_v, in1=dtdec[:, i, :].unsqueeze(2).to_broadcast([128, 4, 64]),
                                                                op=ALU.mult), R=[rpt, r_dtdec], W=[xdd[i][1]])
                yield
                K.op("dve", lambda i=i: nc.vector.tensor_tensor(out=xD[i][0][:], in0=xs_v, in1=dsk[:].unsqueeze(2).to_broadcast([128, 4, 64]),
                                                                op=ALU.mult), R=[rpt, r_c], W=[xD[i][1]])
                yield
                K.op("dve", lambda i=i: nc.vector.tensor_copy(out=Btm[i][0][:], in_=pt_[:, 2, :]), R=[rpt], W=[Btm[i][1]])
            yield
            for i in range(G):
                K.op("pe", lambda i=i: nc.tensor.matmul(X[7][:, i * 128:(i + 1) * 128], lhsT=xc[2][:, cs[i]], rhs=xc[3][:, cs[i]],
                                                        start=True, stop=True), R=[r_xc[2], r_xc[3]], W=[rX[7]])
            yield
            K.op("dve", lambda: nc.vector.tensor_tensor(out=Gm[:], in0=X[7][:, 0:G * 128].rearrange("p (g l) -> p g l", l=128),
                                                        in1=C["tri"].unsqueeze(1).to_broadcast([128, G, 128]), op=ALU.mult),
                 R=[rX[7], C["r"]], W=[r_Gm])
            yield
            for i in range(G):
                K.op("dve", lambda i=i: nc.vector.tensor_tensor(out=scT[i][0][:], in0=Gm[:, i, :].unsqueeze(1).to_broadcast([128, 4, 128]),
                                                                in1=segT[i][0][:], op=ALU.mult), R=[r_Gm, segT[i][1]], W=[scT[i][1]])
            yield
            for i in range(G):
                py_ = X[i % 2][:, (i // 2 % 2) * 256:(i // 2 % 2) * 256 + 256]; rpy = rX[i % 2]
                for h in range(4):
                    K.op("pe", lambda i=i, h=h: nc.tensor.matmul(py_[:, h * 64:(h + 1) * 64], lhsT=scT[i][0][:, h, :], rhs=xdt[i][0][:, h, :],
                                                                 start=True, stop=True), R=[scT[i][1], xdt[i][1]], W=[rpy])
                yield
                K.op("dve", lambda i=i: nc.vector.tensor_tensor(out=t0[i][0][:], in0=py_, in1=fl(xD[i][0]), op=ALU.add),
                     R=[rpy, xD[i][1]], W=[t0[i][1]])
            yield
            yield

        def genB(g0, S_, O_):
            cs = [slice((g0 + i) * 128, (g0 + i + 1) * 128) for i in range(G)]
            sz, r_sz, dtx, r_dtx, ax, r_ax, ex, r_ex, lx, r_lx, dt, r_dt, aa, r_aa, acs, r_acs, nacs, r_nacs, el, r_el, cd, r_cd, dd, r_dd, dec, r_dec, dtdec, r_dtdec, rseg, segT, xdt, xdd, xD, Btm, Gm, r_Gm, scT, t0, Sbf, gg, junk, r_junk, ssq, r_ssq, nscr, yb, r_yb = [S_[n_] for n_ in ['sz', 'r_sz', 'dtx', 'r_dtx', 'ax', 'r_ax', 'ex', 'r_ex', 'lx', 'r_lx', 'dt', 'r_dt', 'aa', 'r_aa', 'acs', 'r_acs', 'nacs', 'r_nacs', 'el', 'r_el', 'cd', 'r_cd', 'dd', 'r_dd', 'dec', 'r_dec', 'dtdec', 'r_dtdec', 'rseg', 'segT', 'xdt', 'xdd', 'xD', 'Btm', 'Gm', 'r_Gm', 'scT', 't0', 'Sbf', 'gg', 'junk', 'r_junk', 'ssq', 'r_ssq', 'nscr', 'yb', 'r_yb']]
            for i in range(G):
                pst_ = X[4][:, (i % 2) * 256:(i % 2) * 256 + 256]; rpst = rX[4]
                K.op("pe", lambda i=i: nc.tensor.matmul(pst_, lhsT=Btm[i][0][:], rhs=fl(xdd[i][0]), start=True, stop=True),
                     R=[Btm[i][1], xdd[i][1]], W=[rpst])
                yield
                K.op("dve", lambda i=i: nc.vector.tensor_tensor(out=Sf[:], in0=Sf[:], in1=cd[:, i, :].unsqueeze(2).to_broadcast([128, 4, 64]),
                                                                op=ALU.mult), R=[r_Sf, r_cd], W=[r_Sf])
                yield
                K.op("dve", lambda: nc.vector.tensor_tensor(out=fl(Sf), in0=fl(Sf), in1=pst_, op=ALU.add),
                     R=[r_Sf, rpst], W=[r_Sf])
                yield
                K.op("act", lambda i=i: nc.scalar.copy(out=Sbf[i + 1][0][:], in_=fl(Sf)), R=[r_Sf], W=[Sbf[i + 1][1]])
            yield
            for i in range(G):
                pyo_ = X[6][:, (i % 2) * 256:(i % 2) * 256 + 256]; rpyo = rX[6]
                K.op("pe", lambda i=i: nc.tensor.matmul(pyo_, lhsT=xc[3][:, cs[i]], rhs=Sbf[i][0][:], start=True, stop=True),
                     R=[r_xc[3], Sbf[i][1]], W=[rpyo])
                yield
                K.op("dve", lambda i=i: nc.vector.tensor_tensor(out=gg[i][0][:].rearrange("p (a b) -> p a b", b=64),
                                                                in0=pyo_.rearrange("p (a b) -> p a b", b=64),
                                                                in1=el[:, i, :].unsqueeze(2).to_broadcast([128, 4, 64]), op=ALU.mult),
                     R=[rpyo, r_el], W=[gg[i][1]])
                yield
                K.op("dve", lambda i=i: nc.vector.tensor_tensor(out=gg[i][0][:], in0=gg[i][0][:], in1=t0[i][0][:], op=ALU.add),
                     R=[gg[i][1], t0[i][1]], W=[gg[i][1]])
                yield
                K.op("dve", lambda i=i: nc.vector.tensor_tensor(out=gg[i][0][:], in0=gg[i][0][:], in1=sz[:, i, :], op=ALU.mult),
                     R=[gg[i][1], r_sz], W=[gg[i][1]])
                yield
                K.op("act", lambda i=i: nc.scalar.activation(out=junk[:], in_=gg[i][0][:], func=AF.Square, accum_out=ssq[:, i:i + 1]),
                     R=[gg[i][1]], W=[r_junk, r_ssq])
            yield
            rstd_from_ssq(K, ssq[:], r_ssq, G, nscr, 1.0 / 256)
            yield
            for i in range(G):
                K.op("dve", lambda i=i: nc.vector.scalar_tensor_tensor(out=yb[i][:], in0=gg[i][0][:], scalar=nscr["rstd"][:, i:i + 1],
                                                                       in1=snw[:], op0=ALU.mult, op1=ALU.mult),
                     R=[gg[i][1], nscr["r_rstd"], r_c], W=[r_yb[i]])
                K.dma("sp", y_d[cs[i], 128:384], yb[i][:], ds_y[i % 2], R=[r_yb[i]])
            yield
            K.op("act", lambda: nc.scalar.copy(out=O_["Sbf"][0][0][:], in_=Sbf[G][0][:]), R=[Sbf[G][1]], W=[O_["Sbf"][0][1]])
            yield

            yield

        groups = list(range(0, nT, G))
        run_gen(genA(groups[0], sets[0]))
        for gi, g0 in enumerate(groups):
            gB = genB(g0, sets[gi % 2], sets[1 - gi % 2])
            gA = genA(groups[gi + 1], sets[1 - gi % 2]) if gi + 1 < len(groups) else None
            while gA is not None or gB is not None:
                for _ in range(AW):
                    if gA is not None:
                        try:
                            next(gA)
                        except StopIteration:
                            gA = None
                if gB is not None:
                    try:
                        next(gB)
                    except StopIteration:
                        gB = None
        K.barrier()
        K.end_phase()
        K.stack = outer


def emit_mlstm2(K, C, hnT, r_hnT, P, y_d, S, W=None):
    nc = K.nc
    nB = S // 512
    outer = K.stack
    with ExitStack() as st:
        K.stack = st
        K.begin_phase()
        wfm, r_wfm = W["wM_fm"] if W is not None else load_w(K, P["wM_fm"], 256, "wMf")
        wg, r_wg = W["wM_g"] if W is not None else load_w(K, P["wM_g"], 4, "wMg")
        wtm, r_wtm = W["wM_tm"] if W is not None else load_w(K, P["wM_tm"], 384, "wMt")
        dsm = K.new_dma_sem()
        r_c = Reg("mconst")
        gbias = K.sb([2, 2], F32, "gbias"); mnw = K.sb([128, 128], F32, "mnw")
        K.dma("act", gbias[:], P["gbias"], dsm, W=[r_c])
        K.dma("act", mnw[:], P["mnw"], dsm, W=[r_c])

        def rh(tok0):
            return r_hnT[tok0 // 1024] if isinstance(r_hnT, (list, tuple)) else r_hnT
        ident_f = C["ident_f"]
        Y = [psbank(K, f"my{i}") for i in range(7)]
        rY = [Reg(f"my{i}", excl=True) for i in range(7)]
        pq, pk, pgi, pgf = Y[0], Y[1], Y[2][0:2, :], Y[3][0:2, :]
        ptl = Y[4][:, 0:32].rearrange("p (q i h) -> p q i h", q=4, i=4)
        pdec = Y[4][:, 32:40]
        pC = Y[4][:, 64:129]
        pD = [Y[2].rearrange("p (i t) -> p i t", t=128), Y[3].rearrange("p (i t) -> p i t", t=128)]
        pS = [Y[0].rearrange("p (i t) -> p i t", t=128), Y[1].rearrange("p (i t) -> p i t", t=128)]
        pQ = [Y[0][:, 0:260].rearrange("p (i v) -> p i v", v=65), Y[1][:, 0:260].rearrange("p (i v) -> p i v", v=65)]
        pN = [Y[5][:, 0:260].rearrange("p (i v) -> p i v", v=65), Y[6][:, 0:260].rearrange("p (i v) -> p i v", v=65)]
        ptm = [Y[5][:, 0:384], Y[6][:, 0:384]]

        def T(shape, dt, name):
            return K.sb(shape, dt, name), Reg(name)
        qTb, r_qTb = T([128, 512], BF16, "mqTb")
        kTb, r_kTb = T([128, 512], BF16, "mkTb")
        rows = {}
        for n_ in ["ipre", "yv", "e", "b", "al", "cma", "mu", "nmu", "wrow", "inter", "en", "tmp"]:
            rows[n_] = T([2, 512], F32, "mr_" + n_)
        rows["nab"] = rows["e"]; rows["l"] = rows["e"]
        rows["logf"] = rows["yv"]
        mnew, r_mnew = T([2, 8], F32, "mnew")
        mprev, r_mprev = T([2, 8], F32, "mprev")
        mcar, r_mcar = T([2, 1], F32, "mcar")
        decay, r_decay = T([2, 8], F32, "mdecay")
        tl, r_tl = T([128, 4, 4, 2], F32, "mtl")
        decr, r_decr = T([128, 8], F32, "mdecr")
        ktm, r_ktm = T([128, 4, 128], F32, "mktm")
        vaug, r_vaug = T([128, 4, 2, 65], BF16, "mvaug")
        og, r_og = T([128, 4, 128], F32, "mog")
        dT, r_dT = T([128, 2, 4, 128], F32, "mdT")
        sdT, r_sdT = T([128, 2, 4, 128], BF16, "msdT")
        nmv, r_nmv = T([128, 2, 4, 65], F32, "mnmv")
        kw, r_kw = T([128, 2, 4, 64], BF16, "mkw")
        Cst, r_Cst = T([128, 65], F32, "mCst")
        Cbf, r_Cbf = T([128, 9, 65], BF16, "mCbf")
        r_Cb = [Reg(f"Cb{i}") for i in range(9)]
        tq, r_tq = T([128, 2, 4, 65], F32, "mtq")
        dn, r_dn = T([128, 2, 4], F32, "mdn")
        rn, r_rn = T([128, 2, 4], F32, "mrn")
        hm, r_hm = T([128, 2, 4, 64], F32, "mhm")
        sqv, r_sqv = T([128, 2, 4, 64], F32, "msqv")
        ssq, r_ssq = T([128, 8], F32, "mssq")
        nscr = mk_scr(K, [128, 8], "mn")
        hn2, r_hn2 = T([128, 2, 4, 64], F32, "mhn2")
        yb = [K.sb([128, 4, 128], BF16, f"myb{i}") for i in range(2)]; r_yb = [Reg() for _ in range(2)]
        ds_y = [K.new_dma_sem() for _ in range(2)]
        K.op("dve", lambda: nc.vector.memset(Cst[:], 0.0), W=[r_Cst])
        K.op("dve", lambda: nc.vector.memset(Cbf[:].rearrange("p a b -> p (a b)"), 0.0), W=r_Cb)
        K.op("dve", lambda: nc.vector.memset(mcar[:], 0.0), W=[r_mcar])
        K.op("dve", lambda: nc.vector.memset(vaug[:].rearrange("p a b c -> p (a b c)"), 1.0), W=[r_vaug])

        def R_(n_):
            return rows[n_][0]

        def rr(n_):
            return rows[n_][1]
        rowc = C["rowc"]
        for b in range(nB):
            bs = slice(b * 512, (b + 1) * 512)
            for (pp_, rp_, c0, dst, rdst, sc) in [(pq, rY[0], 0, qTb, r_qTb, 1.0), (pk, rY[1], 128, kTb, r_kTb, 0.125)]:
                for k in range(8):
                    K.op("pe", lambda k=k: nc.tensor.matmul(pp_, lhsT=wfm[:, k, c0:c0 + 128], rhs=hnT[:, k, bs],
                                                            start=(k == 0), stop=(k == 7)), R=[r_wfm, rh(b * 512)], W=[rp_])
                K.op("act", lambda: nc.scalar.mul(out=dst[:], in_=pp_, mul=sc), R=[rp_], W=[rdst])
            for (pp_, rp_, c0) in [(pgi, rY[2], 0), (pgf, rY[3], 2)]:
                for k in range(8):
                    K.op("pe", lambda k=k: nc.tensor.matmul(pp_, lhsT=wg[:, k, c0:c0 + 2], rhs=hnT[:, k, bs],
                                                            start=(k == 0), stop=(k == 7)), R=[r_wg, rh(b * 512)], W=[rp_])
            K.op("dve", lambda: nc.vector.tensor_scalar(out=R_("ipre")[:], in0=pgi, scalar1=gbias[:, 0:1], scalar2=None,
                                                        op0=ALU.add), R=[rY[2], r_c], W=[rr("ipre")])
            K.op("dve", lambda: nc.vector.tensor_scalar(out=R_("yv")[:], in0=pgf, scalar1=gbias[:, 1:2], scalar2=-1.0,
                                                        op0=ALU.add, op1=ALU.mult), R=[rY[3], r_c], W=[rr("yv")])
            K.op("dve", lambda: nc.vector.scalar_tensor_tensor(out=R_("nab")[:], in0=R_("yv")[:], scalar=-1.0, in1=R_("yv")[:],
                                                               op0=ALU.mult, op1=ALU.min), R=[rr("yv")], W=[rr("nab")])
            K.op("act", lambda: nc.scalar.activation(out=R_("e")[:], in_=R_("nab")[:], func=AF.Exp), R=[rr("nab")], W=[rr("e")])
            K.op("dve", lambda: nc.vector.tensor_scalar(out=R_("e")[:], in0=R_("e")[:], scalar1=1.0, scalar2=None, op0=ALU.add),
                 R=[rr("e")], W=[rr("e")])
            K.op("act", lambda: nc.scalar.activation(out=R_("l")[:], in_=R_("e")[:], func=AF.Ln), R=[rr("e")], W=[rr("l")])
            K.op("dve", lambda: nc.vector.scalar_tensor_tensor(out=R_("logf")[:], in0=R_("yv")[:], scalar=0.0, in1=R_("l")[:],
                                                               op0=ALU.max, op1=ALU.add), R=[rr("yv"), rr("l")], W=[rr("logf")])
            K.op("dve", lambda: nc.vector.tensor_scalar(out=R_("logf")[:], in0=R_("logf")[:], scalar1=-1.0, scalar2=None,
                                                        op0=ALU.mult), R=[rr("logf")], W=[rr("logf")])
            K.op("dve", lambda: nc.vector.tensor_tensor_scan(out=R_("b")[:], data0=rowc[:, 0, :], data1=R_("logf")[:],
                                                             initial=0.0, op0=ALU.mult, op1=ALU.add),
                 R=[rr("logf"), C["r"]], W=[rr("b")])
            K.op("dve", lambda: nc.vector.tensor_tensor(out=R_("al")[:], in0=R_("ipre")[:], in1=R_("b")[:], op=ALU.subtract),
                 R=[rr("ipre"), rr("b")], W=[rr("al")])
            K.op("dve", lambda: nc.vector.tensor_tensor_scan(out=R_("cma")[:], data0=rowc[:, 1, :], data1=R_("al")[:],
                                                             initial=0.0, op0=ALU.add, op1=ALU.max),
                 R=[rr("al"), C["r"]], W=[rr("cma")])
            cma3 = R_("cma")[:].rearrange("p (c l) -> p c l", l=64)
            b3 = R_("b")[:].rearrange("p (c l) -> p c l", l=64)
            al3 = R_("al")[:].rearrange("p (c l) -> p c l", l=64)
            mu3 = R_("mu")[:].rearrange("p (c l) -> p c l", l=64)
            tmp3 = R_("tmp")[:].rearrange("p (c l) -> p c l", l=64)
            K.op("dve", lambda: nc.vector.tensor_tensor_scan(out=mnew[:], data0=cma3[:, :, 63], data1=b3[:, :, 63],
                                                             initial=mcar[:, 0:1], op0=ALU.max, op1=ALU.add),
                 R=[rr("cma"), rr("b"), r_mcar], W=[r_mnew])
            K.op("dve", lambda: nc.vector.tensor_copy(out=mprev[:, 0:1], in_=mcar[:]), R=[r_mcar], W=[r_mprev])
            K.op("dve", lambda: nc.vector.tensor_copy(out=mprev[:, 1:8], in_=mnew[:, 0:7]), R=[r_mnew], W=[r_mprev])
            K.op("dve", lambda: nc.vector.tensor_copy(out=mcar[:], in_=mnew[:, 7:8]), R=[r_mnew, r_mprev], W=[r_mcar])
            mpb = mprev[:].unsqueeze(2).to_broadcast([2, 8, 64])
            K.op("dve", lambda: nc.vector.tensor_tensor(out=mu3, in0=cma3, in1=mpb, op=ALU.max),
                 R=[rr("cma"), r_mprev], W=[rr("mu")])
            K.op("dve", lambda: nc.vector.tensor_scalar(out=R_("nmu")[:], in0=R_("mu")[:], scalar1=-1.0, scalar2=None,
                                                        op0=ALU.mult), R=[rr("mu")], W=[rr("nmu")])
            mcb = mu3[:, :, 63].unsqueeze(2).to_broadcast([2, 8, 64])
            K.op("dve", lambda: nc.vector.tensor_tensor(out=tmp3, in0=al3, in1=mcb, op=ALU.subtract),
                 R=[rr("al"), rr("mu")], W=[rr("tmp")])
            K.op("act", lambda: nc.scalar.activation(out=R_("wrow")[:], in_=R_("tmp")[:], func=AF.Exp), R=[rr("tmp")], W=[rr("wrow")])
            K.op("dve", lambda: nc.vector.tensor_tensor(out=decay[:], in0=mprev[:], in1=mu3[:, :, 63], op=ALU.subtract),
                 R=[r_mprev, rr("mu")], W=[r_decay])
            K.op("act", lambda: nc.scalar.activation(out=decay[:], in_=decay[:], func=AF.Exp), R=[r_decay], W=[r_decay])
            K.op("dve", lambda: nc.vector.tensor_tensor(out=tmp3, in0=mu3, in1=mpb, op=ALU.subtract),
                 R=[rr("mu"), r_mprev, rr("wrow")], W=[rr("tmp")])
            K.op("act", lambda: nc.scalar.activation(out=R_("inter")[:], in_=R_("tmp")[:], func=AF.Exp, scale=-1.0),
                 R=[rr("tmp")], W=[rr("inter")])
            K.op("dve", lambda: nc.vector.tensor_tensor(out=R_("tmp")[:], in0=R_("b")[:], in1=R_("mu")[:], op=ALU.add),
                 R=[rr("b"), rr("mu"), rr("inter")], W=[rr("tmp")])
            K.op("act", lambda: nc.scalar.activation(out=R_("en")[:], in_=R_("tmp")[:], func=AF.Exp, scale=-1.0),
                 R=[rr("tmp")], W=[rr("en")])
            for qi, qn_ in enumerate(["al", "wrow", "inter", "en"]):
                for i in range(4):
                    K.op("pe", lambda qi=qi, i=i, qn_=qn_: nc.tensor.transpose(
                        out=ptl[:, qi, i, :], in_=R_(qn_)[0:2, i * 128:(i + 1) * 128], identity=ident_f[0:2, 0:2]),
                        R=[rr(qn_), C["r"]], W=[rY[4]])
            K.op("pe", lambda: nc.tensor.matmul(pdec, lhsT=C["hsel"][:], rhs=decay[:], start=True, stop=True),
                 R=[r_decay, C["r"]], W=[rY[4]])
            K.op("dve", lambda: nc.vector.tensor_copy(out=tl[:], in_=ptl), R=[rY[4]], W=[r_tl])
            K.op("dve", lambda: nc.vector.tensor_copy(out=decr[:], in_=pdec), R=[rY[4]], W=[r_decr])
            for i in range(4):
                ts = slice((b * 4 + i) * 128, (b * 4 + i + 1) * 128)
                pt_ = ptm[i % 2]; rpt = rY[5 + i % 2]
                for k in range(8):
                    K.op("pe", lambda k=k: nc.tensor.matmul(pt_, lhsT=hnT[:, k, ts], rhs=wtm[:, k, :],
                                                            start=(k == 0), stop=(k == 7)), R=[rh(b * 512), r_wtm], W=[rpt])
                K.op("act", lambda i=i: nc.scalar.mul(out=ktm[:, i, :], in_=pt_[:, 0:128], mul=0.125), R=[rpt], W=[r_ktm])
                K.op("dve", lambda i=i: nc.vector.tensor_copy(out=vaug[:, i, :, 0:64],
                                                              in_=pt_[:, 128:256].rearrange("p (h d) -> p h d", d=64)),
                     R=[rpt], W=[r_vaug])
                K.op("act", lambda i=i: nc.scalar.activation(out=og[:, i, :], in_=pt_[:, 256:384], func=AF.Sigmoid),
                     R=[rpt], W=[r_og])
            for h in range(2):
                K.op("pe", lambda h=h: nc.tensor.matmul(pD[h].rearrange("p i t -> p (i t)"), lhsT=C["sel"][:, h, :],
                                                        rhs=R_("nmu")[:], start=True, stop=False),
                     R=[rr("nmu"), C["r"]], W=[rY[2 + h]])
                K.op("pe", lambda h=h: nc.tensor.matmul(pD[h].rearrange("p i t -> p (i t)"), lhsT=ident_f[:],
                                                        rhs=C["nm2"], start=False, stop=True),
                     R=[C["r"]], W=[rY[2 + h]])
            for h in range(2):
                for i in range(4):
                    K.op("act", lambda h=h, i=i: nc.scalar.activation(out=dT[:, h, i, :], in_=pD[h][:, i, :], func=AF.Exp,
                                                                      bias=tl[:, 0, i, h:h + 1]),
                         R=[rY[2 + h], r_tl], W=[r_dT])
            for h in range(2):
                hp = slice(64 * h, 64 * h + 64)
                for i in range(4):
                    tb = slice(i * 128, (i + 1) * 128)
                    K.op("pe", lambda h=h, i=i: nc.tensor.matmul(pS[h][:, i, :], lhsT=kTb[hp, tb], rhs=qTb[hp, tb],
                                                                 start=True, stop=True), R=[r_kTb, r_qTb], W=[rY[h]])
                K.op("dve", lambda h=h: nc.vector.tensor_tensor(out=sdT[:, h, :, :], in0=pS[h], in1=dT[:, h, :, :], op=ALU.mult),
                     R=[rY[h], r_dT], W=[r_sdT])
            for h in range(2):
                for i in range(4):
                    K.op("pe", lambda h=h, i=i: nc.tensor.matmul(pN[h][:, i, :], lhsT=sdT[:, h, i, :], rhs=vaug[:, i, h, :],
                                                                 start=True, stop=True), R=[r_sdT, r_vaug], W=[rY[5 + h]])
                K.op("dve", lambda h=h: nc.vector.tensor_copy(out=nmv[:, h, :, :], in_=pN[h]), R=[rY[5 + h]], W=[r_nmv])
            for h in range(2):
                hc = slice(64 * h, 64 * h + 64)
                K.op("dve", lambda h=h: nc.vector.tensor_tensor(out=kw[:, h, :, :], in0=ktm[:, :, hc],
                                                                in1=tl[:, 1, :, h].unsqueeze(2).to_broadcast([128, 4, 64]),
                                                                op=ALU.mult), R=[r_ktm, r_tl], W=[r_kw])
            for ce in range(8):
                i, half = ce // 2, ce % 2
                rs_ = slice(64 * half, 64 * half + 64)
                for h in range(2):
                    hp = slice(64 * h, 64 * h + 64)
                    K.op("pe", lambda h=h: nc.tensor.matmul(pC[hp, :], lhsT=kw[rs_, h, i, :], rhs=vaug[rs_, i, h, :],
                                                            start=True, stop=True), R=[r_kw, r_vaug], W=[rY[4]])
                K.op("dve", lambda: nc.vector.scalar_tensor_tensor(out=Cst[:], in0=Cst[:], scalar=decr[:, ce:ce + 1], in1=pC,
                                                                   op0=ALU.mult, op1=ALU.add), R=[r_Cst, r_decr, rY[4]], W=[r_Cst])
                K.op("act", lambda: nc.scalar.copy(out=Cbf[:, ce + 1, :], in_=Cst[:]), R=[r_Cst], W=[r_Cb[ce + 1]])
            for h in range(2):
                hp = slice(64 * h, 64 * h + 64)
                for ce in range(8):
                    i, half = ce // 2, ce % 2
                    rs_ = slice(64 * half, 64 * half + 64)
                    tc = slice(i * 128 + 64 * half, i * 128 + 64 * half + 64)
                    K.op("pe", lambda h=h: nc.tensor.matmul(pQ[h][rs_, i, :], lhsT=qTb[hp, tc], rhs=Cbf[hp, ce, :],
                                                            start=True, stop=True), R=[r_qTb, r_Cb[ce]], W=[rY[h]])
            ybt = yb[b % 2]
            for h in range(2):
                hc = slice(64 * h, 64 * h + 64)
                K.op("dve", lambda h=h: nc.vector.tensor_tensor(out=tq[:, h, :, :], in0=pQ[h],
                                                                in1=tl[:, 2, :, h].unsqueeze(2).to_broadcast([128, 4, 65]),
                                                                op=ALU.mult), R=[rY[h], r_tl], W=[r_tq])
                K.op("dve", lambda h=h: nc.vector.tensor_tensor(out=nmv[:, h, :, :], in0=nmv[:, h, :, :], in1=tq[:, h, :, :],
                                                                op=ALU.add), R=[r_nmv, r_tq], W=[r_nmv])
                K.op("dve", lambda h=h: nc.vector.scalar_tensor_tensor(out=dn[:, h, :], in0=nmv[:, h, :, 64], scalar=-1.0,
                                                                       in1=nmv[:, h, :, 64], op0=ALU.mult, op1=ALU.max),
                     R=[r_nmv], W=[r_dn])
                K.op("dve", lambda h=h: nc.vector.tensor_tensor(out=dn[:, h, :], in0=dn[:, h, :], in1=tl[:, 3, :, h], op=ALU.max),
                     R=[r_dn, r_tl], W=[r_dn])
                K.op("dve", lambda h=h: nc.vector.reciprocal(out=rn[:, h, :], in_=dn[:, h, :]), R=[r_dn], W=[r_rn])
                K.op("dve", lambda h=h: nc.vector.tensor_tensor(out=hm[:, h, :, :], in0=nmv[:, h, :, 0:64],
                                                                in1=rn[:, h, :].unsqueeze(2).to_broadcast([128, 4, 64]),
                                                                op=ALU.mult), R=[r_nmv, r_rn], W=[r_hm])
                K.op("act", lambda h=h: nc.scalar.activation(out=sqv[:, h, :, :], in_=hm[:, h, :, :], func=AF.Square),
                     R=[r_hm], W=[r_sqv])
            K.op("dve", lambda: nc.vector.tensor_reduce(out=ssq[:], in_=sqv[:].rearrange("p h i d -> p (h i) d"),
                                                        axis=AX.X, op=ALU.add), R=[r_sqv], W=[r_ssq])
            rstd_from_ssq(K, ssq[:], r_ssq, 8, nscr, 1.0 / 64)
            rs3 = nscr["rstd"].rearrange("p (h i) -> p h i", i=4)
            for h in range(2):
                hc = slice(64 * h, 64 * h + 64)
                K.op("dve", lambda h=h: nc.vector.tensor_tensor(out=hn2[:, h, :, :], in0=hm[:, h, :, :],
                                                                in1=rs3[:, h, :].unsqueeze(2).to_broadcast([128, 4, 64]),
                                                                op=ALU.mult), R=[r_hm, nscr["r_rstd"]], W=[r_hn2])
                K.op("dve", lambda h=h: nc.vector.tensor_tensor(out=hn2[:, h, :, :], in0=hn2[:, h, :, :],
                                                                in1=mnw[:, hc].unsqueeze(1).to_broadcast([128, 4, 64]),
                                                                op=ALU.mult), R=[r_hn2, r_c], W=[r_hn2])
                K.op("dve", lambda h=h: nc.vector.tensor_tensor(out=ybt[:, :, hc], in0=hn2[:, h, :, :], in1=og[:, :, hc],
                                                                op=ALU.mult), R=[r_hn2, r_og], W=[r_yb[b % 2]])
            K.dma("sp", y_d[b * 512:(b + 1) * 512, 0:128].rearrange("(i p) c -> p i c", p=128), ybt[:], ds_y[b % 2],
                  R=[r_yb[b % 2]])
            K.op("act", lambda: nc.scalar.copy(out=Cbf[:, 0, :], in_=Cbf[:, 8, :]), R=[r_Cb[8]], W=[r_Cb[0]])
        K.barrier()
        K.end_phase()
        K.stack = outer


NEGV = np.float32(-30000.0)
OFF = {}
_names = ["mq", "mk", "mv", "mo", "mi", "mf", "z", "xbc", "dt", "dq", "dk", "dv"]
_sizes = [256, 256, 256, 256, 4, 4, 512, 1024, 8, 256, 256, 256]
_o = 0
for n, s in zip(_names, _sizes):
    OFF[n] = _o
    _o += s


def t5_bucket_np(rel):
    n = np.maximum(rel, 0)
    nf = np.maximum(n, 1).astype(np.float32)
    large = 16 + (np.log(nf / np.float32(16)) / np.float32(math.log(128 / 16)) * np.float32(16)).astype(np.int32)
    large = np.minimum(large, 31)
    return np.where(n < 16, n, large)


def tile_w(w):
    n = w.shape[1]
    return np.ascontiguousarray(w.reshape(8, 128, n).transpose(1, 0, 2))


def rep(v, n=128):
    v = np.asarray(v, dtype=np.float32).reshape(1, -1)
    return np.ascontiguousarray(np.broadcast_to(v, (n, v.shape[1])))


def cols(w, name, a, b):
    return w[:, OFF[name] + a: OFF[name] + b]


def mixer_params(inp, l, h):
    w = np.asarray(inp["w_in"][l])
    P = {}
    P["wD"] = tile_w(np.concatenate([cols(w, "dq", 128 * h, 128 * h + 128), cols(w, "dk", 128 * h, 128 * h + 128),
                                     cols(w, "dv", 128 * h, 128 * h + 128)], axis=1))
    qn = np.asarray(inp["diff_q_norm_w"][l]).reshape(64)
    kn = np.asarray(inp["diff_k_norm_w"][l]).reshape(64)
    P["qkw"] = rep(np.concatenate([qn, qn, kn, kn]))
    P["sw"] = rep(np.asarray(inp["diff_subln_w"][l]))
    P["lamb"] = rep(np.asarray(inp["diff_lambda"][l]).reshape(-1)).reshape(128, 4, 32)
    rb = np.asarray(inp["rel_bias"])
    kl = np.arange(128)[:, None]
    c = np.arange(1024)[None, :]
    rel = c - kl - 384
    bk = t5_bucket_np(rel)
    Bt = np.zeros((128, 2, 1024), np.float32)
    for hl in range(2):
        Bt[:, hl, :] = np.where(rel >= 0, rb[bk, 2 * h + hl], NEGV)
    P["Bt"] = Bt
    P["c31"] = rep(rb[31, 2 * h: 2 * h + 2])
    xb = OFF["xbc"]
    ch = np.concatenate([np.arange(256 * h, 256 * h + 256), 512 + np.arange(128 * h, 128 * h + 128),
                         768 + np.arange(128 * h, 128 * h + 128)])
    P["wS_fm"] = tile_w(w[:, xb + ch])
    P["wS_tm"] = tile_w(np.concatenate([cols(w, "z", 256 * h, 256 * h + 256), cols(w, "dt", 4 * h, 4 * h + 4)], axis=1))
    cw = np.asarray(inp["ssm_conv_w"][l])[:, ch]
    P["cw"] = np.ascontiguousarray(cw.reshape(4, 4, 128).transpose(2, 1, 0))
    P["cb"] = np.ascontiguousarray(np.asarray(inp["ssm_conv_b"][l])[ch].reshape(4, 128).T)
    P["dtb"] = rep(np.asarray(inp["ssm_dt_bias"][l])[4 * h:4 * h + 4])
    P["alog"] = rep(np.asarray(inp["ssm_A_log"][l])[4 * h:4 * h + 4])
    P["dsk"] = rep(np.asarray(inp["ssm_D"][l])[4 * h:4 * h + 4])
    P["snw"] = rep(np.asarray(inp["ssm_norm_w"][l])[256 * h:256 * h + 256])
    P["wM_fm"] = tile_w(np.concatenate([cols(w, "mq", 128 * h, 128 * h + 128), cols(w, "mk", 128 * h, 128 * h + 128)], axis=1))
    P["wM_g"] = tile_w(np.concatenate([cols(w, "mi", 2 * h, 2 * h + 2), cols(w, "mf", 2 * h, 2 * h + 2)], axis=1))
    P["wM_tm"] = tile_w(np.concatenate([cols(w, "mk", 128 * h, 128 * h + 128), cols(w, "mv", 128 * h, 128 * h + 128),
                                        cols(w, "mo", 128 * h, 128 * h + 128)], axis=1))
    gb = np.asarray(inp["mlstm_gate_bias"][l])
    P["gbias"] = np.ascontiguousarray(gb[:, 2 * h:2 * h + 2].T)
    P["mnw"] = rep(np.asarray(inp["mlstm_norm_w"][l])[128 * h:128 * h + 128])
    return {k: np.ascontiguousarray(v, dtype=np.float32) for k, v in P.items()}


def const_arrays():
    Cn = {}
    Cn["ident_bf"] = np.eye(128, dtype=np.float32).astype(ml_dtypes.bfloat16)
    Cn["ident_f"] = np.eye(128, dtype=np.float32)
    j = np.arange(128)
    tri = (j[:, None] <= j[None, :]).astype(np.float32)
    same = (j[:, None] // 64) == (j[None, :] // 64)
    nm2 = np.where(same & (j[:, None] <= j[None, :]), 0.0, NEGV).astype(np.float32)
    sel = np.zeros((2, 2, 128), np.float32)
    sel[0, 0, :] = 1
    sel[1, 1, :] = 1
    hs = np.zeros((2, 128), np.float32)
    hs[0, :64] = 1
    hs[1, 64:] = 1
    nm1 = np.where(j[:, None] <= j[None, :], 0.0, NEGV).astype(np.float32)
    Cn["cf"] = np.ascontiguousarray(np.concatenate([np.ones((128, 128), np.float32), tri, np.tile(nm2, (1, 4)),
                                                    np.tile(nm1, (1, 4))], axis=1))
    Cn["sel"] = sel
    Cn["hsel"] = hs
    t = np.arange(512)
    rm = (t % 64 != 0).astype(np.float32)
    Cn["rowc"] = np.ascontiguousarray(np.stack([np.stack([rm, rm]), np.stack([(1 - rm) * np.float32(-1e30)] * 2)], axis=1))
    return Cn


import math as _math
from concourse.bass_utils import run_bass_kernel_spmd

NCORES = 8
SEQ = 4096
TOK = 2048
DEPTH = 2
PAIRS = [[0, 1], [2, 3], [4, 5], [6, 7]]


def _tile_gu(w):
    return np.ascontiguousarray(np.asarray(w).reshape(8, 128, 22, 128).transpose(2, 1, 0, 3).reshape(22, 128, 1024))


def _tile_nw(w):
    return np.ascontiguousarray(np.asarray(w).reshape(8, 128).T)


def ffn_host(inp, which, l, tag):
    return {
        tag + "nw": _tile_nw(inp[which + "_norm_w"][l]),
        tag + "wg": _tile_gu(inp[which + "_w_gate"][l]),
        tag + "wu": _tile_gu(inp[which + "_w_up"][l]),
        tag + "wd": np.ascontiguousarray(np.asarray(inp[which + "_w_down"][l]).reshape(22, 128, 1024)),
    }


def ffn_decl(nc, tag):
    d = {}
    d["nw"] = nc.dram_tensor(tag + "nw", [128, 8], F32, kind="ExternalInput").ap()
    d["wg"] = nc.dram_tensor(tag + "wg", [22, 128, 1024], F32, kind="ExternalInput").ap()
    d["wu"] = nc.dram_tensor(tag + "wu", [22, 128, 1024], F32, kind="ExternalInput").ap()
    d["wd"] = nc.dram_tensor(tag + "wd", [22, 128, 1024], F32, kind="ExternalInput").ap()
    return d


def wout_host(inp, l):
    perm = []
    for h in range(2):
        perm += list(range(128 * h, 128 * h + 128))
        perm += list(range(256 + 256 * h, 256 + 256 * h + 256))
        perm += list(range(768 + 128 * h, 768 + 128 * h + 128))
    w = np.asarray(inp["w_out"][l])[np.array(perm), :]
    return np.ascontiguousarray(w.reshape(8, 128, 1024))


def lam_init_of(l):
    return 0.8 - 0.6 * _math.exp(-0.3 * l)


def build_fused(Pshapes, Cn):
    nc = bass.Bass("TRN2", target_bir_lowering=False, num_devices=NCORES)
    Dc = {"ident_bf": nc.dram_tensor("ident_bf", [128, 128], BF16, kind="ExternalInput").ap(),
          "ident_f": nc.dram_tensor("ident_f", [128, 128], F32, kind="ExternalInput").ap()}
    Dk = {k: nc.dram_tensor(k, list(Cn[k].shape), F32, kind="ExternalInput").ap() for k in ["cf", "sel", "hsel", "rowc"]}
    x_d = nc.dram_tensor("x", [TOK, 1024], F32, kind="ExternalInput").ap()
    mh_d = nc.dram_tensor("mh", [128, 2], F32, kind="ExternalInput").ap()
    out_d = nc.dram_tensor("out", [TOK, 1024], F32, kind="ExternalOutput").ap()
    L = []
    for l in range(DEPTH):
        d = {"f1": ffn_decl(nc, f"a{l}_"), "f2": ffn_decl(nc, f"c{l}_")}
        d["mixnw"] = nc.dram_tensor(f"mixnw{l}", [128, 8], F32, kind="ExternalInput").ap()
        d["wo"] = nc.dram_tensor(f"wo{l}", [8, 128, 1024], F32, kind="ExternalInput").ap()
        d["P"] = {k: nc.dram_tensor(f"m{l}_{k}", list(shp), F32, kind="ExternalInput").ap() for k, shp in Pshapes.items()}
        d["P"].update(Dk)
        d["x1"] = nc.dram_tensor(f"x1_{l}", [TOK, 1024], F32, kind="Internal").ap()
        d["hn_own"] = [nc.dram_tensor(f"hn_own{l}_{i}", [1024, 1024], BF16, kind="Internal").ap() for i in range(2)]
        d["hn_all"] = [nc.dram_tensor(f"hn_all{l}_{i}", [2 * 1024, 1024], BF16, kind="Internal").ap() for i in range(2)]
        d["y_half"] = [nc.dram_tensor(f"y_half{l}_{i}", [TOK, 512], BF16, kind="Internal").ap() for i in range(2)]
        d["y_all"] = [nc.dram_tensor(f"y_all{l}_{i}", [2 * TOK, 512], BF16, kind="Internal").ap() for i in range(2)]
        d["r_hn_all"] = [Reg(f"hn_all{l}_{i}") for i in range(2)]
        d["r_y_all"] = [Reg(f"y_all{l}_{i}") for i in range(2)]
        d["x2"] = nc.dram_tensor(f"x2_{l}", [TOK, 1024], F32, kind="Internal").ap()
        d["x3"] = out_d if l == DEPTH - 1 else nc.dram_tensor(f"x3_{l}", [TOK, 1024], F32, kind="Internal").ap()
        L.append(d)
    with ExitStack() as st:
        K = KB(nc, st)
        C = load_consts(K, Dc["ident_bf"], Dc["ident_f"])
        load_mixer_consts(K, C, Dk)
        csem = K.new_dma_sem()
        xin = x_d
        for l in range(DEPTH):
            d = L[l]
            f1, f2 = d["f1"], d["f2"]
            def hn_block_done(b, reg, d=d):
                K.collective("AllGather", d["hn_own"][b], d["hn_all"][b], PAIRS, csem, R=[reg], W=[d["r_hn_all"][b]])
            emit_ffn(K, C, xin, d["x1"], f1["nw"], f1["wg"], f1["wu"], f1["wd"], TOK,
                     hn_out=[ho.rearrange("(k p) t -> k p t", p=128) for ho in d["hn_own"]], nw2_d=d["mixnw"],
                     on_block=hn_block_done)
            with ExitStack() as st2:
                outer = K.stack
                K.stack = st2
                K.begin_phase()
                Wm, stg_stack = load_w_all(K, [("wM_fm", d["P"]["wM_fm"], 256), ("wM_g", d["P"]["wM_g"], 4),
                                               ("wM_tm", d["P"]["wM_tm"], 384), ("wS_fm", d["P"]["wS_fm"], 512),
                                               ("wS_tm", d["P"]["wS_tm"], 260), ("wD", d["P"]["wD"], 384)])
                hnT, rq = load_hnT_pair(K, d["hn_all"], SEQ, regs=d["r_hn_all"])
                r_hnT = rq[-1]
                ysp = YSplit(d["y_half"][0], d["y_half"][1], TOK)

                def y_hook(t, d=d, ysp=ysp):
                    if t == SEQ // 1024 - 1:
                        K.collective("AllGather", d["y_half"][0], d["y_all"][0], PAIRS, csem, R=[ysp.regs[0]], W=[d["r_y_all"][0]])
                ysp.hook = y_hook
                emit_mlstm2(K, C, hnT, rq, d["P"], ysp, SEQ, W=Wm)
                emit_ssd2(K, C, hnT, r_hnT, d["P"], ysp, SEQ, W=Wm)
                emit_diff(K, C, hnT, r_hnT, d["P"], ysp, SEQ, lam_init_of(l), W=Wm)
                K.end_phase()
                K.stack = outer
            K.collective("AllGather", d["y_half"][1], d["y_all"][1], PAIRS, csem, W=[d["r_y_all"][1]])
            emit_outproj_sel(K, C, d["x1"], d["y_all"], mh_d, d["wo"], d["x2"], TOK, SEQ, yregs=d["r_y_all"])
            emit_ffn(K, C, d["x2"], d["x3"], f2["nw"], f2["wg"], f2["wu"], f2["wd"], TOK)
            xin = d["x3"]
        K.barrier(full=True)
    return nc


def kernel(**inputs):
    inp = {k: np.asarray(v) for k, v in inputs.items()}
    x = inp["x"]
    cores = list(range(NCORES))
    Cn = const_arrays()
    shared = {"ident_bf": Cn["ident_bf"], "ident_f": Cn["ident_f"]}
    for k in ["cf", "sel", "hsel", "rowc"]:
        shared[k] = Cn[k]
    Ps = [[mixer_params(inp, l, h) for h in range(2)] for l in range(DEPTH)]
    for l in range(DEPTH):
        shared.update(ffn_host(inp, "ffn1", l, f"a{l}_"))
        shared.update(ffn_host(inp, "ffn2", l, f"c{l}_"))
        shared[f"mixnw{l}"] = _tile_nw(inp["mix_norm_w"][l])
        shared[f"wo{l}"] = wout_host(inp, l)
    nc = build_fused({k: v.shape for k, v in Ps[0][0].items()}, Cn)
    maps = []
    for c in cores:
        b, h = c // 2, c % 2
        m = dict(shared)
        m["x"] = np.ascontiguousarray(x[b, h * TOK:(h + 1) * TOK])
        mh = np.zeros((128, 2), np.float32)
        mh[:, h] = 1.0
        m["mh"] = mh
        for l in range(DEPTH):
            for k, v in Ps[l][h].items():
                m[f"m{l}_{k}"] = v
        maps.append(m)
    res = run_bass_kernel_spmd(nc, maps, core_ids=cores).results
    out = np.zeros((4, SEQ, 1024), np.float32)
    for c in cores:
        b, h = c // 2, c % 2
        out[b, h * TOK:(h + 1) * TOK] = np.asarray(res[c]["out"])
    return out
```

```python
import math
import numpy as np
import ml_dtypes
from contextlib import ExitStack
import concourse.bass as bass
import concourse.mybir as mybir


F32 = mybir.dt.float32
BF16 = mybir.dt.bfloat16
AF = mybir.ActivationFunctionType
ALU = mybir.AluOpType
AX = mybir.AxisListType


class Reg:
    __slots__ = ("lw", "rd", "name", "excl")

    def __init__(self, name="", excl=False):
        self.excl = excl
        self.lw = None
        self.rd = {}
        self.name = name


class KB:
    def __init__(self, nc, stack):
        self.nc = nc
        self.stack = stack
        self.eng = {"pe": nc.tensor, "dve": nc.vector, "act": nc.scalar,
                    "pool": nc.gpsimd, "sp": nc.sync}
        self.sem = {}
        self.cnt = {}
        self.waited = {}
        for n in self.eng:
            self.sem[n] = stack.enter_context(nc.semaphore("s_" + n))
            self.cnt[n] = 0
            self.waited[n] = {}
        self.top_stack = stack
        self.free_sems = []
        self.phase_sems = []
        self.lazy_keys = set()
        self.dma_sems = []
        self.dma_by_key = {}
        self.n_dma_sem = 0
        self.uid = 0

    def sb(self, shape, dt, name=None):
        self.uid += 1
        return self.stack.enter_context(
            self.nc.sbuf_tensor(f"sb{self.uid}_{name or ""}", list(shape), dt))

    def ps(self, shape, dt, name=None):
        self.uid += 1
        full = 512 if dt == F32 else 1024
        t = self.stack.enter_context(
            self.nc.psum_tensor(f"ps{self.uid}_{name or ""}", [128, full], dt))
        n = 1
        for d in shape[1:]:
            n *= d
        assert n <= full
        v = t[0:shape[0], 0:n]
        if len(shape) == 3:
            v = v.rearrange("p (a b) -> p a b", b=shape[2])
        elif len(shape) == 4:
            v = v.rearrange("p (a b c) -> p a b c", b=shape[2], c=shape[3])
        return v

    def new_dma_sem(self):
        if self.free_sems:
            ent = self.free_sems.pop()
        else:
            s = self.top_stack.enter_context(self.nc.semaphore(f"d{self.n_dma_sem}"))
            self.n_dma_sem += 1
            ent = [s, 0, f"d{self.n_dma_sem}"]
            self.dma_sems.append(ent)
            self.dma_by_key[ent[2]] = ent
        if self.phase_sems:
            self.phase_sems[-1].append(ent)
        return ent

    def begin_phase(self):
        self.phase_sems.append([])

    def end_phase(self):
        for ent in self.phase_sems.pop():
            self.free_sems.append(ent)

    def collective(self, kind, src, dst, groups, csem, R=(), W=()):
        self._deps("pool", R, W, is_dma=True)
        ins = self.nc.gpsimd.collective_compute(kind, mybir.AluOpType.bypass, replica_groups=groups,
                                                ins=[src], outs=[dst])
        csem[1] += 1
        ins.then_inc(csem[0], 1)
        tag = (csem[2], csem[0], csem[1])
        for w in W:
            w.lw = tag
            w.rd = {}
        self.lazy_keys.add(csem[2])
        return ins

    def _wait(self, e, dep):
        key, sem, val = dep
        if key in self.dma_by_key:
            val = self.dma_by_key[key][1]
        w = self.waited[e]
        if w.get(key, 0) >= val:
            return
        self.eng[e].wait_ge(sem, val)
        w[key] = val

    def _deps(self, e, R, W, is_dma=False):
        deps = []
        for r in R:
            if r.lw is not None:
                deps.append(r.lw)
            if r.excl:
                for d in r.rd.values():
                    if d[0] != e:
                        deps.append(d)
        for w in W:
            if w.lw is not None:
                deps.append(w.lw)
            for d in w.rd.values():
                if d[0] != e or is_dma:
                    deps.append(d)
        for d in deps:
            if d[0] == e and e == "pe" and not is_dma:
                continue
            self._wait(e, d)

    def op(self, e, fn, R=(), W=()):
        self._deps(e, R, W)
        ins = fn()
        self.cnt[e] += 1
        ins.then_inc(self.sem[e], 1)
        tag = (e, self.sem[e], self.cnt[e])
        for w in W:
            w.lw = tag
            w.rd = {}
        for r in R:
            if r not in W:
                r.rd[e] = tag
        return ins

    def dma(self, q, out, in_, dsem, R=(), W=(), **kw):
        self._deps(q, R, W, is_dma=True)
        ins = self.eng[q].dma_start(out=out, in_=in_, **kw)
        dsem[1] += 16
        ins.then_inc(dsem[0], 16)
        tag = (dsem[2], dsem[0], dsem[1])
        for w in W:
            w.lw = tag
            w.rd = {}
        for r in R:
            if r not in W:
                r.rd[dsem[2]] = tag
        return ins

    def barrier(self, full=False):
        deps = [(n, self.sem[n], self.cnt[n]) for n in self.eng if self.cnt[n] > 0]
        deps += [(d[2], d[0], d[1]) for d in self.dma_sems if d[1] > 0 and (full or d[2] not in self.lazy_keys)]
        for e in self.eng:
            for d in deps:
                if d[0] != e:
                    self._wait(e, d)


def psbank(K, name):
    K.uid += 1
    t = K.stack.enter_context(K.nc.psum_tensor(f"pb{K.uid}_{name}", [128, 512], F32))
    return t[:, :]


D = 1024
DFF = 2816
NF = DFF // 128
EPS = 1e-6


def load_consts(K, ident_bf_d, ident_f_d):
    C = {}
    C["dsem"] = K.new_dma_sem()
    C["ident_bf"] = K.sb([128, 128], BF16, "ident_bf")
    C["ident_f"] = K.sb([128, 128], F32, "ident_f")
    C["r"] = Reg("consts")
    K.dma("sp", C["ident_bf"][:], ident_bf_d, C["dsem"], W=[C["r"]])
    K.dma("sp", C["ident_f"][:], ident_f_d, C["dsem"], W=[C["r"]])
    return C


def emit_norm_stats(K, C, xt, r_xt, S):
    nc = K.nc
    K.op("act", lambda: nc.scalar.activation(out=S["junk"][:], in_=xt, func=AF.Square,
                                             accum_out=S["ssq"][:]),
         R=[r_xt], W=[S["r_junk"], S["r_ssq"]])
    K.op("dve", lambda: nc.vector.tensor_scalar(out=S["ms"][:], in0=S["ssq"][:], scalar1=1.0 / D,
                                                scalar2=EPS, op0=ALU.mult, op1=ALU.add),
         R=[S["r_ssq"]], W=[S["r_ms"]])
    K.op("act", lambda: nc.scalar.activation(out=S["sd"][:], in_=S["ms"][:], func=AF.Sqrt),
         R=[S["r_ms"]], W=[S["r_sd"]])
    K.op("dve", lambda: nc.vector.reciprocal(out=S["rstd"][:], in_=S["sd"][:]),
         R=[S["r_sd"]], W=[S["r_rstd"]])
    K.op("dve", lambda: nc.vector.tensor_scalar(out=S["xn"][:], in0=xt, scalar1=S["rstd"][:],
                                                scalar2=None, op0=ALU.mult),
         R=[r_xt, S["r_rstd"]], W=[S["r_xn"]])


def emit_norm_tr(K, C, nw, r_nw, dst_fn, r_dst, S):
    nc = K.nc
    for k in range(8):
        K.op("pe", lambda k=k: nc.tensor.transpose(out=S["ptr"][:, k, :],
                                                   in_=S["xn"][:, k * 128:(k + 1) * 128],
                                                   identity=C["ident_bf"][:]),
             R=[S["r_xn"], C["r"]], W=[S["r_ptr"]])
    K.op("dve", lambda: nc.vector.tensor_tensor(out=dst_fn, in0=S["ptr"][:],
                                                in1=nw.unsqueeze(2).to_broadcast([128, 8, 128]),
                                                op=ALU.mult),
         R=[S["r_ptr"], r_nw], W=[r_dst])


def emit_norm_T(K, C, xt, r_xt, nw, r_nw, dst_fn, r_dst, S):
    emit_norm_stats(K, C, xt, r_xt, S)
    emit_norm_tr(K, C, nw, r_nw, dst_fn, r_dst, S)


def norm_scratch(K, tag, share=0, ptr=None, r_ptr=None):
    S = {}
    if share is not None:
        S["junk"] = K.sb([128, D], BF16, tag + "junk")
    S["ssq"] = K.sb([128, 1], F32, tag + "ssq")
    S["ms"] = K.sb([128, 1], F32, tag + "ms")
    S["sd"] = K.sb([128, 1], F32, tag + "sd")
    S["rstd"] = K.sb([128, 1], F32, tag + "rstd")
    S["xn"] = K.sb([128, D], BF16, tag + "xn")
    S["ptr"] = K.ps([128, 8, 128], BF16, tag + "ptr") if ptr is None else ptr
    for n in ["junk", "ssq", "ms", "sd", "rstd", "xn", "ptr"]:
        S["r_" + n] = Reg(tag + n)
    if r_ptr is not None:
        S["r_ptr"] = r_ptr
    return S


def emit_ffn(K, C, x_in, x_out, nw_d, wg_d, wu_d, wd_d, ntok, hn_out=None, nw2_d=None,
             blk=1024, on_block=None):
    nc = K.nc
    outer = K.stack
    with ExitStack() as st:
        K.stack = st
        K.begin_phase()
        nblk = ntok // blk
        nsub = blk // 128
        ntt = blk // 512
        nw = K.sb([128, 8], F32, "nw")
        r_nw = Reg("nw")
        ds_misc = K.new_dma_sem()
        K.dma("sp", nw[:], nw_d, ds_misc, W=[r_nw])
        if hn_out is not None:
            nw2 = K.sb([128, 8], F32, "nw2")
            r_nw2 = Reg("nw2")
            K.dma("sp", nw2[:], nw2_d, ds_misc, W=[r_nw2])
            hnb = K.sb([128, 8, blk], BF16, "hnb")
            r_hnb = Reg("hnb")
            ds_hn = K.new_dma_sem()
        hT = K.sb([128, 8, blk], BF16, "hT")
        r_hT = [Reg(f"hT{j}") for j in range(nsub)]
        aT = K.sb([128, NF, blk], BF16, "aT")
        r_aT = [[Reg(f"aT{f}_{t}") for t in range(ntt)] for f in range(NF)]
        NX = 3
        xt = [K.sb([128, D], F32, f"xt{i}") for i in range(NX)]
        r_xt = [Reg(f"xt{i}") for i in range(NX)]
        ds_xt = [K.new_dma_sem() for i in range(NX)]
        Ss = [norm_scratch(K, "n1"), norm_scratch(K, "n2", share=None)]
        Ss[1]["junk"] = Ss[0]["junk"]; Ss[1]["r_junk"] = Ss[0]["r_junk"]
        NW = 3
        wst = [[K.sb([128, 8 * 128], F32, f"wst{i}_{g}") for g in range(2)] for i in range(NW)]
        r_wst = [[Reg() for g in range(2)] for i in range(NW)]
        ds_w = [K.new_dma_sem() for i in range(NW)]
        wb = [[K.sb([128, 8, 128], BF16, f"wb{i}_{g}") for g in range(2)] for i in range(NW)]
        r_wb = [[Reg() for g in range(2)] for i in range(NW)]
        wdb = K.sb([128, NF, D], BF16, "wdb")
        r_wdb = [Reg(f"wdb{f}") for f in range(NF)]
        NDS = 2
        wdst = [K.sb([128, D], F32, f"wdst{i}") for i in range(NDS)]
        r_wdst = [Reg() for i in range(NDS)]
        ds_wd = [K.new_dma_sem() for i in range(NDS)]
        pg = [K.ps([128, 512], F32, f"pg{i}") for i in range(2)]
        pu = [K.ps([128, 512], F32, f"pu{i}") for i in range(2)]
        r_pg = [Reg() for i in range(2)]
        r_pu = [Reg() for i in range(2)]
        sg = [K.sb([128, 512], F32, f"sg{i}") for i in range(2)]
        r_sg = [Reg() for i in range(2)]
        po = [K.ps([128, 512], F32, f"po{i}") for i in range(2)]
        r_po = [Reg() for i in range(2)]
        ot = [K.sb([128, D], F32, f"ot{i}") for i in range(2)]
        r_ot = [Reg() for i in range(2)]
        ds_ot = [K.new_dma_sem() for i in range(2)]

        Sn = norm_scratch(K, "n3", share=None, ptr=pg[0].bitcast(BF16).rearrange("p (a b) -> p a b", b=128),
                          r_ptr=r_pg[0])
        Sn["junk"] = Ss[0]["junk"]; Sn["r_junk"] = Ss[0]["r_junk"]
        xs = [0]

        def ld_row(row0):
            slot = xs[0] % NX
            xs[0] += 1
            K.dma("sp", xt[slot][:], x_in[row0: row0 + 128, :], ds_xt[slot], W=[r_xt[slot]])
            return slot
        for b in range(nblk):
            t0 = b * blk
            if b == 0:
                nslot = ld_row(t0)
                for j in range(nsub):
                    slot = nslot
                    if j + 1 < nsub:
                        nslot = ld_row(t0 + (j + 1) * 128)
                    emit_norm_T(K, C, xt[slot][:], r_xt[slot], nw[:], r_nw,
                                hT[:, :, j * 128:(j + 1) * 128], r_hT[j], Ss[j % 2])

            def ld_w(f):
                s = f % NW
                K.dma("sp", wst[s][0][:], wg_d[f], ds_w[s], W=[r_wst[s][0]])
                K.dma("sp", wst[s][1][:], wu_d[f], ds_w[s], W=[r_wst[s][1]])

            def ld_wd(f):
                s = f % NDS
                K.dma("act", wdst[s][:], wd_d[f], ds_wd[s], W=[r_wdst[s]])

            def cast_wd(f):
                s = f % NDS
                K.op("act", lambda: nc.scalar.copy(out=wdb[:, f, :], in_=wdst[s][:]),
                     R=[r_wdst[s]], W=[r_wdb[f]])

            def cast_w(f):
                s = f % NW
                K.op("dve", lambda: nc.vector.tensor_copy(
                    out=wb[s][0][:].rearrange("p k c -> p (k c)"), in_=wst[s][0][:]),
                    R=[r_wst[s][0]], W=[r_wb[s][0]])
                K.op("act", lambda: nc.scalar.copy(
                    out=wb[s][1][:].rearrange("p k c -> p (k c)"), in_=wst[s][1][:]),
                    R=[r_wst[s][1]], W=[r_wb[s][1]])

            ld_w(0)
            ld_w(1)
            ld_wd(0)
            ld_wd(1)
            cast_w(0)
            for f in range(NF):
                s = f % NW
                if f + 2 < NF:
                    ld_w(f + 2)
                if f + 1 < NF:
                    cast_w(f + 1)
                cast_wd(f)
                if f + 2 < NF:
                    ld_wd(f + 2)
                for tt in range(ntt):
                    pb = (f * ntt + tt) % 2
                    rr = [r_hT[j] for j in range(tt * 4, tt * 4 + 4)]
                    for k in range(8):
                        K.op("pe", lambda k=k: nc.tensor.matmul(
                            pg[pb][:], lhsT=wb[s][0][:, k, :], rhs=hT[:, k, tt * 512:(tt + 1) * 512],
                            start=(k == 0), stop=(k == 7)),
                            R=[r_wb[s][0]] + rr, W=[r_pg[pb]])
                    for k in range(8):
                        K.op("pe", lambda k=k: nc.tensor.matmul(
                            pu[pb][:], lhsT=wb[s][1][:, k, :], rhs=hT[:, k, tt * 512:(tt + 1) * 512],
                            start=(k == 0), stop=(k == 7)),
                            R=[r_wb[s][1]] + rr, W=[r_pu[pb]])
                    K.op("act", lambda: nc.scalar.activation(out=sg[pb][:], in_=pg[pb][:], func=AF.Silu),
                         R=[r_pg[pb]], W=[r_sg[pb]])
                    K.op("dve", lambda: nc.vector.tensor_tensor(
                        out=aT[:, f, tt * 512:(tt + 1) * 512], in0=sg[pb][:], in1=pu[pb][:], op=ALU.mult),
                        R=[r_sg[pb], r_pu[pb]], W=[r_aT[f][tt]])

            pend = None
            pend2 = None
            nxtb = (b + 1 < nblk)
            nslot = ld_row(t0)
            for j in range(nsub):
                slot = nslot
                if j + 1 < nsub:
                    nslot = ld_row(t0 + (j + 1) * 128)
                os_ = j % 2
                for half in range(2):
                    pb = (j * 2 + half) % 2
                    for f in range(NF):
                        K.op("pe", lambda f=f: nc.tensor.matmul(
                            po[pb][:], lhsT=aT[:, f, j * 128:(j + 1) * 128],
                            rhs=wdb[:, f, half * 512:(half + 1) * 512],
                            start=(f == 0), stop=(f == NF - 1)),
                            R=[r_aT[f][j // 4], r_wdb[f]], W=[r_po[pb]])
                    K.op("dve", lambda: nc.vector.scalar_tensor_tensor(
                        out=ot[os_][:, half * 512:(half + 1) * 512], in0=po[pb][:], scalar=0.5,
                        in1=xt[slot][:, half * 512:(half + 1) * 512], op0=ALU.mult, op1=ALU.add),
                        R=[r_po[pb], r_xt[slot]], W=[r_ot[os_]])
                K.dma("sp", x_out[t0 + j * 128: t0 + (j + 1) * 128, :], ot[os_][:], ds_ot[os_],
                      R=[r_ot[os_]])
                if hn_out is not None:
                    if pend is not None:
                        emit_norm_tr(K, C, nw2[:], r_nw2, *pend)
                    emit_norm_stats(K, C, ot[os_][:], r_ot[os_], Ss[j % 2])
                    pend = (hnb[:, :, j * 128:(j + 1) * 128], r_hnb, Ss[j % 2])
                if nxtb:
                    if pend2 is not None:
                        emit_norm_tr(K, C, nw[:], r_nw, *pend2)
                    s2 = ld_row(t0 + blk + j * 128)
                    emit_norm_stats(K, C, xt[s2][:], r_xt[s2], Sn)
                    pend2 = (hT[:, :, j * 128:(j + 1) * 128], r_hT[j], Sn)
            if hn_out is not None and pend is not None:
                emit_norm_tr(K, C, nw2[:], r_nw2, *pend)
                pend = None
            if pend2 is not None:
                emit_norm_tr(K, C, nw[:], r_nw, *pend2)
                pend2 = None
            if hn_out is not None:
                if isinstance(hn_out, (list, tuple)):
                    rd_ = Reg(f"hn_dram{b}")
                    K.dma("sp", hn_out[b].rearrange("k p t -> p k t"), hnb[:], ds_hn, R=[r_hnb], W=[rd_])
                    if on_block is not None:
                        on_block(b, rd_)
                else:
                    K.dma("sp", hn_out[:, :, t0:t0 + blk].rearrange("k p t -> p k t"), hnb[:], ds_hn,
                          R=[r_hnb])
        K.barrier()
        K.end_phase()
        K.stack = outer


def emit_outproj(K, C, x_in, y_in, wo_d, x_out, ntok):
    nc = K.nc
    outer = K.stack
    with ExitStack() as st:
        K.stack = st
        K.begin_phase()
        wob = K.sb([128, 8, D], BF16, "wob")
        r_wob = Reg()
        wst = [K.sb([128, D], F32, f"wost{i}") for i in range(2)]
        r_wst = [Reg() for i in range(2)]
        ds_w = [K.new_dma_sem() for i in range(2)]
        for k in range(8):
            K.dma("sp", wst[k % 2][:], wo_d[k], ds_w[k % 2], W=[r_wst[k % 2]])
            K.op("pool", lambda k=k: nc.gpsimd.tensor_copy(out=wob[:, k, :], in_=wst[k % 2][:]),
                 R=[r_wst[k % 2]], W=[r_wob])
        NS = 3
        yt = [K.sb([128, D], BF16, f"yt{i}") for i in range(NS)]
        r_yt = [Reg() for i in range(NS)]
        xt = [K.sb([128, D], F32, f"oxt{i}") for i in range(NS)]
        r_xt = [Reg() for i in range(NS)]
        ds_in = [K.new_dma_sem() for i in range(NS)]
        ptr = [K.ps([128, 8, 128], BF16, f"optr{i}") for i in range(2)]
        r_ptr = [Reg() for i in range(2)]
        yT = [K.sb([128, 8, 128], BF16, f"yT{i}") for i in range(2)]
        r_yT = [Reg() for i in range(2)]
        po = [K.ps([128, 512], F32, f"opo{i}") for i in range(2)]
        r_po = [Reg() for i in range(2)]
        ot = [K.sb([128, D], F32, f"oot{i}") for i in range(2)]
        r_ot = [Reg() for i in range(2)]
        ds_ot = [K.new_dma_sem() for i in range(2)]
        nsub = ntok // 128

        def ld(j):
            s = j % NS
            K.dma("sp", yt[s][:], y_in[j * 128:(j + 1) * 128, :], ds_in[s], W=[r_yt[s]])
            K.dma("sp", xt[s][:], x_in[j * 128:(j + 1) * 128, :], ds_in[s], W=[r_xt[s]])
        ld(0)
        for j in range(nsub):
            s = j % NS
            p2 = j % 2
            if j + 1 < nsub:
                ld(j + 1)
            for k in range(8):
                K.op("pe", lambda k=k: nc.tensor.transpose(out=ptr[p2][:, k, :],
                                                           in_=yt[s][:, k * 128:(k + 1) * 128],
                                                           identity=C["ident_bf"][:]),
                     R=[r_yt[s], C["r"]], W=[r_ptr[p2]])
            K.op("act", lambda: nc.scalar.copy(out=yT[p2][:], in_=ptr[p2][:]),
                 R=[r_ptr[p2]], W=[r_yT[p2]])
            for half in range(2):
                pb = half
                for k in range(8):
                    K.op("pe", lambda k=k: nc.tensor.matmul(
                        po[pb][:], lhsT=yT[p2][:, k, :], rhs=wob[:, k, half * 512:(half + 1) * 512],
                        start=(k == 0), stop=(k == 7)),
                        R=[r_yT[p2], r_wob], W=[r_po[pb]])
                K.op("dve", lambda: nc.vector.tensor_tensor(
                    out=ot[p2][:, half * 512:(half + 1) * 512], in0=po[pb][:],
                    in1=xt[s][:, half * 512:(half + 1) * 512], op=ALU.add),
                    R=[r_po[pb], r_xt[s]], W=[r_ot[p2]])
            K.dma("sp", x_out[j * 128:(j + 1) * 128, :], ot[p2][:], ds_ot[p2], R=[r_ot[p2]])
        K.barrier()
        K.end_phase()
        K.stack = outer


def emit_outproj_sel(K, C, x_in, yall, mh_d, wo_d, x_out, ntok, seq, yregs=None):
    nc = K.nc
    outer = K.stack
    with ExitStack() as st:
        K.stack = st
        K.begin_phase()
        wob = K.sb([128, 8, D], BF16, "wob")
        r_wob = Reg()
        wst = [K.sb([128, D], F32, f"wost{i}") for i in range(2)]
        r_wst = [Reg() for i in range(2)]
        ds_w = [K.new_dma_sem() for i in range(2)]
        mh = K.sb([128, 2], F32, "mh"); r_mh = Reg()
        K.dma("sp", mh[:], mh_d, ds_w[0], W=[r_mh])
        for k in range(8):
            K.dma("sp", wst[k % 2][:], wo_d[k], ds_w[k % 2], W=[r_wst[k % 2]])
            if k % 2 == 0:
                K.op("dve", lambda k=k: nc.vector.tensor_copy(out=wob[:, k, :], in_=wst[k % 2][:]),
                     R=[r_wst[k % 2]], W=[r_wob])
            else:
                K.op("act", lambda k=k: nc.scalar.copy(out=wob[:, k, :], in_=wst[k % 2][:]),
                     R=[r_wst[k % 2]], W=[r_wob])
        NS = 3
        ya = [[K.sb([128, 2, 512], BF16, f"ya{i}_{c}") for c in range(2)] for i in range(NS)]
        r_ya = [[Reg(), Reg()] for i in range(NS)]
        yt = [K.sb([128, D], BF16, f"yt{i}") for i in range(2)]
        r_yt = [Reg() for i in range(2)]
        xt = [K.sb([128, D], F32, f"oxt{i}") for i in range(NS)]
        r_xt = [Reg() for i in range(NS)]
        ds_in = [K.new_dma_sem() for i in range(NS)]
        ptr = [K.ps([128, 8, 128], BF16, f"optr{i}") for i in range(2)]
        r_ptr = [Reg() for i in range(2)]
        yT = [K.sb([128, 8, 128], BF16, f"yT{i}") for i in range(2)]
        r_yT = [Reg() for i in range(2)]
        po = [K.ps([128, 512], F32, f"opo{i}") for i in range(2)]
        r_po = [Reg() for i in range(2)]
        ot = [K.sb([128, D], F32, f"oot{i}") for i in range(2)]
        r_ot = [Reg() for i in range(2)]
        ds_ot = [K.new_dma_sem() for i in range(2)]
        nsub = ntok // 128
        yv = [ya_.rearrange("(r t) c -> t r c", r=2) for ya_ in yall]

        def ld(j):
            s = j % NS
            for c in range(2):
                K.dma("sp" if c == 0 else "act", ya[s][c][:], yv[c][j * 128:(j + 1) * 128, :, :],
                      ds_in[s], W=[r_ya[s][c]], R=([yregs[c]] if yregs is not None else []))
            K.dma("sp", xt[s][:], x_in[j * 128:(j + 1) * 128, :], ds_in[s], W=[r_xt[s]])
        def front(j):
            s = j % NS
            p2 = j % 2
            K.op("dve", lambda: nc.vector.tensor_scalar(out=yt[p2][:], in0=ya[s][0][:].rearrange("p r c -> p (r c)"),
                                                        scalar1=mh[:, 0:1], scalar2=None, op0=ALU.mult),
                 R=[r_ya[s][0], r_mh], W=[r_yt[p2]])
            K.op("dve", lambda: nc.vector.scalar_tensor_tensor(out=yt[p2][:], in0=ya[s][1][:].rearrange("p r c -> p (r c)"),
                                                               scalar=mh[:, 1:2], in1=yt[p2][:], op0=ALU.mult, op1=ALU.add),
                 R=[r_ya[s][1], r_mh, r_yt[p2]], W=[r_yt[p2]])
            for k in range(8):
                K.op("pe", lambda k=k: nc.tensor.transpose(out=ptr[p2][:, k, :],
                                                           in_=yt[p2][:, k * 128:(k + 1) * 128],
                                                           identity=C["ident_bf"][:]),
                     R=[r_yt[p2], C["r"]], W=[r_ptr[p2]])
            K.op("act", lambda: nc.scalar.copy(out=yT[p2][:], in_=ptr[p2][:]),
                 R=[r_ptr[p2]], W=[r_yT[p2]])

        ld(0)
        if nsub > 1:
            ld(1)
        front(0)
        for j in range(nsub):
            s = j % NS
            p2 = j % 2
            if j + 2 < nsub:
                ld(j + 2)
            if j + 1 < nsub:
                front(j + 1)
            for half in range(2):
                pb = half
                for k in range(8):
                    K.op("pe", lambda k=k: nc.tensor.matmul(
                        po[pb][:], lhsT=yT[p2][:, k, :], rhs=wob[:, k, half * 512:(half + 1) * 512],
                        start=(k == 0), stop=(k == 7)),
                        R=[r_yT[p2], r_wob], W=[r_po[pb]])
                K.op("dve", lambda: nc.vector.tensor_tensor(
                    out=ot[p2][:, half * 512:(half + 1) * 512], in0=po[pb][:],
                    in1=xt[s][:, half * 512:(half + 1) * 512], op=ALU.add),
                    R=[r_po[pb], r_xt[s]], W=[r_ot[p2]])
            K.dma("sp", x_out[j * 128:(j + 1) * 128, :], ot[p2][:], ds_ot[p2], R=[r_ot[p2]])
        K.barrier()
        K.end_phase()
        K.stack = outer


EPS = 1e-6
NEG = -30000.0


def load_w(K, w_d, ncols, name):
    nc = K.nc
    wb = K.sb([128, 8, ncols], BF16, name)
    r_wb = Reg(name)
    with ExitStack() as st:
        outer = K.stack
        K.stack = st
        K.begin_phase()
        stg = [K.sb([128, ncols], F32, f"{name}st{i}") for i in range(2)]
        r_st = [Reg() for i in range(2)]
        ds = [K.new_dma_sem() for i in range(2)]
        for k in range(8):
            K.dma("sp", stg[k % 2][:], w_d[:, k, :], ds[k % 2], W=[r_st[k % 2]])
            K.op("pool", lambda k=k: nc.gpsimd.tensor_copy(out=wb[:, k, :], in_=stg[k % 2][:]),
                 R=[r_st[k % 2]], W=[r_wb])
        K.barrier()
        K.end_phase()
        K.stack = outer
    return wb, r_wb


def load_w_all(K, specs, nslots=4):
    nc = K.nc
    out = {}
    for key, w_d, ncols in specs:
        out[key] = (K.sb([128, 8, ncols], BF16, "w_" + key), Reg("w_" + key))
    stg_stack = None
    stg = [K.sb([128, 768], F32, f"wstg{i}") for i in range(nslots)]
    r_st = [Reg() for _ in range(nslots)]
    ds = [K.new_dma_sem() for _ in range(nslots)]
    n = 0
    for key, w_d, ncols in specs:
        wb, r_wb = out[key]
        for k in range(8):
            s = n % nslots
            K.dma("sp" if n % 2 == 0 else "act", stg[s][:, 0:ncols], w_d[:, k, :], ds[s], W=[r_st[s]])
            if n % 2 == 0:
                K.op("dve", lambda k=k: nc.vector.tensor_copy(out=wb[:, k, :], in_=stg[s][:, 0:ncols]),
                     R=[r_st[s]], W=[r_wb])
            else:
                K.op("act", lambda k=k: nc.scalar.copy(out=wb[:, k, :], in_=stg[s][:, 0:ncols]),
                     R=[r_st[s]], W=[r_wb])
            n += 1
    return out, stg_stack


def rstd_from_ssq(K, ssq, r_ssq, n, scr, inv_n):
    nc = K.nc
    K.op("dve", lambda: nc.vector.tensor_scalar(out=scr["ms"], in0=ssq, scalar1=inv_n, scalar2=EPS,
                                                op0=ALU.mult, op1=ALU.add),
         R=[r_ssq], W=[scr["r_ms"]])
    K.op("act", lambda: nc.scalar.activation(out=scr["sd"], in_=scr["ms"], func=AF.Sqrt),
         R=[scr["r_ms"]], W=[scr["r_sd"]])
    K.op("dve", lambda: nc.vector.reciprocal(out=scr["rstd"], in_=scr["sd"]),
         R=[scr["r_sd"]], W=[scr["r_rstd"]])


def rstd_explog(K, ssq, r_ssq, scr, inv_n):
    nc = K.nc
    K.op("dve", lambda: nc.vector.tensor_scalar(out=scr["ms"], in0=ssq, scalar1=inv_n, scalar2=EPS,
                                                op0=ALU.mult, op1=ALU.add),
         R=[r_ssq], W=[scr["r_ms"]])
    K.op("act", lambda: nc.scalar.activation(out=scr["sd"], in_=scr["ms"], func=AF.Ln),
         R=[scr["r_ms"]], W=[scr["r_sd"]])
    K.op("act", lambda: nc.scalar.activation(out=scr["rstd"], in_=scr["sd"], func=AF.Exp, scale=-0.5),
         R=[scr["r_sd"]], W=[scr["r_rstd"]])


def mk_scr(K, shape, tag):
    d = {}
    for n in ["ms", "sd", "rstd"]:
        t = K.sb(shape, F32, tag + n)
        d[n] = t[:]
        d["r_" + n] = Reg(tag + n)
    return d


def emit_diff(K, C, hnT, r_hnT, P, y_d, S, lam_init, W=None):
    nc = K.nc
    nT = S // 128
    nQ = S // 512
    outer = K.stack
    with ExitStack() as st:
        K.stack = st
        K.begin_phase()
        wb, r_wb = W["wD"] if W is not None else load_w(K, P["wD"], 384, "wD")
        dsm = K.new_dma_sem()
        qkw = K.sb([128, 256], F32, "qkw"); r_c = Reg("dconst")
        sw = K.sb([128, 64], F32, "sw")
        lamb = K.sb([128, 4, 32], F32, "lamb")
        Bt = K.sb([128, 2, 1024], F32, "Bt")
        c31 = K.sb([128, 2], F32, "c31")
        K.dma("sp", qkw[:], P["qkw"], dsm, W=[r_c])
        K.dma("sp", sw[:], P["sw"], dsm, W=[r_c])
        K.dma("sp", lamb[:], P["lamb"], dsm, W=[r_c])
        K.dma("sp", Bt[:], P["Bt"], dsm, W=[r_c])
        K.dma("sp", c31[:], P["c31"], dsm, W=[r_c])
        qT = K.sb([128, S], BF16, "dqT"); r_qT = [Reg() for _ in range(nT)]
        kT = K.sb([128, S], BF16, "dkT"); r_kT = [Reg() for _ in range(nT)]
        qTb = K.sb([32, S], BF16, "dqTb")
        kTb = K.sb([32, S], BF16, "dkTb")
        vaug = K.sb([128, nT, 2, 65], BF16, "dvaug"); r_v = [Reg() for _ in range(nT)]
        zer = K.sb([128, 260], BF16, "zer"); r_z = Reg()
        K.op("dve", lambda: nc.vector.memset(zer[:], 0.0), W=[r_z])
        K.op("dve", lambda: nc.vector.memset(vaug[:].rearrange("p a b c -> p (a b c)"), 1.0), W=r_v)
        lt = K.sb([128, 2, 32], F32, "lt"); r_lt = Reg()
        ls = K.sb([128, 2], F32, "ls"); r_ls = Reg()
        le = K.sb([128, 2], F32, "le"); r_le = Reg()
        nlam = K.sb([128, 1], F32, "nlam"); r_nlam = Reg()
        swl = K.sb([128, 64], F32, "swl"); r_swl = Reg()
        K.op("dve", lambda: nc.vector.tensor_tensor(out=lt[:, 0, :], in0=lamb[:, 0, :], in1=lamb[:, 1, :], op=ALU.mult),
             R=[r_c], W=[r_lt])
        K.op("dve", lambda: nc.vector.tensor_tensor(out=lt[:, 1, :], in0=lamb[:, 2, :], in1=lamb[:, 3, :], op=ALU.mult),
             R=[r_c], W=[r_lt])
        K.op("dve", lambda: nc.vector.tensor_reduce(out=ls[:], in_=lt[:], axis=AX.X, op=ALU.add), R=[r_lt], W=[r_ls])
        K.op("act", lambda: nc.scalar.activation(out=le[:], in_=ls[:], func=AF.Exp), R=[r_ls], W=[r_le])
        K.op("dve", lambda: nc.vector.tensor_tensor(out=nlam[:], in0=le[:, 1:2], in1=le[:, 0:1], op=ALU.subtract),
             R=[r_le], W=[r_nlam])
        K.op("dve", lambda: nc.vector.tensor_scalar(out=nlam[:], in0=nlam[:], scalar1=-lam_init, scalar2=None, op0=ALU.add),
             R=[r_nlam], W=[r_nlam])
        K.op("dve", lambda: nc.vector.tensor_scalar(out=swl[:], in0=sw[:], scalar1=1.0 - lam_init, scalar2=None, op0=ALU.mult),
             R=[r_c], W=[r_swl])

        GD = 4
        st_d2 = ExitStack()
        K.stack = st_d2
        ppb = [psbank(K, f"dpp{i}") for i in range(GD)]; r_ppb = [Reg(excl=True) for _ in range(GD)]
        ptrb = [K.ps([128, 8, 128], BF16, f"dptr{i}") for i in range(GD // 2)]; r_ptrb = [Reg() for _ in range(GD // 2)]
        sq = K.sb([128, GD, 256], F32, "dsq"); r_sq = Reg()
        ssq = K.sb([128, GD * 8], F32, "dssq"); r_ssq = Reg()
        scr = mk_scr(K, [128, GD * 8], "dq")
        qn = K.sb([128, GD, 256], F32, "dqn"); r_qn = Reg()
        qnb = K.sb([128, GD, 256], BF16, "dqnb"); r_qnb = Reg()
        K.stack = st
        scale = 32 ** -0.5
        for i0 in range(0, nT, GD):
            tss = [slice((i0 + i) * 128, (i0 + i + 1) * 128) for i in range(GD)]
            for i in range(GD):
                for k in range(8):
                    K.op("pe", lambda k=k, i=i: nc.tensor.matmul(ppb[i][:, 0:384], lhsT=hnT[:, k, tss[i]], rhs=wb[:, k, :],
                                                                 start=(k == 0), stop=(k == 7)),
                         R=[r_hnT, r_wb], W=[r_ppb[i]])
            for i in range(GD):
                K.op("act", lambda i=i: nc.scalar.activation(out=sq[:, i, :], in_=ppb[i][:, 0:256], func=AF.Square),
                     R=[r_ppb[i]], W=[r_sq])
            K.op("dve", lambda: nc.vector.tensor_reduce(out=ssq[:], in_=sq[:].rearrange("p t (g d) -> p (t g) d", d=32),
                                                        axis=AX.X, op=ALU.add), R=[r_sq], W=[r_ssq])
            rstd_explog(K, ssq[:], r_ssq, scr, 1.0 / 32)
            rs3 = scr["rstd"].rearrange("p (t g) -> p t g", g=8)
            K.op("dve", lambda: nc.vector.tensor_scalar(out=rs3[:, :, 0:4], in0=rs3[:, :, 0:4], scalar1=scale,
                                                        scalar2=None, op0=ALU.mult), R=[scr["r_rstd"]], W=[scr["r_rstd"]])
            for i in range(GD):
                K.op("dve", lambda i=i: nc.vector.tensor_tensor(
                    out=qn[:, i, :].rearrange("p (g d) -> p g d", d=32), in0=ppb[i][:, 0:256].rearrange("p (g d) -> p g d", d=32),
                    in1=rs3[:, i, :].unsqueeze(2).to_broadcast([128, 8, 32]), op=ALU.mult),
                    R=[r_ppb[i], scr["r_rstd"]], W=[r_qn])
                K.op("dve", lambda i=i: nc.vector.tensor_copy(out=vaug[:, i0 + i, :, 0:64],
                                                              in_=ppb[i][:, 256:384].rearrange("p (h d) -> p h d", d=64)),
                     R=[r_ppb[i]], W=[r_v[i0 + i]])
            K.op("dve", lambda: nc.vector.tensor_tensor(out=qnb[:], in0=qn[:], in1=qkw[:].unsqueeze(1).to_broadcast([128, GD, 256]),
                                                        op=ALU.mult), R=[r_qn, r_c], W=[r_qnb])
            for i in range(GD):
                ptr = ptrb[i // 2][:, (i % 2) * 4:(i % 2) * 4 + 4, :]; r_ptr = r_ptrb[i // 2]
                for a in range(2):
                    K.op("pe", lambda a=a, i=i: nc.tensor.transpose(out=ptr[0:96, 2 * a, :], in_=qnb[:, i, a * 128:a * 128 + 96],
                                                                    identity=C["ident_bf"][:]),
                         R=[r_qnb, C["r"]], W=[r_ptr])
                    K.op("pe", lambda a=a, i=i: nc.tensor.transpose(out=ptr[0:32, 2 * a + 1, :], in_=qnb[:, i, a * 128 + 96:a * 128 + 128],
                                                                    identity=C["ident_bf"][:]),
                         R=[r_qnb, C["r"]], W=[r_ptr])
            for i in range(GD):
                ptr = ptrb[i // 2][:, (i % 2) * 4:(i % 2) * 4 + 4, :]; r_ptr = r_ptrb[i // 2]
                ti = i0 + i
                K.op("act", lambda: nc.scalar.copy(out=qT[0:96, tss[i]], in_=ptr[0:96, 0, :]), R=[r_ptr], W=[r_qT[ti]])
                K.op("act", lambda: nc.scalar.copy(out=qTb[0:32, tss[i]], in_=ptr[0:32, 1, :]), R=[r_ptr], W=[r_qT[ti]])
                K.op("act", lambda: nc.scalar.copy(out=kT[0:96, tss[i]], in_=ptr[0:96, 2, :]), R=[r_ptr], W=[r_kT[ti]])
                K.op("act", lambda: nc.scalar.copy(out=kTb[0:32, tss[i]], in_=ptr[0:32, 3, :]), R=[r_ptr], W=[r_kT[ti]])

        K.barrier()
        st_d2.close()
        NSB = 4
        sbk = [K.ps([128, 512], F32, f"dsb{i}") for i in range(NSB)]; r_sb = [Reg() for _ in range(NSB)]
        acc = [K.ps([128, 4, 65], F32, f"dacc{i}") for i in range(4)]; r_acc = [Reg() for _ in range(4)]
        NP = 8
        pT = [K.sb([128, 512], BF16, f"dpT{i}") for i in range(NP)]; r_pT = [Reg() for _ in range(NP)]
        tmp = [K.sb([128, 512], F32, f"dtmp{i}") for i in range(2)]; r_tmp = [Reg() for _ in range(2)]
        rd = K.sb([128, 2, 4], F32, "drd"); r_rd = Reg()
        o1 = K.sb([128, 4, 64], F32, "do1"); r_o1 = Reg()
        o2 = K.sb([128, 4, 64], F32, "do2"); r_o2 = Reg()
        osq = K.sb([128, 4, 64], F32, "dosq"); r_osq = Reg()
        oss = K.sb([128, 4], F32, "doss"); r_oss = Reg()
        oscr = mk_scr(K, [128, 4], "do")
        yb = [K.sb([128, 4, 128], BF16, f"dyb{i}") for i in range(2)]; r_yb = [Reg() for _ in range(2)]
        ds_y = [K.new_dma_sem() for _ in range(2)]
        cnt = 0
        ntmp = 0
        ai = 0
        for t in range(nQ):
            ybt = yb[t % 2]
            for hl in range(2):
                accs = []
                items = []
                for c in range(2):
                    g = hl * 2 + c
                    A = acc[ai % 4]; rA = r_acc[ai % 4]; ai += 1
                    accs.append((A, rA))
                    K.op("pe", lambda: nc.tensor.matmul(A[:].rearrange("p a b -> p (a b)"), lhsT=zer[:, 0:128],
                                                        rhs=zer[:, 0:260], start=True, stop=False),
                         R=[r_z], W=[rA])
                for j in range(4 * t + 4):
                    for c in range(2):
                        items.append((c, hl * 2 + c, accs[c][0], accs[c][1], j))

                def emit_S(it):
                    nonlocal cnt, ntmp
                    c, g, A, rA, j = it
                    pr = slice(32 * g, 32 * g + 32) if g < 3 else slice(0, 32)
                    qTg = qT if g < 3 else qTb
                    kTg = kT if g < 3 else kTb
                    m = 4 * t - j
                    c0 = max(0, -m) * 128
                    sb_ = sbk[cnt % NSB]; rs = r_sb[cnt % NSB]
                    p_ = pT[cnt % NP]; rp = r_pT[cnt % NP]
                    cnt += 1
                    K.op("pe", lambda: nc.tensor.matmul(sb_[:, c0:512], lhsT=kTg[pr, j * 128:(j + 1) * 128],
                                                        rhs=qTg[pr, t * 512 + c0:(t + 1) * 512], start=True, stop=True),
                         R=[r_kT[j]] + [r_qT[4 * t + x] for x in range(c0 // 128, 4)], W=[rs])
                    if m >= 2:
                        K.op("act", lambda: nc.scalar.activation(out=p_[:, c0:512], in_=sb_[:, c0:512], func=AF.Exp,
                                                                 bias=c31[:, hl:hl + 1]),
                             R=[rs, r_c], W=[rp])
                    else:
                        tm_ = tmp[ntmp % 2]; rt = r_tmp[ntmp % 2]; ntmp += 1
                        b0 = 128 * m + 384
                        K.op("dve", lambda: nc.vector.tensor_tensor(out=tm_[:, c0:512], in0=sb_[:, c0:512],
                                                                    in1=Bt[:, hl, b0 + c0:b0 + 512], op=ALU.add),
                             R=[rs, r_c], W=[rt])
                        K.op("act", lambda: nc.scalar.activation(out=p_[:, c0:512], in_=tm_[:, c0:512], func=AF.Exp),
                             R=[rt], W=[rp])
                    return (p_, rp, c0)

                def emit_AV(it, pinfo):
                    c, g, A, rA, j = it
                    p_, rp, c0 = pinfo
                    for sub in range(c0 // 128, 4):
                        K.op("pe", lambda sub=sub: nc.tensor.matmul(
                            A[:, sub, :], lhsT=p_[:, sub * 128:(sub + 1) * 128], rhs=vaug[:, j, hl, :],
                            start=False, stop=(j == 4 * t + 3 and sub == 3)),
                            R=[rp, r_v[j]], W=[rA])

                LOOK = 4
                pend = []
                for idx in range(0, len(items), 4):
                    for it in items[idx:idx + 4]:
                        pend.append((it, emit_S(it)))
                    while len(pend) > LOOK:
                        emit_AV(*pend.pop(0))
                while pend:
                    emit_AV(*pend.pop(0))
                (A1, rA1), (A2, rA2) = accs
                K.op("dve", lambda: nc.vector.reciprocal(out=rd[:, 0, :], in_=A1[:, :, 64]), R=[rA1], W=[r_rd])
                K.op("dve", lambda: nc.vector.reciprocal(out=rd[:, 1, :], in_=A2[:, :, 64]), R=[rA2], W=[r_rd])
                K.op("dve", lambda: nc.vector.tensor_scalar(out=rd[:, 1, :], in0=rd[:, 1, :], scalar1=nlam[:, 0:1],
                                                            scalar2=None, op0=ALU.mult), R=[r_rd, r_nlam], W=[r_rd])
                K.op("dve", lambda: nc.vector.tensor_tensor(out=o1[:], in0=A1[:, :, 0:64],
                                                            in1=rd[:, 0, :].unsqueeze(2).to_broadcast([128, 4, 64]),
                                                            op=ALU.mult), R=[rA1, r_rd], W=[r_o1])
                K.op("dve", lambda: nc.vector.tensor_tensor(out=o2[:], in0=A2[:, :, 0:64],
                                                            in1=rd[:, 1, :].unsqueeze(2).to_broadcast([128, 4, 64]),
                                                            op=ALU.mult), R=[rA2, r_rd], W=[r_o2])
                K.op("pool", lambda: nc.gpsimd.tensor_tensor(out=o1[:], in0=o1[:], in1=o2[:], op=ALU.add),
                     R=[r_o1, r_o2], W=[r_o1])
                K.op("dve", lambda: nc.vector.tensor_tensor(out=osq[:], in0=o1[:], in1=o1[:], op=ALU.mult), R=[r_o1], W=[r_osq])
                K.op("dve", lambda: nc.vector.tensor_reduce(out=oss[:], in_=osq[:], axis=AX.X, op=ALU.add),
                     R=[r_osq], W=[r_oss])
                rstd_explog(K, oss[:], r_oss, oscr, 1.0 / 64)
                K.op("dve", lambda: nc.vector.tensor_tensor(out=o2[:], in0=o1[:],
                                                            in1=oscr["rstd"].unsqueeze(2).to_broadcast([128, 4, 64]),
                                                            op=ALU.mult), R=[r_o1, oscr["r_rstd"]], W=[r_o2])
                K.op("dve", lambda: nc.vector.tensor_tensor(out=ybt[:, :, hl * 64:(hl + 1) * 64], in0=o2[:],
                                                            in1=swl[:].unsqueeze(1).to_broadcast([128, 4, 64]),
                                                            op=ALU.mult), R=[r_o2, r_swl], W=[r_yb[t % 2]])
            K.dma("sp", y_d[t * 512:(t + 1) * 512, 384:512].rearrange("(s p) c -> p s c", p=128), ybt[:],
                  ds_y[t % 2], R=[r_yb[t % 2]], W=([y_d.reg(t * 512)] if hasattr(y_d, "reg") else []))
            if getattr(y_d, "hook", None) is not None:
                y_d.hook(t)
        K.barrier()
        K.end_phase()
        K.stack = outer


def load_mixer_consts(K, C, D):
    ds = C["dsem"]
    C["cf"] = K.sb([128, 1280], F32, "cf")
    C["sel"] = K.sb([2, 2, 128], F32, "sel")
    C["hsel"] = K.sb([2, 128], F32, "hsel")
    C["rowc"] = K.sb([2, 2, 512], F32, "rowc")
    K.dma("sp", C["cf"][:], D["cf"], ds, W=[C["r"]])
    K.dma("sp", C["sel"][:], D["sel"], ds, W=[C["r"]])
    K.dma("sp", C["hsel"][:], D["hsel"], ds, W=[C["r"]])
    K.dma("sp", C["rowc"][:], D["rowc"], ds, W=[C["r"]])
    C["ones"] = C["cf"][:, 0:128]
    C["tri"] = C["cf"][:, 128:256]
    C["nm2"] = C["cf"][:, 256:768]
    C["nm1"] = C["cf"][:, 768:1280]


def load_hnT(K, hn_d, S):
    hnT = K.sb([128, 8, S], BF16, "hnT")
    r = Reg("hnT")
    ds = K.new_dma_sem()
    for k in range(8):
        K.dma("sp" if k % 2 == 0 else "act", hnT[:, k, :], hn_d[k], ds, W=[r])
    return hnT, r


def emit_ssd_gen(K, C, hnT, r_hnT, P, y_d, S, W, nsets=2):
    STOP = 99
    nc = K.nc
    nT = S // 128
    nTT = S // 512
    if True:
        wfm, r_wfm = W["wS_fm"]
        wtm, r_wtm = W["wS_tm"]
        dsm = K.new_dma_sem()
        r_c = Reg("sconst")
        cw = K.sb([128, 4, 4], F32, "cw"); cb = K.sb([128, 4], F32, "cb")
        dtb = K.sb([128, 4], F32, "dtb"); alog = K.sb([128, 4], F32, "alog")
        dsk = K.sb([128, 4], F32, "dsk"); snw = K.sb([128, 256], F32, "snw")
        for t_, n_ in [(cw, "cw"), (cb, "cb"), (dtb, "dtb"), (alog, "alog"), (dsk, "dsk"), (snw, "snw")]:
            K.dma("sp", t_[:], P[n_], dsm, W=[r_c])
        Aneg = K.sb([128, 4], F32, "Aneg"); r_A = Reg()
        K.op("act", lambda: nc.scalar.activation(out=Aneg[:], in_=alog[:], func=AF.Exp), R=[r_c], W=[r_A])
        K.op("dve", lambda: nc.vector.tensor_scalar(out=Aneg[:], in0=Aneg[:], scalar1=-1.0, scalar2=None, op0=ALU.mult),
             R=[r_A], W=[r_A])
        xc = [K.sb([128, S], BF16, f"xc{i}") for i in range(4)]
        r_xc = [Reg() for _ in range(4)]
        def T(shape, dt, name):
            return K.sb(shape, dt, name), Reg(name)
        names = [("sz", [128, 256], F32), ("dtx", [128, 4], F32), ("ax", [128, 4], F32), ("ex", [128, 4], F32),
                 ("lx", [128, 4], F32), ("dt", [128, 4], F32), ("aa", [128, 4], F32), ("acs", [128, 4], F32),
                 ("nacs", [128, 4], F32), ("el", [128, 4], F32), ("cd", [128, 4], F32), ("dd", [128, 4], F32),
                 ("dec", [128, 4], F32), ("dtdec", [128, 4], F32), ("rseg", [128, 4, 128], F32),
                 ("segT", [128, 4, 128], F32), ("xdt", [128, 4, 64], BF16), ("xdd", [128, 4, 64], BF16),
                 ("xD", [128, 4, 64], F32), ("Btm", [128, 128], BF16), ("Gm", [128, 128], F32),
                 ("scT", [128, 4, 128], BF16), ("t1", [128, 4, 64], F32), ("gg", [128, 256], F32),
                 ("junk", [128, 256], F32), ("ssq", [128, 1], F32)]
        sets = []
        for par in range(nsets):
            d_ = {}
            for (n_, shp, dt_) in names:
                d_[n_] = T(shp, dt_, f"s{par}{n_}")
            d_["nscr"] = mk_scr(K, [128, 1], f"sn{par}")
            bA = psbank(K, f"sA{par}"); bB = psbank(K, f"sB{par}"); bC = psbank(K, f"sC{par}"); bD = psbank(K, f"sD{par}")
            d_["bA"] = bA; d_["bB"] = bB
            d_["rA"] = Reg(excl=True); d_["rB"] = Reg(excl=True); d_["rC"] = Reg(excl=True); d_["rD"] = Reg(excl=True)
            d_["pz"] = bA[:, 0:256]; d_["pst"] = bA[:, 256:512]
            d_["pseg"] = bB.rearrange("p (a b) -> p a b", b=128)
            d_["ptr"] = bC[:, 0:192].bitcast(BF16).rearrange("p (a b) -> p a b", b=128)
            d_["pG"] = bC[:, 192:320]; d_["pa"] = bC[:, 320:328]; d_["pdt"] = bC[:, 328:332]
            d_["py"] = bD[:, 0:256]; d_["pyo"] = bD[:, 256:512]
            sets.append(d_)
        Sf, r_Sf = T([128, 4, 64], F32, "sSf")
        Sbf, r_Sbf = T([128, 256], BF16, "sSbf")
        yb = [K.sb([128, 256], BF16, f"syb{i}") for i in range(2)]; r_yb = [Reg() for _ in range(2)]
        ds_y = [K.new_dma_sem() for _ in range(2)]
        K.op("dve", lambda: nc.vector.memset(Sf[:].rearrange("p a b -> p (a b)"), 0.0), W=[r_Sf])
        K.op("dve", lambda: nc.vector.memset(Sbf[:], 0.0), W=[r_Sbf])
        ident_f = C["ident_f"]
        if True:
            SH = S // 2
            xpre = K.sb([128, SH + 3], F32, "xpre"); r_xpre = Reg()
            cacc = K.sb([128, SH], F32, "cacc"); r_cacc = Reg()
            pf = [sets[0]["bA"], sets[0]["bB"]]; r_pf = [sets[0]["rA"], sets[0]["rB"]]
            n = 0
            for ct in range(4):
                for hf in range(2):
                    if hf == 0:
                        K.op("dve", lambda: nc.vector.memset(xpre[:, 0:3], 0.0), W=[r_xpre])
                    else:
                        K.op("dve", lambda: nc.vector.tensor_copy(out=xpre[:, 0:3], in_=xpre[:, SH:SH + 3]),
                             R=[r_xpre], W=[r_xpre])
                    for tt in range(nTT // 2):
                        tg_ = hf * (nTT // 2) + tt
                        p_ = pf[n % 2]; rp = r_pf[n % 2]; n += 1
                        for k in range(8):
                            K.op("pe", lambda k=k: nc.tensor.matmul(p_[:], lhsT=wfm[:, k, ct * 128:(ct + 1) * 128],
                                                                    rhs=hnT[:, k, tg_ * 512:(tg_ + 1) * 512],
                                                                    start=(k == 0), stop=(k == 7)),
                                 R=[r_wfm, r_hnT], W=[rp])
                        K.op("act", lambda: nc.scalar.copy(out=xpre[:, 3 + tt * 512:3 + (tt + 1) * 512], in_=p_[:]),
                             R=[rp], W=[r_xpre])
                        yield
                    K.op("dve", lambda: nc.vector.tensor_scalar(out=cacc[:], in0=xpre[:, 0:SH], scalar1=cw[:, ct, 0:1],
                                                                scalar2=None, op0=ALU.mult), R=[r_xpre, r_c], W=[r_cacc])
                    for j in range(1, 4):
                        K.op("dve", lambda j=j: nc.vector.scalar_tensor_tensor(
                            out=cacc[:], in0=xpre[:, j:SH + j], scalar=cw[:, ct, j:j + 1], in1=cacc[:],
                            op0=ALU.mult, op1=ALU.add), R=[r_xpre, r_c, r_cacc], W=[r_cacc])
                        yield
                    K.op("act", lambda: nc.scalar.activation(out=xc[ct][:, hf * SH:(hf + 1) * SH], in_=cacc[:], func=AF.Silu,
                                                             bias=cb[:, ct:ct + 1]),
                         R=[r_cacc, r_c], W=[r_xc[ct]])
                    yield
        state_done = [-1]

        def chunk_flow(c):
                cs = slice(c * 128, (c + 1) * 128)
                S_ = sets[c % nsets]
                (sz, r_sz), (dtx, r_dtx), (ax, r_ax), (ex, r_ex), (lx, r_lx), (dt, r_dt), (aa, r_aa), (acs, r_acs), \
                    (nacs, r_nacs), (el, r_el), (cd, r_cd), (dd, r_dd), (dec, r_dec), (dtdec, r_dtdec), (rseg, r_rseg), \
                    (segT, r_segT), (xdt, r_xdt), (xdd, r_xdd), (xD, r_xD), (Btm, r_Btm), (Gm, r_Gm), (scT, r_scT), \
                    (t1, r_t1), (gg, r_gg), (junk, r_junk), (ssq, r_ssq) = [S_[n_[0]] for n_ in names]
                nscr = S_["nscr"]
                pz, pst, pseg, ptr, pG, pa, pdt, py, pyo = [S_[n_] for n_ in ["pz", "pst", "pseg", "ptr", "pG", "pa", "pdt", "py", "pyo"]]
                r_pz = r_pst = S_["rA"]; r_pseg = S_["rB"]; r_ptr = r_pG = r_pa = r_pdt = S_["rC"]; r_py = r_pyo = S_["rD"]
                for k in range(8):
                    K.op("pe", lambda k=k: nc.tensor.matmul(pz[:], lhsT=hnT[:, k, cs], rhs=wtm[:, k, 0:256],
                                                            start=(k == 0), stop=(k == 7)), R=[r_hnT, r_wtm], W=[r_pz])
                for k in range(8):
                    K.op("pe", lambda k=k: nc.tensor.matmul(pdt[:], lhsT=hnT[:, k, cs], rhs=wtm[:, k, 256:260],
                                                            start=(k == 0), stop=(k == 7)), R=[r_hnT, r_wtm], W=[r_pdt])
                yield
                K.op("act", lambda: nc.scalar.activation(out=sz[:], in_=pz[:, 0:256], func=AF.Silu), R=[r_pz], W=[r_sz])
                yield
                K.op("dve", lambda: nc.vector.tensor_tensor(out=dtx[:], in0=pdt[:], in1=dtb[:], op=ALU.add),
                     R=[r_pdt, r_c], W=[r_dtx])
                yield
                K.op("dve", lambda: nc.vector.scalar_tensor_tensor(out=ax[:], in0=dtx[:], scalar=-1.0, in1=dtx[:],
                                                                   op0=ALU.mult, op1=ALU.min), R=[r_dtx], W=[r_ax])
                yield
                K.op("act", lambda: nc.scalar.activation(out=ex[:], in_=ax[:], func=AF.Exp), R=[r_ax], W=[r_ex])
                yield
                K.op("dve", lambda: nc.vector.tensor_scalar(out=ex[:], in0=ex[:], scalar1=1.0, scalar2=None, op0=ALU.add),
                     R=[r_ex], W=[r_ex])
                yield
                K.op("act", lambda: nc.scalar.activation(out=lx[:], in_=ex[:], func=AF.Ln), R=[r_ex], W=[r_lx])
                yield
                K.op("dve", lambda: nc.vector.scalar_tensor_tensor(out=dt[:], in0=dtx[:], scalar=0.0, in1=lx[:],
                                                                   op0=ALU.max, op1=ALU.add), R=[r_dtx, r_lx], W=[r_dt])
                yield
                K.op("dve", lambda: nc.vector.tensor_tensor(out=aa[:], in0=dt[:], in1=Aneg[:], op=ALU.mult),
                     R=[r_dt, r_A], W=[r_aa])
                yield
                K.op("pe", lambda: nc.tensor.matmul(pa[:, 0:4], lhsT=C["tri"], rhs=aa[:], start=True, stop=True),
                     R=[r_aa, C["r"]], W=[r_pa])
                yield
                K.op("pe", lambda: nc.tensor.matmul(pa[:, 4:8], lhsT=C["ones"], rhs=aa[:], start=True, stop=True),
                     R=[r_aa, C["r"]], W=[r_pa])
                yield
                K.op("dve", lambda: nc.vector.tensor_copy(out=acs[:], in_=pa[:, 0:4]), R=[r_pa], W=[r_acs])
                yield
                K.op("dve", lambda: nc.vector.tensor_scalar(out=nacs[:], in0=pa[:, 0:4], scalar1=-1.0, scalar2=None,
                                                            op0=ALU.mult), R=[r_pa], W=[r_nacs])
                yield
                K.op("act", lambda: nc.scalar.activation(out=el[:], in_=pa[:, 0:4], func=AF.Exp), R=[r_pa], W=[r_el])
                yield
                K.op("act", lambda: nc.scalar.activation(out=cd[:], in_=pa[:, 4:8], func=AF.Exp), R=[r_pa], W=[r_cd])
                yield
                K.op("dve", lambda: nc.vector.tensor_tensor(out=dd[:], in0=pa[:, 4:8], in1=acs[:], op=ALU.subtract),
                     R=[r_pa, r_acs], W=[r_dd])
                yield
                K.op("act", lambda: nc.scalar.activation(out=dec[:], in_=dd[:], func=AF.Exp), R=[r_dd], W=[r_dec])
                yield
                K.op("dve", lambda: nc.vector.tensor_tensor(out=dtdec[:], in0=dt[:], in1=dec[:], op=ALU.mult),
                     R=[r_dt, r_dec], W=[r_dtdec])
                yield
                K.op("dve", lambda: nc.vector.tensor_tensor(out=rseg[:], in0=ident_f[:].unsqueeze(1).to_broadcast([128, 4, 128]),
                                                            in1=acs[:].unsqueeze(2).to_broadcast([128, 4, 128]), op=ALU.mult),
                     R=[C["r"], r_acs], W=[r_rseg])
                yield
                K.op("pe", lambda: nc.tensor.matmul(pseg[:].rearrange("p a b -> p (a b)"), lhsT=C["ones"],
                                                    rhs=rseg[:].rearrange("p a b -> p (a b)"), start=True, stop=False),
                     R=[r_rseg, C["r"]], W=[r_pseg])
                yield
                K.op("pe", lambda: nc.tensor.matmul(pseg[:].rearrange("p a b -> p (a b)"), lhsT=ident_f[:],
                                                    rhs=C["nm1"], start=False, stop=True),
                     R=[C["r"]], W=[r_pseg])
                for h in range(4):
                    K.op("act", lambda h=h: nc.scalar.activation(out=segT[:, h, :], in_=pseg[:, h, :], func=AF.Exp,
                                                                 bias=nacs[:, h:h + 1]), R=[r_pseg, r_nacs], W=[r_segT])
                yield
                for a in range(3):
                    K.op("pe", lambda a=a: nc.tensor.transpose(out=ptr[:, a, :], in_=xc[a][:, cs], identity=C["ident_bf"][:]),
                         R=[r_xc[a], C["r"]], W=[r_ptr])
                xs_v = ptr[:, 0:2, :].rearrange("p a (h d) -> p (a h) d", d=64)
                yield
                K.op("dve", lambda: nc.vector.tensor_tensor(out=xdt[:], in0=xs_v, in1=dt[:].unsqueeze(2).to_broadcast([128, 4, 64]),
                                                            op=ALU.mult), R=[r_ptr, r_dt], W=[r_xdt])
                yield
                K.op("dve", lambda: nc.vector.tensor_tensor(out=xdd[:], in0=xs_v, in1=dtdec[:].unsqueeze(2).to_broadcast([128, 4, 64]),
                                                            op=ALU.mult), R=[r_ptr, r_dtdec], W=[r_xdd])
                yield
                K.op("dve", lambda: nc.vector.tensor_tensor(out=xD[:], in0=xs_v, in1=dsk[:].unsqueeze(2).to_broadcast([128, 4, 64]),
                                                            op=ALU.mult), R=[r_ptr, r_c], W=[r_xD])
                yield
                K.op("act", lambda: nc.scalar.copy(out=Btm[:], in_=ptr[:, 2, :]), R=[r_ptr], W=[r_Btm])
                yield
                K.op("pe", lambda: nc.tensor.matmul(pG[:], lhsT=xc[2][:, cs], rhs=xc[3][:, cs], start=True, stop=True),
                     R=[r_xc[2], r_xc[3]], W=[r_pG])
                yield
                K.op("dve", lambda: nc.vector.tensor_tensor(out=Gm[:], in0=pG[:], in1=C["tri"], op=ALU.mult),
                     R=[r_pG, C["r"]], W=[r_Gm])
                yield
                K.op("dve", lambda: nc.vector.tensor_tensor(out=scT[:], in0=Gm[:].unsqueeze(1).to_broadcast([128, 4, 128]),
                                                            in1=segT[:], op=ALU.mult), R=[r_Gm, r_segT], W=[r_scT])
                for h in range(4):
                    K.op("pe", lambda h=h: nc.tensor.matmul(py[:, h * 64:(h + 1) * 64], lhsT=scT[:, h, :], rhs=xdt[:, h, :],
                                                            start=True, stop=True), R=[r_scT, r_xdt], W=[r_py])
                yield
                while state_done[0] < c - 1:
                    yield
                K.op("pe", lambda: nc.tensor.matmul(pyo[:], lhsT=xc[3][:, cs], rhs=Sbf[:], start=True, stop=True),
                     R=[r_xc[3], r_Sbf], W=[r_pyo])
                yield
                K.op("pe", lambda: nc.tensor.matmul(pst[:], lhsT=Btm[:], rhs=xdd[:].rearrange("p a b -> p (a b)"),
                                                    start=True, stop=True), R=[r_Btm, r_xdd], W=[r_pst])
                yield
                K.op("pool", lambda: nc.gpsimd.tensor_tensor(out=Sf[:], in0=Sf[:], in1=cd[:].unsqueeze(2).to_broadcast([128, 4, 64]),
                                                             op=ALU.mult), R=[r_Sf, r_cd], W=[r_Sf])
                yield
                K.op("dve", lambda: nc.vector.tensor_tensor(out=Sf[:].rearrange("p a b -> p (a b)"),
                                                            in0=Sf[:].rearrange("p a b -> p (a b)"), in1=pst[:], op=ALU.add),
                     R=[r_Sf, r_pst], W=[r_Sf])
                yield
                K.op("act", lambda: nc.scalar.copy(out=Sbf[:], in_=Sf[:].rearrange("p a b -> p (a b)")), R=[r_Sf], W=[r_Sbf])
                state_done[0] = c
                yield
                K.op("dve", lambda: nc.vector.tensor_tensor(out=t1[:], in0=pyo[:].rearrange("p (a b) -> p a b", b=64),
                                                            in1=el[:].unsqueeze(2).to_broadcast([128, 4, 64]), op=ALU.mult),
                     R=[r_pyo, r_el], W=[r_t1])
                yield
                K.op("dve", lambda: nc.vector.tensor_tensor(out=t1[:].rearrange("p a b -> p (a b)"),
                                                            in0=t1[:].rearrange("p a b -> p (a b)"), in1=py[:], op=ALU.add),
                     R=[r_t1, r_py], W=[r_t1])
                yield
                K.op("pool", lambda: nc.gpsimd.tensor_tensor(out=t1[:], in0=t1[:], in1=xD[:], op=ALU.add),
                     R=[r_t1, r_xD], W=[r_t1])
                yield
                K.op("pool", lambda: nc.gpsimd.tensor_tensor(out=gg[:], in0=t1[:].rearrange("p a b -> p (a b)"), in1=sz[:],
                                                             op=ALU.mult), R=[r_t1, r_sz], W=[r_gg])
                yield
                K.op("act", lambda: nc.scalar.activation(out=junk[:], in_=gg[:], func=AF.Square, accum_out=ssq[:]),
                     R=[r_gg], W=[r_junk, r_ssq])
                rstd_from_ssq(K, ssq[:], r_ssq, 1, nscr, 1.0 / 256)
                yb_ = yb[c % 2]
                yield
                K.op("dve", lambda: nc.vector.scalar_tensor_tensor(out=yb_[:], in0=gg[:], scalar=nscr["rstd"], in1=snw[:],
                                                                   op0=ALU.mult, op1=ALU.mult),
                     R=[r_gg, nscr["r_rstd"], r_c], W=[r_yb[c % 2]])
                K.dma("sp", y_d[cs, 128:384], yb_[:], ds_y[c % 2], R=[r_yb[c % 2]])
        nrun = nT if STOP > 1 else 0
        active = []
        nxt_c = 0
        while nxt_c < nrun or active:
            while len(active) < nsets and nxt_c < nrun:
                active.append(chunk_flow(nxt_c))
                nxt_c += 1
            for g_ in list(active):
                try:
                    next(g_)
                except StopIteration:
                    active.remove(g_)
            yield


def run_gen(g):
    for _ in g:
        pass


def emit_ssd(K, C, hnT, r_hnT, P, y_d, S):
    outer = K.stack
    with ExitStack() as st:
        K.stack = st
        K.begin_phase()
        W = {"wS_fm": load_w(K, P["wS_fm"], 512, "wSf"), "wS_tm": load_w(K, P["wS_tm"], 260, "wSt")}
        run_gen(emit_ssd_gen(K, C, hnT, r_hnT, P, y_d, S, W, nsets=2))
        K.barrier()
        K.end_phase()
        K.stack = outer


def emit_mlstm_gen(K, C, hnT, r_hnT, P, y_d, S, W):
    nc = K.nc
    nB = S // 512
    if True:
        wfm, r_wfm = W["wM_fm"]
        wg, r_wg = W["wM_g"]
        wtm, r_wtm = W["wM_tm"]
        dsm = K.new_dma_sem()
        r_c = Reg("mconst")
        gbias = K.sb([2, 2], F32, "gbias"); mnw = K.sb([128, 128], F32, "mnw")
        K.dma("sp", gbias[:], P["gbias"], dsm, W=[r_c])
        K.dma("sp", mnw[:], P["mnw"], dsm, W=[r_c])
        ident_f = C["ident_f"]
        B = [psbank(K, f"mb{i}") for i in range(4)]
        rB = [Reg(f"mb{i}", excl=True) for i in range(4)]
        pq, pk, pgi, pgf = B[2], B[3], B[0][0:2, :], B[1][0:2, :]
        ptl = B[2][:, 0:32].rearrange("p (q i h) -> p q i h", q=4, i=4)
        pdec = B[2][:, 32:40]
        pDt = B[2][:, 0:256].rearrange("p (h t) -> p h t", t=128)
        ptm = B[3][:, 0:384]

        def T(shape, dt, name):
            return K.sb(shape, dt, name), Reg(name)
        qTb, r_qTb = T([128, 512], BF16, "mqTb")
        kTb, r_kTb = T([128, 512], BF16, "mkTb")
        rows = {}
        for n_ in ["ipre", "yv", "e", "b", "al", "cma", "mu", "nmu", "wrow", "inter", "en", "tmp"]:
            rows[n_] = T([2, 512], F32, "mr_" + n_)
        rows["nab"] = rows["e"]; rows["l"] = rows["e"]
        rows["logf"] = rows["yv"]
        mnew, r_mnew = T([2, 8], F32, "mnew")
        mprev, r_mprev = T([2, 8], F32, "mprev")
        mcar, r_mcar = T([2, 1], F32, "mcar")
        decay, r_decay = T([2, 8], F32, "mdecay")
        tl, r_tl = T([128, 4, 4, 2], F32, "mtl")
        decr, r_decr = T([128, 8], F32, "mdecr")
        ktm, r_ktm = T([128, 128], F32, "mktm")
        vaug, r_vaug = T([128, 2, 65], BF16, "mvaug")
        og, r_og = T([128, 128], F32, "mog")
        dT, r_dT = T([128, 128], F32, "mdT")
        sdT, r_sdT = T([128, 128], BF16, "msdT")
        kw, r_kw = T([128, 64], BF16, "mkw")
        Cst, r_Cst = T([128, 65], F32, "mCst")
        Cbf, r_Cbf = T([128, 65], BF16, "mCbf")
        nmv, r_nmv = T([128, 65], F32, "mnmv")
        dn, r_dn = T([128, 1], F32, "mdn")
        rn, r_rn = T([128, 1], F32, "mrn")
        hm, r_hm = T([128, 64], F32, "mhm")
        junk, r_junk = T([128, 64], F32, "mjunk")
        ssq, r_ssq = T([128, 1], F32, "mssq")
        nscr = mk_scr(K, [128, 1], "mn")
        hn2, r_hn2 = T([128, 64], F32, "mhn2")
        yb = [K.sb([128, 128], BF16, f"myb{i}") for i in range(2)]; r_yb = [Reg() for _ in range(2)]
        ds_y = [K.new_dma_sem() for _ in range(2)]
        r_Cst = [Reg("Cst0"), Reg("Cst1")]
        r_Cbf = [Reg("Cbf0"), Reg("Cbf1")]
        K.op("dve", lambda: nc.vector.memset(Cst[:], 0.0), W=r_Cst)
        K.op("dve", lambda: nc.vector.memset(Cbf[:], 0.0), W=r_Cbf)
        K.op("dve", lambda: nc.vector.memset(mcar[:], 0.0), W=[r_mcar])
        ktm2 = [T([128, 128], F32, f"mktm{i}") for i in range(2)]
        vaug2 = [T([128, 2, 65], BF16, f"mvaug{i}") for i in range(2)]
        og2 = [T([128, 128], F32, f"mog{i}") for i in range(2)]
        for i_ in range(2):
            K.op("dve", lambda i_=i_: nc.vector.memset(vaug2[i_][0][:].rearrange("p a b -> p (a b)"), 1.0), W=[vaug2[i_][1]])
        r_yb2 = [[Reg(), Reg()] for _ in range(2)]
        HT = []
        for h_ in range(2):
            d_ = {}
            d_["dT"] = T([128, 128], F32, f"mdT{h_}")
            d_["sdT"] = T([128, 128], BF16, f"msdT{h_}")
            d_["kw"] = T([128, 64], BF16, f"mkw{h_}")
            d_["nmv"] = T([128, 65], F32, f"mnmv{h_}")
            d_["dn"] = T([128, 1], F32, f"mdn{h_}")
            d_["rn"] = T([128, 1], F32, f"mrn{h_}")
            d_["hm"] = T([128, 64], F32, f"mhm{h_}")
            d_["junk"] = T([128, 64], F32, f"mjunk{h_}")
            d_["ssq"] = T([128, 1], F32, f"mssq{h_}")
            d_["hn2"] = T([128, 64], F32, f"mhn2{h_}")
            d_["nscr"] = mk_scr(K, [128, 1], f"mn{h_}")
            HT.append(d_)

        def R_(n_):
            return rows[n_][0]

        def rr(n_):
            return rows[n_][1]
        rowc = C["rowc"]
        ntile = 0
        for b in range(nB):
            bs = slice(b * 512, (b + 1) * 512)
            for (pp_, rp_, c0, dst, rdst, sc) in [(pq, rB[2], 0, qTb, r_qTb, 1.0), (pk, rB[3], 128, kTb, r_kTb, 0.125)]:
                for k in range(8):
                    K.op("pe", lambda k=k: nc.tensor.matmul(pp_, lhsT=wfm[:, k, c0:c0 + 128], rhs=hnT[:, k, bs],
                                                            start=(k == 0), stop=(k == 7)), R=[r_wfm, r_hnT], W=[rp_])
                K.op("act", lambda: nc.scalar.mul(out=dst[:], in_=pp_, mul=sc), R=[rp_], W=[rdst])
            for (pp_, rp_, c0) in [(pgi, rB[0], 0), (pgf, rB[1], 2)]:
                for k in range(8):
                    K.op("pe", lambda k=k: nc.tensor.matmul(pp_, lhsT=wg[:, k, c0:c0 + 2], rhs=hnT[:, k, bs],
                                                            start=(k == 0), stop=(k == 7)), R=[r_wg, r_hnT], W=[rp_])
            yield
            K.op("dve", lambda: nc.vector.tensor_scalar(out=R_("ipre")[:], in0=pgi, scalar1=gbias[:, 0:1], scalar2=None,
                                                        op0=ALU.add), R=[rB[0], r_c], W=[rr("ipre")])
            yield
            K.op("dve", lambda: nc.vector.tensor_scalar(out=R_("yv")[:], in0=pgf, scalar1=gbias[:, 1:2], scalar2=-1.0,
                                                        op0=ALU.add, op1=ALU.mult), R=[rB[1], r_c], W=[rr("yv")])
            yield
            K.op("dve", lambda: nc.vector.scalar_tensor_tensor(out=R_("nab")[:], in0=R_("yv")[:], scalar=-1.0, in1=R_("yv")[:],
                                                               op0=ALU.mult, op1=ALU.min), R=[rr("yv")], W=[rr("nab")])
            yield
            K.op("act", lambda: nc.scalar.activation(out=R_("e")[:], in_=R_("nab")[:], func=AF.Exp), R=[rr("nab")], W=[rr("e")])
            yield
            K.op("dve", lambda: nc.vector.tensor_scalar(out=R_("e")[:], in0=R_("e")[:], scalar1=1.0, scalar2=None, op0=ALU.add),
                 R=[rr("e")], W=[rr("e")])
            yield
            K.op("act", lambda: nc.scalar.activation(out=R_("l")[:], in_=R_("e")[:], func=AF.Ln), R=[rr("e")], W=[rr("l")])
            yield
            K.op("dve", lambda: nc.vector.scalar_tensor_tensor(out=R_("logf")[:], in0=R_("yv")[:], scalar=0.0, in1=R_("l")[:],
                                                               op0=ALU.max, op1=ALU.add), R=[rr("yv"), rr("l")], W=[rr("logf")])
            yield
            K.op("dve", lambda: nc.vector.tensor_scalar(out=R_("logf")[:], in0=R_("logf")[:], scalar1=-1.0, scalar2=None,
                                                        op0=ALU.mult), R=[rr("logf")], W=[rr("logf")])
            yield
            K.op("dve", lambda: nc.vector.tensor_tensor_scan(out=R_("b")[:], data0=rowc[:, 0, :], data1=R_("logf")[:],
                                                             initial=0.0, op0=ALU.mult, op1=ALU.add),
                 R=[rr("logf"), C["r"]], W=[rr("b")])
            yield
            K.op("dve", lambda: nc.vector.tensor_tensor(out=R_("al")[:], in0=R_("ipre")[:], in1=R_("b")[:], op=ALU.subtract),
                 R=[rr("ipre"), rr("b")], W=[rr("al")])
            yield
            K.op("dve", lambda: nc.vector.tensor_tensor_scan(out=R_("cma")[:], data0=rowc[:, 1, :], data1=R_("al")[:],
                                                             initial=0.0, op0=ALU.add, op1=ALU.max),
                 R=[rr("al"), C["r"]], W=[rr("cma")])
            cma3 = R_("cma")[:].rearrange("p (c l) -> p c l", l=64)
            b3 = R_("b")[:].rearrange("p (c l) -> p c l", l=64)
            al3 = R_("al")[:].rearrange("p (c l) -> p c l", l=64)
            mu3 = R_("mu")[:].rearrange("p (c l) -> p c l", l=64)
            tmp3 = R_("tmp")[:].rearrange("p (c l) -> p c l", l=64)
            yield
            K.op("dve", lambda: nc.vector.tensor_tensor_scan(out=mnew[:], data0=cma3[:, :, 63], data1=b3[:, :, 63],
                                                             initial=mcar[:, 0:1], op0=ALU.max, op1=ALU.add),
                 R=[rr("cma"), rr("b"), r_mcar], W=[r_mnew])
            yield
            K.op("dve", lambda: nc.vector.tensor_copy(out=mprev[:, 0:1], in_=mcar[:]), R=[r_mcar], W=[r_mprev])
            yield
            K.op("dve", lambda: nc.vector.tensor_copy(out=mprev[:, 1:8], in_=mnew[:, 0:7]), R=[r_mnew], W=[r_mprev])
            yield
            K.op("dve", lambda: nc.vector.tensor_copy(out=mcar[:], in_=mnew[:, 7:8]), R=[r_mnew, r_mprev], W=[r_mcar])
            mpb = mprev[:].unsqueeze(2).to_broadcast([2, 8, 64])
            yield
            K.op("dve", lambda: nc.vector.tensor_tensor(out=mu3, in0=cma3, in1=mpb, op=ALU.max),
                 R=[rr("cma"), r_mprev], W=[rr("mu")])
            yield
            K.op("dve", lambda: nc.vector.tensor_scalar(out=R_("nmu")[:], in0=R_("mu")[:], scalar1=-1.0, scalar2=None,
                                                        op0=ALU.mult), R=[rr("mu")], W=[rr("nmu")])
            mcb = mu3[:, :, 63].unsqueeze(2).to_broadcast([2, 8, 64])
            yield
            K.op("dve", lambda: nc.vector.tensor_tensor(out=tmp3, in0=al3, in1=mcb, op=ALU.subtract),
                 R=[rr("al"), rr("mu")], W=[rr("tmp")])
            yield
            K.op("act", lambda: nc.scalar.activation(out=R_("wrow")[:], in_=R_("tmp")[:], func=AF.Exp), R=[rr("tmp")], W=[rr("wrow")])
            yield
            K.op("dve", lambda: nc.vector.tensor_tensor(out=decay[:], in0=mprev[:], in1=mu3[:, :, 63], op=ALU.subtract),
                 R=[r_mprev, rr("mu")], W=[r_decay])
            yield
            K.op("act", lambda: nc.scalar.activation(out=decay[:], in_=decay[:], func=AF.Exp), R=[r_decay], W=[r_decay])
            yield
            K.op("dve", lambda: nc.vector.tensor_tensor(out=tmp3, in0=mu3, in1=mpb, op=ALU.subtract),
                 R=[rr("mu"), r_mprev, rr("wrow")], W=[rr("tmp")])
            yield
            K.op("act", lambda: nc.scalar.activation(out=R_("inter")[:], in_=R_("tmp")[:], func=AF.Exp, scale=-1.0),
                 R=[rr("tmp")], W=[rr("inter")])
            yield
            K.op("dve", lambda: nc.vector.tensor_tensor(out=R_("tmp")[:], in0=R_("b")[:], in1=R_("mu")[:], op=ALU.add),
                 R=[rr("b"), rr("mu"), rr("inter")], W=[rr("tmp")])
            yield
            K.op("act", lambda: nc.scalar.activation(out=R_("en")[:], in_=R_("tmp")[:], func=AF.Exp, scale=-1.0),
                 R=[rr("tmp")], W=[rr("en")])
            for qi, qn_ in enumerate(["al", "wrow", "inter", "en"]):
                for i in range(4):
                    K.op("pe", lambda qi=qi, i=i, qn_=qn_: nc.tensor.transpose(
                        out=ptl[:, qi, i, :], in_=R_(qn_)[0:2, i * 128:(i + 1) * 128], identity=ident_f[0:2, 0:2]),
                        R=[rr(qn_), C["r"]], W=[rB[2]])
            yield
            K.op("pe", lambda: nc.tensor.matmul(pdec, lhsT=C["hsel"][:], rhs=decay[:], start=True, stop=True),
                 R=[r_decay, C["r"]], W=[rB[2]])
            yield
            K.op("dve", lambda: nc.vector.tensor_copy(out=tl[:], in_=ptl), R=[rB[2]], W=[r_tl])
            yield
            K.op("dve", lambda: nc.vector.tensor_copy(out=decr[:], in_=pdec), R=[rB[2]], W=[r_decr])
            for i in range(4):
                tg = b * 4 + i
                ts = slice(tg * 128, (tg + 1) * 128)
                tb = slice(i * 128, (i + 1) * 128)
                par = ntile % 2
                ktm_, r_ktm_ = ktm2[par]
                vaug_, r_vaug_ = vaug2[par]
                og_, r_og_ = og2[par]
                for k in range(8):
                    K.op("pe", lambda k=k: nc.tensor.matmul(ptm, lhsT=hnT[:, k, ts], rhs=wtm[:, k, :],
                                                            start=(k == 0), stop=(k == 7)), R=[r_hnT, r_wtm], W=[rB[3]])
                K.op("act", lambda: nc.scalar.mul(out=ktm_[:], in_=ptm[:, 0:128], mul=0.125), R=[rB[3]], W=[r_ktm_])
                K.op("dve", lambda: nc.vector.tensor_copy(out=vaug_[:, :, 0:64],
                                                          in_=ptm[:, 128:256].rearrange("p (h d) -> p h d", d=64)),
                     R=[rB[3]], W=[r_vaug_])
                K.op("act", lambda: nc.scalar.activation(out=og_[:], in_=ptm[:, 256:384], func=AF.Sigmoid), R=[rB[3]], W=[r_og_])
                yb_ = yb[par]
                ryb2 = r_yb2[par]
                for h in range(2):
                    K.op("pe", lambda h=h: nc.tensor.matmul(pDt[:, h, :], lhsT=C["sel"][:, h, :],
                                                            rhs=R_("nmu")[0:2, tb], start=True, stop=False),
                         R=[rr("nmu"), C["r"]], W=[rB[2]])
                    K.op("pe", lambda h=h: nc.tensor.matmul(pDt[:, h, :], lhsT=ident_f[:],
                                                            rhs=C["nm2"][:, 0:128], start=False, stop=True),
                         R=[C["r"]], W=[rB[2]])
                yield

                def head_flow(h):
                    hp = slice(64 * h, 64 * h + 64)
                    hc = slice(64 * h, 64 * h + 64)
                    PB = B[h]; rPB = rB[h]
                    pS = PB[:, 0:128]; pN = PB[:, 128:193]
                    pQs = [PB[:, 200:265], PB[:, 272:337]]
                    pC = PB[:, 344:409]
                    Hh = HT[h]
                    dT, r_dT = Hh["dT"]; sdT, r_sdT = Hh["sdT"]; kw, r_kw = Hh["kw"]; nmv, r_nmv = Hh["nmv"]
                    dn, r_dn = Hh["dn"]; rn, r_rn = Hh["rn"]; hm, r_hm = Hh["hm"]; junk, r_junk = Hh["junk"]
                    ssq, r_ssq = Hh["ssq"]; hn2, r_hn2 = Hh["hn2"]; nscr = Hh["nscr"]
                    K.op("pe", lambda: nc.tensor.matmul(pS, lhsT=kTb[hp, tb], rhs=qTb[hp, tb], start=True, stop=True),
                         R=[r_kTb, r_qTb], W=[rPB])
                    K.op("act", lambda: nc.scalar.activation(out=dT[:], in_=pDt[:, h, :], func=AF.Exp,
                                                             bias=tl[:, 0, i, h:h + 1]), R=[rB[2], r_tl], W=[r_dT])
                    yield
                    K.op("dve", lambda: nc.vector.tensor_tensor(out=sdT[:], in0=pS, in1=dT[:], op=ALU.mult),
                         R=[rPB, r_dT], W=[r_sdT])
                    K.op("pe", lambda: nc.tensor.matmul(pN, lhsT=sdT[:], rhs=vaug_[:, h, :], start=True, stop=True),
                         R=[r_sdT, r_vaug_], W=[rPB])
                    yield
                    K.op("dve", lambda: nc.vector.tensor_copy(out=nmv[:], in_=pN), R=[rPB], W=[r_nmv])
                    for half in range(2):
                        ce = 2 * i + half
                        rs_ = slice(64 * half, 64 * half + 64)
                        pQ = pQs[half]
                        K.op("pe", lambda: nc.tensor.matmul(pQ, lhsT=qTb[hp, tb], rhs=Cbf[hp, :], start=True, stop=True),
                             R=[r_qTb, r_Cbf[h]], W=[rPB])
                        K.op("dve", lambda: nc.vector.tensor_scalar(out=kw[rs_, :], in0=ktm_[rs_, hc], scalar1=tl[rs_, 1, i, h:h + 1],
                                                                    scalar2=None, op0=ALU.mult), R=[r_ktm_, r_tl], W=[r_kw])
                        yield
                        K.op("dve", lambda: nc.vector.scalar_tensor_tensor(
                            out=nmv[rs_, :], in0=pQ[rs_, :], scalar=tl[rs_, 2, i, h:h + 1], in1=nmv[rs_, :],
                            op0=ALU.mult, op1=ALU.add), R=[rPB, r_tl, r_nmv], W=[r_nmv])
                        K.op("pe", lambda: nc.tensor.matmul(pC[hp, :], lhsT=kw[rs_, :], rhs=vaug_[rs_, h, :], start=True, stop=True),
                             R=[r_kw, r_vaug_], W=[rPB])
                        yield
                        K.op("dve", lambda: nc.vector.scalar_tensor_tensor(
                            out=Cst[hp, :], in0=Cst[hp, :], scalar=decr[hp, ce:ce + 1], in1=pC[hp, :],
                            op0=ALU.mult, op1=ALU.add), R=[r_Cst[h], r_decr, rPB], W=[r_Cst[h]])
                        K.op("act", lambda: nc.scalar.copy(out=Cbf[hp, :], in_=Cst[hp, :]), R=[r_Cst[h]], W=[r_Cbf[h]])
                        yield
                    K.op("dve", lambda: nc.vector.scalar_tensor_tensor(out=dn[:], in0=nmv[:, 64:65], scalar=-1.0, in1=nmv[:, 64:65],
                                                                       op0=ALU.mult, op1=ALU.max), R=[r_nmv], W=[r_dn])
                    K.op("dve", lambda: nc.vector.tensor_tensor(out=dn[:], in0=dn[:], in1=tl[:, 3, i, h:h + 1], op=ALU.max),
                         R=[r_dn, r_tl], W=[r_dn])
                    yield
                    K.op("dve", lambda: nc.vector.reciprocal(out=rn[:], in_=dn[:]), R=[r_dn], W=[r_rn])
                    K.op("dve", lambda: nc.vector.tensor_scalar(out=hm[:], in0=nmv[:, 0:64], scalar1=rn[:, 0:1], scalar2=None,
                                                                op0=ALU.mult), R=[r_nmv, r_rn], W=[r_hm])
                    yield
                    K.op("act", lambda: nc.scalar.activation(out=junk[:], in_=hm[:], func=AF.Square, accum_out=ssq[:]),
                         R=[r_hm], W=[r_junk, r_ssq])
                    yield
                    rstd_from_ssq(K, ssq[:], r_ssq, 1, nscr, 1.0 / 64)
                    yield
                    K.op("dve", lambda: nc.vector.scalar_tensor_tensor(out=hn2[:], in0=hm[:], scalar=nscr["rstd"], in1=mnw[:, hc],
                                                                       op0=ALU.mult, op1=ALU.mult),
                         R=[r_hm, nscr["r_rstd"], r_c], W=[r_hn2])
                    K.op("dve", lambda: nc.vector.tensor_tensor(out=yb_[:, hc], in0=hn2[:], in1=og_[:, hc], op=ALU.mult),
                         R=[r_hn2, r_og_], W=[ryb2[h]])

                gens = [head_flow(0), head_flow(1)]
                while gens:
                    for g_ in list(gens):
                        try:
                            next(g_)
                        except StopIteration:
                            gens.remove(g_)
                    yield
                K.dma("sp", y_d[ts, 0:128], yb_[:], ds_y[par], R=ryb2)
                ntile += 1


def emit_mlstm(K, C, hnT, r_hnT, P, y_d, S):
    outer = K.stack
    with ExitStack() as st:
        K.stack = st
        K.begin_phase()
        W = {"wM_fm": load_w(K, P["wM_fm"], 256, "wMf"), "wM_g": load_w(K, P["wM_g"], 4, "wMg"),
             "wM_tm": load_w(K, P["wM_tm"], 384, "wMt")}
        run_gen(emit_mlstm_gen(K, C, hnT, r_hnT, P, y_d, S, W))
        K.barrier()
        K.end_phase()
        K.stack = outer


def load_hnT_pair(K, hn_all, S, blk=1024, regs=None):
    hnT = K.sb([128, 8, S], BF16, "hnT")
    half = S // 2
    nq = S // blk
    rq = [Reg(f"hnT_q{i}") for i in range(nq)]
    dsq = [K.new_dma_sem() for i in range(nq)]
    n = 0
    for b, ha in enumerate(hn_all):
        for rk in range(2):
            c0 = rk * half + b * blk
            q = c0 // blk
            for k in range(8):
                eng = "sp" if (b > 0 or n % 2 == 0) else "act"
                K.dma(eng, hnT[:, k, c0:c0 + blk],
                      ha[rk * 1024 + k * 128: rk * 1024 + (k + 1) * 128, :], dsq[q], W=[rq[q]],
                      R=([regs[b]] if regs is not None else []))
                n += 1
    return hnT, rq


class YSplit:
    def __init__(self, a, b, half):
        self.a, self.b, self.half = a, b, half
        self.regs = [Reg("yhalf0"), Reg("yhalf1")]
        self.hook = None

    def reg(self, row0):
        return self.regs[0 if row0 < self.half else 1]

    def __getitem__(self, key):
        rs, cs = key
        if rs.start < self.half:
            assert rs.stop <= self.half
            return self.a[rs, cs]
        return self.b[slice(rs.start - self.half, rs.stop - self.half), cs]


def emit_ms_concurrent(K, C, hnT, r_hnT, P, y_d, S):
    outer = K.stack
    with ExitStack() as st:
        K.stack = st
        K.begin_phase()
        W = {"wM_fm": load_w(K, P["wM_fm"], 256, "wMf"), "wM_g": load_w(K, P["wM_g"], 4, "wMg"),
             "wM_tm": load_w(K, P["wM_tm"], 384, "wMt"),
             "wS_fm": load_w(K, P["wS_fm"], 512, "wSf"), "wS_tm": load_w(K, P["wS_tm"], 260, "wSt")}
        gens = [emit_mlstm_gen(K, C, hnT, r_hnT, P, y_d, S, W),
                emit_ssd_gen(K, C, hnT, r_hnT, P, y_d, S, W, nsets=1)]
        while gens:
            for g_ in list(gens):
                try:
                    next(g_)
                except StopIteration:
                    gens.remove(g_)
        K.barrier()
        K.end_phase()
        K.stack = outer


def emit_ssd2(K, C, hnT, r_hnT, P, y_d, S, G=4, W=None):
    nc = K.nc
    nT = S // 128
    nTT = S // 512
    outer = K.stack
    with ExitStack() as st:
        K.stack = st
        K.begin_phase()
        wfm, r_wfm = W["wS_fm"] if W is not None else load_w(K, P["wS_fm"], 512, "wSf")
        wtm, r_wtm = W["wS_tm"] if W is not None else load_w(K, P["wS_tm"], 260, "wSt")
        dsm = K.new_dma_sem()
        r_c = Reg("sconst")
        cw = K.sb([128, 4, 4], F32, "cw"); cb = K.sb([128, 4], F32, "cb")
        dtb = K.sb([128, 4], F32, "dtb"); alog = K.sb([128, 4], F32, "alog")
        dsk = K.sb([128, 4], F32, "dsk"); snw = K.sb([128, 256], F32, "snw")
        for t_, n_ in [(cw, "cw"), (cb, "cb"), (dtb, "dtb"), (alog, "alog"), (dsk, "dsk"), (snw, "snw")]:
            K.dma("sp", t_[:], P[n_], dsm, W=[r_c])
        Aneg = K.sb([128, 4], F32, "Aneg"); r_A = Reg()
        K.op("act", lambda: nc.scalar.activation(out=Aneg[:], in_=alog[:], func=AF.Exp), R=[r_c], W=[r_A])
        K.op("dve", lambda: nc.vector.tensor_scalar(out=Aneg[:], in0=Aneg[:], scalar1=-1.0, scalar2=None, op0=ALU.mult),
             R=[r_A], W=[r_A])
        xc = [K.sb([128, S], BF16, f"xc{i}") for i in range(4)]
        r_xc = [Reg() for _ in range(4)]
        X = [psbank(K, f"sx{i}") for i in range(8)]
        rX = [Reg(f"sx{i}", excl=True) for i in range(8)]
        with ExitStack() as st2:
            K.stack = st2
            SH = S // 2
            xpre = K.sb([128, SH + 3], F32, "xpre"); r_xpre = Reg()
            cacc = K.sb([128, SH], F32, "cacc"); r_cacc = Reg()
            n = 0
            for ct in range(4):
                for hf in range(2):
                    if hf == 0:
                        K.op("dve", lambda: nc.vector.memset(xpre[:, 0:3], 0.0), W=[r_xpre])
                    else:
                        K.op("dve", lambda: nc.vector.tensor_copy(out=xpre[:, 0:3], in_=xpre[:, SH:SH + 3]),
                             R=[r_xpre], W=[r_xpre])
                    for tt in range(nTT // 2):
                        tg_ = hf * (nTT // 2) + tt
                        p_ = X[n % 4]; rp = rX[n % 4]; n += 1
                        for k in range(8):
                            K.op("pe", lambda k=k: nc.tensor.matmul(p_, lhsT=wfm[:, k, ct * 128:(ct + 1) * 128],
                                                                    rhs=hnT[:, k, tg_ * 512:(tg_ + 1) * 512],
                                                                    start=(k == 0), stop=(k == 7)),
                                 R=[r_wfm, r_hnT], W=[rp])
                        K.op("act", lambda: nc.scalar.copy(out=xpre[:, 3 + tt * 512:3 + (tt + 1) * 512], in_=p_),
                             R=[rp], W=[r_xpre])
                    K.op("dve", lambda: nc.vector.tensor_scalar(out=cacc[:], in0=xpre[:, 0:SH], scalar1=cw[:, ct, 0:1],
                                                                scalar2=None, op0=ALU.mult), R=[r_xpre, r_c], W=[r_cacc])
                    for j in range(1, 4):
                        K.op("dve", lambda j=j: nc.vector.scalar_tensor_tensor(
                            out=cacc[:], in0=xpre[:, j:SH + j], scalar=cw[:, ct, j:j + 1], in1=cacc[:],
                            op0=ALU.mult, op1=ALU.add), R=[r_xpre, r_c, r_cacc], W=[r_cacc])
                    K.op("act", lambda: nc.scalar.activation(out=xc[ct][:, hf * SH:(hf + 1) * SH], in_=cacc[:], func=AF.Silu,
                                                             bias=cb[:, ct:ct + 1]),
                         R=[r_cacc, r_c], W=[r_xc[ct]])
            K.barrier()
            K.stack = st

        def T(shape, dt, name):
            return K.sb(shape, dt, name), Reg(name)
        sz, r_sz = T([128, G, 256], BF16, "gsz")
        dtx, r_dtx = T([128, G, 4], F32, "gdtx"); ax, r_ax = T([128, G, 4], F32, "gax")
        ex, r_ex = T([128, G, 4], F32, "gex"); lx, r_lx = T([128, G, 4], F32, "glx")
        dt, r_dt = T([128, G, 4], F32, "gdt"); aa, r_aa = T([128, G, 4], F32, "gaa")
        acs, r_acs = T([128, G, 4], F32, "gacs"); nacs, r_nacs = T([128, G, 4], F32, "gnacs")
        el, r_el = T([128, G, 4], F32, "gel"); cd, r_cd = T([128, G, 4], F32, "gcd")
        dd, r_dd = T([128, G, 4], F32, "gdd"); dec, r_dec = T([128, G, 4], F32, "gdec")
        dtdec, r_dtdec = T([128, G, 4], F32, "gdtdec")
        rseg = [T([128, 4, 128], F32, f"grseg{i}") for i in range(2)]
        segT = [T([128, 4, 128], F32, f"gsegT{i}") for i in range(G)]
        xdt = [T([128, 4, 64], BF16, f"gxdt{i}") for i in range(G)]
        xdd = [T([128, 4, 64], BF16, f"gxdd{i}") for i in range(G)]
        xD = [T([128, 4, 64], F32, f"gxD{i}") for i in range(G)]
        Btm = [T([128, 128], BF16, f"gBtm{i}") for i in range(G)]
        Gm, r_Gm = T([128, G, 128], F32, "gGm")
        scT = [T([128, 4, 128], BF16, f"gscT{i}") for i in range(G)]
        t0 = [T([128, 256], F32, f"gt0{i}") for i in range(G)]
        Sf, r_Sf = T([128, 4, 64], F32, "gSf")
        Sbf = [T([128, 256], BF16, f"gSbf{i}") for i in range(G + 1)]
        gg = [T([128, 256], F32, f"ggg{i}") for i in range(G)]
        junk, r_junk = T([128, 256], BF16, "gjunk")
        ssq, r_ssq = T([128, G], F32, "gssq")
        nscr = mk_scr(K, [128, G], "gn")
        yb = [K.sb([128, 256], BF16, f"gyb{i}") for i in range(G)]; r_yb = [Reg() for _ in range(G)]
        ds_y = [K.new_dma_sem() for _ in range(G)]
        K.op("dve", lambda: nc.vector.memset(Sf[:].rearrange("p a b -> p (a b)"), 0.0), W=[r_Sf])
        K.op("dve", lambda: nc.vector.memset(Sbf[0][0][:], 0.0), W=[Sbf[0][1]])
        ident_f = C["ident_f"]
        fl = lambda t: t[:].rearrange("p a b -> p (a b)")
        for g0 in range(0, nT, G):
            cs = [slice((g0 + i) * 128, (g0 + i + 1) * 128) for i in range(G)]
            pz = [X[0][:, 0:256], X[0][:, 256:512], X[1][:, 0:256], X[1][:, 256:512]]
            rpz = [rX[0], rX[0], rX[1], rX[1]]
            pdt = X[2][:, 0:4 * G].rearrange("p (g h) -> p g h", h=4)
            for i in range(G):
                for k in range(8):
                    K.op("pe", lambda k=k, i=i: nc.tensor.matmul(pz[i], lhsT=hnT[:, k, cs[i]], rhs=wtm[:, k, 0:256],
                                                                 start=(k == 0), stop=(k == 7)), R=[r_hnT, r_wtm], W=[rpz[i]])
                for k in range(8):
                    K.op("pe", lambda k=k, i=i: nc.tensor.matmul(pdt[:, i, :], lhsT=hnT[:, k, cs[i]], rhs=wtm[:, k, 256:260],
                                                                 start=(k == 0), stop=(k == 7)), R=[r_hnT, r_wtm], W=[rX[2]])
            for i in range(0, G, 2):
                K.op("act", lambda i=i: nc.scalar.activation(out=sz[:, i:i + 2, :].rearrange("p a b -> p (a b)"),
                                                             in_=X[i // 2], func=AF.Silu), R=[rpz[i]], W=[r_sz])
            K.op("dve", lambda: nc.vector.tensor_tensor(out=dtx[:], in0=pdt, in1=dtb[:].unsqueeze(1).to_broadcast([128, G, 4]),
                                                        op=ALU.add), R=[rX[2], r_c], W=[r_dtx])
            K.op("dve", lambda: nc.vector.scalar_tensor_tensor(out=ax[:], in0=dtx[:], scalar=-1.0, in1=dtx[:],
                                                               op0=ALU.mult, op1=ALU.min), R=[r_dtx], W=[r_ax])
            K.op("act", lambda: nc.scalar.activation(out=ex[:], in_=ax[:], func=AF.Exp), R=[r_ax], W=[r_ex])
            K.op("dve", lambda: nc.vector.tensor_scalar(out=ex[:], in0=ex[:], scalar1=1.0, scalar2=None, op0=ALU.add),
                 R=[r_ex], W=[r_ex])
            K.op("act", lambda: nc.scalar.activation(out=lx[:], in_=ex[:], func=AF.Ln), R=[r_ex], W=[r_lx])
            K.op("dve", lambda: nc.vector.scalar_tensor_tensor(out=dt[:], in0=dtx[:], scalar=0.0, in1=lx[:],
                                                               op0=ALU.max, op1=ALU.add), R=[r_dtx, r_lx], W=[r_dt])
            K.op("dve", lambda: nc.vector.tensor_tensor(out=aa[:], in0=dt[:], in1=Aneg[:].unsqueeze(1).to_broadcast([128, G, 4]),
                                                        op=ALU.mult), R=[r_dt, r_A], W=[r_aa])
            pacs = X[2][:, 64:64 + 4 * G].rearrange("p (g h) -> p g h", h=4)
            plast = X[2][:, 128:128 + 4 * G].rearrange("p (g h) -> p g h", h=4)
            K.op("pe", lambda: nc.tensor.matmul(X[2][:, 64:64 + 4 * G], lhsT=C["tri"], rhs=fl(aa), start=True, stop=True),
                 R=[r_aa, C["r"]], W=[rX[2]])
            K.op("pe", lambda: nc.tensor.matmul(X[2][:, 128:128 + 4 * G], lhsT=C["ones"], rhs=fl(aa), start=True, stop=True),
                 R=[r_aa, C["r"]], W=[rX[2]])
            K.op("dve", lambda: nc.vector.tensor_copy(out=acs[:], in_=pacs), R=[rX[2]], W=[r_acs])
            K.op("dve", lambda: nc.vector.tensor_scalar(out=nacs[:], in0=pacs, scalar1=-1.0, scalar2=None, op0=ALU.mult),
                 R=[rX[2]], W=[r_nacs])
            K.op("dve", lambda: nc.vector.tensor_tensor(out=dd[:], in0=plast, in1=acs[:], op=ALU.subtract),
                 R=[rX[2], r_acs], W=[r_dd])
            K.op("act", lambda: nc.scalar.activation(out=el[:], in_=acs[:], func=AF.Exp), R=[r_acs], W=[r_el])
            K.op("act", lambda: nc.scalar.activation(out=cd[:], in_=plast, func=AF.Exp), R=[rX[2]], W=[r_cd])
            K.op("act", lambda: nc.scalar.activation(out=dec[:], in_=dd[:], func=AF.Exp), R=[r_dd], W=[r_dec])
            K.op("dve", lambda: nc.vector.tensor_tensor(out=dtdec[:], in0=dt[:], in1=dec[:], op=ALU.mult),
                 R=[r_dt, r_dec], W=[r_dtdec])
            for i in range(G):
                rs_, rrs = rseg[i % 2]
                ps_ = X[3 + i % 2]; rps = rX[3 + i % 2]
                K.op("dve", lambda i=i: nc.vector.tensor_tensor(out=rs_[:], in0=ident_f[:].unsqueeze(1).to_broadcast([128, 4, 128]),
                                                                in1=acs[:, i, :].unsqueeze(2).to_broadcast([128, 4, 128]), op=ALU.mult),
                     R=[C["r"], r_acs], W=[rrs])
                K.op("pe", lambda: nc.tensor.matmul(ps_, lhsT=C["ones"], rhs=fl(rs_), start=True, stop=False),
                     R=[rrs, C["r"]], W=[rps])
                K.op("pe", lambda: nc.tensor.matmul(ps_, lhsT=ident_f[:], rhs=C["nm1"], start=False, stop=True),
                     R=[C["r"]], W=[rps])
                for h in range(4):
                    K.op("act", lambda i=i, h=h: nc.scalar.activation(out=segT[i][0][:, h, :], in_=ps_[:, h * 128:(h + 1) * 128],
                                                                      func=AF.Exp, bias=nacs[:, i, h:h + 1]),
                         R=[rps, r_nacs], W=[segT[i][1]])
            for i in range(G):
                pt_ = X[5 + i % 2][:, 0:192].bitcast(BF16).rearrange("p (a b) -> p a b", b=128); rpt = rX[5 + i % 2]
                for a in range(3):
                    K.op("pe", lambda a=a, i=i: nc.tensor.transpose(out=pt_[:, a, :], in_=xc[a][:, cs[i]], identity=C["ident_bf"][:]),
                         R=[r_xc[a], C["r"]], W=[rpt])
                xs_v = pt_[:, 0:2, :].rearrange("p a (h d) -> p (a h) d", d=64)
                K.op("dve", lambda i=i: nc.vector.tensor_tensor(out=xdt[i][0][:], in0=xs_v, in1=dt[:, i, :].unsqueeze(2).to_broadcast([128, 4, 64]),
                                                                op=ALU.mult), R=[rpt, r_dt], W=[xdt[i][1]])
                K.op("dve", lambda i=i: nc.vector.tensor_tensor(out=xdd[i][0][:], in0=xs_v, in1=dtdec[:, i, :].unsqueeze(2).to_broadcast([128, 4, 64]),
                                                                op=ALU.mult), R=[rpt, r_dtdec], W=[xdd[i][1]])
                K.op("dve", lambda i=i: nc.vector.tensor_tensor(out=xD[i][0][:], in0=xs_v, in1=dsk[:].unsqueeze(2).to_broadcast([128, 4, 64]),
                                                                op=ALU.mult), R=[rpt, r_c], W=[xD[i][1]])
                K.op("dve", lambda i=i: nc.vector.tensor_copy(out=Btm[i][0][:], in_=pt_[:, 2, :]), R=[rpt], W=[Btm[i][1]])
            for i in range(G):
                K.op("pe", lambda i=i: nc.tensor.matmul(X[7][:, i * 128:(i + 1) * 128], lhsT=xc[2][:, cs[i]], rhs=xc[3][:, cs[i]],
                                                        start=True, stop=True), R=[r_xc[2], r_xc[3]], W=[rX[7]])
            K.op("dve", lambda: nc.vector.tensor_tensor(out=Gm[:], in0=X[7][:, 0:G * 128].rearrange("p (g l) -> p g l", l=128),
                                                        in1=C["tri"].unsqueeze(1).to_broadcast([128, G, 128]), op=ALU.mult),
                 R=[rX[7], C["r"]], W=[r_Gm])
            for i in range(G):
                K.op("dve", lambda i=i: nc.vector.tensor_tensor(out=scT[i][0][:], in0=Gm[:, i, :].unsqueeze(1).to_broadcast([128, 4, 128]),
                                                                in1=segT[i][0][:], op=ALU.mult), R=[r_Gm, segT[i][1]], W=[scT[i][1]])
            for i in range(G):
                py_ = X[i % 2][:, (i // 2 % 2) * 256:(i // 2 % 2) * 256 + 256]; rpy = rX[i % 2]
                for h in range(4):
                    K.op("pe", lambda i=i, h=h: nc.tensor.matmul(py_[:, h * 64:(h + 1) * 64], lhsT=scT[i][0][:, h, :], rhs=xdt[i][0][:, h, :],
                                                                 start=True, stop=True), R=[scT[i][1], xdt[i][1]], W=[rpy])
                K.op("dve", lambda i=i: nc.vector.tensor_tensor(out=t0[i][0][:], in0=py_, in1=fl(xD[i][0]), op=ALU.add),
                     R=[rpy, xD[i][1]], W=[t0[i][1]])
            for i in range(G):
                pst_ = X[3 + i % 2][:, 0:256]; rpst = rX[3 + i % 2]
                K.op("pe", lambda i=i: nc.tensor.matmul(pst_, lhsT=Btm[i][0][:], rhs=fl(xdd[i][0]), start=True, stop=True),
                     R=[Btm[i][1], xdd[i][1]], W=[rpst])
                K.op("dve", lambda i=i: nc.vector.tensor_tensor(out=Sf[:], in0=Sf[:], in1=cd[:, i, :].unsqueeze(2).to_broadcast([128, 4, 64]),
                                                                op=ALU.mult), R=[r_Sf, r_cd], W=[r_Sf])
                K.op("dve", lambda: nc.vector.tensor_tensor(out=fl(Sf), in0=fl(Sf), in1=pst_, op=ALU.add),
                     R=[r_Sf, rpst], W=[r_Sf])
                K.op("act", lambda i=i: nc.scalar.copy(out=Sbf[i + 1][0][:], in_=fl(Sf)), R=[r_Sf], W=[Sbf[i + 1][1]])
            for i in range(G):
                pyo_ = X[5 + i % 2][:, 256:512]; rpyo = rX[5 + i % 2]
                K.op("pe", lambda i=i: nc.tensor.matmul(pyo_, lhsT=xc[3][:, cs[i]], rhs=Sbf[i][0][:], start=True, stop=True),
                     R=[r_xc[3], Sbf[i][1]], W=[rpyo])
                K.op("dve", lambda i=i: nc.vector.tensor_tensor(out=gg[i][0][:].rearrange("p (a b) -> p a b", b=64),
                                                                in0=pyo_.rearrange("p (a b) -> p a b", b=64),
                                                                in1=el[:, i, :].unsqueeze(2).to_broadcast([128, 4, 64]), op=ALU.mult),
                     R=[rpyo, r_el], W=[gg[i][1]])
                K.op("dve", lambda i=i: nc.vector.tensor_tensor(out=gg[i][0][:], in0=gg[i][0][:], in1=t0[i][0][:], op=ALU.add),
                     R=[gg[i][1], t0[i][1]], W=[gg[i][1]])
                K.op("dve", lambda i=i: nc.vector.tensor_tensor(out=gg[i][0][:], in0=gg[i][0][:], in1=sz[:, i, :], op=ALU.mult),
                     R=[gg[i][1], r_sz], W=[gg[i][1]])
                K.op("act", lambda i=i: nc.scalar.activation(out=junk[:], in_=gg[i][0][:], func=AF.Square, accum_out=ssq[:, i:i + 1]),
                     R=[gg[i][1]], W=[r_junk, r_ssq])
            rstd_from_ssq(K, ssq[:], r_ssq, G, nscr, 1.0 / 256)
            for i in range(G):
                K.op("dve", lambda i=i: nc.vector.scalar_tensor_tensor(out=yb[i][:], in0=gg[i][0][:], scalar=nscr["rstd"][:, i:i + 1],
                                                                       in1=snw[:], op0=ALU.mult, op1=ALU.mult),
                     R=[gg[i][1], nscr["r_rstd"], r_c], W=[r_yb[i]])
                K.dma("sp", y_d[cs[i], 128:384], yb[i][:], ds_y[i], R=[r_yb[i]])
            K.op("act", lambda: nc.scalar.copy(out=Sbf[0][0][:], in_=Sbf[G][0][:]), R=[Sbf[G][1]], W=[Sbf[0][1]])
        K.barrier()
        K.end_phase()
        K.stack = outer


def emit_ssd3(K, C, hnT, r_hnT, P, y_d, S, G=4, AW=3):
    nc = K.nc
    nT = S // 128
    nTT = S // 512
    outer = K.stack
    with ExitStack() as st:
        K.stack = st
        K.begin_phase()
        wfm, r_wfm = load_w(K, P["wS_fm"], 512, "wSf")
        wtm, r_wtm = load_w(K, P["wS_tm"], 260, "wSt")
        dsm = K.new_dma_sem()
        r_c = Reg("sconst")
        cw = K.sb([128, 4, 4], F32, "cw"); cb = K.sb([128, 4], F32, "cb")
        dtb = K.sb([128, 4], F32, "dtb"); alog = K.sb([128, 4], F32, "alog")
        dsk = K.sb([128, 4], F32, "dsk"); snw = K.sb([128, 256], F32, "snw")
        for t_, n_ in [(cw, "cw"), (cb, "cb"), (dtb, "dtb"), (alog, "alog"), (dsk, "dsk"), (snw, "snw")]:
            K.dma("sp", t_[:], P[n_], dsm, W=[r_c])
        Aneg = K.sb([128, 4], F32, "Aneg"); r_A = Reg()
        K.op("act", lambda: nc.scalar.activation(out=Aneg[:], in_=alog[:], func=AF.Exp), R=[r_c], W=[r_A])
        K.op("dve", lambda: nc.vector.tensor_scalar(out=Aneg[:], in0=Aneg[:], scalar1=-1.0, scalar2=None, op0=ALU.mult),
             R=[r_A], W=[r_A])
        xc = [K.sb([128, S], BF16, f"xc{i}") for i in range(4)]
        r_xc = [Reg() for _ in range(4)]
        X = [psbank(K, f"sx{i}") for i in range(8)]
        rX = [Reg(f"sx{i}", excl=True) for i in range(8)]
        with ExitStack() as st2:
            K.stack = st2
            SH = S // 2
            xpre = K.sb([128, SH + 3], F32, "xpre"); r_xpre = Reg()
            cacc = K.sb([128, SH], F32, "cacc"); r_cacc = Reg()
            n = 0
            for ct in range(4):
                for hf in range(2):
                    if hf == 0:
                        K.op("dve", lambda: nc.vector.memset(xpre[:, 0:3], 0.0), W=[r_xpre])
                    else:
                        K.op("dve", lambda: nc.vector.tensor_copy(out=xpre[:, 0:3], in_=xpre[:, SH:SH + 3]),
                             R=[r_xpre], W=[r_xpre])
                    for tt in range(nTT // 2):
                        tg_ = hf * (nTT // 2) + tt
                        p_ = X[n % 4]; rp = rX[n % 4]; n += 1
                        for k in range(8):
                            K.op("pe", lambda k=k: nc.tensor.matmul(p_, lhsT=wfm[:, k, ct * 128:(ct + 1) * 128],
                                                                    rhs=hnT[:, k, tg_ * 512:(tg_ + 1) * 512],
                                                                    start=(k == 0), stop=(k == 7)),
                                 R=[r_wfm, r_hnT], W=[rp])
                        K.op("act", lambda: nc.scalar.copy(out=xpre[:, 3 + tt * 512:3 + (tt + 1) * 512], in_=p_),
                             R=[rp], W=[r_xpre])
                    K.op("dve", lambda: nc.vector.tensor_scalar(out=cacc[:], in0=xpre[:, 0:SH], scalar1=cw[:, ct, 0:1],
                                                                scalar2=None, op0=ALU.mult), R=[r_xpre, r_c], W=[r_cacc])
                    for j in range(1, 4):
                        K.op("dve", lambda j=j: nc.vector.scalar_tensor_tensor(
                            out=cacc[:], in0=xpre[:, j:SH + j], scalar=cw[:, ct, j:j + 1], in1=cacc[:],
                            op0=ALU.mult, op1=ALU.add), R=[r_xpre, r_c, r_cacc], W=[r_cacc])
                    K.op("act", lambda: nc.scalar.activation(out=xc[ct][:, hf * SH:(hf + 1) * SH], in_=cacc[:], func=AF.Silu,
                                                             bias=cb[:, ct:ct + 1]),
                         R=[r_cacc, r_c], W=[r_xc[ct]])
            K.barrier()
            K.stack = st

        def T(shape, dt, name):
            return K.sb(shape, dt, name + SFX[0]), Reg(name)
        SFX = ['']

        def mkset(par):
            SFX[0] = f'_{par}'
            sz, r_sz = T([128, G, 256], BF16, "gsz")
            dtx, r_dtx = T([128, G, 4], F32, "gdtx"); ax, r_ax = T([128, G, 4], F32, "gax")
            ex, r_ex = T([128, G, 4], F32, "gex"); lx, r_lx = T([128, G, 4], F32, "glx")
            dt, r_dt = T([128, G, 4], F32, "gdt"); aa, r_aa = T([128, G, 4], F32, "gaa")
            acs, r_acs = T([128, G, 4], F32, "gacs"); nacs, r_nacs = T([128, G, 4], F32, "gnacs")
            el, r_el = T([128, G, 4], F32, "gel"); cd, r_cd = T([128, G, 4], F32, "gcd")
            dd, r_dd = T([128, G, 4], F32, "gdd"); dec, r_dec = T([128, G, 4], F32, "gdec")
            dtdec, r_dtdec = T([128, G, 4], F32, "gdtdec")
            rseg = [T([128, 4, 128], F32, f"grseg{i}") for i in range(2)]
            segT = [T([128, 4, 128], F32, f"gsegT{i}") for i in range(G)]
            xdt = [T([128, 4, 64], BF16, f"gxdt{i}") for i in range(G)]
            xdd = [T([128, 4, 64], BF16, f"gxdd{i}") for i in range(G)]
            xD = [T([128, 4, 64], F32, f"gxD{i}") for i in range(G)]
            Btm = [T([128, 128], BF16, f"gBtm{i}") for i in range(G)]
            Gm, r_Gm = T([128, G, 128], F32, "gGm")
            scT = [T([128, 4, 128], BF16, f"gscT{i}") for i in range(G)]
            t0 = [T([128, 256], F32, f"gt0{i}") for i in range(G)]
            Sbf = [T([128, 256], BF16, f"gSbf{i}") for i in range(G + 1)]
            gg = [T([128, 256], F32, f"ggg{i}") for i in range(G)]
            junk, r_junk = T([128, 256], BF16, "gjunk")
            ssq, r_ssq = T([128, G], F32, "gssq")
            nscr = mk_scr(K, [128, G], "gn")
            yb = [K.sb([128, 256], BF16, f"gyb{i}") for i in range(G)]; r_yb = [Reg() for _ in range(G)]
            return dict(locals())
        sets = [mkset(0), mkset(1)]
        SFX[0] = ''
        Sf, r_Sf = T([128, 4, 64], F32, "gSf")
        ds_y = [K.new_dma_sem() for _ in range(2)]
        K.op("dve", lambda: nc.vector.memset(Sf[:].rearrange("p a b -> p (a b)"), 0.0), W=[r_Sf])
        K.op("dve", lambda: nc.vector.memset(sets[0]["Sbf"][0][0][:], 0.0), W=[sets[0]["Sbf"][0][1]])
        ident_f = C["ident_f"]
        fl = lambda t: t[:].rearrange("p a b -> p (a b)")

        def genA(g0, S_):
            cs = [slice((g0 + i) * 128, (g0 + i + 1) * 128) for i in range(G)]
            sz, r_sz, dtx, r_dtx, ax, r_ax, ex, r_ex, lx, r_lx, dt, r_dt, aa, r_aa, acs, r_acs, nacs, r_nacs, el, r_el, cd, r_cd, dd, r_dd, dec, r_dec, dtdec, r_dtdec, rseg, segT, xdt, xdd, xD, Btm, Gm, r_Gm, scT, t0, Sbf, gg, junk, r_junk, ssq, r_ssq, nscr, yb, r_yb = [S_[n_] for n_ in ['sz', 'r_sz', 'dtx', 'r_dtx', 'ax', 'r_ax', 'ex', 'r_ex', 'lx', 'r_lx', 'dt', 'r_dt', 'aa', 'r_aa', 'acs', 'r_acs', 'nacs', 'r_nacs', 'el', 'r_el', 'cd', 'r_cd', 'dd', 'r_dd', 'dec', 'r_dec', 'dtdec', 'r_dtdec', 'rseg', 'segT', 'xdt', 'xdd', 'xD', 'Btm', 'Gm', 'r_Gm', 'scT', 't0', 'Sbf', 'gg', 'junk', 'r_junk', 'ssq', 'r_ssq', 'nscr', 'yb', 'r_yb']]
            cs = [slice((g0 + i) * 128, (g0 + i + 1) * 128) for i in range(G)]
            pz = [X[0][:, 0:256], X[0][:, 256:512], X[1][:, 0:256], X[1][:, 256:512]]
            rpz = [rX[0], rX[0], rX[1], rX[1]]
            pdt = X[2][:, 0:4 * G].rearrange("p (g h) -> p g h", h=4)
            yield
            for i in range(G):
                for k in range(8):
                    K.op("pe", lambda k=k, i=i: nc.tensor.matmul(pz[i], lhsT=hnT[:, k, cs[i]], rhs=wtm[:, k, 0:256],
                                                                 start=(k == 0), stop=(k == 7)), R=[r_hnT, r_wtm], W=[rpz[i]])
                yield
                for k in range(8):
                    K.op("pe", lambda k=k, i=i: nc.tensor.matmul(pdt[:, i, :], lhsT=hnT[:, k, cs[i]], rhs=wtm[:, k, 256:260],
                                                                 start=(k == 0), stop=(k == 7)), R=[r_hnT, r_wtm], W=[rX[2]])
            yield
            for i in range(0, G, 2):
                K.op("act", lambda i=i: nc.scalar.activation(out=sz[:, i:i + 2, :].rearrange("p a b -> p (a b)"),
                                                             in_=X[i // 2], func=AF.Silu), R=[rpz[i]], W=[r_sz])
            yield
            K.op("dve", lambda: nc.vector.tensor_tensor(out=dtx[:], in0=pdt, in1=dtb[:].unsqueeze(1).to_broadcast([128, G, 4]),
                                                        op=ALU.add), R=[rX[2], r_c], W=[r_dtx])
            yield
            K.op("dve", lambda: nc.vector.scalar_tensor_tensor(out=ax[:], in0=dtx[:], scalar=-1.0, in1=dtx[:],
                                                               op0=ALU.mult, op1=ALU.min), R=[r_dtx], W=[r_ax])
            yield
            K.op("act", lambda: nc.scalar.activation(out=ex[:], in_=ax[:], func=AF.Exp), R=[r_ax], W=[r_ex])
            yield
            K.op("dve", lambda: nc.vector.tensor_scalar(out=ex[:], in0=ex[:], scalar1=1.0, scalar2=None, op0=ALU.add),
                 R=[r_ex], W=[r_ex])
            yield
            K.op("act", lambda: nc.scalar.activation(out=lx[:], in_=ex[:], func=AF.Ln), R=[r_ex], W=[r_lx])
            yield
            K.op("dve", lambda: nc.vector.scalar_tensor_tensor(out=dt[:], in0=dtx[:], scalar=0.0, in1=lx[:],
                                                               op0=ALU.max, op1=ALU.add), R=[r_dtx, r_lx], W=[r_dt])
            yield
            K.op("dve", lambda: nc.vector.tensor_tensor(out=aa[:], in0=dt[:], in1=Aneg[:].unsqueeze(1).to_broadcast([128, G, 4]),
                                                        op=ALU.mult), R=[r_dt, r_A], W=[r_aa])
            pacs = X[2][:, 64:64 + 4 * G].rearrange("p (g h) -> p g h", h=4)
            plast = X[2][:, 128:128 + 4 * G].rearrange("p (g h) -> p g h", h=4)
            yield
            K.op("pe", lambda: nc.tensor.matmul(X[2][:, 64:64 + 4 * G], lhsT=C["tri"], rhs=fl(aa), start=True, stop=True),
                 R=[r_aa, C["r"]], W=[rX[2]])
            yield
            K.op("pe", lambda: nc.tensor.matmul(X[2][:, 128:128 + 4 * G], lhsT=C["ones"], rhs=fl(aa), start=True, stop=True),
                 R=[r_aa, C["r"]], W=[rX[2]])
            yield
            K.op("dve", lambda: nc.vector.tensor_copy(out=acs[:], in_=pacs), R=[rX[2]], W=[r_acs])
            yield
            K.op("dve", lambda: nc.vector.tensor_scalar(out=nacs[:], in0=pacs, scalar1=-1.0, scalar2=None, op0=ALU.mult),
                 R=[rX[2]], W=[r_nacs])
            yield
            K.op("dve", lambda: nc.vector.tensor_tensor(out=dd[:], in0=plast, in1=acs[:], op=ALU.subtract),
                 R=[rX[2], r_acs], W=[r_dd])
            yield
            K.op("act", lambda: nc.scalar.activation(out=el[:], in_=acs[:], func=AF.Exp), R=[r_acs], W=[r_el])
            yield
            K.op("act", lambda: nc.scalar.activation(out=cd[:], in_=plast, func=AF.Exp), R=[rX[2]], W=[r_cd])
            yield
            K.op("act", lambda: nc.scalar.activation(out=dec[:], in_=dd[:], func=AF.Exp), R=[r_dd], W=[r_dec])
            yield
            K.op("dve", lambda: nc.vector.tensor_tensor(out=dtdec[:], in0=dt[:], in1=dec[:], op=ALU.mult),
                 R=[r_dt, r_dec], W=[r_dtdec])
            yield
            for i in range(G):
                rs_, rrs = rseg[i % 2]
                ps_ = X[3]; rps = rX[3]
                K.op("dve", lambda i=i: nc.vector.tensor_tensor(out=rs_[:], in0=ident_f[:].unsqueeze(1).to_broadcast([128, 4, 128]),
                                                                in1=acs[:, i, :].unsqueeze(2).to_broadcast([128, 4, 128]), op=ALU.mult),
                     R=[C["r"], r_acs], W=[rrs])
                yield
                K.op("pe", lambda: nc.tensor.matmul(ps_, lhsT=C["ones"], rhs=fl(rs_), start=True, stop=False),
                     R=[rrs, C["r"]], W=[rps])
                yield
                K.op("pe", lambda: nc.tensor.matmul(ps_, lhsT=ident_f[:], rhs=C["nm1"], start=False, stop=True),
                     R=[C["r"]], W=[rps])
                yield
                for h in range(4):
                    K.op("act", lambda i=i, h=h: nc.scalar.activation(out=segT[i][0][:, h, :], in_=ps_[:, h * 128:(h + 1) * 128],
                                                                      func=AF.Exp, bias=nacs[:, i, h:h + 1]),
                         R=[rps, r_nacs], W=[segT[i][1]])
            yield
            for i in range(G):
                pt_ = X[5][:, 0:192].bitcast(BF16).rearrange("p (a b) -> p a b", b=128); rpt = rX[5]
                for a in range(3):
                    K.op("pe", lambda a=a, i=i: nc.tensor.transpose(out=pt_[:, a, :], in_=xc[a][:, cs[i]], identity=C["ident_bf"][:]),
                         R=[r_xc[a], C["r"]], W=[rpt])
                xs_v = pt_[:, 0:2, :].rearrange("p a (h d) -> p (a h) d", d=64)
                yield
                K.op("dve", lambda i=i: nc.vector.tensor_tensor(out=xdt[i][0][:], in0=xs_v, in1=dt[:, i, :].unsqueeze(2).to_broadcast([128, 4, 64]),
                                                                op=ALU.mult), R=[rpt, r_dt], W=[xdt[i][1]])
                yield
                K.op("dve", lambda i=i: nc.vector.tensor_tensor(out=xdd[i][0][:], in0=xs_v, in1=dtdec[:, i, :].unsqueeze(2).to_broadcast([128, 4, 64]),
                                                                op=ALU.mult), R=[rpt, r_dtdec], W=[xdd[i][1]])
                yield
                K.op("dve", lambda i=i: nc.vector.tensor_tensor(out=xD[i][0][:], in0=xs_v, in1=dsk[:].unsqueeze(2).to_broadcast([128, 4, 64]),
                                                                op=ALU.mult), R=[rpt, r_c], W=[xD[i][1]])
                yield
                K.op("dve", lambda i=i: nc.vector.tensor_copy(out=Btm[i][0][:], in_=pt_[:, 2, :]), R=[rpt], W=[Btm[i][1]])
            yield
            for i in range(G):
                K.op("pe", lambda i=i: nc.tensor.matmul(X[7][:, i * 128:(i + 1) * 128], lhsT=xc[2][:, cs[i]], rhs=xc[3][:, cs[i]],
                                                        start=True, stop=True), R=[r_xc[2], r_xc[3]], W=[rX[7]])
            yield
            K.op("dve", lambda: nc.vector.tensor_tensor(out=Gm[:], in0=X[7][:, 0:G * 128].rearrange("p (g l) -> p g l", l=128),
                                                        in1=C["tri"].unsqueeze(1).to_broadcast([128, G, 128]), op=ALU.mult),
                 R=[rX[7], C["r"]], W=[r_Gm])
            yield
            for i in range(G):
                K.op("dve", lambda i=i: nc.vector.tensor_tensor(out=scT[i][0][:], in0=Gm[:, i, :].unsqueeze(1).to_broadcast([128, 4, 128]),
                                                                in1=segT[i][0][:], op=ALU.mult), R=[r_Gm, segT[i][1]], W=[scT[i][1]])
            yield
            for i in range(G):
                py_ = X[i % 2][:, (i // 2 % 2) * 256:(i // 2 % 2) * 256 + 256]; rpy = rX[i % 2]
                for h in range(4):
                    K.op("pe", lambda i=i, h=h: nc.tensor.matmul(py_[:, h * 64:(h + 1) * 64], lhsT=scT[i][0][:, h, :], rhs=xdt[i][0][:, h, :],
                                                                 start=True, stop=True), R=[scT[i][1], xdt[i][1]], W=[rpy])
                yield
                K.op("dve", lambda i=i: nc.vector.tensor_tensor(out=t0[i][0][:], in0=py_, in1=fl(xD[i][0]), op=ALU.add),
                     R=[rpy, xD[i][1]], W=[t0[i][1]])
            yield
            yield

        def genB(g0, S_, O_):
            cs = [slice((g0 + i) * 128, (g0 + i + 1) * 128) for i in range(G)]
            sz, r_sz, dtx, r_dtx, ax, r_ax, ex, r_ex, lx, r_lx, dt, r_dt, aa, r_aa, acs, r_acs, nacs, r_nacs, el, r_el, cd, r_cd, dd, r_dd, dec, r_dec, dtdec, r_dtdec, rseg, segT, xdt, xdd, xD, Btm, Gm, r_Gm, scT, t0, Sbf, gg, junk, r_junk, ssq, r_ssq, nscr, yb, r_yb = [S_[n_] for n_ in ['sz', 'r_sz', 'dtx', 'r_dtx', 'ax', 'r_ax', 'ex', 'r_ex', 'lx', 'r_lx', 'dt', 'r_dt', 'aa', 'r_aa', 'acs', 'r_acs', 'nacs', 'r_nacs', 'el', 'r_el', 'cd', 'r_cd', 'dd', 'r_dd', 'dec', 'r_dec', 'dtdec', 'r_dtdec', 'rseg', 'segT', 'xdt', 'xdd', 'xD', 'Btm', 'Gm', 'r_Gm', 'scT', 't0', 'Sbf', 'gg', 'junk', 'r_junk', 'ssq', 'r_ssq', 'nscr', 'yb', 'r_yb']]
            for i in range(G):
                pst_ = X[4][:, (i % 2) * 256:(i % 2) * 256 + 256]; rpst = rX[4]
                K.op("pe", lambda i=i: nc.tensor.matmul(pst_, lhsT=Btm[i][0][:], rhs=fl(xdd[i][0]), start=True, stop=True),
                     R=[Btm[i][1], xdd[i][1]], W=[rpst])
                yield
                K.op("dve", lambda i=i: nc.vector.tensor_tensor(out=Sf[:], in0=Sf[:], in1=cd[:, i, :].unsqueeze(2).to_broadcast([128, 4, 64]),
                                                                op=ALU.mult), R=[r_Sf, r_cd], W=[r_Sf])
                yield
                K.op("dve", lambda: nc.vector.tensor_tensor(out=fl(Sf), in0=fl(Sf), in1=pst_, op=ALU.add),
                     R=[r_Sf, rpst], W=[r_Sf])
                yield
                K.op("act", lambda i=i: nc.scalar.copy(out=Sbf[i + 1][0][:], in_=fl(Sf)), R=[r_Sf], W=[Sbf[i + 1][1]])
            yield
            for i in range(G):
                pyo_ = X[6][:, (i % 2) * 256:(i % 2) * 256 + 256]; rpyo = rX[6]
                K.op("pe", lambda i=i: nc.tensor.matmul(pyo_, lhsT=xc[3][:, cs[i]], rhs=Sbf[i][0][:], start=True, stop=True),
                     R=[r_xc[3], Sbf[i][1]], W=[rpyo])
                yield
                K.op("dve", lambda i=i: nc.vector.tensor_tensor(out=gg[i][0][:].rearrange("p (a b) -> p a b", b=64),
                                                                in0=pyo_.rearrange("p (a b) -> p a b", b=64),
                                                                in1=el[:, i, :].unsqueeze(2).to_broadcast([128, 4, 64]), op=ALU.mult),
                     R=[rpyo, r_el], W=[gg[i][1]])
                yield
                K.op("dve", lambda i=i: nc.vector.tensor_tensor(out=gg[i][0][:], in0=gg[i][0][:], in1=t0[i][0][:], op=ALU.add),
                     R=[gg[i][1], t0[i][1]], W=[gg[i][1]])
                yield
                K.op("dve", lambda i=i: nc.vector.tensor_tensor(out=gg[i][0][:], in0=gg[i][0][:], in1=sz[:, i, :], op=ALU.mult),
                     R=[gg[i][1], r_sz], W=[gg[i][1]])
                yield
                K.op("act", lambda i=i: nc.scalar.activation(out=junk[:], in_=gg[i][0][:], func=AF.Square, accum_out=ssq[:, i:i + 1]),
                     R=[gg[i][1]], W=[r_junk, r_ssq])
            yield
            rstd_from_ssq(K, ssq[:], r_ssq, G, nscr, 1.0 / 256)
            yield
            for i in range(G):
                K.op("dve", lambda i=i: nc.vector.scalar_tensor_tensor(out=yb[i][:], in0=gg[i][0][:], scalar=nscr["rstd"][:, i:i + 1],
                                                                       in1=snw[:], op0=ALU.mult, op1=ALU.mult),
                     R=[gg[i][1], nscr["r_rstd"], r_c], W=[r_yb[i]])
                K.dma("sp", y_d[cs[i], 128:384], yb[i][:], ds_y[i % 2], R=[r_yb[i]])
            yield
            K.op("act", lambda: nc.scalar.copy(out=O_["Sbf"][0][0][:], in_=Sbf[G][0][:]), R=[Sbf[G][1]], W=[O_["Sbf"][0][1]])
            yield

            yield

        groups = list(range(0, nT, G))
        run_gen(genA(groups[0], sets[0]))
        for gi, g0 in enumerate(groups):
            gB = genB(g0, sets[gi % 2], sets[1 - gi % 2])
            gA = genA(groups[gi + 1], sets[1 - gi % 2]) if gi + 1 < len(groups) else None
            while gA is not None or gB is not None:
                for _ in range(AW):
                    if gA is not None:
                        try:
                            next(gA)
                        except StopIteration:
                            gA = None
                if gB is not None:
                    try:
                        next(gB)
                    except StopIteration:
                        gB = None
        K.barrier()
        K.end_phase()
        K.stack = outer


def emit_mlstm2(K, C, hnT, r_hnT, P, y_d, S, W=None):
    nc = K.nc
    nB = S // 512
    outer = K.stack
    with ExitStack() as st:
        K.stack = st
        K.begin_phase()
        wfm, r_wfm = W["wM_fm"] if W is not None else load_w(K, P["wM_fm"], 256, "wMf")
        wg, r_wg = W["wM_g"] if W is not None else load_w(K, P["wM_g"], 4, "wMg")
        wtm, r_wtm = W["wM_tm"] if W is not None else load_w(K, P["wM_tm"], 384, "wMt")
        dsm = K.new_dma_sem()
        r_c = Reg("mconst")
        gbias = K.sb([2, 2], F32, "gbias"); mnw = K.sb([128, 128], F32, "mnw")
        K.dma("act", gbias[:], P["gbias"], dsm, W=[r_c])
        K.dma("act", mnw[:], P["mnw"], dsm, W=[r_c])

        def rh(tok0):
            return r_hnT[tok0 // 1024] if isinstance(r_hnT, (list, tuple)) else r_hnT
        ident_f = C["ident_f"]
        Y = [psbank(K, f"my{i}") for i in range(7)]
        rY = [Reg(f"my{i}", excl=True) for i in range(7)]
        pq, pk, pgi, pgf = Y[0], Y[1], Y[2][0:2, :], Y[3][0:2, :]
        ptl = Y[4][:, 0:32].rearrange("p (q i h) -> p q i h", q=4, i=4)
        pdec = Y[4][:, 32:40]
        pC = Y[4][:, 64:129]
        pD = [Y[2].rearrange("p (i t) -> p i t", t=128), Y[3].rearrange("p (i t) -> p i t", t=128)]
        pS = [Y[0].rearrange("p (i t) -> p i t", t=128), Y[1].rearrange("p (i t) -> p i t", t=128)]
        pQ = [Y[0][:, 0:260].rearrange("p (i v) -> p i v", v=65), Y[1][:, 0:260].rearrange("p (i v) -> p i v", v=65)]
        pN = [Y[5][:, 0:260].rearrange("p (i v) -> p i v", v=65), Y[6][:, 0:260].rearrange("p (i v) -> p i v", v=65)]
        ptm = [Y[5][:, 0:384], Y[6][:, 0:384]]

        def T(shape, dt, name):
            return K.sb(shape, dt, name), Reg(name)
        qTb, r_qTb = T([128, 512], BF16, "mqTb")
        kTb, r_kTb = T([128, 512], BF16, "mkTb")
        rows = {}
        for n_ in ["ipre", "yv", "e", "b", "al", "cma", "mu", "nmu", "wrow", "inter", "en", "tmp"]:
            rows[n_] = T([2, 512], F32, "mr_" + n_)
        rows["nab"] = rows["e"]; rows["l"] = rows["e"]
        rows["logf"] = rows["yv"]
        mnew, r_mnew = T([2, 8], F32, "mnew")
        mprev, r_mprev = T([2, 8], F32, "mprev")
        mcar, r_mcar = T([2, 1], F32, "mcar")
        decay, r_decay = T([2, 8], F32, "mdecay")
        tl, r_tl = T([128, 4, 4, 2], F32, "mtl")
        decr, r_decr = T([128, 8], F32, "mdecr")
        ktm, r_ktm = T([128, 4, 128], F32, "mktm")
        vaug, r_vaug = T([128, 4, 2, 65], BF16, "mvaug")
        og, r_og = T([128, 4, 128], F32, "mog")
        dT, r_dT = T([128, 2, 4, 128], F32, "mdT")
        sdT, r_sdT = T([128, 2, 4, 128], BF16, "msdT")
        nmv, r_nmv = T([128, 2, 4, 65], F32, "mnmv")
        kw, r_kw = T([128, 2, 4, 64], BF16, "mkw")
        Cst, r_Cst = T([128, 65], F32, "mCst")
        Cbf, r_Cbf = T([128, 9, 65], BF16, "mCbf")
        r_Cb = [Reg(f"Cb{i}") for i in range(9)]
        tq, r_tq = T([128, 2, 4, 65], F32, "mtq")
        dn, r_dn = T([128, 2, 4], F32, "mdn")
        rn, r_rn = T([128, 2, 4], F32, "mrn")
        hm, r_hm = T([128, 2, 4, 64], F32, "mhm")
        sqv, r_sqv = T([128, 2, 4, 64], F32, "msqv")
        ssq, r_ssq = T([128, 8], F32, "mssq")
        nscr = mk_scr(K, [128, 8], "mn")
        hn2, r_hn2 = T([128, 2, 4, 64], F32, "mhn2")
        yb = [K.sb([128, 4, 128], BF16, f"myb{i}") for i in range(2)]; r_yb = [Reg() for _ in range(2)]
        ds_y = [K.new_dma_sem() for _ in range(2)]
        K.op("dve", lambda: nc.vector.memset(Cst[:], 0.0), W=[r_Cst])
        K.op("dve", lambda: nc.vector.memset(Cbf[:].rearrange("p a b -> p (a b)"), 0.0), W=r_Cb)
        K.op("dve", lambda: nc.vector.memset(mcar[:], 0.0), W=[r_mcar])
        K.op("dve", lambda: nc.vector.memset(vaug[:].rearrange("p a b c -> p (a b c)"), 1.0), W=[r_vaug])

        def R_(n_):
            return rows[n_][0]

        def rr(n_):
            return rows[n_][1]
        rowc = C["rowc"]
        for b in range(nB):
            bs = slice(b * 512, (b + 1) * 512)
            for (pp_, rp_, c0, dst, rdst, sc) in [(pq, rY[0], 0, qTb, r_qTb, 1.0), (pk, rY[1], 128, kTb, r_kTb, 0.125)]:
                for k in range(8):
                    K.op("pe", lambda k=k: nc.tensor.matmul(pp_, lhsT=wfm[:, k, c0:c0 + 128], rhs=hnT[:, k, bs],
                                                            start=(k == 0), stop=(k == 7)), R=[r_wfm, rh(b * 512)], W=[rp_])
                K.op("act", lambda: nc.scalar.mul(out=dst[:], in_=pp_, mul=sc), R=[rp_], W=[rdst])
            for (pp_, rp_, c0) in [(pgi, rY[2], 0), (pgf, rY[3], 2)]:
                for k in range(8):
                    K.op("pe", lambda k=k: nc.tensor.matmul(pp_, lhsT=wg[:, k, c0:c0 + 2], rhs=hnT[:, k, bs],
                                                            start=(k == 0), stop=(k == 7)), R=[r_wg, rh(b * 512)], W=[rp_])
            K.op("dve", lambda: nc.vector.tensor_scalar(out=R_("ipre")[:], in0=pgi, scalar1=gbias[:, 0:1], scalar2=None,
                                                        op0=ALU.add), R=[rY[2], r_c], W=[rr("ipre")])
            K.op("dve", lambda: nc.vector.tensor_scalar(out=R_("yv")[:], in0=pgf, scalar1=gbias[:, 1:2], scalar2=-1.0,
                                                        op0=ALU.add, op1=ALU.mult), R=[rY[3], r_c], W=[rr("yv")])
            K.op("dve", lambda: nc.vector.scalar_tensor_tensor(out=R_("nab")[:], in0=R_("yv")[:], scalar=-1.0, in1=R_("yv")[:],
                                                               op0=ALU.mult, op1=ALU.min), R=[rr("yv")], W=[rr("nab")])
            K.op("act", lambda: nc.scalar.activation(out=R_("e")[:], in_=R_("nab")[:], func=AF.Exp), R=[rr("nab")], W=[rr("e")])
            K.op("dve", lambda: nc.vector.tensor_scalar(out=R_("e")[:], in0=R_("e")[:], scalar1=1.0, scalar2=None, op0=ALU.add),
                 R=[rr("e")], W=[rr("e")])
            K.op("act", lambda: nc.scalar.activation(out=R_("l")[:], in_=R_("e")[:], func=AF.Ln), R=[rr("e")], W=[rr("l")])
            K.op("dve", lambda: nc.vector.scalar_tensor_tensor(out=R_("logf")[:], in0=R_("yv")[:], scalar=0.0, in1=R_("l")[:],
                                                               op0=ALU.max, op1=ALU.add), R=[rr("yv"), rr("l")], W=[rr("logf")])
            K.op("dve", lambda: nc.vector.tensor_scalar(out=R_("logf")[:], in0=R_("logf")[:], scalar1=-1.0, scalar2=None,
                                                        op0=ALU.mult), R=[rr("logf")], W=[rr("logf")])
            K.op("dve", lambda: nc.vector.tensor_tensor_scan(out=R_("b")[:], data0=rowc[:, 0, :], data1=R_("logf")[:],
                                                             initial=0.0, op0=ALU.mult, op1=ALU.add),
                 R=[rr("logf"), C["r"]], W=[rr("b")])
            K.op("dve", lambda: nc.vector.tensor_tensor(out=R_("al")[:], in0=R_("ipre")[:], in1=R_("b")[:], op=ALU.subtract),
                 R=[rr("ipre"), rr("b")], W=[rr("al")])
            K.op("dve", lambda: nc.vector.tensor_tensor_scan(out=R_("cma")[:], data0=rowc[:, 1, :], data1=R_("al")[:],
                                                             initial=0.0, op0=ALU.add, op1=ALU.max),
                 R=[rr("al"), C["r"]], W=[rr("cma")])
            cma3 = R_("cma")[:].rearrange("p (c l) -> p c l", l=64)
            b3 = R_("b")[:].rearrange("p (c l) -> p c l", l=64)
            al3 = R_("al")[:].rearrange("p (c l) -> p c l", l=64)
            mu3 = R_("mu")[:].rearrange("p (c l) -> p c l", l=64)
            tmp3 = R_("tmp")[:].rearrange("p (c l) -> p c l", l=64)
            K.op("dve", lambda: nc.vector.tensor_tensor_scan(out=mnew[:], data0=cma3[:, :, 63], data1=b3[:, :, 63],
                                                             initial=mcar[:, 0:1], op0=ALU.max, op1=ALU.add),
                 R=[rr("cma"), rr("b"), r_mcar], W=[r_mnew])
            K.op("dve", lambda: nc.vector.tensor_copy(out=mprev[:, 0:1], in_=mcar[:]), R=[r_mcar], W=[r_mprev])
            K.op("dve", lambda: nc.vector.tensor_copy(out=mprev[:, 1:8], in_=mnew[:, 0:7]), R=[r_mnew], W=[r_mprev])
            K.op("dve", lambda: nc.vector.tensor_copy(out=mcar[:], in_=mnew[:, 7:8]), R=[r_mnew, r_mprev], W=[r_mcar])
            mpb = mprev[:].unsqueeze(2).to_broadcast([2, 8, 64])
            K.op("dve", lambda: nc.vector.tensor_tensor(out=mu3, in0=cma3, in1=mpb, op=ALU.max),
                 R=[rr("cma"), r_mprev], W=[rr("mu")])
            K.op("dve", lambda: nc.vector.tensor_scalar(out=R_("nmu")[:], in0=R_("mu")[:], scalar1=-1.0, scalar2=None,
                                                        op0=ALU.mult), R=[rr("mu")], W=[rr("nmu")])
            mcb = mu3[:, :, 63].unsqueeze(2).to_broadcast([2, 8, 64])
            K.op("dve", lambda: nc.vector.tensor_tensor(out=tmp3, in0=al3, in1=mcb, op=ALU.subtract),
                 R=[rr("al"), rr("mu")], W=[rr("tmp")])
            K.op("act", lambda: nc.scalar.activation(out=R_("wrow")[:], in_=R_("tmp")[:], func=AF.Exp), R=[rr("tmp")], W=[rr("wrow")])
            K.op("dve", lambda: nc.vector.tensor_tensor(out=decay[:], in0=mprev[:], in1=mu3[:, :, 63], op=ALU.subtract),
                 R=[r_mprev, rr("mu")], W=[r_decay])
            K.op("act", lambda: nc.scalar.activation(out=decay[:], in_=decay[:], func=AF.Exp), R=[r_decay], W=[r_decay])
            K.op("dve", lambda: nc.vector.tensor_tensor(out=tmp3, in0=mu3, in1=mpb, op=ALU.subtract),
                 R=[rr("mu"), r_mprev, rr("wrow")], W=[rr("tmp")])
            K.op("act", lambda: nc.scalar.activation(out=R_("inter")[:], in_=R_("tmp")[:], func=AF.Exp, scale=-1.0),
                 R=[rr("tmp")], W=[rr("inter")])
            K.op("dve", lambda: nc.vector.tensor_tensor(out=R_("tmp")[:], in0=R_("b")[:], in1=R_("mu")[:], op=ALU.add),
                 R=[rr("b"), rr("mu"), rr("inter")], W=[rr("tmp")])
            K.op("act", lambda: nc.scalar.activation(out=R_("en")[:], in_=R_("tmp")[:], func=AF.Exp, scale=-1.0),
                 R=[rr("tmp")], W=[rr("en")])
            for qi, qn_ in enumerate(["al", "wrow", "inter", "en"]):
                for i in range(4):
                    K.op("pe", lambda qi=qi, i=i, qn_=qn_: nc.tensor.transpose(
                        out=ptl[:, qi, i, :], in_=R_(qn_)[0:2, i * 128:(i + 1) * 128], identity=ident_f[0:2, 0:2]),
                        R=[rr(qn_), C["r"]], W=[rY[4]])
            K.op("pe", lambda: nc.tensor.matmul(pdec, lhsT=C["hsel"][:], rhs=decay[:], start=True, stop=True),
                 R=[r_decay, C["r"]], W=[rY[4]])
            K.op("dve", lambda: nc.vector.tensor_copy(out=tl[:], in_=ptl), R=[rY[4]], W=[r_tl])
            K.op("dve", lambda: nc.vector.tensor_copy(out=decr[:], in_=pdec), R=[rY[4]], W=[r_decr])
            for i in range(4):
                ts = slice((b * 4 + i) * 128, (b * 4 + i + 1) * 128)
                pt_ = ptm[i % 2]; rpt = rY[5 + i % 2]
                for k in range(8):
                    K.op("pe", lambda k=k: nc.tensor.matmul(pt_, lhsT=hnT[:, k, ts], rhs=wtm[:, k, :],
                                                            start=(k == 0), stop=(k == 7)), R=[rh(b * 512), r_wtm], W=[rpt])
                K.op("act", lambda i=i: nc.scalar.mul(out=ktm[:, i, :], in_=pt_[:, 0:128], mul=0.125), R=[rpt], W=[r_ktm])
                K.op("dve", lambda i=i: nc.vector.tensor_copy(out=vaug[:, i, :, 0:64],
                                                              in_=pt_[:, 128:256].rearrange("p (h d) -> p h d", d=64)),
                     R=[rpt], W=[r_vaug])
                K.op("act", lambda i=i: nc.scalar.activation(out=og[:, i, :], in_=pt_[:, 256:384], func=AF.Sigmoid),
                     R=[rpt], W=[r_og])
            for h in range(2):
                K.op("pe", lambda h=h: nc.tensor.matmul(pD[h].rearrange("p i t -> p (i t)"), lhsT=C["sel"][:, h, :],
                                                        rhs=R_("nmu")[:], start=True, stop=False),
                     R=[rr("nmu"), C["r"]], W=[rY[2 + h]])
                K.op("pe", lambda h=h: nc.tensor.matmul(pD[h].rearrange("p i t -> p (i t)"), lhsT=ident_f[:],
                                                        rhs=C["nm2"], start=False, stop=True),
                     R=[C["r"]], W=[rY[2 + h]])
            for h in range(2):
                for i in range(4):
                    K.op("act", lambda h=h, i=i: nc.scalar.activation(out=dT[:, h, i, :], in_=pD[h][:, i, :], func=AF.Exp,
                                                                      bias=tl[:, 0, i, h:h + 1]),
                         R=[rY[2 + h], r_tl], W=[r_dT])
            for h in range(2):
                hp = slice(64 * h, 64 * h + 64)
                for i in range(4):
                    tb = slice(i * 128, (i + 1) * 128)
                    K.op("pe", lambda h=h, i=i: nc.tensor.matmul(pS[h][:, i, :], lhsT=kTb[hp, tb], rhs=qTb[hp, tb],
                                                                 start=True, stop=True), R=[r_kTb, r_qTb], W=[rY[h]])
                K.op("dve", lambda h=h: nc.vector.tensor_tensor(out=sdT[:, h, :, :], in0=pS[h], in1=dT[:, h, :, :], op=ALU.mult),
                     R=[rY[h], r_dT], W=[r_sdT])
            for h in range(2):
                for i in range(4):
                    K.op("pe", lambda h=h, i=i: nc.tensor.matmul(pN[h][:, i, :], lhsT=sdT[:, h, i, :], rhs=vaug[:, i, h, :],
                                                                 start=True, stop=True), R=[r_sdT, r_vaug], W=[rY[5 + h]])
                K.op("dve", lambda h=h: nc.vector.tensor_copy(out=nmv[:, h, :, :], in_=pN[h]), R=[rY[5 + h]], W=[r_nmv])
            for h in range(2):
                hc = slice(64 * h, 64 * h + 64)
                K.op("dve", lambda h=h: nc.vector.tensor_tensor(out=kw[:, h, :, :], in0=ktm[:, :, hc],
                                                                in1=tl[:, 1, :, h].unsqueeze(2).to_broadcast([128, 4, 64]),
                                                                op=ALU.mult), R=[r_ktm, r_tl], W=[r_kw])
            for ce in range(8):
                i, half = ce // 2, ce % 2
                rs_ = slice(64 * half, 64 * half + 64)
                for h in range(2):
                    hp = slice(64 * h, 64 * h + 64)
                    K.op("pe", lambda h=h: nc.tensor.matmul(pC[hp, :], lhsT=kw[rs_, h, i, :], rhs=vaug[rs_, i, h, :],
                                                            start=True, stop=True), R=[r_kw, r_vaug], W=[rY[4]])
                K.op("dve", lambda: nc.vector.scalar_tensor_tensor(out=Cst[:], in0=Cst[:], scalar=decr[:, ce:ce + 1], in1=pC,
                                                                   op0=ALU.mult, op1=ALU.add), R=[r_Cst, r_decr, rY[4]], W=[r_Cst])
                K.op("act", lambda: nc.scalar.copy(out=Cbf[:, ce + 1, :], in_=Cst[:]), R=[r_Cst], W=[r_Cb[ce + 1]])
            for h in range(2):
                hp = slice(64 * h, 64 * h + 64)
                for ce in range(8):
                    i, half = ce // 2, ce % 2
                    rs_ = slice(64 * half, 64 * half + 64)
                    tc = slice(i * 128 + 64 * half, i * 128 + 64 * half + 64)
                    K.op("pe", lambda h=h: nc.tensor.matmul(pQ[h][rs_, i, :], lhsT=qTb[hp, tc], rhs=Cbf[hp, ce, :],
                                                            start=True, stop=True), R=[r_qTb, r_Cb[ce]], W=[rY[h]])
            ybt = yb[b % 2]
            for h in range(2):
                hc = slice(64 * h, 64 * h + 64)
                K.op("dve", lambda h=h: nc.vector.tensor_tensor(out=tq[:, h, :, :], in0=pQ[h],
                                                                in1=tl[:, 2, :, h].unsqueeze(2).to_broadcast([128, 4, 65]),
                                                                op=ALU.mult), R=[rY[h], r_tl], W=[r_tq])
                K.op("dve", lambda h=h: nc.vector.tensor_tensor(out=nmv[:, h, :, :], in0=nmv[:, h, :, :], in1=tq[:, h, :, :],
                                                                op=ALU.add), R=[r_nmv, r_tq], W=[r_nmv])
                K.op("dve", lambda h=h: nc.vector.scalar_tensor_tensor(out=dn[:, h, :], in0=nmv[:, h, :, 64], scalar=-1.0,
                                                                       in1=nmv[:, h, :, 64], op0=ALU.mult, op1=ALU.max),
                     R=[r_nmv], W=[r_dn])
                K.op("dve", lambda h=h: nc.vector.tensor_tensor(out=dn[:, h, :], in0=dn[:, h, :], in1=tl[:, 3, :, h], op=ALU.max),
                     R=[r_dn, r_tl], W=[r_dn])
                K.op("dve", lambda h=h: nc.vector.reciprocal(out=rn[:, h, :], in_=dn[:, h, :]), R=[r_dn], W=[r_rn])
                K.op("dve", lambda h=h: nc.vector.tensor_tensor(out=hm[:, h, :, :], in0=nmv[:, h, :, 0:64],
                                                                in1=rn[:, h, :].unsqueeze(2).to_broadcast([128, 4, 64]),
                                                                op=ALU.mult), R=[r_nmv, r_rn], W=[r_hm])
                K.op("dve", lambda h=h: nc.vector.tensor_tensor(out=sqv[:, h, :, :], in0=hm[:, h, :, :], in1=hm[:, h, :, :],
                                                                op=ALU.mult), R=[r_hm], W=[r_sqv])
            K.op("dve", lambda: nc.vector.tensor_reduce(out=ssq[:], in_=sqv[:].rearrange("p h i d -> p (h i) d"),
                                                        axis=AX.X, op=ALU.add), R=[r_sqv], W=[r_ssq])
            rstd_explog(K, ssq[:], r_ssq, nscr, 1.0 / 64)
            rs3 = nscr["rstd"].rearrange("p (h i) -> p h i", i=4)
            for h in range(2):
                hc = slice(64 * h, 64 * h + 64)
                K.op("dve", lambda h=h: nc.vector.tensor_tensor(out=hn2[:, h, :, :], in0=hm[:, h, :, :],
                                                                in1=rs3[:, h, :].unsqueeze(2).to_broadcast([128, 4, 64]),
                                                                op=ALU.mult), R=[r_hm, nscr["r_rstd"]], W=[r_hn2])
                K.op("dve", lambda h=h: nc.vector.tensor_tensor(out=hn2[:, h, :, :], in0=hn2[:, h, :, :],
                                                                in1=mnw[:, hc].unsqueeze(1).to_broadcast([128, 4, 64]),
                                                                op=ALU.mult), R=[r_hn2, r_c], W=[r_hn2])
                K.op("dve", lambda h=h: nc.vector.tensor_tensor(out=ybt[:, :, hc], in0=hn2[:, h, :, :], in1=og[:, :, hc],
                                                                op=ALU.mult), R=[r_hn2, r_og], W=[r_yb[b % 2]])
            K.dma("sp", y_d[b * 512:(b + 1) * 512, 0:128].rearrange("(i p) c -> p i c", p=128), ybt[:], ds_y[b % 2],
                  R=[r_yb[b % 2]])
            K.op("act", lambda: nc.scalar.copy(out=Cbf[:, 0, :], in_=Cbf[:, 8, :]), R=[r_Cb[8]], W=[r_Cb[0]])
        K.barrier()
        K.end_phase()
        K.stack = outer


NEGV = np.float32(-30000.0)
OFF = {}
_names = ["mq", "mk", "mv", "mo", "mi", "mf", "z", "xbc", "dt", "dq", "dk", "dv"]
_sizes = [256, 256, 256, 256, 4, 4, 512, 1024, 8, 256, 256, 256]
_o = 0
for n, s in zip(_names, _sizes):
    OFF[n] = _o
    _o += s


def t5_bucket_np(rel):
    n = np.maximum(rel, 0)
    nf = np.maximum(n, 1).astype(np.float32)
    large = 16 + (np.log(nf / np.float32(16)) / np.float32(math.log(128 / 16)) * np.float32(16)).astype(np.int32)
    large = np.minimum(large, 31)
    return np.where(n < 16, n, large)


def tile_w(w):
    n = w.shape[1]
    return np.ascontiguousarray(w.reshape(8, 128, n).transpose(1, 0, 2))


def rep(v, n=128):
    v = np.asarray(v, dtype=np.float32).reshape(1, -1)
    return np.ascontiguousarray(np.broadcast_to(v, (n, v.shape[1])))


def cols(w, name, a, b):
    return w[:, OFF[name] + a: OFF[name] + b]


def mixer_params(inp, l, h):
    w = np.asarray(inp["w_in"][l])
    P = {}
    P["wD"] = tile_w(np.concatenate([cols(w, "dq", 128 * h, 128 * h + 128), cols(w, "dk", 128 * h, 128 * h + 128),
                                     cols(w, "dv", 128 * h, 128 * h + 128)], axis=1))
    qn = np.asarray(inp["diff_q_norm_w"][l]).reshape(64)
    kn = np.asarray(inp["diff_k_norm_w"][l]).reshape(64)
    P["qkw"] = rep(np.concatenate([qn, qn, kn, kn]))
    P["sw"] = rep(np.asarray(inp["diff_subln_w"][l]))
    P["lamb"] = rep(np.asarray(inp["diff_lambda"][l]).reshape(-1)).reshape(128, 4, 32)
    rb = np.asarray(inp["rel_bias"])
    kl = np.arange(128)[:, None]
    c = np.arange(1024)[None, :]
    rel = c - kl - 384
    bk = t5_bucket_np(rel)
    Bt = np.zeros((128, 2, 1024), np.float32)
    for hl in range(2):
        Bt[:, hl, :] = np.where(rel >= 0, rb[bk, 2 * h + hl], NEGV)
    P["Bt"] = Bt
    P["c31"] = rep(rb[31, 2 * h: 2 * h + 2])
    xb = OFF["xbc"]
    ch = np.concatenate([np.arange(256 * h, 256 * h + 256), 512 + np.arange(128 * h, 128 * h + 128),
                         768 + np.arange(128 * h, 128 * h + 128)])
    P["wS_fm"] = tile_w(w[:, xb + ch])
    P["wS_tm"] = tile_w(np.concatenate([cols(w, "z", 256 * h, 256 * h + 256), cols(w, "dt", 4 * h, 4 * h + 4)], axis=1))
    cw = np.asarray(inp["ssm_conv_w"][l])[:, ch]
    P["cw"] = np.ascontiguousarray(cw.reshape(4, 4, 128).transpose(2, 1, 0))
    P["cb"] = np.ascontiguousarray(np.asarray(inp["ssm_conv_b"][l])[ch].reshape(4, 128).T)
    P["dtb"] = rep(np.asarray(inp["ssm_dt_bias"][l])[4 * h:4 * h + 4])
    P["alog"] = rep(np.asarray(inp["ssm_A_log"][l])[4 * h:4 * h + 4])
    P["dsk"] = rep(np.asarray(inp["ssm_D"][l])[4 * h:4 * h + 4])
    P["snw"] = rep(np.asarray(inp["ssm_norm_w"][l])[256 * h:256 * h + 256])
    P["wM_fm"] = tile_w(np.concatenate([cols(w, "mq", 128 * h, 128 * h + 128), cols(w, "mk", 128 * h, 128 * h + 128)], axis=1))
    P["wM_g"] = tile_w(np.concatenate([cols(w, "mi", 2 * h, 2 * h + 2), cols(w, "mf", 2 * h, 2 * h + 2)], axis=1))
    P["wM_tm"] = tile_w(np.concatenate([cols(w, "mk", 128 * h, 128 * h + 128), cols(w, "mv", 128 * h, 128 * h + 128),
                                        cols(w, "mo", 128 * h, 128 * h + 128)], axis=1))
    gb = np.asarray(inp["mlstm_gate_bias"][l])
    P["gbias"] = np.ascontiguousarray(gb[:, 2 * h:2 * h + 2].T)
    P["mnw"] = rep(np.asarray(inp["mlstm_norm_w"][l])[128 * h:128 * h + 128])
    return {k: np.ascontiguousarray(v, dtype=np.float32) for k, v in P.items()}


def const_arrays():
    Cn = {}
    Cn["ident_bf"] = np.eye(128, dtype=np.float32).astype(ml_dtypes.bfloat16)
    Cn["ident_f"] = np.eye(128, dtype=np.float32)
    j = np.arange(128)
    tri = (j[:, None] <= j[None, :]).astype(np.float32)
    same = (j[:, None] // 64) == (j[None, :] // 64)
    nm2 = np.where(same & (j[:, None] <= j[None, :]), 0.0, NEGV).astype(np.float32)
    sel = np.zeros((2, 2, 128), np.float32)
    sel[0, 0, :] = 1
    sel[1, 1, :] = 1
    hs = np.zeros((2, 128), np.float32)
    hs[0, :64] = 1
    hs[1, 64:] = 1
    nm1 = np.where(j[:, None] <= j[None, :], 0.0, NEGV).astype(np.float32)
    Cn["cf"] = np.ascontiguousarray(np.concatenate([np.ones((128, 128), np.float32), tri, np.tile(nm2, (1, 4)),
                                                    np.tile(nm1, (1, 4))], axis=1))
    Cn["sel"] = sel
    Cn["hsel"] = hs
    t = np.arange(512)
    rm = (t % 64 != 0).astype(np.float32)
    Cn["rowc"] = np.ascontiguousarray(np.stack([np.stack([rm, rm]), np.stack([(1 - rm) * np.float32(-1e30)] * 2)], axis=1))
    return Cn


import math as _math
from concourse.bass_utils import run_bass_kernel_spmd

NCORES = 8
SEQ = 4096
TOK = 2048
DEPTH = 2
PAIRS = [[0, 1], [2, 3], [4, 5], [6, 7]]


def _tile_gu(w):
    return np.ascontiguousarray(np.asarray(w).reshape(8, 128, 22, 128).transpose(2, 1, 0, 3).reshape(22, 128, 1024))


def _tile_nw(w):
    return np.ascontiguousarray(np.asarray(w).reshape(8, 128).T)


def ffn_host(inp, which, l, tag):
    return {
        tag + "nw": _tile_nw(inp[which + "_norm_w"][l]),
        tag + "wg": _tile_gu(inp[which + "_w_gate"][l]),
        tag + "wu": _tile_gu(inp[which + "_w_up"][l]),
        tag + "wd": np.ascontiguousarray(np.asarray(inp[which + "_w_down"][l]).reshape(22, 128, 1024)),
    }


def ffn_decl(nc, tag):
    d = {}
    d["nw"] = nc.dram_tensor(tag + "nw", [128, 8], F32, kind="ExternalInput").ap()
    d["wg"] = nc.dram_tensor(tag + "wg", [22, 128, 1024], F32, kind="ExternalInput").ap()
    d["wu"] = nc.dram_tensor(tag + "wu", [22, 128, 1024], F32, kind="ExternalInput").ap()
    d["wd"] = nc.dram_tensor(tag + "wd", [22, 128, 1024], F32, kind="ExternalInput").ap()
    return d


def wout_host(inp, l):
    perm = []
    for h in range(2):
        perm += list(range(128 * h, 128 * h + 128))
        perm += list(range(256 + 256 * h, 256 + 256 * h + 256))
        perm += list(range(768 + 128 * h, 768 + 128 * h + 128))
    w = np.asarray(inp["w_out"][l])[np.array(perm), :]
    return np.ascontiguousarray(w.reshape(8, 128, 1024))


def lam_init_of(l):
    return 0.8 - 0.6 * _math.exp(-0.3 * l)


def build_fused(Pshapes, Cn):
    nc = bass.Bass("TRN2", target_bir_lowering=False, num_devices=NCORES)
    Dc = {"ident_bf": nc.dram_tensor("ident_bf", [128, 128], BF16, kind="ExternalInput").ap(),
          "ident_f": nc.dram_tensor("ident_f", [128, 128], F32, kind="ExternalInput").ap()}
    Dk = {k: nc.dram_tensor(k, list(Cn[k].shape), F32, kind="ExternalInput").ap() for k in ["cf", "sel", "hsel", "rowc"]}
    x_d = nc.dram_tensor("x", [TOK, 1024], F32, kind="ExternalInput").ap()
    mh_d = nc.dram_tensor("mh", [128, 2], F32, kind="ExternalInput").ap()
    out_d = nc.dram_tensor("out", [TOK, 1024], F32, kind="ExternalOutput").ap()
    L = []
    for l in range(DEPTH):
        d = {"f1": ffn_decl(nc, f"a{l}_"), "f2": ffn_decl(nc, f"c{l}_")}
        d["mixnw"] = nc.dram_tensor(f"mixnw{l}", [128, 8], F32, kind="ExternalInput").ap()
        d["wo"] = nc.dram_tensor(f"wo{l}", [8, 128, 1024], F32, kind="ExternalInput").ap()
        d["P"] = {k: nc.dram_tensor(f"m{l}_{k}", list(shp), F32, kind="ExternalInput").ap() for k, shp in Pshapes.items()}
        d["P"].update(Dk)
        d["x1"] = nc.dram_tensor(f"x1_{l}", [TOK, 1024], F32, kind="Internal").ap()
        d["hn_own"] = [nc.dram_tensor(f"hn_own{l}_{i}", [1024, 1024], BF16, kind="Internal").ap() for i in range(2)]
        d["hn_all"] = [nc.dram_tensor(f"hn_all{l}_{i}", [2 * 1024, 1024], BF16, kind="Internal").ap() for i in range(2)]
        d["y_half"] = [nc.dram_tensor(f"y_half{l}_{i}", [TOK, 512], BF16, kind="Internal").ap() for i in range(2)]
        d["y_all"] = [nc.dram_tensor(f"y_all{l}_{i}", [2 * TOK, 512], BF16, kind="Internal").ap() for i in range(2)]
        d["r_hn_all"] = [Reg(f"hn_all{l}_{i}") for i in range(2)]
        d["r_y_all"] = [Reg(f"y_all{l}_{i}") for i in range(2)]
        d["x2"] = nc.dram_tensor(f"x2_{l}", [TOK, 1024], F32, kind="Internal").ap()
        d["x3"] = out_d if l == DEPTH - 1 else nc.dram_tensor(f"x3_{l}", [TOK, 1024], F32, kind="Internal").ap()
        L.append(d)
    with ExitStack() as st:
        K = KB(nc, st)
        C = load_consts(K, Dc["ident_bf"], Dc["ident_f"])
        load_mixer_consts(K, C, Dk)
        csem = K.new_dma_sem()
        xin = x_d
        for l in range(DEPTH):
            d = L[l]
            f1, f2 = d["f1"], d["f2"]
            def hn_block_done(b, reg, d=d):
                K.collective("AllGather", d["hn_own"][b], d["hn_all"][b], PAIRS, csem, R=[reg], W=[d["r_hn_all"][b]])
            emit_ffn(K, C, xin, d["x1"], f1["nw"], f1["wg"], f1["wu"], f1["wd"], TOK,
                     hn_out=[ho.rearrange("(k p) t -> k p t", p=128) for ho in d["hn_own"]], nw2_d=d["mixnw"],
                     on_block=hn_block_done)
            with ExitStack() as st2:
                outer = K.stack
                K.stack = st2
                K.begin_phase()
                Wm, stg_stack = load_w_all(K, [("wM_fm", d["P"]["wM_fm"], 256), ("wM_g", d["P"]["wM_g"], 4),
                                               ("wM_tm", d["P"]["wM_tm"], 384), ("wS_fm", d["P"]["wS_fm"], 512),
                                               ("wS_tm", d["P"]["wS_tm"], 260), ("wD", d["P"]["wD"], 384)])
                hnT, rq = load_hnT_pair(K, d["hn_all"], SEQ, regs=d["r_hn_all"])
                r_hnT = rq[-1]
                ysp = YSplit(d["y_half"][0], d["y_half"][1], TOK)

                def y_hook(t, d=d, ysp=ysp):
                    if t == SEQ // 1024 - 1:
                        K.collective("AllGather", d["y_half"][0], d["y_all"][0], PAIRS, csem, R=[ysp.regs[0]], W=[d["r_y_all"][0]])
                ysp.hook = y_hook
                emit_mlstm2(K, C, hnT, rq, d["P"], ysp, SEQ, W=Wm)
                emit_ssd2(K, C, hnT, r_hnT, d["P"], ysp, SEQ, W=Wm)
                emit_diff(K, C, hnT, r_hnT, d["P"], ysp, SEQ, lam_init_of(l), W=Wm)
                K.end_phase()
                K.stack = outer
            K.collective("AllGather", d["y_half"][1], d["y_all"][1], PAIRS, csem, W=[d["r_y_all"][1]])
            emit_outproj_sel(K, C, d["x1"], d["y_all"], mh_d, d["wo"], d["x2"], TOK, SEQ, yregs=d["r_y_all"])
            emit_ffn(K, C, d["x2"], d["x3"], f2["nw"], f2["wg"], f2["wu"], f2["wd"], TOK)
            xin = d["x3"]
        K.barrier(full=True)
    return nc


def kernel(**inputs):
    inp = {k: np.asarray(v) for k, v in inputs.items()}
    x = inp["x"]
    cores = list(range(NCORES))
    Cn = const_arrays()
    shared = {"ident_bf": Cn["ident_bf"], "ident_f": Cn["ident_f"]}
    for k in ["cf", "sel", "hsel", "rowc"]:
        shared[k] = Cn[k]
    Ps = [[mixer_params(inp, l, h) for h in range(2)] for l in range(DEPTH)]
    for l in range(DEPTH):
        shared.update(ffn_host(inp, "ffn1", l, f"a{l}_"))
        shared.update(ffn_host(inp, "ffn2", l, f"c{l}_"))
        shared[f"mixnw{l}"] = _tile_nw(inp["mix_norm_w"][l])
        shared[f"wo{l}"] = wout_host(inp, l)
    nc = build_fused({k: v.shape for k, v in Ps[0][0].items()}, Cn)
    maps = []
    for c in cores:
        b, h = c // 2, c % 2
        m = dict(shared)
        m["x"] = np.ascontiguousarray(x[b, h * TOK:(h + 1) * TOK])
        mh = np.zeros((128, 2), np.float32)
        mh[:, h] = 1.0
        m["mh"] = mh
        for l in range(DEPTH):
            for k, v in Ps[l][h].items():
                m[f"m{l}_{k}"] = v
        maps.append(m)
    res = run_bass_kernel_spmd(nc, maps, core_ids=cores).results
    out = np.zeros((4, SEQ, 1024), np.float32)
    for c in cores:
        b, h = c // 2, c % 2
        out[b, h * TOK:(h + 1) * TOK] = np.asarray(res[c]["out"])
    return out
```
